# Optimizing a Trainium2 kernel written in Bass

```python
import math
import jax, jax.numpy as jnp
from jax import lax
import numpy as np

D_MODEL = 1024
BATCH = 16
SEQ = 2048
DEPTH = 4

ROPE_THETA = 10000.0
QBLOCK = 128
NORM_EPS = 1e-6
NEG_INF = -1e30

MLA_HEADS = 8
MLA_Q_LORA = 384
MLA_KV_LORA = 256
MLA_NOPE = 64
MLA_ROPE = 32
MLA_V = 64
SWA_HEADS = 8
SWA_KV_HEADS = 2
SWA_HD = 64
SWA_WINDOW = 128
DIFF_HEADS = 4
DIFF_HD = 64
FOX_HEADS = 8
FOX_HD = 64
FORGET_BIAS_MEAN = 3.0

EVEN_WIDTH = MLA_HEADS * MLA_V + SWA_HEADS * SWA_HD
ODD_WIDTH = DIFF_HEADS * 2 * DIFF_HD + FOX_HEADS * FOX_HD
EVEN_SPLITS = [MLA_Q_LORA, MLA_KV_LORA, MLA_ROPE, SWA_HEADS * SWA_HD,
               SWA_KV_HEADS * SWA_HD, SWA_KV_HEADS * SWA_HD, EVEN_WIDTH]
ODD_SPLITS = [DIFF_HEADS * 2 * DIFF_HD, DIFF_HEADS * 2 * DIFF_HD, DIFF_HEADS * 2 * DIFF_HD,
              FOX_HEADS * FOX_HD, FOX_HEADS * FOX_HD, FOX_HEADS * FOX_HD, FOX_HEADS, ODD_WIDTH]
EVEN_IN = sum(EVEN_SPLITS)
ODD_IN = sum(ODD_SPLITS)

kernel_name = 'hybrid_mla_swa_diff_fox_block'


def rmsnorm(x, g):
    xf = x.astype(jnp.float32)
    y = xf * lax.rsqrt(jnp.mean(xf * xf, axis=-1, keepdims=True) + NORM_EPS)
    return (y * g.astype(jnp.float32)).astype(x.dtype)


def split_cols(z, sizes):
    offs = [int(o) for o in np.cumsum(sizes)[:-1]]
    return jnp.split(z, offs, axis=-1)


def rope_tables(seq, dim):
    inv = 1.0 / (ROPE_THETA ** (jnp.arange(0, dim, 2, dtype=jnp.float32) / dim))
    ang = jnp.arange(seq, dtype=jnp.float32)[:, None] * inv[None, :]
    return jnp.cos(ang), jnp.sin(ang)


def apply_rope(x, cos, sin):
    half = x.shape[-1] // 2
    x1, x2 = x[..., :half], x[..., half:]
    c = cos[None, :, None, :].astype(x.dtype)
    s = sin[None, :, None, :].astype(x.dtype)
    return jnp.concatenate([x1 * c - x2 * s, x2 * c + x1 * s], axis=-1)


def causal_block_mask(blk, seq):
    t = blk * QBLOCK + jnp.arange(QBLOCK)
    return jnp.arange(seq)[None, :] <= t[:, None]


def sweep_query_blocks(block_fn, q_arrays):
    b, s = q_arrays[0].shape[0], q_arrays[0].shape[1]
    nblk = s // QBLOCK

    def to_blocks(a):
        return jnp.moveaxis(a.reshape(b, nblk, QBLOCK, *a.shape[2:]), 1, 0)

    xs = (jnp.arange(nblk), tuple(to_blocks(a) for a in q_arrays))
    out = lax.map(lambda args: block_fn(args[0], *args[1]), xs)
    out = jnp.moveaxis(out, 0, 1)
    return out.reshape(b, s, *out.shape[3:])


def mla_attention(q_nope, q_rope, k_nope, k_rope, v):
    seq = k_nope.shape[1]
    scale = (MLA_NOPE + MLA_ROPE) ** -0.5

    def block(blk, qn, qr):
        s = jnp.einsum('bqhd,bkhd->bhqk', qn, k_nope) + jnp.einsum('bqhr,bkr->bhqk', qr, k_rope)
        s = jnp.where(causal_block_mask(blk, seq), s.astype(jnp.float32) * scale, NEG_INF)
        p = jax.nn.softmax(s, axis=-1).astype(v.dtype)
        return jnp.einsum('bhqk,bkhd->bqhd', p, v)

    return sweep_query_blocks(block, (q_nope, q_rope))


def swa_sink_attention(q, k, v, sinks):
    b, s, h, d = q.shape
    kvh = k.shape[2]
    g = h // kvh
    w = SWA_WINDOW
    n = s // w
    qb = q.reshape(b, n, w, kvh, g, d)

    def band(a):
        ap = jnp.pad(a, ((0, 0), (w, 0), (0, 0), (0, 0))).reshape(b, n + 1, w, kvh, d)
        return jnp.concatenate([ap[:, :-1], ap[:, 1:]], axis=2)

    kb, vb = band(k), band(v)
    sc = jnp.einsum('bnqkgd,bnjkd->bnkgqj', qb, kb).astype(jnp.float32) * (d ** -0.5)
    qpos = (jnp.arange(n) * w)[:, None, None] + jnp.arange(w)[None, :, None]
    kpos = (jnp.arange(n) * w - w)[:, None, None] + jnp.arange(2 * w)[None, None, :]
    mask = (kpos <= qpos) & (qpos - kpos < w) & (kpos >= 0)
    sc = jnp.where(mask[None, :, None, None], sc, NEG_INF)
    sink = jnp.broadcast_to(sinks.astype(jnp.float32).reshape(1, 1, kvh, g, 1, 1), sc.shape[:-1] + (1,))
    p = jax.nn.softmax(jnp.concatenate([sc, sink], axis=-1), axis=-1)[..., :-1].astype(v.dtype)
    o = jnp.einsum('bnkgqj,bnjkd->bnqkgd', p, vb)
    return o.reshape(b, s, h, d)


def diff_attention(q1, q2, k1, k2, v, lam):
    seq = k1.shape[1]
    scale = DIFF_HD ** -0.5

    def block(blk, a, c):
        mask = causal_block_mask(blk, seq)
        s1 = jnp.where(mask, jnp.einsum('bqhd,bkhd->bhqk', a, k1).astype(jnp.float32) * scale, NEG_INF)
        s2 = jnp.where(mask, jnp.einsum('bqhd,bkhd->bhqk', c, k2).astype(jnp.float32) * scale, NEG_INF)
        p = (jax.nn.softmax(s1, axis=-1) - lam * jax.nn.softmax(s2, axis=-1)).astype(v.dtype)
        return jnp.einsum('bhqk,bkhe->bqhe', p, v)

    return sweep_query_blocks(block, (q1, q2))


def forgetting_attention(q, k, v, fcum):
    seq = k.shape[1]
    scale = FOX_HD ** -0.5
    f_key = jnp.transpose(fcum, (0, 2, 1))

    def block(blk, qb, fq):
        s = jnp.einsum('bqhd,bkhd->bhqk', qb, k).astype(jnp.float32) * scale
        s = s + jnp.transpose(fq, (0, 2, 1))[..., None] - f_key[:, :, None, :]
        s = jnp.where(causal_block_mask(blk, seq), s, NEG_INF)
        p = jax.nn.softmax(s, axis=-1).astype(v.dtype)
        return jnp.einsum('bhqk,bkhd->bqhd', p, v)

    return sweep_query_blocks(block, (q, fcum))


def even_mixer(h, w_in, q_norm, kv_norm, w_uq, w_ukv, sinks, w_out, rope_lat, rope_head):
    b, s, _ = h.shape
    z = h @ w_in
    z_cq, z_ckv, z_kr, z_sq, z_sk, z_sv, z_gate = split_cols(z, EVEN_SPLITS)
    cos_r, sin_r = rope_lat
    cos_h, sin_h = rope_head
    q = (rmsnorm(z_cq, q_norm) @ w_uq).reshape(b, s, MLA_HEADS, MLA_NOPE + MLA_ROPE)
    q_nope = q[..., :MLA_NOPE]
    q_rope = apply_rope(q[..., MLA_NOPE:], cos_r, sin_r)
    kv = (rmsnorm(z_ckv, kv_norm) @ w_ukv).reshape(b, s, MLA_HEADS, MLA_NOPE + MLA_V)
    k_nope = kv[..., :MLA_NOPE]
    v_mla = kv[..., MLA_NOPE:]
    k_rope = apply_rope(z_kr[:, :, None, :], cos_r, sin_r)[:, :, 0, :]
    o_mla = mla_attention(q_nope, q_rope, k_nope, k_rope, v_mla).reshape(b, s, MLA_HEADS * MLA_V)
    q_s = apply_rope(z_sq.reshape(b, s, SWA_HEADS, SWA_HD), cos_h, sin_h)
    k_s = apply_rope(z_sk.reshape(b, s, SWA_KV_HEADS, SWA_HD), cos_h, sin_h)
    v_s = z_sv.reshape(b, s, SWA_KV_HEADS, SWA_HD)
    o_swa = swa_sink_attention(q_s, k_s, v_s, sinks).reshape(b, s, SWA_HEADS * SWA_HD)
    o = jnp.concatenate([o_mla, o_swa], axis=-1) * jax.nn.silu(z_gate)
    return o @ w_out


def odd_mixer(h, w_in, forget_bias, lam_p, subln, w_out, rope_head, layer):
    b, s, _ = h.shape
    z = h @ w_in
    z_dq, z_dk, z_dv, z_fq, z_fk, z_fv, z_ff, z_gate = split_cols(z, ODD_SPLITS)
    cos_h, sin_h = rope_head
    q = apply_rope(z_dq.reshape(b, s, 2 * DIFF_HEADS, DIFF_HD), cos_h, sin_h).reshape(b, s, DIFF_HEADS, 2, DIFF_HD)
    k = apply_rope(z_dk.reshape(b, s, 2 * DIFF_HEADS, DIFF_HD), cos_h, sin_h).reshape(b, s, DIFF_HEADS, 2, DIFF_HD)
    v_d = z_dv.reshape(b, s, DIFF_HEADS, 2 * DIFF_HD)
    lam_init = 0.8 - 0.6 * math.exp(-0.3 * layer)
    lp = lam_p.astype(jnp.float32)
    lam = jnp.exp(jnp.sum(lp[0] * lp[1])) - jnp.exp(jnp.sum(lp[2] * lp[3])) + lam_init
    o_d = diff_attention(q[:, :, :, 0], q[:, :, :, 1], k[:, :, :, 0], k[:, :, :, 1], v_d, lam)
    o_d = (rmsnorm(o_d, subln) * (1.0 - lam_init)).reshape(b, s, DIFF_HEADS * 2 * DIFF_HD)
    fq = z_fq.reshape(b, s, FOX_HEADS, FOX_HD)
    fk = z_fk.reshape(b, s, FOX_HEADS, FOX_HD)
    fv = z_fv.reshape(b, s, FOX_HEADS, FOX_HD)
    logf = jax.nn.log_sigmoid(z_ff.astype(jnp.float32) + forget_bias.astype(jnp.float32))
    fcum = jnp.cumsum(logf, axis=1)
    o_f = forgetting_attention(fq, fk, fv, fcum).reshape(b, s, FOX_HEADS * FOX_HD)
    o = jnp.concatenate([o_d, o_f], axis=-1) * jax.nn.silu(z_gate)
    return o @ w_out


def setup_inputs(seed: int = 0) -> dict:
    key = jax.random.key(seed)
    ks = jax.random.split(key, 20)
    n_even = (DEPTH + 1) // 2
    n_odd = DEPTH // 2

    def nrm(k, shape, std):
        return std * jax.random.normal(k, shape, jnp.float32)

    return {
        'x': nrm(ks[0], (BATCH, SEQ, D_MODEL), 1.0),
        'c': nrm(ks[1], (BATCH, D_MODEL), 1.0),
        'w_ada': nrm(ks[2], (DEPTH, D_MODEL, 3 * D_MODEL), 0.5 * D_MODEL ** -0.5),
        'b_ada': nrm(ks[3], (DEPTH, 3 * D_MODEL), 0.01),
        'g_pre': 1.0 + nrm(ks[4], (DEPTH, D_MODEL), 0.02),
        'g_post': 1.0 + nrm(ks[5], (DEPTH, D_MODEL), 0.02),
        'ev_w_in': nrm(ks[6], (n_even, D_MODEL, EVEN_IN), D_MODEL ** -0.5),
        'ev_q_norm': 1.0 + nrm(ks[7], (n_even, MLA_Q_LORA), 0.02),
        'ev_kv_norm': 1.0 + nrm(ks[8], (n_even, MLA_KV_LORA), 0.02),
        'ev_w_uq': nrm(ks[9], (n_even, MLA_Q_LORA, MLA_HEADS * (MLA_NOPE + MLA_ROPE)), MLA_Q_LORA ** -0.5),
        'ev_w_ukv': nrm(ks[10], (n_even, MLA_KV_LORA, MLA_HEADS * (MLA_NOPE + MLA_V)), MLA_KV_LORA ** -0.5),
        'ev_sinks': nrm(ks[11], (n_even, SWA_HEADS), 0.5),
        'ev_w_out': nrm(ks[12], (n_even, EVEN_WIDTH, D_MODEL), EVEN_WIDTH ** -0.5),
        'od_w_in': nrm(ks[13], (n_odd, D_MODEL, ODD_IN), D_MODEL ** -0.5),
        'od_forget_bias': FORGET_BIAS_MEAN + nrm(ks[14], (n_odd, FOX_HEADS), 0.5),
        'od_lambda': nrm(ks[15], (n_odd, 4, DIFF_HD), 0.1),
        'od_subln': 1.0 + nrm(ks[16], (n_odd, 2 * DIFF_HD), 0.02),
        'od_w_out': nrm(ks[17], (n_odd, ODD_WIDTH, D_MODEL), ODD_WIDTH ** -0.5),
    }


def reference(x, c, w_ada, b_ada, g_pre, g_post, ev_w_in, ev_q_norm, ev_kv_norm, ev_w_uq, ev_w_ukv,
              ev_sinks, ev_w_out, od_w_in, od_forget_bias, od_lambda, od_subln, od_w_out):
    seq = x.shape[1]
    rope_head = rope_tables(seq, SWA_HD)
    rope_lat = rope_tables(seq, MLA_ROPE)
    cond = jax.nn.silu(c)
    for layer in range(DEPTH):
        mod = cond @ w_ada[layer] + b_ada[layer]
        shift, scale, gate = jnp.split(mod, 3, axis=-1)
        h = rmsnorm(x, g_pre[layer]) * (1.0 + scale[:, None, :]) + shift[:, None, :]
        i = layer // 2
        if layer % 2 == 0:
            y = even_mixer(h, ev_w_in[i], ev_q_norm[i], ev_kv_norm[i], ev_w_uq[i], ev_w_ukv[i],
                           ev_sinks[i], ev_w_out[i], rope_lat, rope_head)
        else:
            y = odd_mixer(h, od_w_in[i], od_forget_bias[i], od_lambda[i], od_subln[i], od_w_out[i],
                          rope_head, layer)
        x = x + gate[:, None, :] * rmsnorm(y, g_post[layer])
    return x
```

```python
from contextlib import ExitStack
import math
import numpy as np
import concourse.bass as bass
import concourse.mybir as mybir
from concourse.bass_utils import run_bass_kernel_spmd

F32 = mybir.dt.float32
BF16 = mybir.dt.bfloat16
AF = mybir.ActivationFunctionType
ALU = mybir.AluOpType

S = 2048
TT = 512
NTL = S // TT
NB = S // 128
EPS = 1e-6
NC_CST = 320
NE = 2528
NO = 4112
NEG = -30000.0
STOP = 9
ONE_TILE = False
DEBUG = False
ASTOP = 99
SKIP_XPRO = False


class Buf:
    __slots__ = ("name", "writers", "readers", "excl")

    def __init__(self, name, excl=False):
        self.name = name
        self.excl = excl
        self.writers = {}
        self.readers = {}


class Tens:
    def __init__(self, t, name, parts, excl=False):
        self.t = t
        self.name = name
        self.p = [Buf(f"{name}.{i}", excl) for i in range(parts)]


ENGS = ("pe", "act", "dve", "pool", "sp")


class Prog:
    def __init__(self, nc, same_engine_sync=True):
        self.nc = nc
        self.es = ExitStack()
        self.ops = {e: [] for e in ENGS}
        self.cnt = {e: 0 for e in ENGS}
        self.known = {e: {} for e in ENGS}
        self.snap = {}
        self.chan_cnt = {}
        self.sem = {}
        self.same_engine_sync = same_engine_sync

    def sbuf(self, name, shape, dtype, parts=1):
        t = self.es.enter_context(self.nc.sbuf_tensor("sb_" + name, list(shape), dtype))
        return Tens(t, name, parts)

    def psum(self, name, shape, dtype, parts=1):
        t = self.es.enter_context(self.nc.psum_tensor("ps_" + name, list(shape), dtype))
        return Tens(t, name, parts, excl=True)

    def dram(self, name, shape, dtype, kind="Internal", parts=1):
        t = self.nc.dram_tensor(name, list(shape), dtype, kind=kind).ap()
        return Tens(t, name, parts)

    @staticmethod
    def _flat(xs):
        out = []
        for x in xs:
            if isinstance(x, Tens):
                out.extend(x.p)
            elif isinstance(x, (list, tuple)):
                out.extend(Prog._flat(x))
            else:
                out.append(x)
        return out

    def _deps(self, eng, reads, writes, skip=None):
        deps = {}

        def add(d):
            for k, v in d.items():
                if deps.get(k, 0) < v:
                    deps[k] = v

        for b in reads:
            add(b.writers)
        for b in writes:
            add(b.writers)
            add(b.readers)
        waits = []
        kn = self.known[eng]
        for k, v in deps.items():
            if k == skip:
                continue
            if k == eng and (eng == "pe" or not self.same_engine_sync):
                continue
            if kn.get(k, 0) >= v:
                continue
            waits.append((k, v))
        for k, v in waits:
            kn[k] = max(kn.get(k, 0), v)
            sn = self.snap.get((k, v))
            if sn:
                for k2, v2 in sn.items():
                    if kn.get(k2, 0) < v2:
                        kn[k2] = v2
        return waits

    def _mark(self, tok, reads, writes, partial):
        k, v = tok
        for b in reads:
            if b.readers.get(k, 0) < v:
                b.readers[k] = v
        for b in writes:
            if partial:
                b.writers[k] = v
            else:
                b.writers = {k: v}
                b.readers = {}

    def op(self, eng, fn, reads=(), writes=()):
        reads = self._flat(reads)
        writes = self._flat(writes)
        writes = writes + [b for b in reads if b.excl and b not in writes]
        reads = [b for b in reads if not b.excl]
        waits = self._deps(eng, reads, writes)
        self.cnt[eng] += 1
        tok = (eng, self.cnt[eng])
        sn = dict(self.known[eng])
        sn[eng] = self.cnt[eng]
        self.snap[tok] = sn
        self._mark(tok, reads, writes, False)
        self.ops[eng].append(("op", waits, fn, tok))
        return tok

    def dma(self, eng, out, in_, sb, reads=(), writes=(), partial=False, **kw):
        reads = self._flat(reads)
        writes = self._flat(writes)
        sbb = self._flat([sb])[0]
        chan = "d_" + sbb.name
        waits = self._deps(eng, reads, writes, skip=(chan if partial else None))
        self.chan_cnt[chan] = self.chan_cnt.get(chan, 0) + 16
        tok = (chan, self.chan_cnt[chan])
        self.snap[tok] = dict(self.known[eng])
        self._mark(tok, reads, writes, partial)
        self.ops[eng].append(("dma", waits, (out, in_, kw), tok))
        return tok

    def emit(self):
        nc = self.nc
        for k in list(ENGS) + list(self.chan_cnt):
            if k not in self.sem:
                self.sem[k] = self.es.enter_context(nc.semaphore("s_" + k.replace(".", "_")))
        final = list(self.chan_cnt.items())

        def run(ename, eng):
            for kind, waits, payload, tok in self.ops[ename]:
                for k, v in waits:
                    eng.wait_ge(self.sem[k], v)
                if kind == "op":
                    payload(eng).then_inc(self.sem[tok[0]], 1)
                else:
                    out, in_, kw = payload
                    eng.dma_start(out=out, in_=in_, **kw).then_inc(self.sem[tok[0]], 16)
            if ename == "sp":
                for c, v in final:
                    eng.wait_ge(self.sem[c], v)

        with nc.Block() as block:
            @block.sync
            def _(e):
                run("sp", e)

            @block.scalar
            def _(e):
                run("act", e)

            @block.vector
            def _(e):
                run("dve", e)

            @block.gpsimd
            def _(e):
                run("pool", e)

            @block.tensor
            def _(e):
                run("pe", e)
        self.es.close()


class Rot:
    def __init__(self, items):
        self.items = items
        self.i = 0

    def next(self):
        x = self.items[self.i % len(self.items)]
        self.i += 1
        return x


def build(NL=4):
    nc = bass.Bass("TRN2", target_bir_lowering=False)
    P = Prog(nc)

    def din(name, shape, dt=F32):
        return nc.dram_tensor(name, list(shape), dt, kind="ExternalInput").ap()

    x_d = din("x", [2, S, 1024])
    cT_d = din("cT", [128, 8, 2])
    wada_d = din("wada", [4, 128, 8, 3072])
    cst_d = din("cst", [128, 4, NC_CST])
    cmf_d = din("cmf", [128, 384])
    cmb_d = din("cmb", [128, 896])
    tab_d = din("tab", [128, 4, S])
    wE_d = din("wE", [2, 128, 8, NE])
    wUQ_d = din("wUQ", [2, 128, 3, 800])
    wUK_d = din("wUK", [2, 128, 2, 512])
    wUV_d = din("wUV", [2, 128, 2, 512])
    wOe_d = din("wOe", [2, 128, 8, 1024])
    wO_d = din("wO", [2, 128, 8, NO])
    wOo_d = din("wOo", [2, 128, 8, 1024])
    out_d = nc.dram_tensor("out", [2, S, 1024], F32, kind="ExternalOutput").ap()

    def scr(name, shape, dt, parts=2 * NTL):
        return P.dram(name, shape, dt, kind=("ExternalOutput" if DEBUG else "Internal"), parts=parts)

    xT = scr("xT", [2, 8, 128, S], F32)
    QA = scr("QA", [2, 8, 128, S], BF16)
    KA = scr("KA", [2, 8, 128, S], BF16)
    QB = scr("QB", [2, 4, 128, S], BF16)
    KB = scr("KB", [2, 4, 128, S], BF16)
    VA = scr("VA", [2, NB, 128, 512], BF16)
    VB = scr("VB", [2, NB, 128, 512], BF16)
    GT = scr("GT", [2, 8, 128, S], BF16)
    OT = scr("OT", [2, 8, 128, S], BF16)
    CSP = scr("CSP", [2, 128, NB, 8], F32, parts=2)

    def pt(s, t):
        return s * NTL + t

    def allp(T, s):
        return [T.p[pt(s, t)] for t in range(NTL)]

    WIN = P.sbuf("WIN", [128, 8, NO], BF16)
    WOUT = P.sbuf("WOUT", [128, 8, 1024], BF16)
    WUQ = P.sbuf("WUQ", [128, 3, 800], BF16)
    WUK = P.sbuf("WUK", [128, 2, 512], BF16)
    WUV = P.sbuf("WUV", [128, 2, 512], BF16)
    CST = P.sbuf("CST", [128, 4, NC_CST], F32)
    AD = P.sbuf("AD", [128, 4, 2, 3, 8], F32)
    CMF = P.sbuf("CMF", [128, 384], F32)
    CMB = P.sbuf("CMB", [128, 896], BF16)
    TAB = P.sbuf("TAB", [128, 4, TT], F32)
    XT = P.sbuf("XT", [128, 8, TT], F32)
    HT = P.sbuf("HT", [128, 8, TT], BF16)
    SQ = Rot([P.sbuf(f"SQ{i}", [128, TT], BF16) for i in range(2)])
    FT = Rot([P.sbuf(f"FT{i}", [128, TT], F32) for i in range(4)])
    RSB = Rot([P.sbuf(f"RSB{i}", [128, TT], F32) for i in range(2)])
    STG = Rot([P.sbuf(f"STG{i}", [128, 4096], BF16) for i in range(2)])
    CQC = P.sbuf("CQC", [128, 5, TT], F32)
    CQN = P.sbuf("CQN", [128, 5, TT], BF16)
    XB = Rot([P.sbuf(f"XB{i}", [128, TT], BF16) for i in range(2)])
    Y = P.sbuf("Y", [128, 8, TT], F32, parts=8)
    YB = Y.t.bitcast(BF16)
    QTB = Rot([P.sbuf(f"QTB{i}", [128, TT], BF16) for i in range(4)])
    PT = Rot([P.sbuf(f"PT{i}", [128, TT], BF16) for i in range(4)])
    CSPB = P.sbuf("CSPB", [128, NB, 8], F32)
    SPT = P.sbuf("SPT", [128, NB, 8], F32)
    PREF = P.sbuf("PREF", [128, NB, 8], F32)
    FQ8 = P.sbuf("FQ8", [128, S], BF16)
    SM = P.sbuf("SM", [128, 64], F32)
    CTB = P.sbuf("CTB", [128, 8, 2], BF16)
    CTF = P.sbuf("CTF", [128, 8, 2], F32)
    MODF = P.sbuf("MODF", [128, 4, 24, 2], F32)

    PB = [P.psum(f"B{i}", [128, 512], F32) for i in range(8)]

    ident_f = CMF.t[:, 0:128]
    tri_f = CMF.t[:, 128:256]
    ones_f = CMF.t[:, 256:384]
    ident_b = CMB.t[:, 0:128]
    perm64 = CMB.t[:, 128:256]
    perm32 = CMB.t[:, 256:384]
    mc_b = CMB.t[:, 384:512]
    msw_b = CMB.t[:, 512:768]
    ones_b = CMB.t[:, 768:896]

    YROW = 8192

    def kt_ap(k, rows=128, p0=0, c0=0, n=2048):
        return bass.AP(YB, p0 * YROW + k * 2048 + c0, [[YROW, rows], [1, n]])

    def kt_bufs(k):
        return [Y.p[2 * k], Y.p[2 * k + 1]]

    def va_blk(v, j):
        return bass.AP(YB, 4096 + v * 2048 + j * 128, [[YROW, 128], [1, 128]])

    def va_bufs(v):
        return [Y.p[4 + 2 * v], Y.p[5 + 2 * v]]

    def ACT(out, in_, func, r, w, **kw):
        P.op("act", lambda e: e.activation(out=out, in_=in_, func=func, **kw), r, w)

    def TTO(out, a, b, op, r, w, eng="dve"):
        P.op(eng, lambda e: e.tensor_tensor(out=out, in0=a, in1=b, op=op), r, w)

    def TS(out, a, s1, s2, op0, op1, r, w, eng="dve"):
        P.op(eng, lambda e: e.tensor_scalar(out=out, in0=a, scalar1=s1, scalar2=s2, op0=op0, op1=op1), r, w)

    def STT(out, a, s, b, op0, op1, r, w):
        P.op("dve", lambda e: e.scalar_tensor_tensor(out=out, in0=a, scalar=s, in1=b, op0=op0, op1=op1), r, w)

    def CPY(out, in_, r, w, eng="dve"):
        if eng == "act":
            P.op("act", lambda e: e.activation(out=out, in_=in_, func=AF.Identity), r, w)
        else:
            P.op(eng, lambda e: e.tensor_copy(out=out, in_=in_), r, w)

    def MM(out, pairs, r, w, start=True, stop=True):
        pairs = list(pairs)

        def fn(e):
            n = len(pairs)
            ins = None
            for i, (l, rr) in enumerate(pairs):
                ins = e.matmul(out, lhsT=l, rhs=rr, start=(start and i == 0), stop=(stop and i == n - 1))
            return ins
        P.op("pe", fn, r, w)

    def MMS(specs, r, w):
        specs = list(specs)

        def fn(e):
            ins = None
            for (o, l, rr, st, sp) in specs:
                ins = e.matmul(o, lhsT=l, rhs=rr, start=st, stop=sp)
            return ins
        P.op("pe", fn, r, w)

    projb = Rot(PB[0:5])

    def rstd_from(ps, n_feat):
        t1 = FT.next()
        ACT(t1.t[:], ps.t[:], AF.Ln, [ps], [t1], scale=1.0 / n_feat, bias=EPS)
        t2 = RSB.next()
        ACT(t2.t[:], t1.t[:], AF.Exp, [t1], [t2], scale=-0.5)
        return t2

    CASTKW = dict(max_dma_last_dim=4096)

    def load_weights(l):
        i = l // 2
        if l % 2 == 0:
            for c in range(8):
                P.dma("pool", WIN.t[:, c, 0:NE], wE_d[i, :, c, :], WIN, [], [WIN], partial=(c > 0), **CASTKW)
            P.dma("pool", WUQ.t[:], wUQ_d[i], WUQ, [], [WUQ], **CASTKW)
            P.dma("pool", WUK.t[:], wUK_d[i], WUK, [], [WUK], **CASTKW)
            P.dma("pool", WUV.t[:], wUV_d[i], WUV, [], [WUV], **CASTKW)
        else:
            for c in range(8):
                P.dma("pool", WIN.t[:, c, :], wO_d[i, :, c, :], WIN, [], [WIN], partial=(c > 0), **CASTKW)

    def load_wout(l):
        i = l // 2
        src = wOe_d if l % 2 == 0 else wOo_d
        for c in range(8):
            P.dma("pool", WOUT.t[:, c, :], src[i, :, c, :], WOUT, [], [WOUT], partial=(c > 0), **CASTKW)

    P.dma("sp", CST.t[:], cst_d, CST, [], [CST])
    P.dma("sp", CMF.t[:], cmf_d, CMF, [], [CMF])
    P.dma("pool", CMB.t[:], cmb_d, CMB, [], [CMB])
    P.dma("sp", CTF.t[:], cT_d, CTF, [], [CTF])
    ACT(CTB.t[:], CTF.t[:], AF.Silu, [CTF], [CTB])

    load_weights(0)

    xits = [(s, t, hb) for s in range(0 if not SKIP_XPRO else 2, 2) for t in range(NTL) for hb in range(2)]

    def x_load(it):
        s, t, hb = xits[it]
        stg = XT if it % 2 == 0 else Y
        xin = bass.AP(stg.t, 0, [[8 * TT, 128], [1024, 2], [1, 1024]])
        tok0 = t * TT + hb * 256
        src = x_d[s, tok0:tok0 + 256, :].rearrange("(b p) d -> p b d", p=128)
        P.dma("sp", xin, src, stg.p[0], [], [stg])

    if xits:
        x_load(0)
    for it, (s, t, hb) in enumerate(xits):
        if it + 1 < len(xits):
            x_load(it + 1)
        stg = XT if it % 2 == 0 else Y
        tok0 = t * TT + hb * 256
        for c in range(8):
            pb = projb.next()

            def fn(e, c=c, pb=pb, stg=stg):
                ins = None
                for b in range(2):
                    a_in = bass.AP(stg.t, b * 1024 + c * 128, [[8 * TT, 128], [1, 128]])
                    ins = e.transpose(out=pb.t[:, b * 128:(b + 1) * 128], in_=a_in, identity=ident_f)
                return ins
            P.op("pe", fn, [stg, CMF], [pb])
            ft = FT.next()
            CPY(ft.t[:, 0:256], pb.t[:, 0:256], [pb], [ft], eng=("dve" if c % 2 else "act"))
            P.dma("sp", xT.t[s, c, :, tok0:tok0 + 256], ft.t[:, 0:256], ft, [ft], [xT.p[pt(s, t)]], partial=True)

    def ada_piece_dma(l, pc, dst_ap, sbt, bufs):
        P.dma("pool", dst_ap, wada_d[l, :, :, pc * 512:(pc + 1) * 512], sbt, [], bufs, **CASTKW)

    def ada_piece_mm(l, pc, lw_fn, bufs):
        pb = projb.next()
        specs = []
        for g4 in range(4):
            for kc in range(8):
                specs.append((pb.t[:, 2 * g4:2 * g4 + 2], lw_fn(kc, g4), CTB.t[:, kc, :], kc == 0, kc == 7))
        MMS(specs, list(bufs) + [CTB], [pb])
        for g4 in range(4):
            g = pc * 4 + g4
            TS(MODF.t[:, l, g, :], pb.t[:, 2 * g4:2 * g4 + 2], CST.t[:, l, 16 + g:17 + g], None, ALU.add, ALU.bypass,
               [pb, CST], [MODF])

    def ada_finish(l):
        for b in range(2):
            a1 = AD.t[:, l, b, 0, :]
            TS(a1, MODF.t[:, l, 8:16, b], 1.0, None, ALU.add, ALU.bypass, [MODF], [AD])
            TTO(a1, a1, CST.t[:, l, 0:8], ALU.mult, [AD, CST], [AD])
            CPY(AD.t[:, l, b, 1, :], MODF.t[:, l, 0:8, b], [MODF], [AD])
            TTO(AD.t[:, l, b, 2, :], MODF.t[:, l, 16:24, b], CST.t[:, l, 8:16], ALU.mult, [MODF, CST], [AD])

    if NL > 0:
        for pc in range(6):
            wbuf = STG.next()
            ada_piece_dma(0, pc, bass.AP(wbuf.t, 0, [[4096, 128], [512, 8], [1, 512]]), wbuf, [wbuf])
            ada_piece_mm(0, pc, lambda kc, g4, wbuf=wbuf: bass.AP(wbuf.t, kc * 512 + g4 * 128, [[4096, 128], [1, 128]]), [wbuf])
        ada_finish(0)

    def yhalf_bufs(k):
        return [Y.p[4 * k + j] for j in range(4)]

    def ada_bg_start(l, ti):
        if l >= NL or ti >= 6:
            return
        k = ti % 2
        ada_piece_dma(l, ti, bass.AP(YB, k * 4096, [[YROW, 128], [512, 8], [1, 512]]), Y.p[4 * k], yhalf_bufs(k))

    def ada_bg_end(l, ti):
        if l >= NL or ti >= 6:
            return
        k = ti % 2
        ada_piece_mm(l, ti, lambda kc, g4, k=k: bass.AP(YB, k * 4096 + kc * 512 + g4 * 128, [[YROW, 128], [1, 128]]),
                     yhalf_bufs(k))
        if ti == 5:
            ada_finish(l)

    load_wout(0)

    def tsl(t):
        return slice(t * TT, (t + 1) * TT)

    def load_x_tile(s, t):
        P.dma("sp", XT.t[:], xT.t[s, :, :, tsl(t)].rearrange("c p t -> p c t"), XT, [xT.p[pt(s, t)]], [XT])

    def sumsq_bank(srcs, reads, bank):
        n = len(srcs)
        for i, a in enumerate(srcs):
            sq = SQ.next()
            ACT(sq.t[:], a, AF.Square, reads, [sq])
            MM(bank.t[:], [(ones_b, sq.t[:])], [sq, CMB], [bank], start=(i == 0), stop=(i == n - 1))

    rs_next = {}

    def norm_part(s, t):
        load_x_tile(s, t)
        nb = PB[5]
        sumsq_bank([XT.t[:, c, :] for c in range(8)], [XT], nb)
        rs = rstd_from(nb, 1024.0)
        for c in range(8):
            TTO(XT.t[:, c, :], XT.t[:, c, :], rs.t[:], ALU.mult, [XT, rs], [XT])
        rs_next[(s, t)] = rs

    def next_tile(s, t):
        if t + 1 < NTL:
            return (s, t + 1)
        if s == 0:
            return (1, 0)
        return None

    def phase_A_common(l, s, t):
        if (s, t) not in rs_next:
            norm_part(s, t)
        rs_next.pop((s, t))
        P.dma("sp", TAB.t[:], tab_d[:, :, tsl(t)], TAB, [], [TAB])
        for c in range(8):
            if c % 2 == 0:
                ACT(HT.t[:, c, :], XT.t[:, c, :], AF.Identity, [XT, AD], [HT], scale=AD.t[:, l, s, 0, c:c + 1],
                    bias=AD.t[:, l, s, 1, c:c + 1])
            else:
                TS(HT.t[:, c, :], XT.t[:, c, :], AD.t[:, l, s, 0, c:c + 1], AD.t[:, l, s, 1, c:c + 1], ALU.mult, ALU.add,
                   [XT, AD], [HT])

    def hoist_next(s, t):
        nt = next_tile(s, t)
        if nt is not None and not ONE_TILE:
            norm_part(*nt)

    def proj_fm(col0, M, reads_extra=(), kc=8, w=None, rhs=None):
        w = w or WIN
        pb = projb.next()
        pairs = []
        for c in range(kc):
            r = HT.t[:, c, :] if rhs is None else rhs(c)
            pairs.append((w.t[:, c, col0:col0 + M], r))
        MM(pb.t[0:M, :], pairs, [w, HT] + list(reads_extra), [pb])
        return pb

    def rope(pbx, rows, tabi, perm, out_ap, out_w):
        r0, r1 = rows
        kmax = 128
        xb = XB.next()
        ACT(xb.t[0:kmax, :], pbx.t[0:kmax, :], AF.Identity, [pbx], [xb])
        pb2 = projb.next()
        MM(pb2.t[0:kmax, :], [(perm[0:kmax, 0:kmax], xb.t[0:kmax, :])], [xb, CMB], [pb2])
        f1 = FT.next()
        TTO(f1.t[r0:r1, :], pbx.t[r0:r1, :], TAB.t[r0:r1, tabi, :], ALU.mult, [pbx, TAB], [f1])
        f2 = FT.next()
        TTO(f2.t[r0:r1, :], pb2.t[r0:r1, :], TAB.t[r0:r1, tabi + 1, :], ALU.mult, [pb2, TAB], [f2])
        TTO(out_ap, f1.t[r0:r1, :], f2.t[r0:r1, :], ALU.add, [f1, f2], out_w)

    def store(dst, src, sbt, dparts, partial=True):
        P.dma("pool", dst, src, sbt, [sbt], dparts, partial=partial)

    def gate_proj(l, s, t, col0):
        st = STG.next()
        for c in range(8):
            pb = proj_fm(col0 + c * 128, 128)
            ACT(st.t[:, c * TT:(c + 1) * TT], pb.t[:], AF.Silu, [pb], [st])
        store(GT.t[s, :, :, tsl(t)].rearrange("c p t -> p c t"),
              bass.AP(st.t, 0, [[4096, 128], [TT, 8], [1, TT]]), st, [GT.p[pt(s, t)]])

    def v_proj(s, t, col0, ncol, dst, lhs_src=None, w=None, kc=8):
        w = w or WIN
        st = STG.next()
        for b in range(4):
            pb = projb.next()
            pairs = []
            for c in range(kc):
                lt = HT.t[:, c, b * 128:(b + 1) * 128] if lhs_src is None else lhs_src(c, b)
                pairs.append((lt, w.t[:, c, col0:col0 + ncol]))
            MM(pb.t[:, 0:ncol], pairs, [w, HT, CQN], [pb])
            CPY(st.t[:, b * 512:b * 512 + ncol], pb.t[:, 0:ncol], [pb], [st], eng=("dve" if b % 2 else "act"))
        store(dst.t[s, 4 * t:4 * t + 4, :, 0:ncol].rearrange("b p n -> p b n"),
              bass.AP(st.t, 0, [[4096, 128], [512, 4], [1, ncol]]), st, [dst.p[pt(s, t)]])

    def phase_A_even(l, s, t):
        phase_A_common(l, s, t)
        if ASTOP < 2:
            return
        for (c0, nch, base, ncol, nfeat) in ((0, 3, 0, 40, 384.0), (3, 2, 384, 43, 256.0)):
            nb = PB[6]
            for j in range(nch):
                pb = proj_fm(base + j * 128, 128)
                CPY(CQC.t[:, c0 + j, :], pb.t[:], [pb], [CQC])
                sq = SQ.next()
                ACT(sq.t[:], pb.t[:], AF.Square, [pb], [sq])
                MM(nb.t[:], [(ones_b, sq.t[:])], [sq, CMB], [nb], start=(j == 0), stop=(j == nch - 1))
            rs = rstd_from(nb, nfeat)
            for j in range(nch):
                STT(CQN.t[:, c0 + j, :], CQC.t[:, c0 + j, :], CST.t[:, l, ncol + j:ncol + j + 1], rs.t[:],
                    ALU.mult, ALU.mult, [CQC, CST, rs], [CQN])
        st = STG.next()
        for j in range(4):
            pb = proj_fm(736 + j * 128, 128)
            rope(pb, (0, 128), 0, perm64, st.t[:, j * TT:(j + 1) * TT], [st])
        pb = proj_fm(1248, 128)
        rope(pb, (0, 128), 0, perm64, st.t[:, 4 * TT:5 * TT], [st])
        store(QB.t[s, :, :, tsl(t)].rearrange("c p t -> p c t"),
              bass.AP(st.t, 0, [[4096, 128], [TT, 4], [1, TT]]), st, [QB.p[pt(s, t)]])
        store(KB.t[s, 0, :, tsl(t)], st.t[:, 4 * TT:5 * TT], st, [KB.p[pt(s, t)]])
        if ASTOP < 7:
            return
        v_proj(s, t, 1376, 128, VB)
        if ASTOP < 8:
            return
        hoist_next(s, t)
        gate_proj(l, s, t, 1504)
        if ASTOP < 3:
            return
        st = STG.next()
        for h in range(8):
            pb = proj_fm(h * 96, 128, w=WUQ, kc=3, rhs=lambda c: CQN.t[:, c, :], reads_extra=[CQN])
            CPY(st.t[0:64, h * TT:(h + 1) * TT], pb.t[0:64, :], [pb], [st])
            rope(pb, (64, 96), 2, perm32, st.t[64:96, h * TT:(h + 1) * TT], [st])
        store(QA.t[s, :, 0:96, tsl(t)].rearrange("h p t -> p h t"),
              bass.AP(st.t, 0, [[4096, 96], [TT, 8], [1, TT]]), st, [QA.p[pt(s, t)]])
        if ASTOP < 4:
            return
        st = STG.next()
        for hp in range(4):
            pb = proj_fm(hp * 128, 128, w=WUK, kc=2, rhs=lambda c: CQN.t[:, 3 + c, :], reads_extra=[CQN])
            for e2 in range(2):
                h = 2 * hp + e2
                CPY(st.t[0:64, h * TT:(h + 1) * TT], pb.t[e2 * 64:e2 * 64 + 64, :], [pb], [st],
                    eng=("dve" if e2 else "act"))
        pb = proj_fm(640, 128)
        xk = XB.next()
        rope(pb, (64, 96), 2, perm32, xk.t[64:96, :], [xk])
        for h in range(8):
            CPY(st.t[64:96, h * TT:(h + 1) * TT], xk.t[64:96, :], [xk], [st], eng="pool")
        store(KA.t[s, :, 0:96, tsl(t)].rearrange("h p t -> p h t"),
              bass.AP(st.t, 0, [[4096, 96], [TT, 8], [1, TT]]), st, [KA.p[pt(s, t)]])
        if ASTOP < 5:
            return
        v_proj(s, t, 0, 512, VA, lhs_src=lambda c, b: CQN.t[:, 3 + c, b * 128:(b + 1) * 128], w=WUV, kc=2)
        if ASTOP < 6:
            return

    def phase_A_odd(l, s, t):
        phase_A_common(l, s, t)
        for (col0, dst) in ((0, QB), (512, KB)):
            st = STG.next()
            for j in range(4):
                pb = proj_fm(col0 + j * 128, 128)
                rope(pb, (0, 128), 0, perm64, st.t[:, j * TT:(j + 1) * TT], [st])
            store(dst.t[s, :, :, tsl(t)].rearrange("c p t -> p c t"),
                  bass.AP(st.t, 0, [[4096, 128], [TT, 4], [1, TT]]), st, [dst.p[pt(s, t)]])
        v_proj(s, t, 1024, 512, VB)
        for (col0, dst) in ((1536, QA), (2048, KA)):
            st = STG.next()
            for hp in range(4):
                pb = proj_fm(col0 + hp * 128, 128)
                for e2 in range(2):
                    h = 2 * hp + e2
                    CPY(st.t[0:64, h * TT:(h + 1) * TT], pb.t[e2 * 64:e2 * 64 + 64, :], [pb], [st],
                        eng=("dve" if e2 else "act"))
            store(dst.t[s, :, 0:64, tsl(t)].rearrange("h p t -> p h t"),
                  bass.AP(st.t, 0, [[4096, 64], [TT, 8], [1, TT]]), st, [dst.p[pt(s, t)]])
        v_proj(s, t, 2560, 512, VA)
        pb = projb.next()
        specs = []
        for b in range(4):
            for c in range(8):
                specs.append((pb.t[:, b * 8:(b + 1) * 8], HT.t[:, c, b * 128:(b + 1) * 128], WIN.t[:, c, 3072:3080],
                              c == 0, c == 7))
        MMS(specs, [WIN, HT], [pb])
        f = FT.next()
        for b in range(4):
            TTO(f.t[:, b * 8:(b + 1) * 8], pb.t[:, b * 8:(b + 1) * 8], CST.t[:, l, 54:62], ALU.add, [pb, CST], [f])
        f2 = FT.next()
        ACT(f2.t[:, 0:32], f.t[:, 0:32], AF.Exp, [f], [f2], scale=-1.0)
        ACT(SPT.t[:, 4 * t:4 * t + 4, :].rearrange("p b h -> p (b h)"), f2.t[:, 0:32], AF.Ln, [f2], [SPT], bias=1.0)
        hoist_next(s, t)
        gate_proj(l, s, t, 3088)

    def fox_post(l, s):
        P.op("dve", lambda e: e.memset(PREF.t[:, 0, :], 0.0), [], [PREF])
        for b in range(1, NB):
            TTO(PREF.t[:, b, :], PREF.t[:, b - 1, :], SPT.t[:, b - 1, :], ALU.add, [PREF, SPT], [PREF])
        pb = projb.next()
        specs = []
        for b in range(NB):
            specs.append((pb.t[:, b * 8:(b + 1) * 8], tri_f, SPT.t[:, b, :], True, False))
            specs.append((pb.t[:, b * 8:(b + 1) * 8], ones_f, PREF.t[:, b, :], False, True))
        MMS(specs, [SPT, PREF, CMF], [pb])
        CPY(CSPB.t[:].rearrange("p b h -> p (b h)"), pb.t[:, 0:NB * 8], [pb], [CSPB])
        P.dma("pool", CSP.t[s], CSPB.t[:], CSPB, [CSPB], [CSP.p[s]])
        for g in range(4):
            pb2 = projb.next()

            def fn(e, g=g, pb2=pb2):
                ins = None
                for bb in range(4):
                    ins = e.transpose(out=pb2.t[0:8, bb * 128:(bb + 1) * 128], in_=CSPB.t[:, g * 4 + bb, :],
                                      identity=ident_f)
                return ins
            P.op("pe", fn, [CSPB, CMF], [pb2])
            ACT(FQ8.t[0:8, g * 512:(g + 1) * 512], pb2.t[0:8, :], AF.Identity, [pb2], [FQ8], scale=-8.0)
        P.dma("pool", QA.t[s, :, 64, :], FQ8.t[0:8, :], FQ8, [FQ8], allp(QA, s), partial=True)

    sbk = Rot(PB[0:3])
    sbk4 = Rot(PB[0:3] + [PB[7]])
    obk4 = Rot(PB[3:7])
    obk2 = Rot(PB[3:5])

    SKEW = 2
    pend = []

    def pipe_flush(keep=0):
        while len(pend) > keep:
            fn = pend.pop(0)
            fn()

    def attn_core(ktf, ktb, krows, kbase, qt, vaf, vab, scale, pO, i, bias=None, swa=False, lbank=None, fin=None, deep=True):
        if swa:
            js = [j for j in range(4 * i - 1, 4 * i + 4) if j >= 0]
        else:
            js = list(range(0, 4 * i + 4))
        for idx, j in enumerate(js):
            m = j - 4 * i
            c0 = 0 if m < 0 else 128 * m
            if swa:
                n = 128 if (m < 0 or m == 3) else 256
            else:
                n = TT - c0
            ps = (sbk4 if deep else sbk).next()
            need_mask = swa or m >= 0
            specs = [(ps.t[:, c0:c0 + n], ktf(j), qt.t[kbase:kbase + krows, c0:c0 + n], True, not need_mask)]
            if swa:
                mk = msw_b[:, 128:256] if m < 0 else msw_b[:, 0:n]
                specs.append((ps.t[:, c0:c0 + n], ident_b, mk, False, True))
            elif m >= 0:
                specs.append((ps.t[:, c0:c0 + 128], ident_b, mc_b, False, True))
            MMS(specs, [qt, CMB] + ktb, [ps])
            p = PT.next()
            kw = dict(scale=scale)
            rd = [ps]
            if bias is not None:
                kw["bias"] = bias(j)
                rd.append(CSPB)
            ACT(p.t[:, c0:c0 + n], ps.t[:, c0:c0 + n], AF.Exp, rd, [p], **kw)
            first = (idx == 0)
            last = (idx == len(js) - 1)

            def pv(p=p, c0=c0, n=n, j=j, first=first, last=last):
                specs = [(pO.t[:, c0:c0 + n], vaf(j), p.t[:, c0:c0 + n], first, last)]
                wr = [pO]
                if lbank is not None:
                    specs.append((lbank.t[:, c0:c0 + n], ones_b, p.t[:, c0:c0 + n], first, last))
                    wr.append(lbank)
                MMS(specs, [p, CMB] + vab, wr)
                if last and fin is not None:
                    fin()
            pend.append(pv)
            pipe_flush(keep=(3 if deep else SKEW))

    def load_va64(src, s, col0, v):
        ap = bass.AP(YB, 4096 + v * 2048 + v * 64, [[YROW, 128], [128, NB], [1, 64]])
        P.dma("sp", ap, src.t[s, :, :, col0:col0 + 64].rearrange("b p n -> p b n"), Y.p[4 + 2 * v],
              allp(src, s), va_bufs(v), partial=True)

    def set_ones():
        for v in range(2):
            ap = bass.AP(YB, 4096 + v * 2048 + (1 - v) * 64, [[YROW, 128], [128, NB], [1, 64]])
            P.op("dve", lambda e, ap=ap: e.memset(ap, 1.0), [], va_bufs(v))

    def load_q_half(src, s, ch, half, t):
        q = QTB.next()
        o0 = (1 - half) * 64
        P.op("pool", lambda e, q=q, o0=o0: e.memset(q.t[o0:o0 + 64, :], 0.0), [], [q])
        r0 = half * 64
        P.dma("sp", q.t[r0:r0 + 64, :], src.t[s, ch, r0:r0 + 64, tsl(t)], q, [src.p[pt(s, t)]], [q], partial=True)
        return q

    def finalize64(pO, par, lbias, r_extra, s, t, chunk):
        n0, l0 = par * 64, (1 - par) * 64
        f2 = FT.next()
        if lbias is not None:
            f1 = FT.next()
            ACT(f1.t[n0:n0 + 64, :], pO.t[l0:l0 + 64, :], AF.Ln, [pO] + list(r_extra), [f1], bias=lbias)
            ACT(f2.t[n0:n0 + 64, :], f1.t[n0:n0 + 64, :], AF.Exp, [f1], [f2], scale=-1.0)
        else:
            P.op("dve", lambda e: e.reciprocal(out=f2.t[n0:n0 + 64, :], in_=pO.t[l0:l0 + 64, :]), [pO], [f2])
        xb = XB.next()
        TTO(xb.t[n0:n0 + 64, :], pO.t[n0:n0 + 64, :], f2.t[n0:n0 + 64, :], ALU.mult, [pO, f2], [xb])
        store(OT.t[s, chunk, n0:n0 + 64, tsl(t)], xb.t[n0:n0 + 64, :], xb, [OT.p[pt(s, t)]])

    def phase_B_even(l, s):
        sc_m = 96.0 ** -0.5
        pipe_flush()
        set_ones()
        for h in range(8):
            par = h % 2
            k = par
            P.dma("sp", kt_ap(k, rows=96), KA.t[s, h, 0:96, :], Y.p[2 * k], allp(KA, s), kt_bufs(k))
            load_va64(VA, s, h * 64, par)
            for t in range(NTL):
                q = QTB.next()
                P.dma("sp", q.t[0:96, :], QA.t[s, h, 0:96, tsl(t)], q, [QA.p[pt(s, t)]], [q])
                pO = obk4.next()
                attn_core(lambda j, k=k: kt_ap(k, rows=96, c0=j * 128, n=128), kt_bufs(k), 96, 0, q,
                          lambda j, v=par: va_blk(v, j), va_bufs(par), sc_m, pO, t,
                          fin=lambda pO=pO, par=par, t=t, h=h: finalize64(pO, par, None, [], s, t, h // 2))
        for grp in range(2):
            k = grp
            P.dma("sp", kt_ap(k), KB.t[s, 0, :, :], Y.p[2 * k], allp(KB, s), kt_bufs(k))
            load_va64(VB, s, grp * 64, grp)
            for j4 in range(4):
                horig = grp * 4 + j4
                for t in range(NTL):
                    q = load_q_half(QB, s, j4, grp, t)
                    pO = obk4.next()
                    attn_core(lambda j, k=k: kt_ap(k, c0=j * 128, n=128), kt_bufs(k),
                              128, 0, q, lambda j, v=grp: va_blk(v, j), va_bufs(grp), 0.125, pO, t, swa=True,
                              fin=lambda pO=pO, grp=grp, horig=horig, t=t, j4=j4: finalize64(
                                  pO, grp, SM.t[grp * 64:grp * 64 + 64, horig:horig + 1], [SM], s, t, 4 + j4))

    def phase_B_odd(l, s):
        i = l // 2
        lam_init = 0.8 - 0.6 * math.exp(-0.3 * l)
        P.dma("sp", CSPB.t[:], CSP.t[s], CSPB, [CSP.p[s]], [CSPB])
        for h in range(4):
            k = h % 2
            v = h % 2
            P.dma("sp", kt_ap(k), KB.t[s, h, :, :], Y.p[2 * k], allp(KB, s), kt_bufs(k))
            vap = bass.AP(YB, 4096 + v * 2048, [[YROW, 128], [128, NB], [1, 128]])
            P.dma("sp", vap, VB.t[s, :, :, h * 128:(h + 1) * 128].rearrange("b p n -> p b n"), Y.p[4 + 2 * v],
                  allp(VB, s), va_bufs(v))
            for t in range(NTL):
                for mp in range(2):
                    q = load_q_half(QB, s, h, mp, t)
                    pO = obk2.next()
                    pL = PB[5 + mp]

                    def fin(pO=pO, pL=pL, mp=mp, t=t, h=h):
                        f1 = FT.next()
                        ACT(f1.t[:], pL.t[:], AF.Ln, [pL], [f1])
                        f2 = FT.next()
                        ACT(f2.t[:], f1.t[:], AF.Exp, [f1], [f2], scale=-1.0)
                        TTO(CQC.t[:, mp, :], pO.t[:], f2.t[:], ALU.mult, [pO, f2], [CQC])
                        if mp == 0:
                            return
                        od = CQC.t[:, 2, :]
                        STT(od, CQC.t[:, 1, :], SM.t[:, 16 + i:17 + i], CQC.t[:, 0, :], ALU.mult, ALU.add, [CQC, SM], [CQC])
                        nb = PB[7]
                        sumsq_bank([od], [CQC], nb)
                        rs = rstd_from(nb, 128.0)
                        xb = XB.next()
                        STT(xb.t[:], od, CST.t[:, l, 53:54], rs.t[:], ALU.mult, ALU.mult, [CQC, rs, CST], [xb])
                        xb2 = XB.next()
                        TS(xb2.t[:], xb.t[:], 1.0 - lam_init, None, ALU.mult, ALU.bypass, [xb], [xb2])
                        store(OT.t[s, h, :, tsl(t)], xb2.t[:], xb2, [OT.p[pt(s, t)]])
                    attn_core(lambda j, k=k: kt_ap(k, c0=j * 128, n=128), kt_bufs(k),
                              128, 0, q, lambda j, v=v: va_blk(v, j), va_bufs(v), 0.125, pO, t, lbank=pL, fin=fin, deep=False)
        pipe_flush()
        for k in range(2):
            ap = bass.AP(YB, 64 * YROW + k * 2048, [[YROW, 1], [1, 2048]])
            P.op("dve", lambda e, ap=ap: e.memset(ap, 1.0), [], kt_bufs(k))
        set_ones()
        for h in range(8):
            par = h % 2
            k = par
            P.dma("sp", kt_ap(k, rows=64), KA.t[s, h, 0:64, :], Y.p[2 * k], allp(KA, s), kt_bufs(k), partial=True)
            load_va64(VA, s, h * 64, par)
            for t in range(NTL):
                q = QTB.next()
                P.dma("sp", q.t[0:65, :], QA.t[s, h, 0:65, tsl(t)], q, [QA.p[pt(s, t)]], [q])
                pO = obk4.next()
                attn_core(lambda j, k=k: kt_ap(k, rows=65, c0=j * 128, n=128), kt_bufs(k), 65, 0, q,
                          lambda j, v=par: va_blk(v, j), va_bufs(par), 0.125, pO, t,
                          bias=lambda j, h=h: CSPB.t[:, j, h:h + 1],
                          fin=lambda pO=pO, par=par, t=t, h=h: finalize64(pO, par, None, [], s, t, 4 + h // 2))

    def phase_C1(l, s, t):
        st_o = STG.next()
        P.dma("sp", bass.AP(st_o.t, 0, [[4096, 128], [TT, 8], [1, TT]]),
              OT.t[s, :, :, tsl(t)].rearrange("c p t -> p c t"), st_o, [OT.p[pt(s, t)]], [st_o])
        st_g = STG.next()
        P.dma("sp", bass.AP(st_g.t, 0, [[4096, 128], [TT, 8], [1, TT]]),
              GT.t[s, :, :, tsl(t)].rearrange("c p t -> p c t"), st_g, [GT.p[pt(s, t)]], [st_g])
        for c in range(8):
            TTO(HT.t[:, c, :], st_o.t[:, c * TT:(c + 1) * TT], st_g.t[:, c * TT:(c + 1) * TT], ALU.mult,
                [st_o, st_g], [HT])

    def phase_C2(l, s, t):
        load_x_tile(s, t)
        nb = PB[6]
        for g in range(8):
            pb = proj_fm(g * 128, 128, w=WOUT)
            CPY(Y.t[:, g, :], pb.t[:], [pb], [Y.p[g]])
            sq = SQ.next()
            ACT(sq.t[:], pb.t[:], AF.Square, [pb], [sq])
            MM(nb.t[:], [(ones_b, sq.t[:])], [sq, CMB], [nb], start=(g == 0), stop=(g == 7))
        return rstd_from(nb, 1024.0)

    def phase_C3(l, s, t, rs):
        for c in range(8):
            f = FT.next()
            TTO(f.t[:], Y.t[:, c, :], rs.t[:], ALU.mult, [Y.p[c], rs], [f])
            STT(XT.t[:, c, :], f.t[:], AD.t[:, l, s, 2, c:c + 1], XT.t[:, c, :], ALU.mult, ALU.add, [f, AD, XT], [XT])
        P.dma("pool", xT.t[s, :, :, tsl(t)].rearrange("c p t -> p c t"), XT.t[:], XT, [XT], [xT.p[pt(s, t)]])

    def phase_C_all(l):
        tiles = [(s, t) for s in range(2) for t in range(NTL)]
        phase_C1(l, *tiles[0])
        for idx, (s, t) in enumerate(tiles):
            rs = phase_C2(l, s, t)
            if idx + 1 < len(tiles):
                phase_C1(l, *tiles[idx + 1])
            phase_C3(l, s, t, rs)

    for l in range(NL):
        i = l // 2
        even = (l % 2 == 0)
        if even:
            ACT(SM.t[:, 0:8], CST.t[:, l, 45:53], AF.Exp, [CST], [SM])
        else:
            lam_init = 0.8 - 0.6 * math.exp(-0.3 * l)
            f = FT.next()
            TTO(f.t[:, 0:64], CST.t[:, l, 62:126], CST.t[:, l, 126:190], ALU.mult, [CST], [f])
            TTO(f.t[:, 64:128], CST.t[:, l, 190:254], CST.t[:, l, 254:318], ALU.mult, [CST], [f])
            P.op("dve", lambda e, f=f: e.reduce_sum(out=SM.t[:, 20:22],
                                                 in_=f.t[:, 0:128].rearrange("p (a b) -> p a b", a=2),
                                                 axis=mybir.AxisListType.X), [f], [SM])
            ACT(SM.t[:, 22:24], SM.t[:, 20:22], AF.Exp, [SM], [SM])
            TTO(SM.t[:, 24:25], SM.t[:, 23:24], SM.t[:, 22:23], ALU.subtract, [SM], [SM])
            TS(SM.t[:, 16 + i:17 + i], SM.t[:, 24:25], -lam_init, None, ALU.add, ALU.bypass, [SM], [SM])
        if STOP < 2:
            continue
        for s in range(2):
            for t in range(NTL):
                if ONE_TILE and (s, t) != (0, 0):
                    continue
                ada_bg_start(l + 1, s * NTL + t)
                (phase_A_even if even else phase_A_odd)(l, s, t)
                ada_bg_end(l + 1, s * NTL + t)
            if not even and not ONE_TILE:
                fox_post(l, s)
        if l + 1 < NL:
            load_weights(l + 1)
        if STOP < 3:
            continue
        for s in range(2):
            (phase_B_even if even else phase_B_odd)(l, s)
        pipe_flush()
        if STOP < 4:
            continue
        phase_C_all(l)
        if l + 1 < NL:
            load_wout(l + 1)

    for s in range(0 if not SKIP_XPRO else 2, 2):
        for t in range(NTL):
            load_x_tile(s, t)
            for b in range(4):
                for half in range(2):
                    pb = projb.next()

                    def fn(e, b=b, half=half, pb=pb):
                        ins = None
                        for cc in range(4):
                            c = half * 4 + cc
                            ins = e.transpose(out=pb.t[:, cc * 128:(cc + 1) * 128],
                                              in_=XT.t[:, c, b * 128:(b + 1) * 128], identity=ident_f)
                        return ins
                    P.op("pe", fn, [XT, CMF], [pb])
                    ft = FT.next()
                    CPY(ft.t[:], pb.t[:], [pb], [ft], eng=("dve" if half else "act"))
                    tok0 = t * TT + b * 128
                    P.dma("pool", out_d[s, tok0:tok0 + 128, half * 512:(half + 1) * 512], ft.t[:], ft, [ft], [])
    P.emit()
    return nc


def _kc(w):
    K, N = w.shape
    return np.ascontiguousarray(w.reshape(K // 128, 128, N).transpose(1, 0, 2))


def _o_perm_even():
    perm = np.zeros(1024, np.int64)
    for f in range(1024):
        cf, r = divmod(f, 128)
        if cf < 4:
            perm[f] = f
        else:
            j = cf - 4
            head = j if r < 64 else 4 + j
            perm[f] = 512 + head * 64 + (r % 64)
    return perm


def _host_consts():
    pos = np.arange(S, dtype=np.float32)
    tab = np.zeros((128, 4, S), np.float32)
    inv64 = (1.0 / (np.float32(10000.0) ** (np.arange(0, 64, 2, dtype=np.float32) / np.float32(64)))).astype(np.float32)
    inv32 = (1.0 / (np.float32(10000.0) ** (np.arange(0, 32, 2, dtype=np.float32) / np.float32(32)))).astype(np.float32)
    for r in range(128):
        i = r % 64
        ang = (pos * inv64[i % 32]).astype(np.float32)
        tab[r, 0] = np.cos(ang)
        tab[r, 1] = -np.sin(ang) if i < 32 else np.sin(ang)
    for r in range(64, 96):
        j = r - 64
        ang = (pos * inv32[j % 16]).astype(np.float32)
        tab[r, 2] = np.cos(ang)
        tab[r, 3] = -np.sin(ang) if j < 16 else np.sin(ang)
    cmf = np.zeros((128, 384), np.float32)
    cmf[:, 0:128] = np.eye(128, dtype=np.float32)
    k = np.arange(128)
    cmf[:, 128:256] = (k[:, None] <= k[None, :]).astype(np.float32)
    cmf[:, 256:384] = 1.0
    cmb = np.zeros((128, 896), np.float32)
    cmb[:, 0:128] = np.eye(128, dtype=np.float32)
    for m in range(128):
        sw = m + 32 if (m % 64) < 32 else m - 32
        cmb[sw, 128 + m] = 1.0
    for m in range(128):
        if 64 <= m < 96:
            sw = m + 16 if (m - 64) < 16 else m - 16
        else:
            sw = m
        cmb[sw, 256 + m] = 1.0
    cmb[:, 384:512] = np.where(k[:, None] > k[None, :], NEG, 0.0)
    cmb[:, 512:640] = np.where(k[:, None] > k[None, :], NEG, 0.0)
    cmb[:, 640:768] = np.where(k[:, None] <= k[None, :], NEG, 0.0)
    cmb[:, 768:896] = 1.0
    return tab, cmf, cmb


def _prep_shared(inp):
    f = lambda a: np.asarray(a, dtype=np.float32)
    sh = {}
    w_ada = f(inp["w_ada"])
    sh["wada"] = np.ascontiguousarray(w_ada.reshape(4, 8, 128, 3072).transpose(0, 2, 1, 3))
    perm_e = _o_perm_even()
    wE, wUQ, wUK, wUV, wOe = [], [], [], [], []
    for i in range(2):
        w = f(inp["ev_w_in"][i])
        cols = [w[:, 0:384], w[:, 384:640], np.zeros((1024, 64), np.float32), w[:, 640:672]]
        sq = w[:, 672:1184].reshape(1024, 8, 64)
        order = []
        for j in range(4):
            order += [j, 4 + j]
        cols.append(sq[:, order, :].reshape(1024, 512))
        cols.append(w[:, 1184:1312])
        cols.append(w[:, 1312:1440])
        cols.append(w[:, 1440:2464][:, perm_e])
        wE.append(_kc(np.concatenate(cols, axis=1)))
        wUQ.append(_kc(np.concatenate([f(inp["ev_w_uq"][i]), np.zeros((384, 32), np.float32)], axis=1)))
        ukv = f(inp["ev_w_ukv"][i]).reshape(256, 8, 128)
        wUK.append(_kc(np.ascontiguousarray(ukv[:, :, 0:64]).reshape(256, 512)))
        wUV.append(_kc(np.ascontiguousarray(ukv[:, :, 64:128]).reshape(256, 512)))
        wOe.append(_kc(f(inp["ev_w_out"][i])[perm_e, :]))
    sh["wE"] = np.stack(wE)
    sh["wUQ"] = np.stack(wUQ)
    sh["wUK"] = np.stack(wUK)
    sh["wUV"] = np.stack(wUV)
    sh["wOe"] = np.stack(wOe)
    wO, wOo = [], []
    for i in range(2):
        w = f(inp["od_w_in"][i])
        cols = [w[:, 0:3080], np.zeros((1024, 8), np.float32), w[:, 3080:4104]]
        wO.append(_kc(np.concatenate(cols, axis=1)))
        wOo.append(_kc(f(inp["od_w_out"][i])))
    sh["wO"] = np.stack(wO)
    sh["wOo"] = np.stack(wOo)
    cst = np.zeros((128, 4, NC_CST), np.float32)
    for l in range(4):
        i = l // 2
        cst[:, l, 0:8] = f(inp["g_pre"][l]).reshape(8, 128).T
        cst[:, l, 8:16] = f(inp["g_post"][l]).reshape(8, 128).T
        cst[:, l, 16:40] = f(inp["b_ada"][l]).reshape(24, 128).T
        if l % 2 == 0:
            cst[:, l, 40:43] = f(inp["ev_q_norm"][i]).reshape(3, 128).T
            cst[:, l, 43:45] = f(inp["ev_kv_norm"][i]).reshape(2, 128).T
            cst[:, l, 45:53] = f(inp["ev_sinks"][i])[None, :]
        else:
            cst[:, l, 53] = f(inp["od_subln"][i])
            cst[:, l, 54:62] = f(inp["od_forget_bias"][i])[None, :]
            cst[:, l, 62:318] = f(inp["od_lambda"][i]).reshape(256)[None, :]
    sh["cst"] = cst
    tab, cmf, cmb = _host_consts()
    sh["tab"] = tab
    sh["cmf"] = cmf
    sh["cmb"] = cmb
    return sh


_NC_CACHE = {}


def run(inputs, NL=4, cores=8):
    x = np.asarray(inputs["x"], dtype=np.float32)
    c = np.asarray(inputs["c"], dtype=np.float32)
    sh = _prep_shared(inputs)
    in_maps = []
    for core in range(cores):
        m = dict(sh)
        m["x"] = np.ascontiguousarray(x[2 * core:2 * core + 2])
        cc = c[2 * core:2 * core + 2]
        m["cT"] = np.ascontiguousarray(cc.T.reshape(8, 128, 2).transpose(1, 0, 2))
        in_maps.append(m)
    if NL not in _NC_CACHE:
        _NC_CACHE[NL] = build(NL)
    res = run_bass_kernel_spmd(_NC_CACHE[NL], in_maps, core_ids=list(range(cores)))
    if DEBUG:
        return res.results
    return np.concatenate([r["out"] for r in res.results], axis=0).astype(np.float32)


def kernel(**inputs):
    return run(inputs, NL=4, cores=8)
```

```python
from contextlib import ExitStack
import math
import numpy as np
import concourse.bass as bass
import concourse.mybir as mybir
from concourse.bass_utils import run_bass_kernel_spmd

F32 = mybir.dt.float32
BF16 = mybir.dt.bfloat16
AF = mybir.ActivationFunctionType
ALU = mybir.AluOpType

S = 2048
TT = 512
NTL = S // TT
NB = S // 128
EPS = 1e-6
NC_CST = 320
NE = 2528
NO = 4112
NEG = -30000.0
STOP = 9
ONE_TILE = False
DEBUG = False
ASTOP = 99
SKIP_XPRO = False


class Buf:
    __slots__ = ("name", "writers", "readers", "excl")

    def __init__(self, name, excl=False):
        self.name = name
        self.excl = excl
        self.writers = {}
        self.readers = {}


class Tens:
    def __init__(self, t, name, parts, excl=False):
        self.t = t
        self.name = name
        self.p = [Buf(f"{name}.{i}", excl) for i in range(parts)]


ENGS = ("pe", "act", "dve", "pool", "sp")


class Prog:
    def __init__(self, nc, same_engine_sync=True):
        self.nc = nc
        self.es = ExitStack()
        self.ops = {e: [] for e in ENGS}
        self.cnt = {e: 0 for e in ENGS}
        self.known = {e: {} for e in ENGS}
        self.snap = {}
        self.chan_cnt = {}
        self.sem = {}
        self.same_engine_sync = same_engine_sync

    def sbuf(self, name, shape, dtype, parts=1):
        t = self.es.enter_context(self.nc.sbuf_tensor("sb_" + name, list(shape), dtype))
        return Tens(t, name, parts)

    def psum(self, name, shape, dtype, parts=1):
        t = self.es.enter_context(self.nc.psum_tensor("ps_" + name, list(shape), dtype))
        return Tens(t, name, parts, excl=True)

    def dram(self, name, shape, dtype, kind="Internal", parts=1):
        t = self.nc.dram_tensor(name, list(shape), dtype, kind=kind).ap()
        return Tens(t, name, parts)

    @staticmethod
    def _flat(xs):
        out = []
        for x in xs:
            if isinstance(x, Tens):
                out.extend(x.p)
            elif isinstance(x, (list, tuple)):
                out.extend(Prog._flat(x))
            else:
                out.append(x)
        return out

    def _deps(self, eng, reads, writes, skip=None):
        deps = {}

        def add(d):
            for k, v in d.items():
                if deps.get(k, 0) < v:
                    deps[k] = v

        for b in reads:
            add(b.writers)
        for b in writes:
            add(b.writers)
            add(b.readers)
        waits = []
        kn = self.known[eng]
        for k, v in deps.items():
            if k == skip:
                continue
            if k == eng and (eng == "pe" or not self.same_engine_sync):
                continue
            if kn.get(k, 0) >= v:
                continue
            waits.append((k, v))
        for k, v in waits:
            kn[k] = max(kn.get(k, 0), v)
            sn = self.snap.get((k, v))
            if sn:
                for k2, v2 in sn.items():
                    if kn.get(k2, 0) < v2:
                        kn[k2] = v2
        return waits

    def _mark(self, tok, reads, writes, partial):
        k, v = tok
        for b in reads:
            if b.readers.get(k, 0) < v:
                b.readers[k] = v
        for b in writes:
            if partial:
                b.writers[k] = v
            else:
                b.writers = {k: v}
                b.readers = {}

    def op(self, eng, fn, reads=(), writes=()):
        reads = self._flat(reads)
        writes = self._flat(writes)
        writes = writes + [b for b in reads if b.excl and b not in writes]
        reads = [b for b in reads if not b.excl]
        waits = self._deps(eng, reads, writes)
        self.cnt[eng] += 1
        tok = (eng, self.cnt[eng])
        sn = dict(self.known[eng])
        sn[eng] = self.cnt[eng]
        self.snap[tok] = sn
        self._mark(tok, reads, writes, False)
        self.ops[eng].append(("op", waits, fn, tok))
        return tok

    def dma(self, eng, out, in_, sb, reads=(), writes=(), partial=False, **kw):
        reads = self._flat(reads)
        writes = self._flat(writes)
        sbb = self._flat([sb])[0]
        chan = "d_" + sbb.name
        waits = self._deps(eng, reads, writes, skip=(chan if partial else None))
        self.chan_cnt[chan] = self.chan_cnt.get(chan, 0) + 16
        tok = (chan, self.chan_cnt[chan])
        self.snap[tok] = dict(self.known[eng])
        self._mark(tok, reads, writes, partial)
        self.ops[eng].append(("dma", waits, (out, in_, kw), tok))
        return tok

    def emit(self):
        nc = self.nc
        for k in list(ENGS) + list(self.chan_cnt):
            if k not in self.sem:
                self.sem[k] = self.es.enter_context(nc.semaphore("s_" + k.replace(".", "_")))
        final = list(self.chan_cnt.items())

        def run(ename, eng):
            for kind, waits, payload, tok in self.ops[ename]:
                for k, v in waits:
                    eng.wait_ge(self.sem[k], v)
                if kind == "op":
                    payload(eng).then_inc(self.sem[tok[0]], 1)
                else:
                    out, in_, kw = payload
                    eng.dma_start(out=out, in_=in_, **kw).then_inc(self.sem[tok[0]], 16)
            if ename == "sp":
                for c, v in final:
                    eng.wait_ge(self.sem[c], v)

        with nc.Block() as block:
            @block.sync
            def _(e):
                run("sp", e)

            @block.scalar
            def _(e):
                run("act", e)

            @block.vector
            def _(e):
                run("dve", e)

            @block.gpsimd
            def _(e):
                run("pool", e)

            @block.tensor
            def _(e):
                run("pe", e)
        self.es.close()


class Rot:
    def __init__(self, items):
        self.items = items
        self.i = 0

    def next(self):
        x = self.items[self.i % len(self.items)]
        self.i += 1
        return x


def build(NL=4):
    nc = bass.Bass("TRN2", target_bir_lowering=False)
    P = Prog(nc)

    def din(name, shape, dt=F32):
        return nc.dram_tensor(name, list(shape), dt, kind="ExternalInput").ap()

    x_d = din("x", [2, S, 1024])
    cT_d = din("cT", [128, 8, 2])
    wada_d = din("wada", [4, 128, 8, 3072])
    cst_d = din("cst", [128, 4, NC_CST])
    cmf_d = din("cmf", [128, 384])
    cmb_d = din("cmb", [128, 896])
    tab_d = din("tab", [128, 4, S])
    wE_d = din("wE", [2, 128, 8, NE])
    wUQ_d = din("wUQ", [2, 128, 3, 800])
    wUK_d = din("wUK", [2, 128, 2, 512])
    wUV_d = din("wUV", [2, 128, 2, 512])
    wOe_d = din("wOe", [2, 128, 8, 1024])
    wO_d = din("wO", [2, 128, 8, NO])
    wOo_d = din("wOo", [2, 128, 8, 1024])
    out_d = nc.dram_tensor("out", [2, S, 1024], F32, kind="ExternalOutput").ap()

    def scr(name, shape, dt, parts=2 * NTL):
        return P.dram(name, shape, dt, kind=("ExternalOutput" if DEBUG else "Internal"), parts=parts)

    xT = scr("xT", [2, 8, 128, S], F32)
    QA = scr("QA", [2, 8, 128, S], BF16)
    KA = scr("KA", [2, 8, 128, S], BF16)
    QB = scr("QB", [2, 4, 128, S], BF16)
    KB = scr("KB", [2, 4, 128, S], BF16)
    VA = scr("VA", [2, NB, 128, 512], BF16)
    VB = scr("VB", [2, NB, 128, 512], BF16)
    GT = scr("GT", [2, 8, 128, S], BF16)
    OT = scr("OT", [2, 8, 128, S], BF16)
    CSP = scr("CSP", [2, 128, NB, 8], F32, parts=2)

    def pt(s, t):
        return s * NTL + t

    def allp(T, s):
        return [T.p[pt(s, t)] for t in range(NTL)]

    WIN = P.sbuf("WIN", [128, 8, NO], BF16)
    WOUT = P.sbuf("WOUT", [128, 8, 1024], BF16)
    WUQ = P.sbuf("WUQ", [128, 3, 800], BF16)
    WUK = P.sbuf("WUK", [128, 2, 512], BF16)
    WUV = P.sbuf("WUV", [128, 2, 512], BF16)
    CST = P.sbuf("CST", [128, 4, NC_CST], F32)
    AD = P.sbuf("AD", [128, 4, 2, 3, 8], F32)
    CMF = P.sbuf("CMF", [128, 384], F32)
    CMB = P.sbuf("CMB", [128, 896], BF16)
    TAB = P.sbuf("TAB", [128, 4, TT], F32)
    XT = P.sbuf("XT", [128, 8, TT], F32)
    HT = P.sbuf("HT", [128, 8, TT], BF16)
    SQ = Rot([P.sbuf(f"SQ{i}", [128, TT], BF16) for i in range(2)])
    FT = Rot([P.sbuf(f"FT{i}", [128, TT], F32) for i in range(4)])
    RSB = Rot([P.sbuf(f"RSB{i}", [128, TT], F32) for i in range(2)])
    STG = Rot([P.sbuf(f"STG{i}", [128, 4096], BF16) for i in range(2)])
    CQC = P.sbuf("CQC", [128, 5, TT], F32)
    CQN = P.sbuf("CQN", [128, 5, TT], BF16)
    XB = Rot([P.sbuf(f"XB{i}", [128, TT], BF16) for i in range(2)])
    Y = P.sbuf("Y", [128, 8, TT], F32, parts=8)
    YB = Y.t.bitcast(BF16)
    QTBL = [P.sbuf(f"QTB{i}", [128, TT], BF16) for i in range(4)]
    QTB = Rot(QTBL)
    PT = Rot([P.sbuf(f"PT{i}", [128, TT], BF16) for i in range(4)])
    CSPB = P.sbuf("CSPB", [128, NB, 8], F32)
    SPT = P.sbuf("SPT", [128, NB, 8], F32)
    PREF = P.sbuf("PREF", [128, NB, 8], F32)
    FQ8 = P.sbuf("FQ8", [128, S], BF16)
    SM = P.sbuf("SM", [128, 64], F32)
    CTB = P.sbuf("CTB", [128, 8, 2], BF16)
    CTF = P.sbuf("CTF", [128, 8, 2], F32)
    MODF = P.sbuf("MODF", [128, 4, 24, 2], F32)

    PB = [P.psum(f"B{i}", [128, 512], F32) for i in range(8)]

    ident_f = CMF.t[:, 0:128]
    tri_f = CMF.t[:, 128:256]
    ones_f = CMF.t[:, 256:384]
    ident_b = CMB.t[:, 0:128]
    perm64 = CMB.t[:, 128:256]
    perm32 = CMB.t[:, 256:384]
    mc_b = CMB.t[:, 384:512]
    msw_b = CMB.t[:, 512:768]
    ones_b = CMB.t[:, 768:896]

    YROW = 8192

    def kt_ap(k, rows=128, p0=0, c0=0, n=2048):
        return bass.AP(YB, p0 * YROW + k * 2048 + c0, [[YROW, rows], [1, n]])

    def kt_bufs(k):
        return [Y.p[2 * k], Y.p[2 * k + 1]]

    def va_blk(v, j):
        return bass.AP(YB, 4096 + v * 2048 + j * 128, [[YROW, 128], [1, 128]])

    def va_bufs(v):
        return [Y.p[4 + 2 * v], Y.p[5 + 2 * v]]

    def ACT(out, in_, func, r, w, **kw):
        P.op("act", lambda e: e.activation(out=out, in_=in_, func=func, **kw), r, w)

    def TTO(out, a, b, op, r, w, eng="dve"):
        P.op(eng, lambda e: e.tensor_tensor(out=out, in0=a, in1=b, op=op), r, w)

    def TS(out, a, s1, s2, op0, op1, r, w, eng="dve"):
        P.op(eng, lambda e: e.tensor_scalar(out=out, in0=a, scalar1=s1, scalar2=s2, op0=op0, op1=op1), r, w)

    def STT(out, a, s, b, op0, op1, r, w):
        P.op("dve", lambda e: e.scalar_tensor_tensor(out=out, in0=a, scalar=s, in1=b, op0=op0, op1=op1), r, w)

    def CPY(out, in_, r, w, eng="dve"):
        if eng == "act":
            P.op("act", lambda e: e.activation(out=out, in_=in_, func=AF.Identity), r, w)
        else:
            P.op(eng, lambda e: e.tensor_copy(out=out, in_=in_), r, w)

    def MM(out, pairs, r, w, start=True, stop=True):
        pairs = list(pairs)

        def fn(e):
            n = len(pairs)
            ins = None
            for i, (l, rr) in enumerate(pairs):
                ins = e.matmul(out, lhsT=l, rhs=rr, start=(start and i == 0), stop=(stop and i == n - 1))
            return ins
        P.op("pe", fn, r, w)

    def MMS(specs, r, w):
        specs = list(specs)

        def fn(e):
            ins = None
            for (o, l, rr, st, sp) in specs:
                ins = e.matmul(o, lhsT=l, rhs=rr, start=st, stop=sp)
            return ins
        P.op("pe", fn, r, w)

    projb = Rot(PB[0:5])

    def rstd_from(ps, n_feat):
        t1 = FT.next()
        ACT(t1.t[:], ps.t[:], AF.Ln, [ps], [t1], scale=1.0 / n_feat, bias=EPS)
        t2 = RSB.next()
        ACT(t2.t[:], t1.t[:], AF.Exp, [t1], [t2], scale=-0.5)
        return t2

    CASTKW = dict(max_dma_last_dim=4096)

    def load_weights(l):
        i = l // 2
        if l % 2 == 0:
            for c in range(8):
                P.dma("pool", WIN.t[:, c, 0:NE], wE_d[i, :, c, :], WIN, [], [WIN], partial=(c > 0), **CASTKW)
            P.dma("pool", WUQ.t[:], wUQ_d[i], WUQ, [], [WUQ], **CASTKW)
            P.dma("pool", WUK.t[:], wUK_d[i], WUK, [], [WUK], **CASTKW)
            P.dma("pool", WUV.t[:], wUV_d[i], WUV, [], [WUV], **CASTKW)
        else:
            for c in range(8):
                P.dma("pool", WIN.t[:, c, :], wO_d[i, :, c, :], WIN, [], [WIN], partial=(c > 0), **CASTKW)

    def load_wout(l):
        i = l // 2
        src = wOe_d if l % 2 == 0 else wOo_d
        for c in range(8):
            P.dma("pool", WOUT.t[:, c, :], src[i, :, c, :], WOUT, [], [WOUT], partial=(c > 0), **CASTKW)

    P.dma("sp", CST.t[:], cst_d, CST, [], [CST])
    P.dma("sp", CMF.t[:], cmf_d, CMF, [], [CMF])
    P.dma("pool", CMB.t[:], cmb_d, CMB, [], [CMB])
    P.dma("sp", CTF.t[:], cT_d, CTF, [], [CTF])
    ACT(CTB.t[:], CTF.t[:], AF.Silu, [CTF], [CTB])

    load_weights(0)

    xits = [(s, t, hb) for s in range(0 if not SKIP_XPRO else 2, 2) for t in range(NTL) for hb in range(2)]

    def x_load(it):
        s, t, hb = xits[it]
        stg = XT if it % 2 == 0 else Y
        xin = bass.AP(stg.t, 0, [[8 * TT, 128], [1024, 2], [1, 1024]])
        tok0 = t * TT + hb * 256
        src = x_d[s, tok0:tok0 + 256, :].rearrange("(b p) d -> p b d", p=128)
        P.dma("sp", xin, src, stg.p[0], [], [stg])

    if xits:
        x_load(0)
    for it, (s, t, hb) in enumerate(xits):
        if it + 1 < len(xits):
            x_load(it + 1)
        stg = XT if it % 2 == 0 else Y
        tok0 = t * TT + hb * 256
        for c in range(8):
            pb = projb.next()

            def fn(e, c=c, pb=pb, stg=stg):
                ins = None
                for b in range(2):
                    a_in = bass.AP(stg.t, b * 1024 + c * 128, [[8 * TT, 128], [1, 128]])
                    ins = e.transpose(out=pb.t[:, b * 128:(b + 1) * 128], in_=a_in, identity=ident_f)
                return ins
            P.op("pe", fn, [stg, CMF], [pb])
            ft = FT.next()
            CPY(ft.t[:, 0:256], pb.t[:, 0:256], [pb], [ft], eng=("dve" if c % 2 else "act"))
            P.dma("sp", xT.t[s, c, :, tok0:tok0 + 256], ft.t[:, 0:256], ft, [ft], [xT.p[pt(s, t)]], partial=True)

    def ada_piece_dma(l, pc, dst_ap, sbt, bufs):
        P.dma("pool", dst_ap, wada_d[l, :, :, pc * 512:(pc + 1) * 512], sbt, [], bufs, **CASTKW)

    def ada_piece_mm(l, pc, lw_fn, bufs):
        pb = projb.next()
        specs = []
        for g4 in range(4):
            for kc in range(8):
                specs.append((pb.t[:, 2 * g4:2 * g4 + 2], lw_fn(kc, g4), CTB.t[:, kc, :], kc == 0, kc == 7))
        MMS(specs, list(bufs) + [CTB], [pb])
        for g4 in range(4):
            g = pc * 4 + g4
            TS(MODF.t[:, l, g, :], pb.t[:, 2 * g4:2 * g4 + 2], CST.t[:, l, 16 + g:17 + g], None, ALU.add, ALU.bypass,
               [pb, CST], [MODF])

    def ada_finish(l):
        for b in range(2):
            a1 = AD.t[:, l, b, 0, :]
            TS(a1, MODF.t[:, l, 8:16, b], 1.0, None, ALU.add, ALU.bypass, [MODF], [AD])
            TTO(a1, a1, CST.t[:, l, 0:8], ALU.mult, [AD, CST], [AD])
            CPY(AD.t[:, l, b, 1, :], MODF.t[:, l, 0:8, b], [MODF], [AD])
            TTO(AD.t[:, l, b, 2, :], MODF.t[:, l, 16:24, b], CST.t[:, l, 8:16], ALU.mult, [MODF, CST], [AD])

    if NL > 0:
        for pc in range(6):
            wbuf = STG.next()
            ada_piece_dma(0, pc, bass.AP(wbuf.t, 0, [[4096, 128], [512, 8], [1, 512]]), wbuf, [wbuf])
            ada_piece_mm(0, pc, lambda kc, g4, wbuf=wbuf: bass.AP(wbuf.t, kc * 512 + g4 * 128, [[4096, 128], [1, 128]]), [wbuf])
        ada_finish(0)

    def yhalf_bufs(k):
        return [Y.p[4 * k + j] for j in range(4)]

    def ada_bg_start(l, ti):
        if l >= NL or ti >= 6:
            return
        k = ti % 2
        ada_piece_dma(l, ti, bass.AP(YB, k * 4096, [[YROW, 128], [512, 8], [1, 512]]), Y.p[4 * k], yhalf_bufs(k))

    def ada_bg_end(l, ti):
        if l >= NL or ti >= 6:
            return
        k = ti % 2
        ada_piece_mm(l, ti, lambda kc, g4, k=k: bass.AP(YB, k * 4096 + kc * 512 + g4 * 128, [[YROW, 128], [1, 128]]),
                     yhalf_bufs(k))
        if ti == 5:
            ada_finish(l)

    load_wout(0)

    def tsl(t):
        return slice(t * TT, (t + 1) * TT)

    def load_x_tile(s, t):
        P.dma("sp", XT.t[:], xT.t[s, :, :, tsl(t)].rearrange("c p t -> p c t"), XT, [xT.p[pt(s, t)]], [XT])

    def sumsq_bank(srcs, reads, bank):
        n = len(srcs)
        for i, a in enumerate(srcs):
            sq = SQ.next()
            ACT(sq.t[:], a, AF.Square, reads, [sq])
            MM(bank.t[:], [(ones_b, sq.t[:])], [sq, CMB], [bank], start=(i == 0), stop=(i == n - 1))

    rs_next = {}

    def norm_part(s, t):
        load_x_tile(s, t)
        nb = PB[5]
        sumsq_bank([XT.t[:, c, :] for c in range(8)], [XT], nb)
        rs = rstd_from(nb, 1024.0)
        for c in range(8):
            TTO(XT.t[:, c, :], XT.t[:, c, :], rs.t[:], ALU.mult, [XT, rs], [XT])
        rs_next[(s, t)] = rs

    def next_tile(s, t):
        if t + 1 < NTL:
            return (s, t + 1)
        if s == 0:
            return (1, 0)
        return None

    def phase_A_common(l, s, t):
        if (s, t) not in rs_next:
            norm_part(s, t)
        rs_next.pop((s, t))
        P.dma("sp", TAB.t[:], tab_d[:, :, tsl(t)], TAB, [], [TAB])
        for c in range(8):
            if c % 2 == 0:
                ACT(HT.t[:, c, :], XT.t[:, c, :], AF.Identity, [XT, AD], [HT], scale=AD.t[:, l, s, 0, c:c + 1],
                    bias=AD.t[:, l, s, 1, c:c + 1])
            else:
                TS(HT.t[:, c, :], XT.t[:, c, :], AD.t[:, l, s, 0, c:c + 1], AD.t[:, l, s, 1, c:c + 1], ALU.mult, ALU.add,
                   [XT, AD], [HT])

    def hoist_next(s, t):
        nt = next_tile(s, t)
        if nt is not None and not ONE_TILE:
            norm_part(*nt)

    def proj_fm(col0, M, reads_extra=(), kc=8, w=None, rhs=None):
        w = w or WIN
        pb = projb.next()
        pairs = []
        for c in range(kc):
            r = HT.t[:, c, :] if rhs is None else rhs(c)
            pairs.append((w.t[:, c, col0:col0 + M], r))
        MM(pb.t[0:M, :], pairs, [w, HT] + list(reads_extra), [pb])
        return pb

    def rope(pbx, rows, tabi, perm, out_ap, out_w):
        r0, r1 = rows
        kmax = 128
        xb = XB.next()
        ACT(xb.t[0:kmax, :], pbx.t[0:kmax, :], AF.Identity, [pbx], [xb])
        pb2 = projb.next()
        MM(pb2.t[0:kmax, :], [(perm[0:kmax, 0:kmax], xb.t[0:kmax, :])], [xb, CMB], [pb2])
        f1 = FT.next()
        TTO(f1.t[r0:r1, :], pbx.t[r0:r1, :], TAB.t[r0:r1, tabi, :], ALU.mult, [pbx, TAB], [f1])
        f2 = FT.next()
        TTO(f2.t[r0:r1, :], pb2.t[r0:r1, :], TAB.t[r0:r1, tabi + 1, :], ALU.mult, [pb2, TAB], [f2])
        TTO(out_ap, f1.t[r0:r1, :], f2.t[r0:r1, :], ALU.add, [f1, f2], out_w)

    def store(dst, src, sbt, dparts, partial=True):
        P.dma("pool", dst, src, sbt, [sbt], dparts, partial=partial)

    def gate_proj(l, s, t, col0):
        st = STG.next()
        for c in range(8):
            pb = proj_fm(col0 + c * 128, 128)
            ACT(st.t[:, c * TT:(c + 1) * TT], pb.t[:], AF.Silu, [pb], [st])
        store(GT.t[s, :, :, tsl(t)].rearrange("c p t -> p c t"),
              bass.AP(st.t, 0, [[4096, 128], [TT, 8], [1, TT]]), st, [GT.p[pt(s, t)]])

    def v_proj(s, t, col0, ncol, dst, lhs_src=None, w=None, kc=8):
        w = w or WIN
        st = STG.next()
        for b in range(4):
            pb = projb.next()
            pairs = []
            for c in range(kc):
                lt = HT.t[:, c, b * 128:(b + 1) * 128] if lhs_src is None else lhs_src(c, b)
                pairs.append((lt, w.t[:, c, col0:col0 + ncol]))
            MM(pb.t[:, 0:ncol], pairs, [w, HT, CQN], [pb])
            CPY(st.t[:, b * 512:b * 512 + ncol], pb.t[:, 0:ncol], [pb], [st], eng=("dve" if b % 2 else "act"))
        store(dst.t[s, 4 * t:4 * t + 4, :, 0:ncol].rearrange("b p n -> p b n"),
              bass.AP(st.t, 0, [[4096, 128], [512, 4], [1, ncol]]), st, [dst.p[pt(s, t)]])

    def phase_A_even(l, s, t):
        phase_A_common(l, s, t)
        if ASTOP < 2:
            return
        for (c0, nch, base, ncol, nfeat) in ((0, 3, 0, 40, 384.0), (3, 2, 384, 43, 256.0)):
            nb = PB[6]
            for j in range(nch):
                pb = proj_fm(base + j * 128, 128)
                CPY(CQC.t[:, c0 + j, :], pb.t[:], [pb], [CQC])
                sq = SQ.next()
                ACT(sq.t[:], pb.t[:], AF.Square, [pb], [sq])
                MM(nb.t[:], [(ones_b, sq.t[:])], [sq, CMB], [nb], start=(j == 0), stop=(j == nch - 1))
            rs = rstd_from(nb, nfeat)
            for j in range(nch):
                STT(CQN.t[:, c0 + j, :], CQC.t[:, c0 + j, :], CST.t[:, l, ncol + j:ncol + j + 1], rs.t[:],
                    ALU.mult, ALU.mult, [CQC, CST, rs], [CQN])
        st = STG.next()
        for j in range(4):
            pb = proj_fm(736 + j * 128, 128)
            rope(pb, (0, 128), 0, perm64, st.t[:, j * TT:(j + 1) * TT], [st])
        pb = proj_fm(1248, 128)
        rope(pb, (0, 128), 0, perm64, st.t[:, 4 * TT:5 * TT], [st])
        store(QB.t[s, :, :, tsl(t)].rearrange("c p t -> p c t"),
              bass.AP(st.t, 0, [[4096, 128], [TT, 4], [1, TT]]), st, [QB.p[pt(s, t)]])
        store(KB.t[s, 0, :, tsl(t)], st.t[:, 4 * TT:5 * TT], st, [KB.p[pt(s, t)]])
        if ASTOP < 7:
            return
        v_proj(s, t, 1376, 128, VB)
        if ASTOP < 8:
            return
        hoist_next(s, t)
        gate_proj(l, s, t, 1504)
        if ASTOP < 3:
            return
        st = STG.next()
        for h in range(8):
            pb = proj_fm(h * 96, 128, w=WUQ, kc=3, rhs=lambda c: CQN.t[:, c, :], reads_extra=[CQN])
            CPY(st.t[0:64, h * TT:(h + 1) * TT], pb.t[0:64, :], [pb], [st])
            rope(pb, (64, 96), 2, perm32, st.t[64:96, h * TT:(h + 1) * TT], [st])
        store(QA.t[s, :, 0:96, tsl(t)].rearrange("h p t -> p h t"),
              bass.AP(st.t, 0, [[4096, 96], [TT, 8], [1, TT]]), st, [QA.p[pt(s, t)]])
        if ASTOP < 4:
            return
        st = STG.next()
        for hp in range(4):
            pb = proj_fm(hp * 128, 128, w=WUK, kc=2, rhs=lambda c: CQN.t[:, 3 + c, :], reads_extra=[CQN])
            for e2 in range(2):
                h = 2 * hp + e2
                CPY(st.t[0:64, h * TT:(h + 1) * TT], pb.t[e2 * 64:e2 * 64 + 64, :], [pb], [st],
                    eng=("dve" if e2 else "act"))
        pb = proj_fm(640, 128)
        xk = XB.next()
        rope(pb, (64, 96), 2, perm32, xk.t[64:96, :], [xk])
        for h in range(8):
            CPY(st.t[64:96, h * TT:(h + 1) * TT], xk.t[64:96, :], [xk], [st], eng="pool")
        store(KA.t[s, :, 0:96, tsl(t)].rearrange("h p t -> p h t"),
              bass.AP(st.t, 0, [[4096, 96], [TT, 8], [1, TT]]), st, [KA.p[pt(s, t)]])
        if ASTOP < 5:
            return
        v_proj(s, t, 0, 512, VA, lhs_src=lambda c, b: CQN.t[:, 3 + c, b * 128:(b + 1) * 128], w=WUV, kc=2)
        if ASTOP < 6:
            return

    def phase_A_odd(l, s, t):
        phase_A_common(l, s, t)
        for (col0, dst) in ((0, QB), (512, KB)):
            st = STG.next()
            for j in range(4):
                pb = proj_fm(col0 + j * 128, 128)
                rope(pb, (0, 128), 0, perm64, st.t[:, j * TT:(j + 1) * TT], [st])
            store(dst.t[s, :, :, tsl(t)].rearrange("c p t -> p c t"),
                  bass.AP(st.t, 0, [[4096, 128], [TT, 4], [1, TT]]), st, [dst.p[pt(s, t)]])
        v_proj(s, t, 1024, 512, VB)
        for (col0, dst) in ((1536, QA), (2048, KA)):
            st = STG.next()
            for hp in range(4):
                pb = proj_fm(col0 + hp * 128, 128)
                for e2 in range(2):
                    h = 2 * hp + e2
                    CPY(st.t[0:64, h * TT:(h + 1) * TT], pb.t[e2 * 64:e2 * 64 + 64, :], [pb], [st],
                        eng=("dve" if e2 else "act"))
            store(dst.t[s, :, 0:64, tsl(t)].rearrange("h p t -> p h t"),
                  bass.AP(st.t, 0, [[4096, 64], [TT, 8], [1, TT]]), st, [dst.p[pt(s, t)]])
        v_proj(s, t, 2560, 512, VA)
        pb = projb.next()
        specs = []
        for b in range(4):
            for c in range(8):
                specs.append((pb.t[:, b * 8:(b + 1) * 8], HT.t[:, c, b * 128:(b + 1) * 128], WIN.t[:, c, 3072:3080],
                              c == 0, c == 7))
        MMS(specs, [WIN, HT], [pb])
        f = FT.next()
        for b in range(4):
            TTO(f.t[:, b * 8:(b + 1) * 8], pb.t[:, b * 8:(b + 1) * 8], CST.t[:, l, 54:62], ALU.add, [pb, CST], [f])
        f2 = FT.next()
        ACT(f2.t[:, 0:32], f.t[:, 0:32], AF.Exp, [f], [f2], scale=-1.0)
        ACT(SPT.t[:, 4 * t:4 * t + 4, :].rearrange("p b h -> p (b h)"), f2.t[:, 0:32], AF.Ln, [f2], [SPT], bias=1.0)
        hoist_next(s, t)
        gate_proj(l, s, t, 3088)

    def fox_post(l, s):
        P.op("dve", lambda e: e.memset(PREF.t[:, 0, :], 0.0), [], [PREF])
        for b in range(1, NB):
            TTO(PREF.t[:, b, :], PREF.t[:, b - 1, :], SPT.t[:, b - 1, :], ALU.add, [PREF, SPT], [PREF])
        pb = projb.next()
        specs = []
        for b in range(NB):
            specs.append((pb.t[:, b * 8:(b + 1) * 8], tri_f, SPT.t[:, b, :], True, False))
            specs.append((pb.t[:, b * 8:(b + 1) * 8], ones_f, PREF.t[:, b, :], False, True))
        MMS(specs, [SPT, PREF, CMF], [pb])
        CPY(CSPB.t[:].rearrange("p b h -> p (b h)"), pb.t[:, 0:NB * 8], [pb], [CSPB])
        P.dma("pool", CSP.t[s], CSPB.t[:], CSPB, [CSPB], [CSP.p[s]])
        for g in range(4):
            pb2 = projb.next()

            def fn(e, g=g, pb2=pb2):
                ins = None
                for bb in range(4):
                    ins = e.transpose(out=pb2.t[0:8, bb * 128:(bb + 1) * 128], in_=CSPB.t[:, g * 4 + bb, :],
                                      identity=ident_f)
                return ins
            P.op("pe", fn, [CSPB, CMF], [pb2])
            ACT(FQ8.t[0:8, g * 512:(g + 1) * 512], pb2.t[0:8, :], AF.Identity, [pb2], [FQ8], scale=-8.0)
        P.dma("pool", QA.t[s, :, 64, :], FQ8.t[0:8, :], FQ8, [FQ8], allp(QA, s), partial=True)

    sbk = Rot(PB[0:3])
    sbk4 = Rot(PB[0:3] + [PB[7]])
    obk4 = Rot(PB[3:7])
    obk2 = Rot(PB[3:5])

    SKEW = 2
    pend = []

    def pipe_flush(keep=0):
        while len(pend) > keep:
            fn = pend.pop(0)
            fn()

    def attn_core(ktf, ktb, krows, kbase, qt, vaf, vab, scale, pO, i, bias=None, swa=False, lbank=None, fin=None, deep=True):
        if swa:
            js = [j for j in range(4 * i - 1, 4 * i + 4) if j >= 0]
        else:
            js = list(range(0, 4 * i + 4))
        for idx, j in enumerate(js):
            m = j - 4 * i
            c0 = 0 if m < 0 else 128 * m
            if swa:
                n = 128 if (m < 0 or m == 3) else 256
            else:
                n = TT - c0
            ps = (sbk4 if deep else sbk).next()
            need_mask = swa or m >= 0
            specs = [(ps.t[:, c0:c0 + n], ktf(j), qt.t[kbase:kbase + krows, c0:c0 + n], True, not need_mask)]
            if swa:
                mk = msw_b[:, 128:256] if m < 0 else msw_b[:, 0:n]
                specs.append((ps.t[:, c0:c0 + n], ident_b, mk, False, True))
            elif m >= 0:
                specs.append((ps.t[:, c0:c0 + 128], ident_b, mc_b, False, True))
            MMS(specs, [qt, CMB] + ktb, [ps])
            p = PT.next()
            kw = dict(scale=scale)
            rd = [ps]
            if bias is not None:
                kw["bias"] = bias(j)
                rd.append(CSPB)
            ACT(p.t[:, c0:c0 + n], ps.t[:, c0:c0 + n], AF.Exp, rd, [p], **kw)
            first = (idx == 0)
            last = (idx == len(js) - 1)

            def pv(p=p, c0=c0, n=n, j=j, first=first, last=last):
                specs = [(pO.t[:, c0:c0 + n], vaf(j), p.t[:, c0:c0 + n], first, last)]
                wr = [pO]
                if lbank is not None:
                    specs.append((lbank.t[:, c0:c0 + n], ones_b, p.t[:, c0:c0 + n], first, last))
                    wr.append(lbank)
                MMS(specs, [p, CMB] + vab, wr)
                if last and fin is not None:
                    fin()
            pend.append(pv)
            pipe_flush(keep=(3 if deep else SKEW))

    def load_va64(src, s, col0, v):
        ap = bass.AP(YB, 4096 + v * 2048 + v * 64, [[YROW, 128], [128, NB], [1, 64]])
        P.dma("sp", ap, src.t[s, :, :, col0:col0 + 64].rearrange("b p n -> p b n"), Y.p[4 + 2 * v],
              allp(src, s), va_bufs(v), partial=True)

    def set_ones():
        for v in range(2):
            ap = bass.AP(YB, 4096 + v * 2048 + (1 - v) * 64, [[YROW, 128], [128, NB], [1, 64]])
            P.op("dve", lambda e, ap=ap: e.memset(ap, 1.0), [], va_bufs(v))

    def zero_q_half(q, half):
        P.op("dve", lambda e, q=q, half=half: e.memset(q.t[half * 64:half * 64 + 64, :], 0.0), [], [q])

    def load_q_half(src, s, ch, half, t, q):
        r0 = half * 64
        P.dma("sp", q.t[r0:r0 + 64, :], src.t[s, ch, r0:r0 + 64, tsl(t)], q, [src.p[pt(s, t)]], [q], partial=True)
        return q

    def finalize64(pO, par, lbias, r_extra, s, t, chunk):
        n0, l0 = par * 64, (1 - par) * 64
        f2 = FT.next()
        if lbias is not None:
            f1 = FT.next()
            ACT(f1.t[n0:n0 + 64, :], pO.t[l0:l0 + 64, :], AF.Ln, [pO] + list(r_extra), [f1], bias=lbias)
            ACT(f2.t[n0:n0 + 64, :], f1.t[n0:n0 + 64, :], AF.Exp, [f1], [f2], scale=-1.0)
        else:
            P.op("dve", lambda e: e.reciprocal(out=f2.t[n0:n0 + 64, :], in_=pO.t[l0:l0 + 64, :]), [pO], [f2])
        xb = XB.next()
        TTO(xb.t[n0:n0 + 64, :], pO.t[n0:n0 + 64, :], f2.t[n0:n0 + 64, :], ALU.mult, [pO, f2], [xb])
        store(OT.t[s, chunk, n0:n0 + 64, tsl(t)], xb.t[n0:n0 + 64, :], xb, [OT.p[pt(s, t)]])

    def phase_B_even(l, s):
        sc_m = 96.0 ** -0.5
        pipe_flush()
        set_ones()
        for h in range(8):
            par = h % 2
            k = par
            P.dma("sp", kt_ap(k, rows=96), KA.t[s, h, 0:96, :], Y.p[2 * k], allp(KA, s), kt_bufs(k))
            load_va64(VA, s, h * 64, par)
            for t in range(NTL):
                q = QTB.next()
                P.dma("sp", q.t[0:96, :], QA.t[s, h, 0:96, tsl(t)], q, [QA.p[pt(s, t)]], [q])
                pO = obk4.next()
                attn_core(lambda j, k=k: kt_ap(k, rows=96, c0=j * 128, n=128), kt_bufs(k), 96, 0, q,
                          lambda j, v=par: va_blk(v, j), va_bufs(par), sc_m, pO, t,
                          fin=lambda pO=pO, par=par, t=t, h=h: finalize64(pO, par, None, [], s, t, h // 2))
        for grp in range(2):
            k = grp
            P.dma("sp", kt_ap(k), KB.t[s, 0, :, :], Y.p[2 * k], allp(KB, s), kt_bufs(k))
            load_va64(VB, s, grp * 64, grp)
            pipe_flush()
            for q in QTBL:
                zero_q_half(q, 1 - grp)
            for j4 in range(4):
                horig = grp * 4 + j4
                for t in range(NTL):
                    q = load_q_half(QB, s, j4, grp, t, QTB.next())
                    pO = obk4.next()
                    attn_core(lambda j, k=k: kt_ap(k, c0=j * 128, n=128), kt_bufs(k),
                              128, 0, q, lambda j, v=grp: va_blk(v, j), va_bufs(grp), 0.125, pO, t, swa=True,
                              fin=lambda pO=pO, grp=grp, horig=horig, t=t, j4=j4: finalize64(
                                  pO, grp, SM.t[grp * 64:grp * 64 + 64, horig:horig + 1], [SM], s, t, 4 + j4))

    def phase_B_odd(l, s):
        i = l // 2
        lam_init = 0.8 - 0.6 * math.exp(-0.3 * l)
        P.dma("sp", CSPB.t[:], CSP.t[s], CSPB, [CSP.p[s]], [CSPB])
        pipe_flush()
        for qi, q in enumerate(QTBL):
            zero_q_half(q, 1 - (qi % 2))
        dcnt = 0
        for h in range(4):
            k = h % 2
            v = h % 2
            P.dma("sp", kt_ap(k), KB.t[s, h, :, :], Y.p[2 * k], allp(KB, s), kt_bufs(k))
            vap = bass.AP(YB, 4096 + v * 2048, [[YROW, 128], [128, NB], [1, 128]])
            P.dma("sp", vap, VB.t[s, :, :, h * 128:(h + 1) * 128].rearrange("b p n -> p b n"), Y.p[4 + 2 * v],
                  allp(VB, s), va_bufs(v))
            for t in range(NTL):
                dcnt += 1
                for mp in range(2):
                    q = load_q_half(QB, s, h, mp, t, QTBL[2 * (dcnt % 2) + mp])
                    pO = obk2.next()
                    pL = PB[5 + mp]

                    def fin(pO=pO, pL=pL, mp=mp, t=t, h=h):
                        f1 = FT.next()
                        ACT(f1.t[:], pL.t[:], AF.Ln, [pL], [f1])
                        f2 = FT.next()
                        ACT(f2.t[:], f1.t[:], AF.Exp, [f1], [f2], scale=-1.0)
                        TTO(CQC.t[:, mp, :], pO.t[:], f2.t[:], ALU.mult, [pO, f2], [CQC])
                        if mp == 0:
                            return
                        od = CQC.t[:, 2, :]
                        STT(od, CQC.t[:, 1, :], SM.t[:, 16 + i:17 + i], CQC.t[:, 0, :], ALU.mult, ALU.add, [CQC, SM], [CQC])
                        nb = PB[7]
                        sumsq_bank([od], [CQC], nb)
                        rs = rstd_from(nb, 128.0)
                        xb = XB.next()
                        STT(xb.t[:], od, CST.t[:, l, 53:54], rs.t[:], ALU.mult, ALU.mult, [CQC, rs, CST], [xb])
                        xb2 = XB.next()
                        TS(xb2.t[:], xb.t[:], 1.0 - lam_init, None, ALU.mult, ALU.bypass, [xb], [xb2])
                        store(OT.t[s, h, :, tsl(t)], xb2.t[:], xb2, [OT.p[pt(s, t)]])
                    attn_core(lambda j, k=k: kt_ap(k, c0=j * 128, n=128), kt_bufs(k),
                              128, 0, q, lambda j, v=v: va_blk(v, j), va_bufs(v), 0.125, pO, t, lbank=pL, fin=fin, deep=False)
        pipe_flush()
        for k in range(2):
            ap = bass.AP(YB, 64 * YROW + k * 2048, [[YROW, 1], [1, 2048]])
            P.op("dve", lambda e, ap=ap: e.memset(ap, 1.0), [], kt_bufs(k))
        set_ones()
        for h in range(8):
            par = h % 2
            k = par
            P.dma("sp", kt_ap(k, rows=64), KA.t[s, h, 0:64, :], Y.p[2 * k], allp(KA, s), kt_bufs(k), partial=True)
            load_va64(VA, s, h * 64, par)
            for t in range(NTL):
                q = QTB.next()
                P.dma("sp", q.t[0:65, :], QA.t[s, h, 0:65, tsl(t)], q, [QA.p[pt(s, t)]], [q])
                pO = obk4.next()
                attn_core(lambda j, k=k: kt_ap(k, rows=65, c0=j * 128, n=128), kt_bufs(k), 65, 0, q,
                          lambda j, v=par: va_blk(v, j), va_bufs(par), 0.125, pO, t,
                          bias=lambda j, h=h: CSPB.t[:, j, h:h + 1],
                          fin=lambda pO=pO, par=par, t=t, h=h: finalize64(pO, par, None, [], s, t, 4 + h // 2))

    def phase_C1(l, s, t):
        st_o = STG.next()
        P.dma("sp", bass.AP(st_o.t, 0, [[4096, 128], [TT, 8], [1, TT]]),
              OT.t[s, :, :, tsl(t)].rearrange("c p t -> p c t"), st_o, [OT.p[pt(s, t)]], [st_o])
        st_g = STG.next()
        P.dma("sp", bass.AP(st_g.t, 0, [[4096, 128], [TT, 8], [1, TT]]),
              GT.t[s, :, :, tsl(t)].rearrange("c p t -> p c t"), st_g, [GT.p[pt(s, t)]], [st_g])
        for c in range(8):
            TTO(HT.t[:, c, :], st_o.t[:, c * TT:(c + 1) * TT], st_g.t[:, c * TT:(c + 1) * TT], ALU.mult,
                [st_o, st_g], [HT])

    def phase_C2(l, s, t):
        load_x_tile(s, t)
        nb = PB[6]
        for g in range(8):
            pb = proj_fm(g * 128, 128, w=WOUT)
            CPY(Y.t[:, g, :], pb.t[:], [pb], [Y.p[g]])
            sq = SQ.next()
            ACT(sq.t[:], pb.t[:], AF.Square, [pb], [sq])
            MM(nb.t[:], [(ones_b, sq.t[:])], [sq, CMB], [nb], start=(g == 0), stop=(g == 7))
        return rstd_from(nb, 1024.0)

    def phase_C3(l, s, t, rs):
        for c in range(8):
            f = FT.next()
            TTO(f.t[:], Y.t[:, c, :], rs.t[:], ALU.mult, [Y.p[c], rs], [f])
            STT(XT.t[:, c, :], f.t[:], AD.t[:, l, s, 2, c:c + 1], XT.t[:, c, :], ALU.mult, ALU.add, [f, AD, XT], [XT])
        P.dma("pool", xT.t[s, :, :, tsl(t)].rearrange("c p t -> p c t"), XT.t[:], XT, [XT], [xT.p[pt(s, t)]])

    def phase_C_all(l):
        tiles = [(s, t) for s in range(2) for t in range(NTL)]
        phase_C1(l, *tiles[0])
        for idx, (s, t) in enumerate(tiles):
            rs = phase_C2(l, s, t)
            if idx + 1 < len(tiles):
                phase_C1(l, *tiles[idx + 1])
            phase_C3(l, s, t, rs)

    for l in range(NL):
        i = l // 2
        even = (l % 2 == 0)
        if even:
            ACT(SM.t[:, 0:8], CST.t[:, l, 45:53], AF.Exp, [CST], [SM])
        else:
            lam_init = 0.8 - 0.6 * math.exp(-0.3 * l)
            f = FT.next()
            TTO(f.t[:, 0:64], CST.t[:, l, 62:126], CST.t[:, l, 126:190], ALU.mult, [CST], [f])
            TTO(f.t[:, 64:128], CST.t[:, l, 190:254], CST.t[:, l, 254:318], ALU.mult, [CST], [f])
            P.op("dve", lambda e, f=f: e.reduce_sum(out=SM.t[:, 20:22],
                                                 in_=f.t[:, 0:128].rearrange("p (a b) -> p a b", a=2),
                                                 axis=mybir.AxisListType.X), [f], [SM])
            ACT(SM.t[:, 22:24], SM.t[:, 20:22], AF.Exp, [SM], [SM])
            TTO(SM.t[:, 24:25], SM.t[:, 23:24], SM.t[:, 22:23], ALU.subtract, [SM], [SM])
            TS(SM.t[:, 16 + i:17 + i], SM.t[:, 24:25], -lam_init, None, ALU.add, ALU.bypass, [SM], [SM])
        if STOP < 2:
            continue
        for s in range(2):
            for t in range(NTL):
                if ONE_TILE and (s, t) != (0, 0):
                    continue
                ada_bg_start(l + 1, s * NTL + t)
                (phase_A_even if even else phase_A_odd)(l, s, t)
                ada_bg_end(l + 1, s * NTL + t)
            if not even and not ONE_TILE:
                fox_post(l, s)
        if l + 1 < NL:
            load_weights(l + 1)
        if STOP < 3:
            continue
        for s in range(2):
            (phase_B_even if even else phase_B_odd)(l, s)
        pipe_flush()
        if STOP < 4:
            continue
        phase_C_all(l)
        if l + 1 < NL:
            load_wout(l + 1)

    for s in range(0 if not SKIP_XPRO else 2, 2):
        for t in range(NTL):
            load_x_tile(s, t)
            for b in range(4):
                for half in range(2):
                    pb = projb.next()

                    def fn(e, b=b, half=half, pb=pb):
                        ins = None
                        for cc in range(4):
                            c = half * 4 + cc
                            ins = e.transpose(out=pb.t[:, cc * 128:(cc + 1) * 128],
                                              in_=XT.t[:, c, b * 128:(b + 1) * 128], identity=ident_f)
                        return ins
                    P.op("pe", fn, [XT, CMF], [pb])
                    ft = FT.next()
                    CPY(ft.t[:], pb.t[:], [pb], [ft], eng=("dve" if half else "act"))
                    tok0 = t * TT + b * 128
                    P.dma("pool", out_d[s, tok0:tok0 + 128, half * 512:(half + 1) * 512], ft.t[:], ft, [ft], [])
    P.emit()
    return nc


def _kc(w):
    K, N = w.shape
    return np.ascontiguousarray(w.reshape(K // 128, 128, N).transpose(1, 0, 2))


def _o_perm_even():
    perm = np.zeros(1024, np.int64)
    for f in range(1024):
        cf, r = divmod(f, 128)
        if cf < 4:
            perm[f] = f
        else:
            j = cf - 4
            head = j if r < 64 else 4 + j
            perm[f] = 512 + head * 64 + (r % 64)
    return perm


def _host_consts():
    pos = np.arange(S, dtype=np.float32)
    tab = np.zeros((128, 4, S), np.float32)
    inv64 = (1.0 / (np.float32(10000.0) ** (np.arange(0, 64, 2, dtype=np.float32) / np.float32(64)))).astype(np.float32)
    inv32 = (1.0 / (np.float32(10000.0) ** (np.arange(0, 32, 2, dtype=np.float32) / np.float32(32)))).astype(np.float32)
    for r in range(128):
        i = r % 64
        ang = (pos * inv64[i % 32]).astype(np.float32)
        tab[r, 0] = np.cos(ang)
        tab[r, 1] = -np.sin(ang) if i < 32 else np.sin(ang)
    for r in range(64, 96):
        j = r - 64
        ang = (pos * inv32[j % 16]).astype(np.float32)
        tab[r, 2] = np.cos(ang)
        tab[r, 3] = -np.sin(ang) if j < 16 else np.sin(ang)
    cmf = np.zeros((128, 384), np.float32)
    cmf[:, 0:128] = np.eye(128, dtype=np.float32)
    k = np.arange(128)
    cmf[:, 128:256] = (k[:, None] <= k[None, :]).astype(np.float32)
    cmf[:, 256:384] = 1.0
    cmb = np.zeros((128, 896), np.float32)
    cmb[:, 0:128] = np.eye(128, dtype=np.float32)
    for m in range(128):
        sw = m + 32 if (m % 64) < 32 else m - 32
        cmb[sw, 128 + m] = 1.0
    for m in range(128):
        if 64 <= m < 96:
            sw = m + 16 if (m - 64) < 16 else m - 16
        else:
            sw = m
        cmb[sw, 256 + m] = 1.0
    cmb[:, 384:512] = np.where(k[:, None] > k[None, :], NEG, 0.0)
    cmb[:, 512:640] = np.where(k[:, None] > k[None, :], NEG, 0.0)
    cmb[:, 640:768] = np.where(k[:, None] <= k[None, :], NEG, 0.0)
    cmb[:, 768:896] = 1.0
    return tab, cmf, cmb


def _prep_shared(inp):
    f = lambda a: np.asarray(a, dtype=np.float32)
    sh = {}
    w_ada = f(inp["w_ada"])
    sh["wada"] = np.ascontiguousarray(w_ada.reshape(4, 8, 128, 3072).transpose(0, 2, 1, 3))
    perm_e = _o_perm_even()
    wE, wUQ, wUK, wUV, wOe = [], [], [], [], []
    for i in range(2):
        w = f(inp["ev_w_in"][i])
        cols = [w[:, 0:384], w[:, 384:640], np.zeros((1024, 64), np.float32), w[:, 640:672]]
        sq = w[:, 672:1184].reshape(1024, 8, 64)
        order = []
        for j in range(4):
            order += [j, 4 + j]
        cols.append(sq[:, order, :].reshape(1024, 512))
        cols.append(w[:, 1184:1312])
        cols.append(w[:, 1312:1440])
        cols.append(w[:, 1440:2464][:, perm_e])
        wE.append(_kc(np.concatenate(cols, axis=1)))
        wUQ.append(_kc(np.concatenate([f(inp["ev_w_uq"][i]), np.zeros((384, 32), np.float32)], axis=1)))
        ukv = f(inp["ev_w_ukv"][i]).reshape(256, 8, 128)
        wUK.append(_kc(np.ascontiguousarray(ukv[:, :, 0:64]).reshape(256, 512)))
        wUV.append(_kc(np.ascontiguousarray(ukv[:, :, 64:128]).reshape(256, 512)))
        wOe.append(_kc(f(inp["ev_w_out"][i])[perm_e, :]))
    sh["wE"] = np.stack(wE)
    sh["wUQ"] = np.stack(wUQ)
    sh["wUK"] = np.stack(wUK)
    sh["wUV"] = np.stack(wUV)
    sh["wOe"] = np.stack(wOe)
    wO, wOo = [], []
    for i in range(2):
        w = f(inp["od_w_in"][i])
        cols = [w[:, 0:3080], np.zeros((1024, 8), np.float32), w[:, 3080:4104]]
        wO.append(_kc(np.concatenate(cols, axis=1)))
        wOo.append(_kc(f(inp["od_w_out"][i])))
    sh["wO"] = np.stack(wO)
    sh["wOo"] = np.stack(wOo)
    cst = np.zeros((128, 4, NC_CST), np.float32)
    for l in range(4):
        i = l // 2
        cst[:, l, 0:8] = f(inp["g_pre"][l]).reshape(8, 128).T
        cst[:, l, 8:16] = f(inp["g_post"][l]).reshape(8, 128).T
        cst[:, l, 16:40] = f(inp["b_ada"][l]).reshape(24, 128).T
        if l % 2 == 0:
            cst[:, l, 40:43] = f(inp["ev_q_norm"][i]).reshape(3, 128).T
            cst[:, l, 43:45] = f(inp["ev_kv_norm"][i]).reshape(2, 128).T
            cst[:, l, 45:53] = f(inp["ev_sinks"][i])[None, :]
        else:
            cst[:, l, 53] = f(inp["od_subln"][i])
            cst[:, l, 54:62] = f(inp["od_forget_bias"][i])[None, :]
            cst[:, l, 62:318] = f(inp["od_lambda"][i]).reshape(256)[None, :]
    sh["cst"] = cst
    tab, cmf, cmb = _host_consts()
    sh["tab"] = tab
    sh["cmf"] = cmf
    sh["cmb"] = cmb
    return sh


_NC_CACHE = {}


def run(inputs, NL=4, cores=8):
    x = np.asarray(inputs["x"], dtype=np.float32)
    c = np.asarray(inputs["c"], dtype=np.float32)
    sh = _prep_shared(inputs)
    in_maps = []
    for core in range(cores):
        m = dict(sh)
        m["x"] = np.ascontiguousarray(x[2 * core:2 * core + 2])
        cc = c[2 * core:2 * core + 2]
        m["cT"] = np.ascontiguousarray(cc.T.reshape(8, 128, 2).transpose(1, 0, 2))
        in_maps.append(m)
    if NL not in _NC_CACHE:
        _NC_CACHE[NL] = build(NL)
    res = run_bass_kernel_spmd(_NC_CACHE[NL], in_maps, core_ids=list(range(cores)))
    if DEBUG:
        return res.results
    return np.concatenate([r["out"] for r in res.results], axis=0).astype(np.float32)


def kernel(**inputs):
    return run(inputs, NL=4, cores=8)
```

```python
from contextlib import ExitStack
import math
import numpy as np
import concourse.bass as bass
import concourse.mybir as mybir
from concourse.bass_utils import run_bass_kernel_spmd

F32 = mybir.dt.float32
BF16 = mybir.dt.bfloat16
AF = mybir.ActivationFunctionType
ALU = mybir.AluOpType

S = 2048
TT = 512
NTL = S // TT
NB = S // 128
EPS = 1e-6
NC_CST = 320
NE = 2528
NO = 4112
NEG = -30000.0
STOP = 9
ONE_TILE = False
DEBUG = False
ASTOP = 99
SKIP_XPRO = False


class Buf:
    __slots__ = ("name", "writers", "readers", "excl")

    def __init__(self, name, excl=False):
        self.name = name
        self.excl = excl
        self.writers = {}
        self.readers = {}


class Tens:
    def __init__(self, t, name, parts, excl=False):
        self.t = t
        self.name = name
        self.p = [Buf(f"{name}.{i}", excl) for i in range(parts)]


ENGS = ("pe", "act", "dve", "pool", "sp")


class Prog:
    def __init__(self, nc, same_engine_sync=True):
        self.nc = nc
        self.es = ExitStack()
        self.ops = {e: [] for e in ENGS}
        self.cnt = {e: 0 for e in ENGS}
        self.known = {e: {} for e in ENGS}
        self.snap = {}
        self.chan_cnt = {}
        self.sem = {}
        self.same_engine_sync = same_engine_sync

    def sbuf(self, name, shape, dtype, parts=1):
        t = self.es.enter_context(self.nc.sbuf_tensor("sb_" + name, list(shape), dtype))
        return Tens(t, name, parts)

    def psum(self, name, shape, dtype, parts=1):
        t = self.es.enter_context(self.nc.psum_tensor("ps_" + name, list(shape), dtype))
        return Tens(t, name, parts, excl=True)

    def dram(self, name, shape, dtype, kind="Internal", parts=1):
        t = self.nc.dram_tensor(name, list(shape), dtype, kind=kind).ap()
        return Tens(t, name, parts)

    @staticmethod
    def _flat(xs):
        out = []
        for x in xs:
            if isinstance(x, Tens):
                out.extend(x.p)
            elif isinstance(x, (list, tuple)):
                out.extend(Prog._flat(x))
            else:
                out.append(x)
        return out

    def _deps(self, eng, reads, writes, skip=None):
        deps = {}

        def add(d):
            for k, v in d.items():
                if deps.get(k, 0) < v:
                    deps[k] = v

        for b in reads:
            add(b.writers)
        for b in writes:
            add(b.writers)
            add(b.readers)
        waits = []
        kn = self.known[eng]
        for k, v in deps.items():
            if k == skip:
                continue
            if k == eng and (eng == "pe" or not self.same_engine_sync):
                continue
            if kn.get(k, 0) >= v:
                continue
            waits.append((k, v))
        for k, v in waits:
            kn[k] = max(kn.get(k, 0), v)
            sn = self.snap.get((k, v))
            if sn:
                for k2, v2 in sn.items():
                    if kn.get(k2, 0) < v2:
                        kn[k2] = v2
        return waits

    def _mark(self, tok, reads, writes, partial):
        k, v = tok
        for b in reads:
            if b.readers.get(k, 0) < v:
                b.readers[k] = v
        for b in writes:
            if partial:
                b.writers[k] = v
            else:
                b.writers = {k: v}
                b.readers = {}

    def op(self, eng, fn, reads=(), writes=()):
        reads = self._flat(reads)
        writes = self._flat(writes)
        writes = writes + [b for b in reads if b.excl and b not in writes]
        reads = [b for b in reads if not b.excl]
        waits = self._deps(eng, reads, writes)
        self.cnt[eng] += 1
        tok = (eng, self.cnt[eng])
        sn = dict(self.known[eng])
        sn[eng] = self.cnt[eng]
        self.snap[tok] = sn
        self._mark(tok, reads, writes, False)
        self.ops[eng].append(("op", waits, fn, tok))
        return tok

    def dma(self, eng, out, in_, sb, reads=(), writes=(), partial=False, **kw):
        reads = self._flat(reads)
        writes = self._flat(writes)
        sbb = self._flat([sb])[0]
        chan = "d_" + sbb.name
        waits = self._deps(eng, reads, writes, skip=(chan if partial else None))
        self.chan_cnt[chan] = self.chan_cnt.get(chan, 0) + 16
        tok = (chan, self.chan_cnt[chan])
        self.snap[tok] = dict(self.known[eng])
        self._mark(tok, reads, writes, partial)
        self.ops[eng].append(("dma", waits, (out, in_, kw), tok))
        return tok

    def emit(self):
        nc = self.nc
        for k in list(ENGS) + list(self.chan_cnt):
            if k not in self.sem:
                self.sem[k] = self.es.enter_context(nc.semaphore("s_" + k.replace(".", "_")))
        final = list(self.chan_cnt.items())

        def run(ename, eng):
            for kind, waits, payload, tok in self.ops[ename]:
                for k, v in waits:
                    eng.wait_ge(self.sem[k], v)
                if kind == "op":
                    payload(eng).then_inc(self.sem[tok[0]], 1)
                else:
                    out, in_, kw = payload
                    eng.dma_start(out=out, in_=in_, **kw).then_inc(self.sem[tok[0]], 16)
            if ename == "sp":
                for c, v in final:
                    eng.wait_ge(self.sem[c], v)

        with nc.Block() as block:
            @block.sync
            def _(e):
                run("sp", e)

            @block.scalar
            def _(e):
                run("act", e)

            @block.vector
            def _(e):
                run("dve", e)

            @block.gpsimd
            def _(e):
                run("pool", e)

            @block.tensor
            def _(e):
                run("pe", e)
        self.es.close()


class Rot:
    def __init__(self, items):
        self.items = items
        self.i = 0

    def next(self):
        x = self.items[self.i % len(self.items)]
        self.i += 1
        return x


def build(NL=4):
    nc = bass.Bass("TRN2", target_bir_lowering=False)
    P = Prog(nc)

    def din(name, shape, dt=F32):
        return nc.dram_tensor(name, list(shape), dt, kind="ExternalInput").ap()

    x_d = din("x", [2, S, 1024])
    cT_d = din("cT", [128, 8, 2])
    wada_d = din("wada", [4, 128, 8, 3072])
    cst_d = din("cst", [128, 4, NC_CST])
    cmf_d = din("cmf", [128, 384])
    cmb_d = din("cmb", [128, 896])
    tab_d = din("tab", [128, 4, S])
    wE_d = din("wE", [2, 128, 8, NE])
    wUQ_d = din("wUQ", [2, 128, 3, 800])
    wUK_d = din("wUK", [2, 128, 2, 512])
    wUV_d = din("wUV", [2, 128, 2, 512])
    wOe_d = din("wOe", [2, 128, 8, 1024])
    wO_d = din("wO", [2, 128, 8, NO])
    wOo_d = din("wOo", [2, 128, 8, 1024])
    out_d = nc.dram_tensor("out", [2, S, 1024], F32, kind="ExternalOutput").ap()

    def scr(name, shape, dt, parts=2 * NTL):
        return P.dram(name, shape, dt, kind=("ExternalOutput" if DEBUG else "Internal"), parts=parts)

    xT = scr("xT", [2, 8, 128, S], F32)
    QA = scr("QA", [2, 8, 128, S], BF16)
    KA = scr("KA", [2, 8, 128, S], BF16)
    QB = scr("QB", [2, 4, 128, S], BF16)
    KB = scr("KB", [2, 4, 128, S], BF16)
    VA = scr("VA", [2, NB, 128, 512], BF16)
    VB = scr("VB", [2, NB, 128, 512], BF16)
    GT = scr("GT", [2, 8, 128, S], BF16)
    OT = scr("OT", [2, 8, 128, S], BF16)
    CSP = scr("CSP", [2, 128, NB, 8], F32, parts=2)

    def pt(s, t):
        return s * NTL + t

    def allp(T, s):
        return [T.p[pt(s, t)] for t in range(NTL)]

    WIN = P.sbuf("WIN", [128, 8, NO], BF16)
    WOUT = P.sbuf("WOUT", [128, 8, 1024], BF16)
    WUQ = P.sbuf("WUQ", [128, 3, 800], BF16)
    WUK = P.sbuf("WUK", [128, 2, 512], BF16)
    WUV = P.sbuf("WUV", [128, 2, 512], BF16)
    CST = P.sbuf("CST", [128, 4, NC_CST], F32)
    AD = P.sbuf("AD", [128, 4, 2, 3, 8], F32)
    CMF = P.sbuf("CMF", [128, 384], F32)
    CMB = P.sbuf("CMB", [128, 896], BF16)
    TAB = P.sbuf("TAB", [128, 4, TT], F32)
    XT = P.sbuf("XT", [128, 8, TT], F32)
    HT = P.sbuf("HT", [128, 8, TT], BF16)
    SQ = Rot([P.sbuf(f"SQ{i}", [128, TT], BF16) for i in range(2)])
    FT = Rot([P.sbuf(f"FT{i}", [128, TT], F32) for i in range(4)])
    RSB = Rot([P.sbuf(f"RSB{i}", [128, TT], F32) for i in range(2)])
    STG = Rot([P.sbuf(f"STG{i}", [128, 4096], BF16) for i in range(2)])
    CQC = P.sbuf("CQC", [128, 5, TT], F32)
    CQN = P.sbuf("CQN", [128, 5, TT], BF16)
    XB = Rot([P.sbuf(f"XB{i}", [128, TT], BF16) for i in range(2)])
    Y = P.sbuf("Y", [128, 8, TT], F32, parts=8)
    YB = Y.t.bitcast(BF16)
    QTBL = [P.sbuf(f"QTB{i}", [128, TT], BF16) for i in range(4)]
    QTB = Rot(QTBL)
    PT = Rot([P.sbuf(f"PT{i}", [128, TT], BF16) for i in range(4)])
    CSPB = P.sbuf("CSPB", [128, NB, 8], F32)
    SPT = P.sbuf("SPT", [128, NB, 8], F32)
    PREF = P.sbuf("PREF", [128, NB, 8], F32)
    FQ8 = P.sbuf("FQ8", [128, S], BF16)
    SM = P.sbuf("SM", [128, 64], F32)
    CTB = P.sbuf("CTB", [128, 8, 2], BF16)
    CTF = P.sbuf("CTF", [128, 8, 2], F32)
    MODF = P.sbuf("MODF", [128, 4, 24, 2], F32)

    PB = [P.psum(f"B{i}", [128, 512], F32) for i in range(8)]

    ident_f = CMF.t[:, 0:128]
    tri_f = CMF.t[:, 128:256]
    ones_f = CMF.t[:, 256:384]
    ident_b = CMB.t[:, 0:128]
    perm64 = CMB.t[:, 128:256]
    perm32 = CMB.t[:, 256:384]
    mc_b = CMB.t[:, 384:512]
    msw_b = CMB.t[:, 512:768]
    ones_b = CMB.t[:, 768:896]

    YROW = 8192

    def kt_ap(k, rows=128, p0=0, c0=0, n=2048):
        return bass.AP(YB, p0 * YROW + k * 2048 + c0, [[YROW, rows], [1, n]])

    def kt_bufs(k):
        return [Y.p[2 * k], Y.p[2 * k + 1]]

    def va_blk(v, j):
        return bass.AP(YB, 4096 + v * 2048 + j * 128, [[YROW, 128], [1, 128]])

    def va_bufs(v):
        return [Y.p[4 + 2 * v], Y.p[5 + 2 * v]]

    def ACT(out, in_, func, r, w, **kw):
        P.op("act", lambda e: e.activation(out=out, in_=in_, func=func, **kw), r, w)

    def TTO(out, a, b, op, r, w, eng="dve"):
        P.op(eng, lambda e: e.tensor_tensor(out=out, in0=a, in1=b, op=op), r, w)

    def TS(out, a, s1, s2, op0, op1, r, w, eng="dve"):
        P.op(eng, lambda e: e.tensor_scalar(out=out, in0=a, scalar1=s1, scalar2=s2, op0=op0, op1=op1), r, w)

    def STT(out, a, s, b, op0, op1, r, w):
        P.op("dve", lambda e: e.scalar_tensor_tensor(out=out, in0=a, scalar=s, in1=b, op0=op0, op1=op1), r, w)

    def CPY(out, in_, r, w, eng="dve"):
        if eng == "act":
            P.op("act", lambda e: e.activation(out=out, in_=in_, func=AF.Identity), r, w)
        else:
            P.op(eng, lambda e: e.tensor_copy(out=out, in_=in_), r, w)

    def MM(out, pairs, r, w, start=True, stop=True):
        pairs = list(pairs)

        def fn(e):
            n = len(pairs)
            ins = None
            for i, (l, rr) in enumerate(pairs):
                ins = e.matmul(out, lhsT=l, rhs=rr, start=(start and i == 0), stop=(stop and i == n - 1))
            return ins
        P.op("pe", fn, r, w)

    def MMS(specs, r, w):
        specs = list(specs)

        def fn(e):
            ins = None
            for (o, l, rr, st, sp) in specs:
                ins = e.matmul(o, lhsT=l, rhs=rr, start=st, stop=sp)
            return ins
        P.op("pe", fn, r, w)

    projb = Rot(PB[0:5])

    def rstd_from(ps, n_feat):
        t1 = FT.next()
        ACT(t1.t[:], ps.t[:], AF.Ln, [ps], [t1], scale=1.0 / n_feat, bias=EPS)
        t2 = RSB.next()
        ACT(t2.t[:], t1.t[:], AF.Exp, [t1], [t2], scale=-0.5)
        return t2

    CASTKW = dict(max_dma_last_dim=4096)

    def load_weights(l):
        i = l // 2
        if l % 2 == 0:
            for c in range(8):
                P.dma("pool", WIN.t[:, c, 0:NE], wE_d[i, :, c, :], WIN, [], [WIN], partial=(c > 0), **CASTKW)
            P.dma("pool", WUQ.t[:], wUQ_d[i], WUQ, [], [WUQ], **CASTKW)
            P.dma("pool", WUK.t[:], wUK_d[i], WUK, [], [WUK], **CASTKW)
            P.dma("pool", WUV.t[:], wUV_d[i], WUV, [], [WUV], **CASTKW)
        else:
            for c in range(8):
                P.dma("pool", WIN.t[:, c, :], wO_d[i, :, c, :], WIN, [], [WIN], partial=(c > 0), **CASTKW)

    def load_wout(l):
        i = l // 2
        src = wOe_d if l % 2 == 0 else wOo_d
        for c in range(8):
            P.dma("pool", WOUT.t[:, c, :], src[i, :, c, :], WOUT, [], [WOUT], partial=(c > 0), **CASTKW)

    P.dma("sp", CST.t[:], cst_d, CST, [], [CST])
    P.dma("sp", CMF.t[:], cmf_d, CMF, [], [CMF])
    P.dma("pool", CMB.t[:], cmb_d, CMB, [], [CMB])
    P.dma("sp", CTF.t[:], cT_d, CTF, [], [CTF])
    ACT(CTB.t[:], CTF.t[:], AF.Silu, [CTF], [CTB])

    load_weights(0)

    xits = [(s, t, hb) for s in range(0 if not SKIP_XPRO else 2, 2) for t in range(NTL) for hb in range(2)]

    def x_load(it):
        s, t, hb = xits[it]
        stg = XT if it % 2 == 0 else Y
        xin = bass.AP(stg.t, 0, [[8 * TT, 128], [1024, 2], [1, 1024]])
        tok0 = t * TT + hb * 256
        src = x_d[s, tok0:tok0 + 256, :].rearrange("(b p) d -> p b d", p=128)
        P.dma("sp", xin, src, stg.p[0], [], [stg])

    if xits:
        x_load(0)
    for it, (s, t, hb) in enumerate(xits):
        if it + 1 < len(xits):
            x_load(it + 1)
        stg = XT if it % 2 == 0 else Y
        tok0 = t * TT + hb * 256
        for c in range(8):
            pb = projb.next()

            def fn(e, c=c, pb=pb, stg=stg):
                ins = None
                for b in range(2):
                    a_in = bass.AP(stg.t, b * 1024 + c * 128, [[8 * TT, 128], [1, 128]])
                    ins = e.transpose(out=pb.t[:, b * 128:(b + 1) * 128], in_=a_in, identity=ident_f)
                return ins
            P.op("pe", fn, [stg, CMF], [pb])
            ft = FT.next()
            CPY(ft.t[:, 0:256], pb.t[:, 0:256], [pb], [ft], eng=("dve" if c % 2 else "act"))
            P.dma("sp", xT.t[s, c, :, tok0:tok0 + 256], ft.t[:, 0:256], ft, [ft], [xT.p[pt(s, t)]], partial=True)

    def ada_piece_dma(l, pc, dst_ap, sbt, bufs):
        P.dma("pool", dst_ap, wada_d[l, :, :, pc * 512:(pc + 1) * 512], sbt, [], bufs, **CASTKW)

    def ada_piece_mm(l, pc, lw_fn, bufs):
        pb = projb.next()
        specs = []
        for g4 in range(4):
            for kc in range(8):
                specs.append((pb.t[:, 2 * g4:2 * g4 + 2], lw_fn(kc, g4), CTB.t[:, kc, :], kc == 0, kc == 7))
        MMS(specs, list(bufs) + [CTB], [pb])
        for g4 in range(4):
            g = pc * 4 + g4
            TS(MODF.t[:, l, g, :], pb.t[:, 2 * g4:2 * g4 + 2], CST.t[:, l, 16 + g:17 + g], None, ALU.add, ALU.bypass,
               [pb, CST], [MODF])

    def ada_finish(l):
        for b in range(2):
            a1 = AD.t[:, l, b, 0, :]
            TS(a1, MODF.t[:, l, 8:16, b], 1.0, None, ALU.add, ALU.bypass, [MODF], [AD])
            TTO(a1, a1, CST.t[:, l, 0:8], ALU.mult, [AD, CST], [AD])
            CPY(AD.t[:, l, b, 1, :], MODF.t[:, l, 0:8, b], [MODF], [AD])
            TTO(AD.t[:, l, b, 2, :], MODF.t[:, l, 16:24, b], CST.t[:, l, 8:16], ALU.mult, [MODF, CST], [AD])

    if NL > 0:
        for pc in range(6):
            wbuf = STG.next()
            ada_piece_dma(0, pc, bass.AP(wbuf.t, 0, [[4096, 128], [512, 8], [1, 512]]), wbuf, [wbuf])
            ada_piece_mm(0, pc, lambda kc, g4, wbuf=wbuf: bass.AP(wbuf.t, kc * 512 + g4 * 128, [[4096, 128], [1, 128]]), [wbuf])
        ada_finish(0)

    def yhalf_bufs(k):
        return [Y.p[4 * k + j] for j in range(4)]

    def ada_bg_start(l, ti):
        if l >= NL or ti >= 6:
            return
        k = ti % 2
        ada_piece_dma(l, ti, bass.AP(YB, k * 4096, [[YROW, 128], [512, 8], [1, 512]]), Y.p[4 * k], yhalf_bufs(k))

    def ada_bg_end(l, ti):
        if l >= NL or ti >= 6:
            return
        k = ti % 2
        ada_piece_mm(l, ti, lambda kc, g4, k=k: bass.AP(YB, k * 4096 + kc * 512 + g4 * 128, [[YROW, 128], [1, 128]]),
                     yhalf_bufs(k))
        if ti == 5:
            ada_finish(l)

    load_wout(0)

    def tsl(t):
        return slice(t * TT, (t + 1) * TT)

    def load_x_tile(s, t):
        P.dma("sp", XT.t[:], xT.t[s, :, :, tsl(t)].rearrange("c p t -> p c t"), XT, [xT.p[pt(s, t)]], [XT])

    def sumsq_bank(srcs, reads, bank):
        n = len(srcs)
        for i, a in enumerate(srcs):
            sq = SQ.next()
            ACT(sq.t[:], a, AF.Square, reads, [sq])
            MM(bank.t[:], [(ones_b, sq.t[:])], [sq, CMB], [bank], start=(i == 0), stop=(i == n - 1))

    rs_next = {}

    SQ8 = None

    def norm_sq(s, t):
        load_x_tile(s, t)
        bufs = PT.items + QTBL
        for c in range(8):
            ACT(bufs[c].t[:], XT.t[:, c, :], AF.Square, [XT], [bufs[c]])

    def norm_mm(s, t):
        bufs = PT.items + QTBL
        nb = PB[5]
        for c in range(8):
            MM(nb.t[:], [(ones_b, bufs[c].t[:])], [bufs[c], CMB], [nb], start=(c == 0), stop=(c == 7))
        rs = rstd_from(nb, 1024.0)
        for c in range(8):
            TTO(XT.t[:, c, :], XT.t[:, c, :], rs.t[:], ALU.mult, [XT, rs], [XT])
        rs_next[(s, t)] = rs

    def norm_part(s, t):
        norm_sq(s, t)
        norm_mm(s, t)

    def next_tile(s, t):
        if t + 1 < NTL:
            return (s, t + 1)
        if s == 0:
            return (1, 0)
        return None

    def phase_A_common(l, s, t):
        if (s, t) not in rs_next:
            norm_part(s, t)
        rs_next.pop((s, t))
        P.dma("sp", TAB.t[:], tab_d[:, :, tsl(t)], TAB, [], [TAB])
        for c in range(8):
            if c % 2 == 0:
                ACT(HT.t[:, c, :], XT.t[:, c, :], AF.Identity, [XT, AD], [HT], scale=AD.t[:, l, s, 0, c:c + 1],
                    bias=AD.t[:, l, s, 1, c:c + 1])
            else:
                TS(HT.t[:, c, :], XT.t[:, c, :], AD.t[:, l, s, 0, c:c + 1], AD.t[:, l, s, 1, c:c + 1], ALU.mult, ALU.add,
                   [XT, AD], [HT])

    def hoist_sq(s, t):
        nt = next_tile(s, t)
        if nt is not None and not ONE_TILE:
            norm_sq(*nt)

    def hoist_mm(s, t):
        nt = next_tile(s, t)
        if nt is not None and not ONE_TILE:
            norm_mm(*nt)

    def proj_fm(col0, M, reads_extra=(), kc=8, w=None, rhs=None):
        w = w or WIN
        pb = projb.next()
        pairs = []
        for c in range(kc):
            r = HT.t[:, c, :] if rhs is None else rhs(c)
            pairs.append((w.t[:, c, col0:col0 + M], r))
        MM(pb.t[0:M, :], pairs, [w, HT] + list(reads_extra), [pb])
        return pb

    def rope(pbx, rows, tabi, perm, out_ap, out_w):
        r0, r1 = rows
        kmax = 128
        xb = XB.next()
        ACT(xb.t[0:kmax, :], pbx.t[0:kmax, :], AF.Identity, [pbx], [xb])
        pb2 = projb.next()
        MM(pb2.t[0:kmax, :], [(perm[0:kmax, 0:kmax], xb.t[0:kmax, :])], [xb, CMB], [pb2])
        f1 = FT.next()
        TTO(f1.t[r0:r1, :], pbx.t[r0:r1, :], TAB.t[r0:r1, tabi, :], ALU.mult, [pbx, TAB], [f1])
        f2 = FT.next()
        TTO(f2.t[r0:r1, :], pb2.t[r0:r1, :], TAB.t[r0:r1, tabi + 1, :], ALU.mult, [pb2, TAB], [f2])
        TTO(out_ap, f1.t[r0:r1, :], f2.t[r0:r1, :], ALU.add, [f1, f2], out_w)

    def store(dst, src, sbt, dparts, partial=True):
        P.dma("pool", dst, src, sbt, [sbt], dparts, partial=partial)

    def gate_proj(l, s, t, col0):
        st = STG.next()
        for c in range(8):
            pb = proj_fm(col0 + c * 128, 128)
            ACT(st.t[:, c * TT:(c + 1) * TT], pb.t[:], AF.Silu, [pb], [st])
        store(GT.t[s, :, :, tsl(t)].rearrange("c p t -> p c t"),
              bass.AP(st.t, 0, [[4096, 128], [TT, 8], [1, TT]]), st, [GT.p[pt(s, t)]])

    def v_proj(s, t, col0, ncol, dst, lhs_src=None, w=None, kc=8):
        w = w or WIN
        st = STG.next()
        for b in range(4):
            pb = projb.next()
            pairs = []
            for c in range(kc):
                lt = HT.t[:, c, b * 128:(b + 1) * 128] if lhs_src is None else lhs_src(c, b)
                pairs.append((lt, w.t[:, c, col0:col0 + ncol]))
            MM(pb.t[:, 0:ncol], pairs, [w, HT, CQN], [pb])
            CPY(st.t[:, b * 512:b * 512 + ncol], pb.t[:, 0:ncol], [pb], [st], eng=("dve" if b % 2 else "act"))
        store(dst.t[s, 4 * t:4 * t + 4, :, 0:ncol].rearrange("b p n -> p b n"),
              bass.AP(st.t, 0, [[4096, 128], [512, 4], [1, ncol]]), st, [dst.p[pt(s, t)]])

    def phase_A_even(l, s, t):
        phase_A_common(l, s, t)
        if ASTOP < 2:
            return
        sqb = PT.items + QTBL
        for j in range(5):
            pb = proj_fm((0 if j < 3 else 384 - 3 * 128) + j * 128, 128)
            CPY(CQC.t[:, j, :], pb.t[:], [pb], [CQC])
            ACT(sqb[j].t[:], pb.t[:], AF.Square, [pb], [sqb[j]])
        for (c0, nch, ncol, nfeat) in ((0, 3, 40, 384.0), (3, 2, 43, 256.0)):
            nb = PB[6]
            for j in range(nch):
                MM(nb.t[:], [(ones_b, sqb[c0 + j].t[:])], [sqb[c0 + j], CMB], [nb], start=(j == 0), stop=(j == nch - 1))
            rs = rstd_from(nb, nfeat)
            for j in range(nch):
                STT(CQN.t[:, c0 + j, :], CQC.t[:, c0 + j, :], CST.t[:, l, ncol + j:ncol + j + 1], rs.t[:],
                    ALU.mult, ALU.mult, [CQC, CST, rs], [CQN])
        st = STG.next()
        for j in range(4):
            pb = proj_fm(736 + j * 128, 128)
            rope(pb, (0, 128), 0, perm64, st.t[:, j * TT:(j + 1) * TT], [st])
        pb = proj_fm(1248, 128)
        rope(pb, (0, 128), 0, perm64, st.t[:, 4 * TT:5 * TT], [st])
        store(QB.t[s, :, :, tsl(t)].rearrange("c p t -> p c t"),
              bass.AP(st.t, 0, [[4096, 128], [TT, 4], [1, TT]]), st, [QB.p[pt(s, t)]])
        store(KB.t[s, 0, :, tsl(t)], st.t[:, 4 * TT:5 * TT], st, [KB.p[pt(s, t)]])
        if ASTOP < 7:
            return
        hoist_sq(s, t)
        v_proj(s, t, 1376, 128, VB)
        if ASTOP < 8:
            return
        gate_proj(l, s, t, 1504)
        hoist_mm(s, t)
        if ASTOP < 3:
            return
        st = STG.next()
        for h in range(8):
            pb = proj_fm(h * 96, 128, w=WUQ, kc=3, rhs=lambda c: CQN.t[:, c, :], reads_extra=[CQN])
            CPY(st.t[0:64, h * TT:(h + 1) * TT], pb.t[0:64, :], [pb], [st])
            rope(pb, (64, 96), 2, perm32, st.t[64:96, h * TT:(h + 1) * TT], [st])
        store(QA.t[s, :, 0:96, tsl(t)].rearrange("h p t -> p h t"),
              bass.AP(st.t, 0, [[4096, 96], [TT, 8], [1, TT]]), st, [QA.p[pt(s, t)]])
        if ASTOP < 4:
            return
        st = STG.next()
        for hp in range(4):
            pb = proj_fm(hp * 128, 128, w=WUK, kc=2, rhs=lambda c: CQN.t[:, 3 + c, :], reads_extra=[CQN])
            for e2 in range(2):
                h = 2 * hp + e2
                CPY(st.t[0:64, h * TT:(h + 1) * TT], pb.t[e2 * 64:e2 * 64 + 64, :], [pb], [st],
                    eng=("dve" if e2 else "act"))
        pb = proj_fm(640, 128)
        xk = XB.next()
        rope(pb, (64, 96), 2, perm32, xk.t[64:96, :], [xk])
        for h in range(8):
            CPY(st.t[64:96, h * TT:(h + 1) * TT], xk.t[64:96, :], [xk], [st], eng="pool")
        store(KA.t[s, :, 0:96, tsl(t)].rearrange("h p t -> p h t"),
              bass.AP(st.t, 0, [[4096, 96], [TT, 8], [1, TT]]), st, [KA.p[pt(s, t)]])
        if ASTOP < 5:
            return
        v_proj(s, t, 0, 512, VA, lhs_src=lambda c, b: CQN.t[:, 3 + c, b * 128:(b + 1) * 128], w=WUV, kc=2)
        if ASTOP < 6:
            return

    def phase_A_odd(l, s, t):
        phase_A_common(l, s, t)
        for (col0, dst) in ((0, QB), (512, KB)):
            st = STG.next()
            for j in range(4):
                pb = proj_fm(col0 + j * 128, 128)
                rope(pb, (0, 128), 0, perm64, st.t[:, j * TT:(j + 1) * TT], [st])
            store(dst.t[s, :, :, tsl(t)].rearrange("c p t -> p c t"),
                  bass.AP(st.t, 0, [[4096, 128], [TT, 4], [1, TT]]), st, [dst.p[pt(s, t)]])
        v_proj(s, t, 1024, 512, VB)
        hoist_sq(s, t)
        for (col0, dst) in ((1536, QA), (2048, KA)):
            st = STG.next()
            for hp in range(4):
                pb = proj_fm(col0 + hp * 128, 128)
                for e2 in range(2):
                    h = 2 * hp + e2
                    CPY(st.t[0:64, h * TT:(h + 1) * TT], pb.t[e2 * 64:e2 * 64 + 64, :], [pb], [st],
                        eng=("dve" if e2 else "act"))
            store(dst.t[s, :, 0:64, tsl(t)].rearrange("h p t -> p h t"),
                  bass.AP(st.t, 0, [[4096, 64], [TT, 8], [1, TT]]), st, [dst.p[pt(s, t)]])
        v_proj(s, t, 2560, 512, VA)
        pb = projb.next()
        specs = []
        for b in range(4):
            for c in range(8):
                specs.append((pb.t[:, b * 8:(b + 1) * 8], HT.t[:, c, b * 128:(b + 1) * 128], WIN.t[:, c, 3072:3080],
                              c == 0, c == 7))
        MMS(specs, [WIN, HT], [pb])
        f = FT.next()
        for b in range(4):
            TTO(f.t[:, b * 8:(b + 1) * 8], pb.t[:, b * 8:(b + 1) * 8], CST.t[:, l, 54:62], ALU.add, [pb, CST], [f])
        f2 = FT.next()
        ACT(f2.t[:, 0:32], f.t[:, 0:32], AF.Exp, [f], [f2], scale=-1.0)
        ACT(SPT.t[:, 4 * t:4 * t + 4, :].rearrange("p b h -> p (b h)"), f2.t[:, 0:32], AF.Ln, [f2], [SPT], bias=1.0)
        hoist_mm(s, t)
        gate_proj(l, s, t, 3088)

    def fox_post(l, s):
        P.op("dve", lambda e: e.memset(PREF.t[:, 0, :], 0.0), [], [PREF])
        for b in range(1, NB):
            TTO(PREF.t[:, b, :], PREF.t[:, b - 1, :], SPT.t[:, b - 1, :], ALU.add, [PREF, SPT], [PREF])
        pb = projb.next()
        specs = []
        for b in range(NB):
            specs.append((pb.t[:, b * 8:(b + 1) * 8], tri_f, SPT.t[:, b, :], True, False))
            specs.append((pb.t[:, b * 8:(b + 1) * 8], ones_f, PREF.t[:, b, :], False, True))
        MMS(specs, [SPT, PREF, CMF], [pb])
        CPY(CSPB.t[:].rearrange("p b h -> p (b h)"), pb.t[:, 0:NB * 8], [pb], [CSPB])
        P.dma("pool", CSP.t[s], CSPB.t[:], CSPB, [CSPB], [CSP.p[s]])
        for g in range(4):
            pb2 = projb.next()

            def fn(e, g=g, pb2=pb2):
                ins = None
                for bb in range(4):
                    ins = e.transpose(out=pb2.t[0:8, bb * 128:(bb + 1) * 128], in_=CSPB.t[:, g * 4 + bb, :],
                                      identity=ident_f)
                return ins
            P.op("pe", fn, [CSPB, CMF], [pb2])
            ACT(FQ8.t[0:8, g * 512:(g + 1) * 512], pb2.t[0:8, :], AF.Identity, [pb2], [FQ8], scale=-8.0)
        P.dma("pool", QA.t[s, :, 64, :], FQ8.t[0:8, :], FQ8, [FQ8], allp(QA, s), partial=True)

    sbk = Rot(PB[0:3])
    sbk4 = Rot(PB[0:3] + [PB[7]])
    obk4 = Rot(PB[3:7])
    obk2 = Rot(PB[3:5])

    SKEW = 2
    pend = []

    def pipe_flush(keep=0):
        while len(pend) > keep:
            fn = pend.pop(0)
            fn()

    def attn_core(ktf, ktb, krows, kbase, qt, vaf, vab, scale, pO, i, bias=None, swa=False, lbank=None, fin=None, deep=True):
        if swa:
            js = [j for j in range(4 * i - 1, 4 * i + 4) if j >= 0]
        else:
            js = list(range(0, 4 * i + 4))
        for idx, j in enumerate(js):
            m = j - 4 * i
            c0 = 0 if m < 0 else 128 * m
            if swa:
                n = 128 if (m < 0 or m == 3) else 256
            else:
                n = TT - c0
            ps = (sbk4 if deep else sbk).next()
            need_mask = swa or m >= 0
            specs = [(ps.t[:, c0:c0 + n], ktf(j), qt.t[kbase:kbase + krows, c0:c0 + n], True, not need_mask)]
            if swa:
                mk = msw_b[:, 128:256] if m < 0 else msw_b[:, 0:n]
                specs.append((ps.t[:, c0:c0 + n], ident_b, mk, False, True))
            elif m >= 0:
                specs.append((ps.t[:, c0:c0 + 128], ident_b, mc_b, False, True))
            MMS(specs, [qt, CMB] + ktb, [ps])
            p = PT.next()
            kw = dict(scale=scale)
            rd = [ps]
            if bias is not None:
                kw["bias"] = bias(j)
                rd.append(CSPB)
            ACT(p.t[:, c0:c0 + n], ps.t[:, c0:c0 + n], AF.Exp, rd, [p], **kw)
            first = (idx == 0)
            last = (idx == len(js) - 1)

            def pv(p=p, c0=c0, n=n, j=j, first=first, last=last):
                specs = [(pO.t[:, c0:c0 + n], vaf(j), p.t[:, c0:c0 + n], first, last)]
                wr = [pO]
                if lbank is not None:
                    specs.append((lbank.t[:, c0:c0 + n], ones_b, p.t[:, c0:c0 + n], first, last))
                    wr.append(lbank)
                MMS(specs, [p, CMB] + vab, wr)
                if last and fin is not None:
                    fin()
            pend.append(pv)
            pipe_flush(keep=(3 if deep else SKEW))

    def load_va64(src, s, col0, v):
        ap = bass.AP(YB, 4096 + v * 2048 + v * 64, [[YROW, 128], [128, NB], [1, 64]])
        P.dma("sp", ap, src.t[s, :, :, col0:col0 + 64].rearrange("b p n -> p b n"), Y.p[4 + 2 * v],
              allp(src, s), va_bufs(v), partial=True)

    def set_ones():
        for v in range(2):
            ap = bass.AP(YB, 4096 + v * 2048 + (1 - v) * 64, [[YROW, 128], [128, NB], [1, 64]])
            P.op("dve", lambda e, ap=ap: e.memset(ap, 1.0), [], va_bufs(v))

    def zero_q_half(q, half):
        P.op("dve", lambda e, q=q, half=half: e.memset(q.t[half * 64:half * 64 + 64, :], 0.0), [], [q])

    def load_q_half(src, s, ch, half, t, q):
        r0 = half * 64
        P.dma("sp", q.t[r0:r0 + 64, :], src.t[s, ch, r0:r0 + 64, tsl(t)], q, [src.p[pt(s, t)]], [q], partial=True)
        return q

    def finalize64(pO, par, lbias, r_extra, s, t, chunk):
        n0, l0 = par * 64, (1 - par) * 64
        f2 = FT.next()
        if lbias is not None:
            f1 = FT.next()
            ACT(f1.t[n0:n0 + 64, :], pO.t[l0:l0 + 64, :], AF.Ln, [pO] + list(r_extra), [f1], bias=lbias)
            ACT(f2.t[n0:n0 + 64, :], f1.t[n0:n0 + 64, :], AF.Exp, [f1], [f2], scale=-1.0)
        else:
            P.op("dve", lambda e: e.reciprocal(out=f2.t[n0:n0 + 64, :], in_=pO.t[l0:l0 + 64, :]), [pO], [f2])
        xb = XB.next()
        TTO(xb.t[n0:n0 + 64, :], pO.t[n0:n0 + 64, :], f2.t[n0:n0 + 64, :], ALU.mult, [pO, f2], [xb])
        store(OT.t[s, chunk, n0:n0 + 64, tsl(t)], xb.t[n0:n0 + 64, :], xb, [OT.p[pt(s, t)]])

    def phase_B_even(l, s):
        sc_m = 96.0 ** -0.5
        pipe_flush()
        set_ones()
        for h in range(8):
            par = h % 2
            k = par
            P.dma("sp", kt_ap(k, rows=96), KA.t[s, h, 0:96, :], Y.p[2 * k], allp(KA, s), kt_bufs(k))
            load_va64(VA, s, h * 64, par)
            for t in range(NTL):
                q = QTB.next()
                P.dma("sp", q.t[0:96, :], QA.t[s, h, 0:96, tsl(t)], q, [QA.p[pt(s, t)]], [q])
                pO = obk4.next()
                attn_core(lambda j, k=k: kt_ap(k, rows=96, c0=j * 128, n=128), kt_bufs(k), 96, 0, q,
                          lambda j, v=par: va_blk(v, j), va_bufs(par), sc_m, pO, t,
                          fin=lambda pO=pO, par=par, t=t, h=h: finalize64(pO, par, None, [], s, t, h // 2))
        for grp in range(2):
            k = grp
            P.dma("sp", kt_ap(k), KB.t[s, 0, :, :], Y.p[2 * k], allp(KB, s), kt_bufs(k))
            load_va64(VB, s, grp * 64, grp)
            pipe_flush()
            for q in QTBL:
                zero_q_half(q, 1 - grp)
            for j4 in range(4):
                horig = grp * 4 + j4
                for t in range(NTL):
                    q = load_q_half(QB, s, j4, grp, t, QTB.next())
                    pO = obk4.next()
                    attn_core(lambda j, k=k: kt_ap(k, c0=j * 128, n=128), kt_bufs(k),
                              128, 0, q, lambda j, v=grp: va_blk(v, j), va_bufs(grp), 0.125, pO, t, swa=True,
                              fin=lambda pO=pO, grp=grp, horig=horig, t=t, j4=j4: finalize64(
                                  pO, grp, SM.t[grp * 64:grp * 64 + 64, horig:horig + 1], [SM], s, t, 4 + j4))

    def phase_B_odd(l, s):
        i = l // 2
        lam_init = 0.8 - 0.6 * math.exp(-0.3 * l)
        P.dma("sp", CSPB.t[:], CSP.t[s], CSPB, [CSP.p[s]], [CSPB])
        pipe_flush()
        for qi, q in enumerate(QTBL):
            zero_q_half(q, 1 - (qi % 2))
        dcnt = 0
        for h in range(4):
            k = h % 2
            v = h % 2
            P.dma("sp", kt_ap(k), KB.t[s, h, :, :], Y.p[2 * k], allp(KB, s), kt_bufs(k))
            vap = bass.AP(YB, 4096 + v * 2048, [[YROW, 128], [128, NB], [1, 128]])
            P.dma("sp", vap, VB.t[s, :, :, h * 128:(h + 1) * 128].rearrange("b p n -> p b n"), Y.p[4 + 2 * v],
                  allp(VB, s), va_bufs(v))
            for t in range(NTL):
                dcnt += 1
                for mp in range(2):
                    q = load_q_half(QB, s, h, mp, t, QTBL[2 * (dcnt % 2) + mp])
                    pO = obk2.next()
                    pL = PB[5 + mp]

                    def fin(pO=pO, pL=pL, mp=mp, t=t, h=h):
                        f1 = FT.next()
                        ACT(f1.t[:], pL.t[:], AF.Ln, [pL], [f1])
                        f2 = FT.next()
                        ACT(f2.t[:], f1.t[:], AF.Exp, [f1], [f2], scale=-1.0)
                        TTO(CQC.t[:, mp, :], pO.t[:], f2.t[:], ALU.mult, [pO, f2], [CQC])
                        if mp == 0:
                            return
                        od = CQC.t[:, 2, :]
                        STT(od, CQC.t[:, 1, :], SM.t[:, 16 + i:17 + i], CQC.t[:, 0, :], ALU.mult, ALU.add, [CQC, SM], [CQC])
                        nb = PB[7]
                        sumsq_bank([od], [CQC], nb)
                        rs = rstd_from(nb, 128.0)
                        xb = XB.next()
                        STT(xb.t[:], od, CST.t[:, l, 53:54], rs.t[:], ALU.mult, ALU.mult, [CQC, rs, CST], [xb])
                        xb2 = XB.next()
                        TS(xb2.t[:], xb.t[:], 1.0 - lam_init, None, ALU.mult, ALU.bypass, [xb], [xb2])
                        store(OT.t[s, h, :, tsl(t)], xb2.t[:], xb2, [OT.p[pt(s, t)]])
                    attn_core(lambda j, k=k: kt_ap(k, c0=j * 128, n=128), kt_bufs(k),
                              128, 0, q, lambda j, v=v: va_blk(v, j), va_bufs(v), 0.125, pO, t, lbank=pL, fin=fin, deep=False)
        pipe_flush()
        for k in range(2):
            ap = bass.AP(YB, 64 * YROW + k * 2048, [[YROW, 1], [1, 2048]])
            P.op("dve", lambda e, ap=ap: e.memset(ap, 1.0), [], kt_bufs(k))
        set_ones()
        for h in range(8):
            par = h % 2
            k = par
            P.dma("sp", kt_ap(k, rows=64), KA.t[s, h, 0:64, :], Y.p[2 * k], allp(KA, s), kt_bufs(k), partial=True)
            load_va64(VA, s, h * 64, par)
            for t in range(NTL):
                q = QTB.next()
                P.dma("sp", q.t[0:65, :], QA.t[s, h, 0:65, tsl(t)], q, [QA.p[pt(s, t)]], [q])
                pO = obk4.next()
                attn_core(lambda j, k=k: kt_ap(k, rows=65, c0=j * 128, n=128), kt_bufs(k), 65, 0, q,
                          lambda j, v=par: va_blk(v, j), va_bufs(par), 0.125, pO, t,
                          bias=lambda j, h=h: CSPB.t[:, j, h:h + 1],
                          fin=lambda pO=pO, par=par, t=t, h=h: finalize64(pO, par, None, [], s, t, 4 + h // 2))

    def phase_C1(l, s, t):
        st_o = STG.next()
        P.dma("sp", bass.AP(st_o.t, 0, [[4096, 128], [TT, 8], [1, TT]]),
              OT.t[s, :, :, tsl(t)].rearrange("c p t -> p c t"), st_o, [OT.p[pt(s, t)]], [st_o])
        st_g = STG.next()
        P.dma("sp", bass.AP(st_g.t, 0, [[4096, 128], [TT, 8], [1, TT]]),
              GT.t[s, :, :, tsl(t)].rearrange("c p t -> p c t"), st_g, [GT.p[pt(s, t)]], [st_g])
        for c in range(8):
            TTO(HT.t[:, c, :], st_o.t[:, c * TT:(c + 1) * TT], st_g.t[:, c * TT:(c + 1) * TT], ALU.mult,
                [st_o, st_g], [HT])

    def phase_C2(l, s, t):
        load_x_tile(s, t)
        nb = PB[6]
        sqb = PT.items + QTBL
        for g in range(8):
            pb = proj_fm(g * 128, 128, w=WOUT)
            CPY(Y.t[:, g, :], pb.t[:], [pb], [Y.p[g]])
            ACT(sqb[g].t[:], pb.t[:], AF.Square, [pb], [sqb[g]])
            if g >= 2:
                MM(nb.t[:], [(ones_b, sqb[g - 2].t[:])], [sqb[g - 2], CMB], [nb], start=(g == 2), stop=False)
        for g in (6, 7):
            MM(nb.t[:], [(ones_b, sqb[g].t[:])], [sqb[g], CMB], [nb], start=False, stop=(g == 7))
        return rstd_from(nb, 1024.0)

    def phase_C3(l, s, t, rs):
        for c in range(8):
            f = FT.next()
            TTO(f.t[:], Y.t[:, c, :], rs.t[:], ALU.mult, [Y.p[c], rs], [f])
            STT(XT.t[:, c, :], f.t[:], AD.t[:, l, s, 2, c:c + 1], XT.t[:, c, :], ALU.mult, ALU.add, [f, AD, XT], [XT])
        P.dma("pool", xT.t[s, :, :, tsl(t)].rearrange("c p t -> p c t"), XT.t[:], XT, [XT], [xT.p[pt(s, t)]])

    def phase_C_all(l):
        tiles = [(s, t) for s in range(2) for t in range(NTL)]
        phase_C1(l, *tiles[0])
        for idx, (s, t) in enumerate(tiles):
            rs = phase_C2(l, s, t)
            if idx + 1 < len(tiles):
                phase_C1(l, *tiles[idx + 1])
            phase_C3(l, s, t, rs)

    for l in range(NL):
        i = l // 2
        even = (l % 2 == 0)
        if even:
            ACT(SM.t[:, 0:8], CST.t[:, l, 45:53], AF.Exp, [CST], [SM])
        else:
            lam_init = 0.8 - 0.6 * math.exp(-0.3 * l)
            f = FT.next()
            TTO(f.t[:, 0:64], CST.t[:, l, 62:126], CST.t[:, l, 126:190], ALU.mult, [CST], [f])
            TTO(f.t[:, 64:128], CST.t[:, l, 190:254], CST.t[:, l, 254:318], ALU.mult, [CST], [f])
            P.op("dve", lambda e, f=f: e.reduce_sum(out=SM.t[:, 20:22],
                                                 in_=f.t[:, 0:128].rearrange("p (a b) -> p a b", a=2),
                                                 axis=mybir.AxisListType.X), [f], [SM])
            ACT(SM.t[:, 22:24], SM.t[:, 20:22], AF.Exp, [SM], [SM])
            TTO(SM.t[:, 24:25], SM.t[:, 23:24], SM.t[:, 22:23], ALU.subtract, [SM], [SM])
            TS(SM.t[:, 16 + i:17 + i], SM.t[:, 24:25], -lam_init, None, ALU.add, ALU.bypass, [SM], [SM])
        if STOP < 2:
            continue
        for s in range(2):
            for t in range(NTL):
                if ONE_TILE and (s, t) != (0, 0):
                    continue
                ada_bg_start(l + 1, s * NTL + t)
                (phase_A_even if even else phase_A_odd)(l, s, t)
                ada_bg_end(l + 1, s * NTL + t)
            if not even and not ONE_TILE:
                fox_post(l, s)
        if l + 1 < NL:
            load_weights(l + 1)
        if STOP < 3:
            continue
        for s in range(2):
            (phase_B_even if even else phase_B_odd)(l, s)
        pipe_flush()
        if STOP < 4:
            continue
        phase_C_all(l)
        if l + 1 < NL:
            load_wout(l + 1)

    for s in range(0 if not SKIP_XPRO else 2, 2):
        for t in range(NTL):
            load_x_tile(s, t)
            for b in range(4):
                for half in range(2):
                    pb = projb.next()

                    def fn(e, b=b, half=half, pb=pb):
                        ins = None
                        for cc in range(4):
                            c = half * 4 + cc
                            ins = e.transpose(out=pb.t[:, cc * 128:(cc + 1) * 128],
                                              in_=XT.t[:, c, b * 128:(b + 1) * 128], identity=ident_f)
                        return ins
                    P.op("pe", fn, [XT, CMF], [pb])
                    ft = FT.next()
                    CPY(ft.t[:], pb.t[:], [pb], [ft], eng=("dve" if half else "act"))
                    tok0 = t * TT + b * 128
                    P.dma("pool", out_d[s, tok0:tok0 + 128, half * 512:(half + 1) * 512], ft.t[:], ft, [ft], [])
    P.emit()
    return nc


def _kc(w):
    K, N = w.shape
    return np.ascontiguousarray(w.reshape(K // 128, 128, N).transpose(1, 0, 2))


def _o_perm_even():
    perm = np.zeros(1024, np.int64)
    for f in range(1024):
        cf, r = divmod(f, 128)
        if cf < 4:
            perm[f] = f
        else:
            j = cf - 4
            head = j if r < 64 else 4 + j
            perm[f] = 512 + head * 64 + (r % 64)
    return perm


def _host_consts():
    pos = np.arange(S, dtype=np.float32)
    tab = np.zeros((128, 4, S), np.float32)
    inv64 = (1.0 / (np.float32(10000.0) ** (np.arange(0, 64, 2, dtype=np.float32) / np.float32(64)))).astype(np.float32)
    inv32 = (1.0 / (np.float32(10000.0) ** (np.arange(0, 32, 2, dtype=np.float32) / np.float32(32)))).astype(np.float32)
    for r in range(128):
        i = r % 64
        ang = (pos * inv64[i % 32]).astype(np.float32)
        tab[r, 0] = np.cos(ang)
        tab[r, 1] = -np.sin(ang) if i < 32 else np.sin(ang)
    for r in range(64, 96):
        j = r - 64
        ang = (pos * inv32[j % 16]).astype(np.float32)
        tab[r, 2] = np.cos(ang)
        tab[r, 3] = -np.sin(ang) if j < 16 else np.sin(ang)
    cmf = np.zeros((128, 384), np.float32)
    cmf[:, 0:128] = np.eye(128, dtype=np.float32)
    k = np.arange(128)
    cmf[:, 128:256] = (k[:, None] <= k[None, :]).astype(np.float32)
    cmf[:, 256:384] = 1.0
    cmb = np.zeros((128, 896), np.float32)
    cmb[:, 0:128] = np.eye(128, dtype=np.float32)
    for m in range(128):
        sw = m + 32 if (m % 64) < 32 else m - 32
        cmb[sw, 128 + m] = 1.0
    for m in range(128):
        if 64 <= m < 96:
            sw = m + 16 if (m - 64) < 16 else m - 16
        else:
            sw = m
        cmb[sw, 256 + m] = 1.0
    cmb[:, 384:512] = np.where(k[:, None] > k[None, :], NEG, 0.0)
    cmb[:, 512:640] = np.where(k[:, None] > k[None, :], NEG, 0.0)
    cmb[:, 640:768] = np.where(k[:, None] <= k[None, :], NEG, 0.0)
    cmb[:, 768:896] = 1.0
    return tab, cmf, cmb


def _prep_shared(inp):
    f = lambda a: np.asarray(a, dtype=np.float32)
    sh = {}
    w_ada = f(inp["w_ada"])
    sh["wada"] = np.ascontiguousarray(w_ada.reshape(4, 8, 128, 3072).transpose(0, 2, 1, 3))
    perm_e = _o_perm_even()
    wE, wUQ, wUK, wUV, wOe = [], [], [], [], []
    for i in range(2):
        w = f(inp["ev_w_in"][i])
        cols = [w[:, 0:384], w[:, 384:640], np.zeros((1024, 64), np.float32), w[:, 640:672]]
        sq = w[:, 672:1184].reshape(1024, 8, 64)
        order = []
        for j in range(4):
            order += [j, 4 + j]
        cols.append(sq[:, order, :].reshape(1024, 512))
        cols.append(w[:, 1184:1312])
        cols.append(w[:, 1312:1440])
        cols.append(w[:, 1440:2464][:, perm_e])
        wE.append(_kc(np.concatenate(cols, axis=1)))
        wUQ.append(_kc(np.concatenate([f(inp["ev_w_uq"][i]), np.zeros((384, 32), np.float32)], axis=1)))
        ukv = f(inp["ev_w_ukv"][i]).reshape(256, 8, 128)
        wUK.append(_kc(np.ascontiguousarray(ukv[:, :, 0:64]).reshape(256, 512)))
        wUV.append(_kc(np.ascontiguousarray(ukv[:, :, 64:128]).reshape(256, 512)))
        wOe.append(_kc(f(inp["ev_w_out"][i])[perm_e, :]))
    sh["wE"] = np.stack(wE)
    sh["wUQ"] = np.stack(wUQ)
    sh["wUK"] = np.stack(wUK)
    sh["wUV"] = np.stack(wUV)
    sh["wOe"] = np.stack(wOe)
    wO, wOo = [], []
    for i in range(2):
        w = f(inp["od_w_in"][i])
        cols = [w[:, 0:3080], np.zeros((1024, 8), np.float32), w[:, 3080:4104]]
        wO.append(_kc(np.concatenate(cols, axis=1)))
        wOo.append(_kc(f(inp["od_w_out"][i])))
    sh["wO"] = np.stack(wO)
    sh["wOo"] = np.stack(wOo)
    cst = np.zeros((128, 4, NC_CST), np.float32)
    for l in range(4):
        i = l // 2
        cst[:, l, 0:8] = f(inp["g_pre"][l]).reshape(8, 128).T
        cst[:, l, 8:16] = f(inp["g_post"][l]).reshape(8, 128).T
        cst[:, l, 16:40] = f(inp["b_ada"][l]).reshape(24, 128).T
        if l % 2 == 0:
            cst[:, l, 40:43] = f(inp["ev_q_norm"][i]).reshape(3, 128).T
            cst[:, l, 43:45] = f(inp["ev_kv_norm"][i]).reshape(2, 128).T
            cst[:, l, 45:53] = f(inp["ev_sinks"][i])[None, :]
        else:
            cst[:, l, 53] = f(inp["od_subln"][i])
            cst[:, l, 54:62] = f(inp["od_forget_bias"][i])[None, :]
            cst[:, l, 62:318] = f(inp["od_lambda"][i]).reshape(256)[None, :]
    sh["cst"] = cst
    tab, cmf, cmb = _host_consts()
    sh["tab"] = tab
    sh["cmf"] = cmf
    sh["cmb"] = cmb
    return sh


_NC_CACHE = {}


def run(inputs, NL=4, cores=8):
    x = np.asarray(inputs["x"], dtype=np.float32)
    c = np.asarray(inputs["c"], dtype=np.float32)
    sh = _prep_shared(inputs)
    in_maps = []
    for core in range(cores):
        m = dict(sh)
        m["x"] = np.ascontiguousarray(x[2 * core:2 * core + 2])
        cc = c[2 * core:2 * core + 2]
        m["cT"] = np.ascontiguousarray(cc.T.reshape(8, 128, 2).transpose(1, 0, 2))
        in_maps.append(m)
    if NL not in _NC_CACHE:
        _NC_CACHE[NL] = build(NL)
    res = run_bass_kernel_spmd(_NC_CACHE[NL], in_maps, core_ids=list(range(cores)))
    if DEBUG:
        return res.results
    return np.concatenate([r["out"] for r in res.results], axis=0).astype(np.float32)


def kernel(**inputs):
    return run(inputs, NL=4, cores=8)
```

```python
from contextlib import ExitStack
import math
import numpy as np
import concourse.bass as bass
import concourse.mybir as mybir
from concourse.bass_utils import run_bass_kernel_spmd

F32 = mybir.dt.float32
BF16 = mybir.dt.bfloat16
AF = mybir.ActivationFunctionType
ALU = mybir.AluOpType

S = 2048
TT = 512
NTL = S // TT
NB = S // 128
EPS = 1e-6
NC_CST = 320
NE = 2528
NO = 4112
NEG = -30000.0
STOP = 9
ONE_TILE = False
DEBUG = False
ASTOP = 99
SKIP_XPRO = False


class Buf:
    __slots__ = ("name", "writers", "readers", "excl")

    def __init__(self, name, excl=False):
        self.name = name
        self.excl = excl
        self.writers = {}
        self.readers = {}


class Tens:
    def __init__(self, t, name, parts, excl=False):
        self.t = t
        self.name = name
        self.p = [Buf(f"{name}.{i}", excl) for i in range(parts)]


ENGS = ("pe", "act", "dve", "pool", "sp")


class Prog:
    def __init__(self, nc, same_engine_sync=True):
        self.nc = nc
        self.es = ExitStack()
        self.ops = {e: [] for e in ENGS}
        self.cnt = {e: 0 for e in ENGS}
        self.known = {e: {} for e in ENGS}
        self.snap = {}
        self.chan_cnt = {}
        self.sem = {}
        self.same_engine_sync = same_engine_sync

    def sbuf(self, name, shape, dtype, parts=1):
        t = self.es.enter_context(self.nc.sbuf_tensor("sb_" + name, list(shape), dtype))
        return Tens(t, name, parts)

    def psum(self, name, shape, dtype, parts=1):
        t = self.es.enter_context(self.nc.psum_tensor("ps_" + name, list(shape), dtype))
        return Tens(t, name, parts, excl=True)

    def dram(self, name, shape, dtype, kind="Internal", parts=1):
        t = self.nc.dram_tensor(name, list(shape), dtype, kind=kind).ap()
        return Tens(t, name, parts)

    @staticmethod
    def _flat(xs):
        out = []
        for x in xs:
            if isinstance(x, Tens):
                out.extend(x.p)
            elif isinstance(x, (list, tuple)):
                out.extend(Prog._flat(x))
            else:
                out.append(x)
        return out

    def _deps(self, eng, reads, writes, skip=None):
        deps = {}

        def add(d):
            for k, v in d.items():
                if deps.get(k, 0) < v:
                    deps[k] = v

        for b in reads:
            add(b.writers)
        for b in writes:
            add(b.writers)
            add(b.readers)
        waits = []
        kn = self.known[eng]
        for k, v in deps.items():
            if k == skip:
                continue
            if k == eng and (eng == "pe" or not self.same_engine_sync):
                continue
            if kn.get(k, 0) >= v:
                continue
            waits.append((k, v))
        for k, v in waits:
            kn[k] = max(kn.get(k, 0), v)
            sn = self.snap.get((k, v))
            if sn:
                for k2, v2 in sn.items():
                    if kn.get(k2, 0) < v2:
                        kn[k2] = v2
        return waits

    def _mark(self, tok, reads, writes, partial):
        k, v = tok
        for b in reads:
            if b.readers.get(k, 0) < v:
                b.readers[k] = v
        for b in writes:
            if partial:
                b.writers[k] = v
            else:
                b.writers = {k: v}
                b.readers = {}

    def op(self, eng, fn, reads=(), writes=()):
        reads = self._flat(reads)
        writes = self._flat(writes)
        writes = writes + [b for b in reads if b.excl and b not in writes]
        reads = [b for b in reads if not b.excl]
        waits = self._deps(eng, reads, writes)
        self.cnt[eng] += 1
        tok = (eng, self.cnt[eng])
        sn = dict(self.known[eng])
        sn[eng] = self.cnt[eng]
        self.snap[tok] = sn
        self._mark(tok, reads, writes, False)
        self.ops[eng].append(("op", waits, fn, tok))
        return tok

    def dma(self, eng, out, in_, sb, reads=(), writes=(), partial=False, **kw):
        reads = self._flat(reads)
        writes = self._flat(writes)
        sbb = self._flat([sb])[0]
        chan = "d_" + sbb.name
        waits = self._deps(eng, reads, writes, skip=(chan if partial else None))
        self.chan_cnt[chan] = self.chan_cnt.get(chan, 0) + 16
        tok = (chan, self.chan_cnt[chan])
        self.snap[tok] = dict(self.known[eng])
        self._mark(tok, reads, writes, partial)
        self.ops[eng].append(("dma", waits, (out, in_, kw), tok))
        return tok

    def emit(self):
        nc = self.nc
        for k in list(ENGS) + list(self.chan_cnt):
            if k not in self.sem:
                self.sem[k] = self.es.enter_context(nc.semaphore("s_" + k.replace(".", "_")))
        final = list(self.chan_cnt.items())

        def run(ename, eng):
            for kind, waits, payload, tok in self.ops[ename]:
                for k, v in waits:
                    eng.wait_ge(self.sem[k], v)
                if kind == "op":
                    payload(eng).then_inc(self.sem[tok[0]], 1)
                else:
                    out, in_, kw = payload
                    eng.dma_start(out=out, in_=in_, **kw).then_inc(self.sem[tok[0]], 16)
            if ename == "sp":
                for c, v in final:
                    eng.wait_ge(self.sem[c], v)

        with nc.Block() as block:
            @block.sync
            def _(e):
                run("sp", e)

            @block.scalar
            def _(e):
                run("act", e)

            @block.vector
            def _(e):
                run("dve", e)

            @block.gpsimd
            def _(e):
                run("pool", e)

            @block.tensor
            def _(e):
                run("pe", e)
        self.es.close()


class Rot:
    def __init__(self, items):
        self.items = items
        self.i = 0

    def next(self):
        x = self.items[self.i % len(self.items)]
        self.i += 1
        return x


def build(NL=4):
    nc = bass.Bass("TRN2", target_bir_lowering=False)
    P = Prog(nc)

    def din(name, shape, dt=F32):
        return nc.dram_tensor(name, list(shape), dt, kind="ExternalInput").ap()

    x_d = din("x", [2, S, 1024])
    cT_d = din("cT", [128, 8, 2])
    wada_d = din("wada", [4, 128, 8, 3072])
    cst_d = din("cst", [128, 4, NC_CST])
    cmf_d = din("cmf", [128, 384])
    cmb_d = din("cmb", [128, 896])
    tab_d = din("tab", [128, 4, S])
    wE_d = din("wE", [2, 128, 8, NE])
    wUQ_d = din("wUQ", [2, 128, 3, 800])
    wUK_d = din("wUK", [2, 128, 2, 512])
    wUV_d = din("wUV", [2, 128, 2, 512])
    wOe_d = din("wOe", [2, 128, 8, 1024])
    wO_d = din("wO", [2, 128, 8, NO])
    wOo_d = din("wOo", [2, 128, 8, 1024])
    out_d = nc.dram_tensor("out", [2, S, 1024], F32, kind="ExternalOutput").ap()

    def scr(name, shape, dt, parts=2 * NTL):
        return P.dram(name, shape, dt, kind=("ExternalOutput" if DEBUG else "Internal"), parts=parts)

    xT = scr("xT", [2, 8, 128, S], F32)
    QA = scr("QA", [2, 8, 128, S], BF16)
    KA = scr("KA", [2, 8, 128, S], BF16)
    QB = scr("QB", [2, 4, 128, S], BF16)
    KB = scr("KB", [2, 4, 128, S], BF16)
    VA = scr("VA", [2, NB, 128, 512], BF16)
    VB = scr("VB", [2, NB, 128, 512], BF16)
    GT = scr("GT", [2, 8, 128, S], BF16)
    OT = scr("OT", [2, 8, 128, S], BF16)
    CSP = scr("CSP", [2, 128, NB, 8], F32, parts=2)

    def pt(s, t):
        return s * NTL + t

    def allp(T, s):
        return [T.p[pt(s, t)] for t in range(NTL)]

    WIN = P.sbuf("WIN", [128, 8, NO], BF16)
    WOUT = P.sbuf("WOUT", [128, 8, 1024], BF16)
    WUQ = P.sbuf("WUQ", [128, 3, 800], BF16)
    WUK = P.sbuf("WUK", [128, 2, 512], BF16)
    WUV = P.sbuf("WUV", [128, 2, 512], BF16)
    CST = P.sbuf("CST", [128, 4, NC_CST], F32)
    AD = P.sbuf("AD", [128, 4, 2, 3, 8], F32)
    CMF = P.sbuf("CMF", [128, 384], F32)
    CMB = P.sbuf("CMB", [128, 896], BF16)
    TAB = P.sbuf("TAB", [128, 4, TT], F32)
    XT = P.sbuf("XT", [128, 8, TT], F32)
    HT = P.sbuf("HT", [128, 8, TT], BF16)
    SQ = Rot([P.sbuf(f"SQ{i}", [128, TT], BF16) for i in range(2)])
    FT = Rot([P.sbuf(f"FT{i}", [128, TT], F32) for i in range(4)])
    RSB = Rot([P.sbuf(f"RSB{i}", [128, TT], F32) for i in range(2)])
    STG = Rot([P.sbuf(f"STG{i}", [128, 4096], BF16) for i in range(2)])
    CQC = P.sbuf("CQC", [128, 5, TT], F32)
    CQN = P.sbuf("CQN", [128, 5, TT], BF16)
    XB = Rot([P.sbuf(f"XB{i}", [128, TT], BF16) for i in range(2)])
    Y = P.sbuf("Y", [128, 8, TT], F32, parts=8)
    YB = Y.t.bitcast(BF16)
    QTBL = [P.sbuf(f"QTB{i}", [128, TT], BF16) for i in range(4)]
    QTB = Rot(QTBL)
    PT = Rot([P.sbuf(f"PT{i}", [128, TT], BF16) for i in range(4)])
    CSPB = P.sbuf("CSPB", [128, NB, 8], F32)
    SPT = P.sbuf("SPT", [128, NB, 8], F32)
    PREF = P.sbuf("PREF", [128, NB, 8], F32)
    FQ8 = P.sbuf("FQ8", [128, S], BF16)
    SM = P.sbuf("SM", [128, 64], F32)
    CTB = P.sbuf("CTB", [128, 8, 2], BF16)
    CTF = P.sbuf("CTF", [128, 8, 2], F32)
    MODF = P.sbuf("MODF", [128, 4, 24, 2], F32)

    PB = [P.psum(f"B{i}", [128, 512], F32) for i in range(8)]

    ident_f = CMF.t[:, 0:128]
    tri_f = CMF.t[:, 128:256]
    ones_f = CMF.t[:, 256:384]
    ident_b = CMB.t[:, 0:128]
    perm64 = CMB.t[:, 128:256]
    perm32 = CMB.t[:, 256:384]
    mc_b = CMB.t[:, 384:512]
    msw_b = CMB.t[:, 512:768]
    ones_b = CMB.t[:, 768:896]

    YROW = 8192

    def kt_ap(k, rows=128, p0=0, c0=0, n=2048):
        return bass.AP(YB, p0 * YROW + k * 2048 + c0, [[YROW, rows], [1, n]])

    def kt_bufs(k):
        return [Y.p[2 * k], Y.p[2 * k + 1]]

    def va_blk(v, j):
        return bass.AP(YB, 4096 + v * 2048 + j * 128, [[YROW, 128], [1, 128]])

    def va_bufs(v):
        return [Y.p[4 + 2 * v], Y.p[5 + 2 * v]]

    def ACT(out, in_, func, r, w, **kw):
        P.op("act", lambda e: e.activation(out=out, in_=in_, func=func, **kw), r, w)

    def TTO(out, a, b, op, r, w, eng="dve"):
        P.op(eng, lambda e: e.tensor_tensor(out=out, in0=a, in1=b, op=op), r, w)

    def TS(out, a, s1, s2, op0, op1, r, w, eng="dve"):
        P.op(eng, lambda e: e.tensor_scalar(out=out, in0=a, scalar1=s1, scalar2=s2, op0=op0, op1=op1), r, w)

    def STT(out, a, s, b, op0, op1, r, w):
        P.op("dve", lambda e: e.scalar_tensor_tensor(out=out, in0=a, scalar=s, in1=b, op0=op0, op1=op1), r, w)

    def CPY(out, in_, r, w, eng="dve"):
        if eng == "act":
            P.op("act", lambda e: e.activation(out=out, in_=in_, func=AF.Identity), r, w)
        else:
            P.op(eng, lambda e: e.tensor_copy(out=out, in_=in_), r, w)

    def MM(out, pairs, r, w, start=True, stop=True):
        pairs = list(pairs)

        def fn(e):
            n = len(pairs)
            ins = None
            for i, (l, rr) in enumerate(pairs):
                ins = e.matmul(out, lhsT=l, rhs=rr, start=(start and i == 0), stop=(stop and i == n - 1))
            return ins
        P.op("pe", fn, r, w)

    def MMS(specs, r, w):
        specs = list(specs)

        def fn(e):
            ins = None
            for (o, l, rr, st, sp) in specs:
                ins = e.matmul(o, lhsT=l, rhs=rr, start=st, stop=sp)
            return ins
        P.op("pe", fn, r, w)

    projb = Rot(PB[0:5])

    def rstd_from(ps, n_feat):
        t1 = FT.next()
        ACT(t1.t[:], ps.t[:], AF.Ln, [ps], [t1], scale=1.0 / n_feat, bias=EPS)
        t2 = RSB.next()
        ACT(t2.t[:], t1.t[:], AF.Exp, [t1], [t2], scale=-0.5)
        return t2

    CASTKW = dict(max_dma_last_dim=4096)

    def load_weights(l):
        i = l // 2
        if l % 2 == 0:
            for c in range(8):
                P.dma("pool", WIN.t[:, c, 0:NE], wE_d[i, :, c, :], WIN, [], [WIN], partial=(c > 0), **CASTKW)
            P.dma("pool", WUQ.t[:], wUQ_d[i], WUQ, [], [WUQ], **CASTKW)
            P.dma("pool", WUK.t[:], wUK_d[i], WUK, [], [WUK], **CASTKW)
            P.dma("pool", WUV.t[:], wUV_d[i], WUV, [], [WUV], **CASTKW)
        else:
            for c in range(8):
                P.dma("pool", WIN.t[:, c, :], wO_d[i, :, c, :], WIN, [], [WIN], partial=(c > 0), **CASTKW)

    def load_wout(l):
        i = l // 2
        src = wOe_d if l % 2 == 0 else wOo_d
        for c in range(8):
            P.dma("pool", WOUT.t[:, c, :], src[i, :, c, :], WOUT, [], [WOUT], partial=(c > 0), **CASTKW)

    P.dma("sp", CST.t[:], cst_d, CST, [], [CST])
    P.dma("sp", CMF.t[:], cmf_d, CMF, [], [CMF])
    P.dma("pool", CMB.t[:], cmb_d, CMB, [], [CMB])
    P.dma("sp", CTF.t[:], cT_d, CTF, [], [CTF])
    ACT(CTB.t[:], CTF.t[:], AF.Silu, [CTF], [CTB])

    load_weights(0)

    xits = [(s, t, hb) for s in range(0 if not SKIP_XPRO else 2, 2) for t in range(NTL) for hb in range(2)]

    def x_load(it):
        s, t, hb = xits[it]
        stg = XT if it % 2 == 0 else Y
        xin = bass.AP(stg.t, 0, [[8 * TT, 128], [1024, 2], [1, 1024]])
        tok0 = t * TT + hb * 256
        src = x_d[s, tok0:tok0 + 256, :].rearrange("(b p) d -> p b d", p=128)
        P.dma("sp", xin, src, stg.p[0], [], [stg])

    if xits:
        x_load(0)
    for it, (s, t, hb) in enumerate(xits):
        if it + 1 < len(xits):
            x_load(it + 1)
        stg = XT if it % 2 == 0 else Y
        tok0 = t * TT + hb * 256
        for c in range(8):
            pb = projb.next()

            def fn(e, c=c, pb=pb, stg=stg):
                ins = None
                for b in range(2):
                    a_in = bass.AP(stg.t, b * 1024 + c * 128, [[8 * TT, 128], [1, 128]])
                    ins = e.transpose(out=pb.t[:, b * 128:(b + 1) * 128], in_=a_in, identity=ident_f)
                return ins
            P.op("pe", fn, [stg, CMF], [pb])
            ft = FT.next()
            CPY(ft.t[:, 0:256], pb.t[:, 0:256], [pb], [ft], eng=("dve" if c % 2 else "act"))
            P.dma("sp", xT.t[s, c, :, tok0:tok0 + 256], ft.t[:, 0:256], ft, [ft], [xT.p[pt(s, t)]], partial=True)

    def ada_piece_dma(l, pc, dst_ap, sbt, bufs):
        P.dma("pool", dst_ap, wada_d[l, :, :, pc * 512:(pc + 1) * 512], sbt, [], bufs, **CASTKW)

    def ada_piece_mm(l, pc, lw_fn, bufs):
        pb = projb.next()
        specs = []
        for g4 in range(4):
            for kc in range(8):
                specs.append((pb.t[:, 2 * g4:2 * g4 + 2], lw_fn(kc, g4), CTB.t[:, kc, :], kc == 0, kc == 7))
        MMS(specs, list(bufs) + [CTB], [pb])
        for g4 in range(4):
            g = pc * 4 + g4
            TS(MODF.t[:, l, g, :], pb.t[:, 2 * g4:2 * g4 + 2], CST.t[:, l, 16 + g:17 + g], None, ALU.add, ALU.bypass,
               [pb, CST], [MODF])

    def ada_finish(l):
        for b in range(2):
            a1 = AD.t[:, l, b, 0, :]
            TS(a1, MODF.t[:, l, 8:16, b], 1.0, None, ALU.add, ALU.bypass, [MODF], [AD])
            TTO(a1, a1, CST.t[:, l, 0:8], ALU.mult, [AD, CST], [AD])
            CPY(AD.t[:, l, b, 1, :], MODF.t[:, l, 0:8, b], [MODF], [AD])
            TTO(AD.t[:, l, b, 2, :], MODF.t[:, l, 16:24, b], CST.t[:, l, 8:16], ALU.mult, [MODF, CST], [AD])

    if NL > 0:
        for pc in range(6):
            wbuf = STG.next()
            ada_piece_dma(0, pc, bass.AP(wbuf.t, 0, [[4096, 128], [512, 8], [1, 512]]), wbuf, [wbuf])
            ada_piece_mm(0, pc, lambda kc, g4, wbuf=wbuf: bass.AP(wbuf.t, kc * 512 + g4 * 128, [[4096, 128], [1, 128]]), [wbuf])
        ada_finish(0)

    def yhalf_bufs(k):
        return [Y.p[4 * k + j] for j in range(4)]

    def ada_bg_start(l, ti):
        if l >= NL or ti >= 6:
            return
        k = ti % 2
        ada_piece_dma(l, ti, bass.AP(YB, k * 4096, [[YROW, 128], [512, 8], [1, 512]]), Y.p[4 * k], yhalf_bufs(k))

    def ada_bg_end(l, ti):
        if l >= NL or ti >= 6:
            return
        k = ti % 2
        ada_piece_mm(l, ti, lambda kc, g4, k=k: bass.AP(YB, k * 4096 + kc * 512 + g4 * 128, [[YROW, 128], [1, 128]]),
                     yhalf_bufs(k))
        if ti == 5:
            ada_finish(l)

    load_wout(0)

    def tsl(t):
        return slice(t * TT, (t + 1) * TT)

    def load_x_tile(s, t):
        P.dma("sp", XT.t[:], xT.t[s, :, :, tsl(t)].rearrange("c p t -> p c t"), XT, [xT.p[pt(s, t)]], [XT])

    def sumsq_bank(srcs, reads, bank):
        n = len(srcs)
        for i, a in enumerate(srcs):
            sq = SQ.next()
            ACT(sq.t[:], a, AF.Square, reads, [sq])
            MM(bank.t[:], [(ones_b, sq.t[:])], [sq, CMB], [bank], start=(i == 0), stop=(i == n - 1))

    rs_next = {}

    pend_mult = []

    def pop_mult(n=1):
        for _ in range(n):
            if pend_mult:
                pend_mult.pop(0)()

    def norm_sq(s, t):
        load_x_tile(s, t)
        bufs = PT.items + QTBL
        for c in range(8):
            ACT(bufs[c].t[:], XT.t[:, c, :], AF.Square, [XT], [bufs[c]])

    def norm_mm(s, t):
        bufs = PT.items + QTBL
        nb = PB[5]
        for c in range(8):
            MM(nb.t[:], [(ones_b, bufs[c].t[:])], [bufs[c], CMB], [nb], start=(c == 0), stop=(c == 7))
        rs = rstd_from(nb, 1024.0)
        for c in range(8):
            pend_mult.append(lambda c=c, rs=rs: TTO(XT.t[:, c, :], XT.t[:, c, :], rs.t[:], ALU.mult, [XT, rs], [XT]))
        rs_next[(s, t)] = rs

    def norm_part(s, t):
        norm_sq(s, t)
        norm_mm(s, t)

    def next_tile(s, t):
        if t + 1 < NTL:
            return (s, t + 1)
        if s == 0:
            return (1, 0)
        return None

    def phase_A_common(l, s, t):
        if (s, t) not in rs_next:
            norm_part(s, t)
        pop_mult(8)
        rs_next.pop((s, t))
        P.dma("sp", TAB.t[:], tab_d[:, :, tsl(t)], TAB, [], [TAB])
        for c in range(8):
            if c % 2 == 0:
                ACT(HT.t[:, c, :], XT.t[:, c, :], AF.Identity, [XT, AD], [HT], scale=AD.t[:, l, s, 0, c:c + 1],
                    bias=AD.t[:, l, s, 1, c:c + 1])
            else:
                TS(HT.t[:, c, :], XT.t[:, c, :], AD.t[:, l, s, 0, c:c + 1], AD.t[:, l, s, 1, c:c + 1], ALU.mult, ALU.add,
                   [XT, AD], [HT])

    def hoist_sq(s, t):
        nt = next_tile(s, t)
        if nt is not None and not ONE_TILE:
            norm_sq(*nt)

    def hoist_mm(s, t):
        nt = next_tile(s, t)
        if nt is not None and not ONE_TILE:
            norm_mm(*nt)

    def proj_fm(col0, M, reads_extra=(), kc=8, w=None, rhs=None):
        w = w or WIN
        pb = projb.next()
        pairs = []
        for c in range(kc):
            r = HT.t[:, c, :] if rhs is None else rhs(c)
            pairs.append((w.t[:, c, col0:col0 + M], r))
        MM(pb.t[0:M, :], pairs, [w, HT] + list(reads_extra), [pb])
        return pb

    def rope(pbx, rows, tabi, perm, out_ap, out_w):
        r0, r1 = rows
        kmax = 128
        xb = XB.next()
        ACT(xb.t[0:kmax, :], pbx.t[0:kmax, :], AF.Identity, [pbx], [xb])
        pb2 = projb.next()
        MM(pb2.t[0:kmax, :], [(perm[0:kmax, 0:kmax], xb.t[0:kmax, :])], [xb, CMB], [pb2])
        f1 = FT.next()
        TTO(f1.t[r0:r1, :], pbx.t[r0:r1, :], TAB.t[r0:r1, tabi, :], ALU.mult, [pbx, TAB], [f1])
        f2 = FT.next()
        TTO(f2.t[r0:r1, :], pb2.t[r0:r1, :], TAB.t[r0:r1, tabi + 1, :], ALU.mult, [pb2, TAB], [f2])
        TTO(out_ap, f1.t[r0:r1, :], f2.t[r0:r1, :], ALU.add, [f1, f2], out_w)

    def store(dst, src, sbt, dparts, partial=True):
        P.dma("pool", dst, src, sbt, [sbt], dparts, partial=partial)

    def gate_proj(l, s, t, col0):
        st = STG.next()
        for c in range(8):
            pb = proj_fm(col0 + c * 128, 128)
            ACT(st.t[:, c * TT:(c + 1) * TT], pb.t[:], AF.Silu, [pb], [st])
            pop_mult()
        store(GT.t[s, :, :, tsl(t)].rearrange("c p t -> p c t"),
              bass.AP(st.t, 0, [[4096, 128], [TT, 8], [1, TT]]), st, [GT.p[pt(s, t)]])

    def v_proj(s, t, col0, ncol, dst, lhs_src=None, w=None, kc=8):
        w = w or WIN
        st = STG.next()
        for b in range(4):
            pb = projb.next()
            pairs = []
            for c in range(kc):
                lt = HT.t[:, c, b * 128:(b + 1) * 128] if lhs_src is None else lhs_src(c, b)
                pairs.append((lt, w.t[:, c, col0:col0 + ncol]))
            MM(pb.t[:, 0:ncol], pairs, [w, HT, CQN], [pb])
            CPY(st.t[:, b * 512:b * 512 + ncol], pb.t[:, 0:ncol], [pb], [st], eng=("dve" if b % 2 else "act"))
        store(dst.t[s, 4 * t:4 * t + 4, :, 0:ncol].rearrange("b p n -> p b n"),
              bass.AP(st.t, 0, [[4096, 128], [512, 4], [1, ncol]]), st, [dst.p[pt(s, t)]])

    def phase_A_even(l, s, t):
        phase_A_common(l, s, t)
        if ASTOP < 2:
            return
        sqb = PT.items + QTBL
        for j in range(5):
            pb = proj_fm((0 if j < 3 else 384 - 3 * 128) + j * 128, 128)
            CPY(CQC.t[:, j, :], pb.t[:], [pb], [CQC])
            ACT(sqb[j].t[:], pb.t[:], AF.Square, [pb], [sqb[j]])
        for (c0, nch, ncol, nfeat) in ((0, 3, 40, 384.0), (3, 2, 43, 256.0)):
            nb = PB[6]
            for j in range(nch):
                MM(nb.t[:], [(ones_b, sqb[c0 + j].t[:])], [sqb[c0 + j], CMB], [nb], start=(j == 0), stop=(j == nch - 1))
            rs = rstd_from(nb, nfeat)
            for j in range(nch):
                STT(CQN.t[:, c0 + j, :], CQC.t[:, c0 + j, :], CST.t[:, l, ncol + j:ncol + j + 1], rs.t[:],
                    ALU.mult, ALU.mult, [CQC, CST, rs], [CQN])
        st = STG.next()
        for j in range(4):
            pb = proj_fm(736 + j * 128, 128)
            rope(pb, (0, 128), 0, perm64, st.t[:, j * TT:(j + 1) * TT], [st])
        pb = proj_fm(1248, 128)
        rope(pb, (0, 128), 0, perm64, st.t[:, 4 * TT:5 * TT], [st])
        store(QB.t[s, :, :, tsl(t)].rearrange("c p t -> p c t"),
              bass.AP(st.t, 0, [[4096, 128], [TT, 4], [1, TT]]), st, [QB.p[pt(s, t)]])
        store(KB.t[s, 0, :, tsl(t)], st.t[:, 4 * TT:5 * TT], st, [KB.p[pt(s, t)]])
        if ASTOP < 7:
            return
        hoist_sq(s, t)
        v_proj(s, t, 1376, 128, VB)
        if ASTOP < 8:
            return
        gate_proj(l, s, t, 1504)
        hoist_mm(s, t)
        if ASTOP < 3:
            return
        st = STG.next()
        for h in range(8):
            pb = proj_fm(h * 96, 128, w=WUQ, kc=3, rhs=lambda c: CQN.t[:, c, :], reads_extra=[CQN])
            CPY(st.t[0:64, h * TT:(h + 1) * TT], pb.t[0:64, :], [pb], [st])
            rope(pb, (64, 96), 2, perm32, st.t[64:96, h * TT:(h + 1) * TT], [st])
            pop_mult()
        store(QA.t[s, :, 0:96, tsl(t)].rearrange("h p t -> p h t"),
              bass.AP(st.t, 0, [[4096, 96], [TT, 8], [1, TT]]), st, [QA.p[pt(s, t)]])
        if ASTOP < 4:
            return
        st = STG.next()
        for hp in range(4):
            pb = proj_fm(hp * 128, 128, w=WUK, kc=2, rhs=lambda c: CQN.t[:, 3 + c, :], reads_extra=[CQN])
            for e2 in range(2):
                h = 2 * hp + e2
                CPY(st.t[0:64, h * TT:(h + 1) * TT], pb.t[e2 * 64:e2 * 64 + 64, :], [pb], [st],
                    eng=("dve" if e2 else "act"))
        pb = proj_fm(640, 128)
        xk = XB.next()
        rope(pb, (64, 96), 2, perm32, xk.t[64:96, :], [xk])
        store(KA.t[s, :, 0:64, tsl(t)].rearrange("h p t -> p h t"),
              bass.AP(st.t, 0, [[4096, 64], [TT, 8], [1, TT]]), st, [KA.p[pt(s, t)]])
        store(KA.t[s, :, 64:96, tsl(t)].rearrange("h p t -> p h t"),
              bass.AP(xk.t, 64 * TT, [[TT, 32], [0, 8], [1, TT]]), xk, [KA.p[pt(s, t)]])
        if ASTOP < 5:
            return
        v_proj(s, t, 0, 512, VA, lhs_src=lambda c, b: CQN.t[:, 3 + c, b * 128:(b + 1) * 128], w=WUV, kc=2)
        if ASTOP < 6:
            return

    def phase_A_odd(l, s, t):
        phase_A_common(l, s, t)
        for (col0, dst) in ((0, QB), (512, KB)):
            st = STG.next()
            for j in range(4):
                pb = proj_fm(col0 + j * 128, 128)
                rope(pb, (0, 128), 0, perm64, st.t[:, j * TT:(j + 1) * TT], [st])
            store(dst.t[s, :, :, tsl(t)].rearrange("c p t -> p c t"),
                  bass.AP(st.t, 0, [[4096, 128], [TT, 4], [1, TT]]), st, [dst.p[pt(s, t)]])
        v_proj(s, t, 1024, 512, VB)
        hoist_sq(s, t)
        for (col0, dst) in ((1536, QA), (2048, KA)):
            st = STG.next()
            for hp in range(4):
                pb = proj_fm(col0 + hp * 128, 128)
                for e2 in range(2):
                    h = 2 * hp + e2
                    CPY(st.t[0:64, h * TT:(h + 1) * TT], pb.t[e2 * 64:e2 * 64 + 64, :], [pb], [st],
                        eng=("dve" if e2 else "act"))
            store(dst.t[s, :, 0:64, tsl(t)].rearrange("h p t -> p h t"),
                  bass.AP(st.t, 0, [[4096, 64], [TT, 8], [1, TT]]), st, [dst.p[pt(s, t)]])
        v_proj(s, t, 2560, 512, VA)
        pb = projb.next()
        specs = []
        for b in range(4):
            for c in range(8):
                specs.append((pb.t[:, b * 8:(b + 1) * 8], HT.t[:, c, b * 128:(b + 1) * 128], WIN.t[:, c, 3072:3080],
                              c == 0, c == 7))
        MMS(specs, [WIN, HT], [pb])
        f = FT.next()
        for b in range(4):
            TTO(f.t[:, b * 8:(b + 1) * 8], pb.t[:, b * 8:(b + 1) * 8], CST.t[:, l, 54:62], ALU.add, [pb, CST], [f])
        f2 = FT.next()
        ACT(f2.t[:, 0:32], f.t[:, 0:32], AF.Exp, [f], [f2], scale=-1.0)
        ACT(SPT.t[:, 4 * t:4 * t + 4, :].rearrange("p b h -> p (b h)"), f2.t[:, 0:32], AF.Ln, [f2], [SPT], bias=1.0)
        hoist_mm(s, t)
        gate_proj(l, s, t, 3088)

    def fox_post(l, s):
        P.op("dve", lambda e: e.memset(PREF.t[:, 0, :], 0.0), [], [PREF])
        for b in range(1, NB):
            TTO(PREF.t[:, b, :], PREF.t[:, b - 1, :], SPT.t[:, b - 1, :], ALU.add, [PREF, SPT], [PREF])
        pb = projb.next()
        specs = []
        for b in range(NB):
            specs.append((pb.t[:, b * 8:(b + 1) * 8], tri_f, SPT.t[:, b, :], True, False))
            specs.append((pb.t[:, b * 8:(b + 1) * 8], ones_f, PREF.t[:, b, :], False, True))
        MMS(specs, [SPT, PREF, CMF], [pb])
        CPY(CSPB.t[:].rearrange("p b h -> p (b h)"), pb.t[:, 0:NB * 8], [pb], [CSPB])
        P.dma("pool", CSP.t[s], CSPB.t[:], CSPB, [CSPB], [CSP.p[s]])
        for g in range(4):
            pb2 = projb.next()

            def fn(e, g=g, pb2=pb2):
                ins = None
                for bb in range(4):
                    ins = e.transpose(out=pb2.t[0:8, bb * 128:(bb + 1) * 128], in_=CSPB.t[:, g * 4 + bb, :],
                                      identity=ident_f)
                return ins
            P.op("pe", fn, [CSPB, CMF], [pb2])
            ACT(FQ8.t[0:8, g * 512:(g + 1) * 512], pb2.t[0:8, :], AF.Identity, [pb2], [FQ8], scale=-8.0)
        P.dma("pool", QA.t[s, :, 64, :], FQ8.t[0:8, :], FQ8, [FQ8], allp(QA, s), partial=True)

    sbk = Rot(PB[0:3])
    sbk4 = Rot(PB[0:3] + [PB[7]])
    obk4 = Rot(PB[3:7])
    obk2 = Rot(PB[3:5])

    SKEW = 2
    pend = []

    def pipe_flush(keep=0):
        while len(pend) > keep:
            fn = pend.pop(0)
            fn()

    def attn_core(ktf, ktb, krows, kbase, qt, vaf, vab, scale, pO, i, bias=None, swa=False, lbank=None, fin=None, deep=True):
        if swa:
            js = [j for j in range(4 * i - 1, 4 * i + 4) if j >= 0]
        else:
            js = list(range(0, 4 * i + 4))
        for idx, j in enumerate(js):
            m = j - 4 * i
            c0 = 0 if m < 0 else 128 * m
            if swa:
                n = 128 if (m < 0 or m == 3) else 256
            else:
                n = TT - c0
            ps = (sbk4 if deep else sbk).next()
            need_mask = swa or m >= 0
            specs = [(ps.t[:, c0:c0 + n], ktf(j), qt.t[kbase:kbase + krows, c0:c0 + n], True, not need_mask)]
            if swa:
                mk = msw_b[:, 128:256] if m < 0 else msw_b[:, 0:n]
                specs.append((ps.t[:, c0:c0 + n], ident_b, mk, False, True))
            elif m >= 0:
                specs.append((ps.t[:, c0:c0 + 128], ident_b, mc_b, False, True))
            MMS(specs, [qt, CMB] + ktb, [ps])
            p = PT.next()
            kw = dict(scale=scale)
            rd = [ps]
            if bias is not None:
                kw["bias"] = bias(j)
                rd.append(CSPB)
            ACT(p.t[:, c0:c0 + n], ps.t[:, c0:c0 + n], AF.Exp, rd, [p], **kw)
            first = (idx == 0)
            last = (idx == len(js) - 1)

            def pv(p=p, c0=c0, n=n, j=j, first=first, last=last):
                specs = [(pO.t[:, c0:c0 + n], vaf(j), p.t[:, c0:c0 + n], first, last)]
                wr = [pO]
                if lbank is not None:
                    specs.append((lbank.t[:, c0:c0 + n], ones_b, p.t[:, c0:c0 + n], first, last))
                    wr.append(lbank)
                MMS(specs, [p, CMB] + vab, wr)
                if last and fin is not None:
                    fin()
            pend.append(pv)
            pipe_flush(keep=(3 if deep else SKEW))

    def load_va64(src, s, col0, v):
        ap = bass.AP(YB, 4096 + v * 2048 + v * 64, [[YROW, 128], [128, NB], [1, 64]])
        P.dma("sp", ap, src.t[s, :, :, col0:col0 + 64].rearrange("b p n -> p b n"), Y.p[4 + 2 * v],
              allp(src, s), va_bufs(v), partial=True)

    def set_ones():
        for v in range(2):
            ap = bass.AP(YB, 4096 + v * 2048 + (1 - v) * 64, [[YROW, 128], [128, NB], [1, 64]])
            P.op("dve", lambda e, ap=ap: e.memset(ap, 1.0), [], va_bufs(v))

    def zero_q_half(q, half):
        P.op("dve", lambda e, q=q, half=half: e.memset(q.t[half * 64:half * 64 + 64, :], 0.0), [], [q])

    def load_q_half(src, s, ch, half, t, q):
        r0 = half * 64
        P.dma("sp", q.t[r0:r0 + 64, :], src.t[s, ch, r0:r0 + 64, tsl(t)], q, [src.p[pt(s, t)]], [q], partial=True)
        return q

    def finalize64(pO, par, lbias, r_extra, s, t, chunk):
        n0, l0 = par * 64, (1 - par) * 64
        f2 = FT.next()
        if lbias is not None:
            f1 = FT.next()
            ACT(f1.t[n0:n0 + 64, :], pO.t[l0:l0 + 64, :], AF.Ln, [pO] + list(r_extra), [f1], bias=lbias)
            ACT(f2.t[n0:n0 + 64, :], f1.t[n0:n0 + 64, :], AF.Exp, [f1], [f2], scale=-1.0)
        else:
            P.op("dve", lambda e: e.reciprocal(out=f2.t[n0:n0 + 64, :], in_=pO.t[l0:l0 + 64, :]), [pO], [f2])
        xb = XB.next()
        TTO(xb.t[n0:n0 + 64, :], pO.t[n0:n0 + 64, :], f2.t[n0:n0 + 64, :], ALU.mult, [pO, f2], [xb])
        store(OT.t[s, chunk, n0:n0 + 64, tsl(t)], xb.t[n0:n0 + 64, :], xb, [OT.p[pt(s, t)]])

    def phase_B_even(l, s):
        sc_m = 96.0 ** -0.5
        pipe_flush()
        set_ones()
        for h in range(8):
            par = h % 2
            k = par
            P.dma("sp", kt_ap(k, rows=96), KA.t[s, h, 0:96, :], Y.p[2 * k], allp(KA, s), kt_bufs(k))
            load_va64(VA, s, h * 64, par)
            for t in range(NTL):
                q = QTB.next()
                P.dma("sp", q.t[0:96, :], QA.t[s, h, 0:96, tsl(t)], q, [QA.p[pt(s, t)]], [q])
                pO = obk4.next()
                attn_core(lambda j, k=k: kt_ap(k, rows=96, c0=j * 128, n=128), kt_bufs(k), 96, 0, q,
                          lambda j, v=par: va_blk(v, j), va_bufs(par), sc_m, pO, t,
                          fin=lambda pO=pO, par=par, t=t, h=h: finalize64(pO, par, None, [], s, t, h // 2))
        for grp in range(2):
            k = grp
            P.dma("sp", kt_ap(k), KB.t[s, 0, :, :], Y.p[2 * k], allp(KB, s), kt_bufs(k))
            load_va64(VB, s, grp * 64, grp)
            pipe_flush()
            for q in QTBL:
                zero_q_half(q, 1 - grp)
            for j4 in range(4):
                horig = grp * 4 + j4
                for t in range(NTL):
                    q = load_q_half(QB, s, j4, grp, t, QTB.next())
                    pO = obk4.next()
                    attn_core(lambda j, k=k: kt_ap(k, c0=j * 128, n=128), kt_bufs(k),
                              128, 0, q, lambda j, v=grp: va_blk(v, j), va_bufs(grp), 0.125, pO, t, swa=True,
                              fin=lambda pO=pO, grp=grp, horig=horig, t=t, j4=j4: finalize64(
                                  pO, grp, SM.t[grp * 64:grp * 64 + 64, horig:horig + 1], [SM], s, t, 4 + j4))

    def phase_B_odd(l, s):
        i = l // 2
        lam_init = 0.8 - 0.6 * math.exp(-0.3 * l)
        P.dma("sp", CSPB.t[:], CSP.t[s], CSPB, [CSP.p[s]], [CSPB])
        pipe_flush()
        for qi, q in enumerate(QTBL):
            zero_q_half(q, 1 - (qi % 2))
        dcnt = 0
        for h in range(4):
            k = h % 2
            v = h % 2
            P.dma("sp", kt_ap(k), KB.t[s, h, :, :], Y.p[2 * k], allp(KB, s), kt_bufs(k))
            vap = bass.AP(YB, 4096 + v * 2048, [[YROW, 128], [128, NB], [1, 128]])
            P.dma("sp", vap, VB.t[s, :, :, h * 128:(h + 1) * 128].rearrange("b p n -> p b n"), Y.p[4 + 2 * v],
                  allp(VB, s), va_bufs(v))
            for t in range(NTL):
                dcnt += 1
                for mp in range(2):
                    q = load_q_half(QB, s, h, mp, t, QTBL[2 * (dcnt % 2) + mp])
                    pO = obk2.next()
                    pL = PB[5 + mp]

                    def fin(pO=pO, pL=pL, mp=mp, t=t, h=h):
                        f1 = FT.next()
                        ACT(f1.t[:], pL.t[:], AF.Ln, [pL], [f1])
                        f2 = FT.next()
                        ACT(f2.t[:], f1.t[:], AF.Exp, [f1], [f2], scale=-1.0)
                        TTO(CQC.t[:, mp, :], pO.t[:], f2.t[:], ALU.mult, [pO, f2], [CQC])
                        if mp == 0:
                            return
                        od = CQC.t[:, 2, :]
                        STT(od, CQC.t[:, 1, :], SM.t[:, 16 + i:17 + i], CQC.t[:, 0, :], ALU.mult, ALU.add, [CQC, SM], [CQC])
                        nb = PB[7]
                        sumsq_bank([od], [CQC], nb)
                        rs = rstd_from(nb, 128.0)
                        xb = XB.next()
                        STT(xb.t[:], od, CST.t[:, l, 53:54], rs.t[:], ALU.mult, ALU.mult, [CQC, rs, CST], [xb])
                        xb2 = XB.next()
                        TS(xb2.t[:], xb.t[:], 1.0 - lam_init, None, ALU.mult, ALU.bypass, [xb], [xb2])
                        store(OT.t[s, h, :, tsl(t)], xb2.t[:], xb2, [OT.p[pt(s, t)]])
                    attn_core(lambda j, k=k: kt_ap(k, c0=j * 128, n=128), kt_bufs(k),
                              128, 0, q, lambda j, v=v: va_blk(v, j), va_bufs(v), 0.125, pO, t, lbank=pL, fin=fin, deep=False)
        pipe_flush()
        for k in range(2):
            ap = bass.AP(YB, 64 * YROW + k * 2048, [[YROW, 1], [1, 2048]])
            P.op("dve", lambda e, ap=ap: e.memset(ap, 1.0), [], kt_bufs(k))
        set_ones()
        for h in range(8):
            par = h % 2
            k = par
            P.dma("sp", kt_ap(k, rows=64), KA.t[s, h, 0:64, :], Y.p[2 * k], allp(KA, s), kt_bufs(k), partial=True)
            load_va64(VA, s, h * 64, par)
            for t in range(NTL):
                q = QTB.next()
                P.dma("sp", q.t[0:65, :], QA.t[s, h, 0:65, tsl(t)], q, [QA.p[pt(s, t)]], [q])
                pO = obk4.next()
                attn_core(lambda j, k=k: kt_ap(k, rows=65, c0=j * 128, n=128), kt_bufs(k), 65, 0, q,
                          lambda j, v=par: va_blk(v, j), va_bufs(par), 0.125, pO, t,
                          bias=lambda j, h=h: CSPB.t[:, j, h:h + 1],
                          fin=lambda pO=pO, par=par, t=t, h=h: finalize64(pO, par, None, [], s, t, 4 + h // 2))

    def phase_C1(l, s, t):
        st_o = STG.next()
        P.dma("sp", bass.AP(st_o.t, 0, [[4096, 128], [TT, 8], [1, TT]]),
              OT.t[s, :, :, tsl(t)].rearrange("c p t -> p c t"), st_o, [OT.p[pt(s, t)]], [st_o])
        st_g = STG.next()
        P.dma("sp", bass.AP(st_g.t, 0, [[4096, 128], [TT, 8], [1, TT]]),
              GT.t[s, :, :, tsl(t)].rearrange("c p t -> p c t"), st_g, [GT.p[pt(s, t)]], [st_g])
        for c in range(8):
            TTO(HT.t[:, c, :], st_o.t[:, c * TT:(c + 1) * TT], st_g.t[:, c * TT:(c + 1) * TT], ALU.mult,
                [st_o, st_g], [HT])

    def phase_C2(l, s, t):
        load_x_tile(s, t)
        nb = PB[6]
        sqb = PT.items + QTBL
        for g in range(8):
            pb = proj_fm(g * 128, 128, w=WOUT)
            CPY(Y.t[:, g, :], pb.t[:], [pb], [Y.p[g]])
            ACT(sqb[g].t[:], pb.t[:], AF.Square, [pb], [sqb[g]])
            if g >= 2:
                MM(nb.t[:], [(ones_b, sqb[g - 2].t[:])], [sqb[g - 2], CMB], [nb], start=(g == 2), stop=False)
        for g in (6, 7):
            MM(nb.t[:], [(ones_b, sqb[g].t[:])], [sqb[g], CMB], [nb], start=False, stop=(g == 7))
        return rstd_from(nb, 1024.0)

    def phase_C3(l, s, t, rs):
        for c in range(8):
            f = FT.next()
            TTO(f.t[:], Y.t[:, c, :], rs.t[:], ALU.mult, [Y.p[c], rs], [f])
            STT(XT.t[:, c, :], f.t[:], AD.t[:, l, s, 2, c:c + 1], XT.t[:, c, :], ALU.mult, ALU.add, [f, AD, XT], [XT])
        P.dma("pool", xT.t[s, :, :, tsl(t)].rearrange("c p t -> p c t"), XT.t[:], XT, [XT], [xT.p[pt(s, t)]])

    def phase_C_all(l):
        tiles = [(s, t) for s in range(2) for t in range(NTL)]
        phase_C1(l, *tiles[0])
        for idx, (s, t) in enumerate(tiles):
            rs = phase_C2(l, s, t)
            if idx + 1 < len(tiles):
                phase_C1(l, *tiles[idx + 1])
            phase_C3(l, s, t, rs)

    for l in range(NL):
        i = l // 2
        even = (l % 2 == 0)
        if even:
            ACT(SM.t[:, 0:8], CST.t[:, l, 45:53], AF.Exp, [CST], [SM])
        else:
            lam_init = 0.8 - 0.6 * math.exp(-0.3 * l)
            f = FT.next()
            TTO(f.t[:, 0:64], CST.t[:, l, 62:126], CST.t[:, l, 126:190], ALU.mult, [CST], [f])
            TTO(f.t[:, 64:128], CST.t[:, l, 190:254], CST.t[:, l, 254:318], ALU.mult, [CST], [f])
            P.op("dve", lambda e, f=f: e.reduce_sum(out=SM.t[:, 20:22],
                                                 in_=f.t[:, 0:128].rearrange("p (a b) -> p a b", a=2),
                                                 axis=mybir.AxisListType.X), [f], [SM])
            ACT(SM.t[:, 22:24], SM.t[:, 20:22], AF.Exp, [SM], [SM])
            TTO(SM.t[:, 24:25], SM.t[:, 23:24], SM.t[:, 22:23], ALU.subtract, [SM], [SM])
            TS(SM.t[:, 16 + i:17 + i], SM.t[:, 24:25], -lam_init, None, ALU.add, ALU.bypass, [SM], [SM])
        if STOP < 2:
            continue
        for s in range(2):
            for t in range(NTL):
                if ONE_TILE and (s, t) != (0, 0):
                    continue
                ada_bg_start(l + 1, s * NTL + t)
                (phase_A_even if even else phase_A_odd)(l, s, t)
                ada_bg_end(l + 1, s * NTL + t)
            if not even and not ONE_TILE:
                fox_post(l, s)
        if l + 1 < NL:
            load_weights(l + 1)
        if STOP < 3:
            continue
        for s in range(2):
            (phase_B_even if even else phase_B_odd)(l, s)
        pipe_flush()
        if STOP < 4:
            continue
        phase_C_all(l)
        if l + 1 < NL:
            load_wout(l + 1)

    for s in range(0 if not SKIP_XPRO else 2, 2):
        for t in range(NTL):
            load_x_tile(s, t)
            for b in range(4):
                for half in range(2):
                    pb = projb.next()

                    def fn(e, b=b, half=half, pb=pb):
                        ins = None
                        for cc in range(4):
                            c = half * 4 + cc
                            ins = e.transpose(out=pb.t[:, cc * 128:(cc + 1) * 128],
                                              in_=XT.t[:, c, b * 128:(b + 1) * 128], identity=ident_f)
                        return ins
                    P.op("pe", fn, [XT, CMF], [pb])
                    ft = FT.next()
                    CPY(ft.t[:], pb.t[:], [pb], [ft], eng=("dve" if half else "act"))
                    tok0 = t * TT + b * 128
                    P.dma("pool", out_d[s, tok0:tok0 + 128, half * 512:(half + 1) * 512], ft.t[:], ft, [ft], [])
    P.emit()
    return nc


def _kc(w):
    K, N = w.shape
    return np.ascontiguousarray(w.reshape(K // 128, 128, N).transpose(1, 0, 2))


def _o_perm_even():
    perm = np.zeros(1024, np.int64)
    for f in range(1024):
        cf, r = divmod(f, 128)
        if cf < 4:
            perm[f] = f
        else:
            j = cf - 4
            head = j if r < 64 else 4 + j
            perm[f] = 512 + head * 64 + (r % 64)
    return perm


def _host_consts():
    pos = np.arange(S, dtype=np.float32)
    tab = np.zeros((128, 4, S), np.float32)
    inv64 = (1.0 / (np.float32(10000.0) ** (np.arange(0, 64, 2, dtype=np.float32) / np.float32(64)))).astype(np.float32)
    inv32 = (1.0 / (np.float32(10000.0) ** (np.arange(0, 32, 2, dtype=np.float32) / np.float32(32)))).astype(np.float32)
    for r in range(128):
        i = r % 64
        ang = (pos * inv64[i % 32]).astype(np.float32)
        tab[r, 0] = np.cos(ang)
        tab[r, 1] = -np.sin(ang) if i < 32 else np.sin(ang)
    for r in range(64, 96):
        j = r - 64
        ang = (pos * inv32[j % 16]).astype(np.float32)
        tab[r, 2] = np.cos(ang)
        tab[r, 3] = -np.sin(ang) if j < 16 else np.sin(ang)
    cmf = np.zeros((128, 384), np.float32)
    cmf[:, 0:128] = np.eye(128, dtype=np.float32)
    k = np.arange(128)
    cmf[:, 128:256] = (k[:, None] <= k[None, :]).astype(np.float32)
    cmf[:, 256:384] = 1.0
    cmb = np.zeros((128, 896), np.float32)
    cmb[:, 0:128] = np.eye(128, dtype=np.float32)
    for m in range(128):
        sw = m + 32 if (m % 64) < 32 else m - 32
        cmb[sw, 128 + m] = 1.0
    for m in range(128):
        if 64 <= m < 96:
            sw = m + 16 if (m - 64) < 16 else m - 16
        else:
            sw = m
        cmb[sw, 256 + m] = 1.0
    cmb[:, 384:512] = np.where(k[:, None] > k[None, :], NEG, 0.0)
    cmb[:, 512:640] = np.where(k[:, None] > k[None, :], NEG, 0.0)
    cmb[:, 640:768] = np.where(k[:, None] <= k[None, :], NEG, 0.0)
    cmb[:, 768:896] = 1.0
    return tab, cmf, cmb


def _prep_shared(inp):
    f = lambda a: np.asarray(a, dtype=np.float32)
    sh = {}
    w_ada = f(inp["w_ada"])
    sh["wada"] = np.ascontiguousarray(w_ada.reshape(4, 8, 128, 3072).transpose(0, 2, 1, 3))
    perm_e = _o_perm_even()
    wE, wUQ, wUK, wUV, wOe = [], [], [], [], []
    for i in range(2):
        w = f(inp["ev_w_in"][i])
        cols = [w[:, 0:384], w[:, 384:640], np.zeros((1024, 64), np.float32), w[:, 640:672]]
        sq = w[:, 672:1184].reshape(1024, 8, 64)
        order = []
        for j in range(4):
            order += [j, 4 + j]
        cols.append(sq[:, order, :].reshape(1024, 512))
        cols.append(w[:, 1184:1312])
        cols.append(w[:, 1312:1440])
        cols.append(w[:, 1440:2464][:, perm_e])
        wE.append(_kc(np.concatenate(cols, axis=1)))
        wUQ.append(_kc(np.concatenate([f(inp["ev_w_uq"][i]), np.zeros((384, 32), np.float32)], axis=1)))
        ukv = f(inp["ev_w_ukv"][i]).reshape(256, 8, 128)
        wUK.append(_kc(np.ascontiguousarray(ukv[:, :, 0:64]).reshape(256, 512)))
        wUV.append(_kc(np.ascontiguousarray(ukv[:, :, 64:128]).reshape(256, 512)))
        wOe.append(_kc(f(inp["ev_w_out"][i])[perm_e, :]))
    sh["wE"] = np.stack(wE)
    sh["wUQ"] = np.stack(wUQ)
    sh["wUK"] = np.stack(wUK)
    sh["wUV"] = np.stack(wUV)
    sh["wOe"] = np.stack(wOe)
    wO, wOo = [], []
    for i in range(2):
        w = f(inp["od_w_in"][i])
        cols = [w[:, 0:3080], np.zeros((1024, 8), np.float32), w[:, 3080:4104]]
        wO.append(_kc(np.concatenate(cols, axis=1)))
        wOo.append(_kc(f(inp["od_w_out"][i])))
    sh["wO"] = np.stack(wO)
    sh["wOo"] = np.stack(wOo)
    cst = np.zeros((128, 4, NC_CST), np.float32)
    for l in range(4):
        i = l // 2
        cst[:, l, 0:8] = f(inp["g_pre"][l]).reshape(8, 128).T
        cst[:, l, 8:16] = f(inp["g_post"][l]).reshape(8, 128).T
        cst[:, l, 16:40] = f(inp["b_ada"][l]).reshape(24, 128).T
        if l % 2 == 0:
            cst[:, l, 40:43] = f(inp["ev_q_norm"][i]).reshape(3, 128).T
            cst[:, l, 43:45] = f(inp["ev_kv_norm"][i]).reshape(2, 128).T
            cst[:, l, 45:53] = f(inp["ev_sinks"][i])[None, :]
        else:
            cst[:, l, 53] = f(inp["od_subln"][i])
            cst[:, l, 54:62] = f(inp["od_forget_bias"][i])[None, :]
            cst[:, l, 62:318] = f(inp["od_lambda"][i]).reshape(256)[None, :]
    sh["cst"] = cst
    tab, cmf, cmb = _host_consts()
    sh["tab"] = tab
    sh["cmf"] = cmf
    sh["cmb"] = cmb
    return sh


_NC_CACHE = {}


def run(inputs, NL=4, cores=8):
    x = np.asarray(inputs["x"], dtype=np.float32)
    c = np.asarray(inputs["c"], dtype=np.float32)
    sh = _prep_shared(inputs)
    in_maps = []
    for core in range(cores):
        m = dict(sh)
        m["x"] = np.ascontiguousarray(x[2 * core:2 * core + 2])
        cc = c[2 * core:2 * core + 2]
        m["cT"] = np.ascontiguousarray(cc.T.reshape(8, 128, 2).transpose(1, 0, 2))
        in_maps.append(m)
    if NL not in _NC_CACHE:
        _NC_CACHE[NL] = build(NL)
    res = run_bass_kernel_spmd(_NC_CACHE[NL], in_maps, core_ids=list(range(cores)))
    if DEBUG:
        return res.results
    return np.concatenate([r["out"] for r in res.results], axis=0).astype(np.float32)


def kernel(**inputs):
    return run(inputs, NL=4, cores=8)
```

```python
from contextlib import ExitStack
import math
import numpy as np
import concourse.bass as bass
import concourse.mybir as mybir
from concourse.bass_utils import run_bass_kernel_spmd

F32 = mybir.dt.float32
BF16 = mybir.dt.bfloat16
AF = mybir.ActivationFunctionType
ALU = mybir.AluOpType

S = 2048
TT = 512
NTL = S // TT
NB = S // 128
EPS = 1e-6
NC_CST = 320
NE = 2528
NO = 4112
NEG = -30000.0
STOP = 9
ONE_TILE = False
DEBUG = False
ASTOP = 99
SKIP_XPRO = False


class Buf:
    __slots__ = ("name", "writers", "readers", "excl")

    def __init__(self, name, excl=False):
        self.name = name
        self.excl = excl
        self.writers = {}
        self.readers = {}


class Tens:
    def __init__(self, t, name, parts, excl=False):
        self.t = t
        self.name = name
        self.p = [Buf(f"{name}.{i}", excl) for i in range(parts)]


ENGS = ("pe", "act", "dve", "pool", "sp")


class Prog:
    def __init__(self, nc, same_engine_sync=True):
        self.nc = nc
        self.es = ExitStack()
        self.ops = {e: [] for e in ENGS}
        self.cnt = {e: 0 for e in ENGS}
        self.known = {e: {} for e in ENGS}
        self.snap = {}
        self.chan_cnt = {}
        self.sem = {}
        self.same_engine_sync = same_engine_sync

    def sbuf(self, name, shape, dtype, parts=1):
        t = self.es.enter_context(self.nc.sbuf_tensor("sb_" + name, list(shape), dtype))
        return Tens(t, name, parts)

    def psum(self, name, shape, dtype, parts=1):
        t = self.es.enter_context(self.nc.psum_tensor("ps_" + name, list(shape), dtype))
        return Tens(t, name, parts, excl=True)

    def dram(self, name, shape, dtype, kind="Internal", parts=1):
        t = self.nc.dram_tensor(name, list(shape), dtype, kind=kind).ap()
        return Tens(t, name, parts)

    @staticmethod
    def _flat(xs):
        out = []
        for x in xs:
            if isinstance(x, Tens):
                out.extend(x.p)
            elif isinstance(x, (list, tuple)):
                out.extend(Prog._flat(x))
            else:
                out.append(x)
        return out

    def _deps(self, eng, reads, writes, skip=None):
        deps = {}

        def add(d):
            for k, v in d.items():
                if deps.get(k, 0) < v:
                    deps[k] = v

        for b in reads:
            add(b.writers)
        for b in writes:
            add(b.writers)
            add(b.readers)
        waits = []
        kn = self.known[eng]
        for k, v in deps.items():
            if k == skip:
                continue
            if k == eng and (eng == "pe" or not self.same_engine_sync):
                continue
            if kn.get(k, 0) >= v:
                continue
            waits.append((k, v))
        for k, v in waits:
            kn[k] = max(kn.get(k, 0), v)
            sn = self.snap.get((k, v))
            if sn:
                for k2, v2 in sn.items():
                    if kn.get(k2, 0) < v2:
                        kn[k2] = v2
        return waits

    def _mark(self, tok, reads, writes, partial):
        k, v = tok
        for b in reads:
            if b.readers.get(k, 0) < v:
                b.readers[k] = v
        for b in writes:
            if partial:
                b.writers[k] = v
            else:
                b.writers = {k: v}
                b.readers = {}

    def op(self, eng, fn, reads=(), writes=()):
        reads = self._flat(reads)
        writes = self._flat(writes)
        writes = writes + [b for b in reads if b.excl and b not in writes]
        reads = [b for b in reads if not b.excl]
        waits = self._deps(eng, reads, writes)
        self.cnt[eng] += 1
        tok = (eng, self.cnt[eng])
        sn = dict(self.known[eng])
        sn[eng] = self.cnt[eng]
        self.snap[tok] = sn
        self._mark(tok, reads, writes, False)
        self.ops[eng].append(("op", waits, fn, tok))
        return tok

    def dma(self, eng, out, in_, sb, reads=(), writes=(), partial=False, **kw):
        reads = self._flat(reads)
        writes = self._flat(writes)
        sbb = self._flat([sb])[0]
        chan = "d_" + sbb.name
        waits = self._deps(eng, reads, writes, skip=(chan if partial else None))
        self.chan_cnt[chan] = self.chan_cnt.get(chan, 0) + 16
        tok = (chan, self.chan_cnt[chan])
        self.snap[tok] = dict(self.known[eng])
        self._mark(tok, reads, writes, partial)
        self.ops[eng].append(("dma", waits, (out, in_, kw), tok))
        return tok

    def emit(self):
        nc = self.nc
        for k in list(ENGS) + list(self.chan_cnt):
            if k not in self.sem:
                self.sem[k] = self.es.enter_context(nc.semaphore("s_" + k.replace(".", "_")))
        final = list(self.chan_cnt.items())

        def run(ename, eng):
            for kind, waits, payload, tok in self.ops[ename]:
                for k, v in waits:
                    eng.wait_ge(self.sem[k], v)
                if kind == "op":
                    payload(eng).then_inc(self.sem[tok[0]], 1)
                else:
                    out, in_, kw = payload
                    eng.dma_start(out=out, in_=in_, **kw).then_inc(self.sem[tok[0]], 16)
            if ename == "sp":
                for c, v in final:
                    eng.wait_ge(self.sem[c], v)

        with nc.Block() as block:
            @block.sync
            def _(e):
                run("sp", e)

            @block.scalar
            def _(e):
                run("act", e)

            @block.vector
            def _(e):
                run("dve", e)

            @block.gpsimd
            def _(e):
                run("pool", e)

            @block.tensor
            def _(e):
                run("pe", e)
        self.es.close()


class Rot:
    def __init__(self, items):
        self.items = items
        self.i = 0

    def next(self):
        x = self.items[self.i % len(self.items)]
        self.i += 1
        return x


def build(NL=4):
    nc = bass.Bass("TRN2", target_bir_lowering=False)
    P = Prog(nc)

    def din(name, shape, dt=F32):
        return nc.dram_tensor(name, list(shape), dt, kind="ExternalInput").ap()

    x_d = din("x", [2, S, 1024])
    cT_d = din("cT", [128, 8, 2])
    wada_d = din("wada", [4, 128, 8, 3072])
    cst_d = din("cst", [128, 4, NC_CST])
    cmf_d = din("cmf", [128, 384])
    cmb_d = din("cmb", [128, 896])
    tab_d = din("tab", [128, 4, S])
    wE_d = din("wE", [2, 128, 8, NE])
    wUQ_d = din("wUQ", [2, 128, 3, 800])
    wUK_d = din("wUK", [2, 128, 2, 512])
    wUV_d = din("wUV", [2, 128, 2, 512])
    wOe_d = din("wOe", [2, 128, 8, 1024])
    wO_d = din("wO", [2, 128, 8, NO])
    wOo_d = din("wOo", [2, 128, 8, 1024])
    out_d = nc.dram_tensor("out", [2, S, 1024], F32, kind="ExternalOutput").ap()

    def scr(name, shape, dt, parts=2 * NTL):
        return P.dram(name, shape, dt, kind=("ExternalOutput" if DEBUG else "Internal"), parts=parts)

    xT = scr("xT", [2, 8, 128, S], F32)
    QA = scr("QA", [2, 8, 128, S], BF16)
    KA = scr("KA", [2, 8, 128, S], BF16)
    QB = scr("QB", [2, 4, 128, S], BF16)
    KB = scr("KB", [2, 4, 128, S], BF16)
    VA = scr("VA", [2, NB, 128, 512], BF16)
    VB = scr("VB", [2, NB, 128, 512], BF16)
    GT = scr("GT", [2, 8, 128, S], BF16)
    OT = scr("OT", [2, 8, 128, S], BF16)
    CSP = scr("CSP", [2, 128, NB, 8], F32, parts=2)

    def pt(s, t):
        return s * NTL + t

    def allp(T, s):
        return [T.p[pt(s, t)] for t in range(NTL)]

    WIN = P.sbuf("WIN", [128, 8, NO], BF16)
    WOUT = P.sbuf("WOUT", [128, 8, 1024], BF16)
    WUQ = P.sbuf("WUQ", [128, 3, 800], BF16)
    WUK = P.sbuf("WUK", [128, 2, 512], BF16)
    WUV = P.sbuf("WUV", [128, 2, 512], BF16)
    CST = P.sbuf("CST", [128, 4, NC_CST], F32)
    AD = P.sbuf("AD", [128, 4, 2, 3, 8], F32)
    CMF = P.sbuf("CMF", [128, 384], F32)
    CMB = P.sbuf("CMB", [128, 896], BF16)
    TAB = P.sbuf("TAB", [128, 4, TT], F32)
    XT = P.sbuf("XT", [128, 8, TT], F32)
    HT = P.sbuf("HT", [128, 8, TT], BF16)
    SQ = Rot([P.sbuf(f"SQ{i}", [128, TT], BF16) for i in range(2)])
    FT = Rot([P.sbuf(f"FT{i}", [128, TT], F32) for i in range(4)])
    RSB = Rot([P.sbuf(f"RSB{i}", [128, TT], F32) for i in range(2)])
    STG = Rot([P.sbuf(f"STG{i}", [128, 4096], BF16) for i in range(2)])
    CQC = P.sbuf("CQC", [128, 5, TT], F32)
    CQN = P.sbuf("CQN", [128, 5, TT], BF16)
    XB = Rot([P.sbuf(f"XB{i}", [128, TT], BF16) for i in range(2)])
    Y = P.sbuf("Y", [128, 8, TT], F32, parts=8)
    YB = Y.t.bitcast(BF16)
    QTBL = [P.sbuf(f"QTB{i}", [128, TT], BF16) for i in range(4)]
    QTB = Rot(QTBL)
    PT = Rot([P.sbuf(f"PT{i}", [128, TT], BF16) for i in range(4)])
    CSPB = P.sbuf("CSPB", [128, NB, 8], F32)
    SPT = P.sbuf("SPT", [128, NB, 8], F32)
    PREF = P.sbuf("PREF", [128, NB, 8], F32)
    FQ8 = P.sbuf("FQ8", [128, S], BF16)
    SM = P.sbuf("SM", [128, 64], F32)
    CTB = P.sbuf("CTB", [128, 8, 2], BF16)
    CTF = P.sbuf("CTF", [128, 8, 2], F32)
    MODF = P.sbuf("MODF", [128, 4, 24, 2], F32)

    PB = [P.psum(f"B{i}", [128, 512], F32) for i in range(8)]

    ident_f = CMF.t[:, 0:128]
    tri_f = CMF.t[:, 128:256]
    ones_f = CMF.t[:, 256:384]
    ident_b = CMB.t[:, 0:128]
    perm64 = CMB.t[:, 128:256]
    perm32 = CMB.t[:, 256:384]
    mc_b = CMB.t[:, 384:512]
    msw_b = CMB.t[:, 512:768]
    ones_b = CMB.t[:, 768:896]

    YROW = 8192

    def kt_ap(k, rows=128, p0=0, c0=0, n=2048):
        return bass.AP(YB, p0 * YROW + k * 2048 + c0, [[YROW, rows], [1, n]])

    def kt_bufs(k):
        return [Y.p[2 * k], Y.p[2 * k + 1]]

    def va_blk(v, j):
        return bass.AP(YB, 4096 + v * 2048 + j * 128, [[YROW, 128], [1, 128]])

    def va_bufs(v):
        return [Y.p[4 + 2 * v], Y.p[5 + 2 * v]]

    def ACT(out, in_, func, r, w, **kw):
        P.op("act", lambda e: e.activation(out=out, in_=in_, func=func, **kw), r, w)

    def TTO(out, a, b, op, r, w, eng="dve"):
        P.op(eng, lambda e: e.tensor_tensor(out=out, in0=a, in1=b, op=op), r, w)

    def TS(out, a, s1, s2, op0, op1, r, w, eng="dve"):
        P.op(eng, lambda e: e.tensor_scalar(out=out, in0=a, scalar1=s1, scalar2=s2, op0=op0, op1=op1), r, w)

    def STT(out, a, s, b, op0, op1, r, w):
        P.op("dve", lambda e: e.scalar_tensor_tensor(out=out, in0=a, scalar=s, in1=b, op0=op0, op1=op1), r, w)

    def CPY(out, in_, r, w, eng="dve"):
        if eng == "act":
            P.op("act", lambda e: e.activation(out=out, in_=in_, func=AF.Identity), r, w)
        else:
            P.op(eng, lambda e: e.tensor_copy(out=out, in_=in_), r, w)

    def MM(out, pairs, r, w, start=True, stop=True):
        pairs = list(pairs)

        def fn(e):
            n = len(pairs)
            ins = None
            for i, (l, rr) in enumerate(pairs):
                ins = e.matmul(out, lhsT=l, rhs=rr, start=(start and i == 0), stop=(stop and i == n - 1))
            return ins
        P.op("pe", fn, r, w)

    def MMS(specs, r, w):
        specs = list(specs)

        def fn(e):
            ins = None
            for (o, l, rr, st, sp) in specs:
                ins = e.matmul(o, lhsT=l, rhs=rr, start=st, stop=sp)
            return ins
        P.op("pe", fn, r, w)

    projb = Rot(PB[0:5])

    def rstd_from(ps, n_feat):
        t1 = FT.next()
        ACT(t1.t[:], ps.t[:], AF.Ln, [ps], [t1], scale=1.0 / n_feat, bias=EPS)
        t2 = RSB.next()
        ACT(t2.t[:], t1.t[:], AF.Exp, [t1], [t2], scale=-0.5)
        return t2

    CASTKW = dict(max_dma_last_dim=4096)

    def load_weights(l):
        i = l // 2
        if l % 2 == 0:
            for c in range(8):
                P.dma("pool", WIN.t[:, c, 0:NE], wE_d[i, :, c, :], WIN, [], [WIN], partial=(c > 0), **CASTKW)
            P.dma("pool", WUQ.t[:], wUQ_d[i], WUQ, [], [WUQ], **CASTKW)
            P.dma("pool", WUK.t[:], wUK_d[i], WUK, [], [WUK], **CASTKW)
            P.dma("pool", WUV.t[:], wUV_d[i], WUV, [], [WUV], **CASTKW)
        else:
            for c in range(8):
                P.dma("pool", WIN.t[:, c, :], wO_d[i, :, c, :], WIN, [], [WIN], partial=(c > 0), **CASTKW)

    def load_wout(l):
        i = l // 2
        src = wOe_d if l % 2 == 0 else wOo_d
        for c in range(8):
            P.dma("pool", WOUT.t[:, c, :], src[i, :, c, :], WOUT, [], [WOUT], partial=(c > 0), **CASTKW)

    P.dma("sp", CST.t[:], cst_d, CST, [], [CST])
    P.dma("sp", CMF.t[:], cmf_d, CMF, [], [CMF])
    P.dma("pool", CMB.t[:], cmb_d, CMB, [], [CMB])
    P.dma("sp", CTF.t[:], cT_d, CTF, [], [CTF])
    ACT(CTB.t[:], CTF.t[:], AF.Silu, [CTF], [CTB])

    load_weights(0)

    xits = [(s, t, hb) for s in range(0 if not SKIP_XPRO else 2, 2) for t in range(NTL) for hb in range(2)]

    def x_load(it):
        s, t, hb = xits[it]
        stg = XT if it % 2 == 0 else Y
        xin = bass.AP(stg.t, 0, [[8 * TT, 128], [1024, 2], [1, 1024]])
        tok0 = t * TT + hb * 256
        src = x_d[s, tok0:tok0 + 256, :].rearrange("(b p) d -> p b d", p=128)
        P.dma("sp", xin, src, stg.p[0], [], [stg])

    if xits:
        x_load(0)
    for it, (s, t, hb) in enumerate(xits):
        if it + 1 < len(xits):
            x_load(it + 1)
        stg = XT if it % 2 == 0 else Y
        tok0 = t * TT + hb * 256
        for c in range(8):
            pb = projb.next()

            def fn(e, c=c, pb=pb, stg=stg):
                ins = None
                for b in range(2):
                    a_in = bass.AP(stg.t, b * 1024 + c * 128, [[8 * TT, 128], [1, 128]])
                    ins = e.transpose(out=pb.t[:, b * 128:(b + 1) * 128], in_=a_in, identity=ident_f)
                return ins
            P.op("pe", fn, [stg, CMF], [pb])
            ft = FT.next()
            CPY(ft.t[:, 0:256], pb.t[:, 0:256], [pb], [ft], eng=("dve" if c % 2 else "act"))
            P.dma("sp", xT.t[s, c, :, tok0:tok0 + 256], ft.t[:, 0:256], ft, [ft], [xT.p[pt(s, t)]], partial=True)

    def ada_piece_dma(l, pc, dst_ap, sbt, bufs):
        P.dma("pool", dst_ap, wada_d[l, :, :, pc * 512:(pc + 1) * 512], sbt, [], bufs, **CASTKW)

    def ada_piece_mm(l, pc, lw_fn, bufs):
        pb = projb.next()
        specs = []
        for g4 in range(4):
            for kc in range(8):
                specs.append((pb.t[:, 2 * g4:2 * g4 + 2], lw_fn(kc, g4), CTB.t[:, kc, :], kc == 0, kc == 7))
        MMS(specs, list(bufs) + [CTB], [pb])
        for g4 in range(4):
            g = pc * 4 + g4
            TS(MODF.t[:, l, g, :], pb.t[:, 2 * g4:2 * g4 + 2], CST.t[:, l, 16 + g:17 + g], None, ALU.add, ALU.bypass,
               [pb, CST], [MODF])

    def ada_finish(l):
        for b in range(2):
            a1 = AD.t[:, l, b, 0, :]
            TS(a1, MODF.t[:, l, 8:16, b], 1.0, None, ALU.add, ALU.bypass, [MODF], [AD])
            TTO(a1, a1, CST.t[:, l, 0:8], ALU.mult, [AD, CST], [AD])
            CPY(AD.t[:, l, b, 1, :], MODF.t[:, l, 0:8, b], [MODF], [AD])
            TTO(AD.t[:, l, b, 2, :], MODF.t[:, l, 16:24, b], CST.t[:, l, 8:16], ALU.mult, [MODF, CST], [AD])

    if NL > 0:
        for pc in range(6):
            wbuf = STG.next()
            ada_piece_dma(0, pc, bass.AP(wbuf.t, 0, [[4096, 128], [512, 8], [1, 512]]), wbuf, [wbuf])
            ada_piece_mm(0, pc, lambda kc, g4, wbuf=wbuf: bass.AP(wbuf.t, kc * 512 + g4 * 128, [[4096, 128], [1, 128]]), [wbuf])
        ada_finish(0)

    def yhalf_bufs(k):
        return [Y.p[4 * k + j] for j in range(4)]

    def ada_bg_start(l, ti):
        if l >= NL or ti >= 6:
            return
        k = ti % 2
        ada_piece_dma(l, ti, bass.AP(YB, k * 4096, [[YROW, 128], [512, 8], [1, 512]]), Y.p[4 * k], yhalf_bufs(k))

    def ada_bg_end(l, ti):
        if l >= NL or ti >= 6:
            return
        k = ti % 2
        ada_piece_mm(l, ti, lambda kc, g4, k=k: bass.AP(YB, k * 4096 + kc * 512 + g4 * 128, [[YROW, 128], [1, 128]]),
                     yhalf_bufs(k))
        if ti == 5:
            ada_finish(l)

    load_wout(0)

    def tsl(t):
        return slice(t * TT, (t + 1) * TT)

    def load_x_tile(s, t):
        P.dma("sp", XT.t[:], xT.t[s, :, :, tsl(t)].rearrange("c p t -> p c t"), XT, [xT.p[pt(s, t)]], [XT])

    def sumsq_bank(srcs, reads, bank):
        n = len(srcs)
        for i, a in enumerate(srcs):
            sq = SQ.next()
            ACT(sq.t[:], a, AF.Square, reads, [sq])
            MM(bank.t[:], [(ones_b, sq.t[:])], [sq, CMB], [bank], start=(i == 0), stop=(i == n - 1))

    rs_next = {}

    pend_mult = []

    def pop_mult(n=1):
        for _ in range(n):
            if pend_mult:
                pend_mult.pop(0)()

    def norm_sq(s, t):
        load_x_tile(s, t)
        bufs = PT.items + QTBL
        for c in range(8):
            ACT(bufs[c].t[:], XT.t[:, c, :], AF.Square, [XT], [bufs[c]])

    def norm_mm(s, t):
        bufs = PT.items + QTBL
        nb = PB[5]
        for c in range(8):
            MM(nb.t[:], [(ones_b, bufs[c].t[:])], [bufs[c], CMB], [nb], start=(c == 0), stop=(c == 7))
        rs = rstd_from(nb, 1024.0)
        for c in range(8):
            pend_mult.append(lambda c=c, rs=rs: TTO(XT.t[:, c, :], XT.t[:, c, :], rs.t[:], ALU.mult, [XT, rs], [XT]))
        rs_next[(s, t)] = rs

    def norm_part(s, t):
        norm_sq(s, t)
        norm_mm(s, t)

    def next_tile(s, t):
        if t + 1 < NTL:
            return (s, t + 1)
        if s == 0:
            return (1, 0)
        return None

    def phase_A_common(l, s, t):
        if (s, t) not in rs_next:
            norm_part(s, t)
        pop_mult(8)
        rs_next.pop((s, t))
        P.dma("sp", TAB.t[:], tab_d[:, :, tsl(t)], TAB, [], [TAB])
        for c in range(8):
            if c % 2 == 0:
                ACT(HT.t[:, c, :], XT.t[:, c, :], AF.Identity, [XT, AD], [HT], scale=AD.t[:, l, s, 0, c:c + 1],
                    bias=AD.t[:, l, s, 1, c:c + 1])
            else:
                TS(HT.t[:, c, :], XT.t[:, c, :], AD.t[:, l, s, 0, c:c + 1], AD.t[:, l, s, 1, c:c + 1], ALU.mult, ALU.add,
                   [XT, AD], [HT])

    def hoist_sq(s, t):
        nt = next_tile(s, t)
        if nt is not None and not ONE_TILE:
            norm_sq(*nt)

    def hoist_mm(s, t):
        nt = next_tile(s, t)
        if nt is not None and not ONE_TILE:
            norm_mm(*nt)

    def proj_fm(col0, M, reads_extra=(), kc=8, w=None, rhs=None):
        w = w or WIN
        pb = projb.next()
        pairs = []
        for c in range(kc):
            r = HT.t[:, c, :] if rhs is None else rhs(c)
            pairs.append((w.t[:, c, col0:col0 + M], r))
        MM(pb.t[0:M, :], pairs, [w, HT] + list(reads_extra), [pb])
        return pb

    def rope(pbx, rows, tabi, perm, out_ap, out_w):
        r0, r1 = rows
        kmax = 128
        xb = XB.next()
        ACT(xb.t[0:kmax, :], pbx.t[0:kmax, :], AF.Identity, [pbx], [xb])
        pb2 = projb.next()
        MM(pb2.t[0:kmax, :], [(perm[0:kmax, 0:kmax], xb.t[0:kmax, :])], [xb, CMB], [pb2])
        f1 = FT.next()
        TTO(f1.t[r0:r1, :], pbx.t[r0:r1, :], TAB.t[r0:r1, tabi, :], ALU.mult, [pbx, TAB], [f1])
        f2 = FT.next()
        TTO(f2.t[r0:r1, :], pb2.t[r0:r1, :], TAB.t[r0:r1, tabi + 1, :], ALU.mult, [pb2, TAB], [f2])
        TTO(out_ap, f1.t[r0:r1, :], f2.t[r0:r1, :], ALU.add, [f1, f2], out_w)

    def store(dst, src, sbt, dparts, partial=True):
        P.dma("pool", dst, src, sbt, [sbt], dparts, partial=partial)

    def gate_proj(l, s, t, col0):
        st = STG.next()
        for c in range(8):
            pb = proj_fm(col0 + c * 128, 128)
            ACT(st.t[:, c * TT:(c + 1) * TT], pb.t[:], AF.Silu, [pb], [st])
            pop_mult()
        store(GT.t[s, :, :, tsl(t)].rearrange("c p t -> p c t"),
              bass.AP(st.t, 0, [[4096, 128], [TT, 8], [1, TT]]), st, [GT.p[pt(s, t)]])

    def v_proj(s, t, col0, ncol, dst, lhs_src=None, w=None, kc=8):
        w = w or WIN
        st = STG.next()
        for b in range(4):
            pb = projb.next()
            pairs = []
            for c in range(kc):
                lt = HT.t[:, c, b * 128:(b + 1) * 128] if lhs_src is None else lhs_src(c, b)
                pairs.append((lt, w.t[:, c, col0:col0 + ncol]))
            MM(pb.t[:, 0:ncol], pairs, [w, HT, CQN], [pb])
            CPY(st.t[:, b * 512:b * 512 + ncol], pb.t[:, 0:ncol], [pb], [st], eng=("dve" if b % 2 else "act"))
        store(dst.t[s, 4 * t:4 * t + 4, :, 0:ncol].rearrange("b p n -> p b n"),
              bass.AP(st.t, 0, [[4096, 128], [512, 4], [1, ncol]]), st, [dst.p[pt(s, t)]])

    def phase_A_even(l, s, t):
        phase_A_common(l, s, t)
        if ASTOP < 2:
            return
        sqb = PT.items + QTBL
        for j in range(5):
            pb = proj_fm((0 if j < 3 else 384 - 3 * 128) + j * 128, 128)
            CPY(CQC.t[:, j, :], pb.t[:], [pb], [CQC], eng="act")
            ACT(sqb[j].t[:], pb.t[:], AF.Square, [pb], [sqb[j]])
        for (c0, nch, ncol, nfeat) in ((0, 3, 40, 384.0), (3, 2, 43, 256.0)):
            nb = PB[6]
            for j in range(nch):
                MM(nb.t[:], [(ones_b, sqb[c0 + j].t[:])], [sqb[c0 + j], CMB], [nb], start=(j == 0), stop=(j == nch - 1))
            rs = rstd_from(nb, nfeat)
            for j in range(nch):
                STT(CQN.t[:, c0 + j, :], CQC.t[:, c0 + j, :], CST.t[:, l, ncol + j:ncol + j + 1], rs.t[:],
                    ALU.mult, ALU.mult, [CQC, CST, rs], [CQN])
        st = STG.next()
        for j in range(4):
            pb = proj_fm(736 + j * 128, 128)
            rope(pb, (0, 128), 0, perm64, st.t[:, j * TT:(j + 1) * TT], [st])
        pb = proj_fm(1248, 128)
        rope(pb, (0, 128), 0, perm64, st.t[:, 4 * TT:5 * TT], [st])
        store(QB.t[s, :, :, tsl(t)].rearrange("c p t -> p c t"),
              bass.AP(st.t, 0, [[4096, 128], [TT, 4], [1, TT]]), st, [QB.p[pt(s, t)]])
        store(KB.t[s, 0, :, tsl(t)], st.t[:, 4 * TT:5 * TT], st, [KB.p[pt(s, t)]])
        if ASTOP < 7:
            return
        hoist_sq(s, t)
        v_proj(s, t, 1376, 128, VB)
        if ASTOP < 8:
            return
        gate_proj(l, s, t, 1504)
        hoist_mm(s, t)
        if ASTOP < 3:
            return
        st = STG.next()
        for h in range(8):
            pb = proj_fm(h * 96, 128, w=WUQ, kc=3, rhs=lambda c: CQN.t[:, c, :], reads_extra=[CQN])
            CPY(st.t[0:64, h * TT:(h + 1) * TT], pb.t[0:64, :], [pb], [st], eng="act")
            rope(pb, (64, 96), 2, perm32, st.t[64:96, h * TT:(h + 1) * TT], [st])
            pop_mult()
        store(QA.t[s, :, 0:96, tsl(t)].rearrange("h p t -> p h t"),
              bass.AP(st.t, 0, [[4096, 96], [TT, 8], [1, TT]]), st, [QA.p[pt(s, t)]])
        if ASTOP < 4:
            return
        st = STG.next()
        for hp in range(4):
            pb = proj_fm(hp * 128, 128, w=WUK, kc=2, rhs=lambda c: CQN.t[:, 3 + c, :], reads_extra=[CQN])
            for e2 in range(2):
                h = 2 * hp + e2
                CPY(st.t[0:64, h * TT:(h + 1) * TT], pb.t[e2 * 64:e2 * 64 + 64, :], [pb], [st],
                    eng=("dve" if e2 else "act"))
        pb = proj_fm(640, 128)
        xk = XB.next()
        rope(pb, (64, 96), 2, perm32, xk.t[64:96, :], [xk])
        store(KA.t[s, :, 0:64, tsl(t)].rearrange("h p t -> p h t"),
              bass.AP(st.t, 0, [[4096, 64], [TT, 8], [1, TT]]), st, [KA.p[pt(s, t)]])
        store(KA.t[s, :, 64:96, tsl(t)].rearrange("h p t -> p h t"),
              bass.AP(xk.t, 64 * TT, [[TT, 32], [0, 8], [1, TT]]), xk, [KA.p[pt(s, t)]])
        if ASTOP < 5:
            return
        v_proj(s, t, 0, 512, VA, lhs_src=lambda c, b: CQN.t[:, 3 + c, b * 128:(b + 1) * 128], w=WUV, kc=2)
        if ASTOP < 6:
            return

    def phase_A_odd(l, s, t):
        phase_A_common(l, s, t)
        for (col0, dst) in ((0, QB), (512, KB)):
            st = STG.next()
            for j in range(4):
                pb = proj_fm(col0 + j * 128, 128)
                rope(pb, (0, 128), 0, perm64, st.t[:, j * TT:(j + 1) * TT], [st])
            store(dst.t[s, :, :, tsl(t)].rearrange("c p t -> p c t"),
                  bass.AP(st.t, 0, [[4096, 128], [TT, 4], [1, TT]]), st, [dst.p[pt(s, t)]])
        v_proj(s, t, 1024, 512, VB)
        hoist_sq(s, t)
        for (col0, dst) in ((1536, QA), (2048, KA)):
            st = STG.next()
            for hp in range(4):
                pb = proj_fm(col0 + hp * 128, 128)
                for e2 in range(2):
                    h = 2 * hp + e2
                    CPY(st.t[0:64, h * TT:(h + 1) * TT], pb.t[e2 * 64:e2 * 64 + 64, :], [pb], [st],
                        eng=("dve" if e2 else "act"))
            store(dst.t[s, :, 0:64, tsl(t)].rearrange("h p t -> p h t"),
                  bass.AP(st.t, 0, [[4096, 64], [TT, 8], [1, TT]]), st, [dst.p[pt(s, t)]])
        v_proj(s, t, 2560, 512, VA)
        pb = projb.next()
        specs = []
        for b in range(4):
            for c in range(8):
                specs.append((pb.t[:, b * 8:(b + 1) * 8], HT.t[:, c, b * 128:(b + 1) * 128], WIN.t[:, c, 3072:3080],
                              c == 0, c == 7))
        MMS(specs, [WIN, HT], [pb])
        f = FT.next()
        for b in range(4):
            TTO(f.t[:, b * 8:(b + 1) * 8], pb.t[:, b * 8:(b + 1) * 8], CST.t[:, l, 54:62], ALU.add, [pb, CST], [f])
        f2 = FT.next()
        ACT(f2.t[:, 0:32], f.t[:, 0:32], AF.Exp, [f], [f2], scale=-1.0)
        ACT(SPT.t[:, 4 * t:4 * t + 4, :].rearrange("p b h -> p (b h)"), f2.t[:, 0:32], AF.Ln, [f2], [SPT], bias=1.0)
        hoist_mm(s, t)
        gate_proj(l, s, t, 3088)

    def fox_post(l, s):
        P.op("dve", lambda e: e.memset(PREF.t[:, 0, :], 0.0), [], [PREF])
        for b in range(1, NB):
            TTO(PREF.t[:, b, :], PREF.t[:, b - 1, :], SPT.t[:, b - 1, :], ALU.add, [PREF, SPT], [PREF])
        pb = projb.next()
        specs = []
        for b in range(NB):
            specs.append((pb.t[:, b * 8:(b + 1) * 8], tri_f, SPT.t[:, b, :], True, False))
            specs.append((pb.t[:, b * 8:(b + 1) * 8], ones_f, PREF.t[:, b, :], False, True))
        MMS(specs, [SPT, PREF, CMF], [pb])
        CPY(CSPB.t[:].rearrange("p b h -> p (b h)"), pb.t[:, 0:NB * 8], [pb], [CSPB])
        P.dma("pool", CSP.t[s], CSPB.t[:], CSPB, [CSPB], [CSP.p[s]])
        for g in range(4):
            pb2 = projb.next()

            def fn(e, g=g, pb2=pb2):
                ins = None
                for bb in range(4):
                    ins = e.transpose(out=pb2.t[0:8, bb * 128:(bb + 1) * 128], in_=CSPB.t[:, g * 4 + bb, :],
                                      identity=ident_f)
                return ins
            P.op("pe", fn, [CSPB, CMF], [pb2])
            ACT(FQ8.t[0:8, g * 512:(g + 1) * 512], pb2.t[0:8, :], AF.Identity, [pb2], [FQ8], scale=-8.0)
        P.dma("pool", QA.t[s, :, 64, :], FQ8.t[0:8, :], FQ8, [FQ8], allp(QA, s), partial=True)

    sbk = Rot(PB[0:3])
    sbk4 = Rot(PB[0:3] + [PB[7]])
    obk4 = Rot(PB[3:7])
    obk2 = Rot(PB[3:5])

    SKEW = 2
    pend = []

    def pipe_flush(keep=0):
        while len(pend) > keep:
            fn = pend.pop(0)
            fn()

    def attn_core(ktf, ktb, krows, kbase, qt, vaf, vab, scale, pO, i, bias=None, swa=False, lbank=None, fin=None, deep=True):
        if swa:
            js = [j for j in range(4 * i - 1, 4 * i + 4) if j >= 0]
        else:
            js = list(range(0, 4 * i + 4))
        for idx, j in enumerate(js):
            m = j - 4 * i
            c0 = 0 if m < 0 else 128 * m
            if swa:
                n = 128 if (m < 0 or m == 3) else 256
            else:
                n = TT - c0
            ps = (sbk4 if deep else sbk).next()
            need_mask = swa or m >= 0
            specs = [(ps.t[:, c0:c0 + n], ktf(j), qt.t[kbase:kbase + krows, c0:c0 + n], True, not need_mask)]
            if swa:
                mk = msw_b[:, 128:256] if m < 0 else msw_b[:, 0:n]
                specs.append((ps.t[:, c0:c0 + n], ident_b, mk, False, True))
            elif m >= 0:
                specs.append((ps.t[:, c0:c0 + 128], ident_b, mc_b, False, True))
            MMS(specs, [qt, CMB] + ktb, [ps])
            p = PT.next()
            kw = dict(scale=scale)
            rd = [ps]
            if bias is not None:
                kw["bias"] = bias(j)
                rd.append(CSPB)
            ACT(p.t[:, c0:c0 + n], ps.t[:, c0:c0 + n], AF.Exp, rd, [p], **kw)
            first = (idx == 0)
            last = (idx == len(js) - 1)

            def pv(p=p, c0=c0, n=n, j=j, first=first, last=last):
                specs = [(pO.t[:, c0:c0 + n], vaf(j), p.t[:, c0:c0 + n], first, last)]
                wr = [pO]
                if lbank is not None:
                    specs.append((lbank.t[:, c0:c0 + n], ones_b, p.t[:, c0:c0 + n], first, last))
                    wr.append(lbank)
                MMS(specs, [p, CMB] + vab, wr)
                if last and fin is not None:
                    fin()
            pend.append(pv)
            pipe_flush(keep=(3 if deep else SKEW))

    def load_va64(src, s, col0, v):
        ap = bass.AP(YB, 4096 + v * 2048 + v * 64, [[YROW, 128], [128, NB], [1, 64]])
        P.dma("sp", ap, src.t[s, :, :, col0:col0 + 64].rearrange("b p n -> p b n"), Y.p[4 + 2 * v],
              allp(src, s), va_bufs(v), partial=True)

    def set_ones():
        for v in range(2):
            ap = bass.AP(YB, 4096 + v * 2048 + (1 - v) * 64, [[YROW, 128], [128, NB], [1, 64]])
            P.op("dve", lambda e, ap=ap: e.memset(ap, 1.0), [], va_bufs(v))

    def zero_q_half(q, half):
        P.op("dve", lambda e, q=q, half=half: e.memset(q.t[half * 64:half * 64 + 64, :], 0.0), [], [q])

    def load_q_half(src, s, ch, half, t, q):
        r0 = half * 64
        P.dma("sp", q.t[r0:r0 + 64, :], src.t[s, ch, r0:r0 + 64, tsl(t)], q, [src.p[pt(s, t)]], [q], partial=True)
        return q

    def finalize64(pO, par, lbias, r_extra, s, t, chunk):
        n0, l0 = par * 64, (1 - par) * 64
        f2 = FT.next()
        if lbias is not None:
            f1 = FT.next()
            ACT(f1.t[n0:n0 + 64, :], pO.t[l0:l0 + 64, :], AF.Ln, [pO] + list(r_extra), [f1], bias=lbias)
            ACT(f2.t[n0:n0 + 64, :], f1.t[n0:n0 + 64, :], AF.Exp, [f1], [f2], scale=-1.0)
        else:
            P.op("dve", lambda e: e.reciprocal(out=f2.t[n0:n0 + 64, :], in_=pO.t[l0:l0 + 64, :]), [pO], [f2])
        xb = XB.next()
        TTO(xb.t[n0:n0 + 64, :], pO.t[n0:n0 + 64, :], f2.t[n0:n0 + 64, :], ALU.mult, [pO, f2], [xb])
        store(OT.t[s, chunk, n0:n0 + 64, tsl(t)], xb.t[n0:n0 + 64, :], xb, [OT.p[pt(s, t)]])

    def phase_B_even(l, s):
        sc_m = 96.0 ** -0.5
        pipe_flush()
        set_ones()
        for h in range(8):
            par = h % 2
            k = par
            P.dma("sp", kt_ap(k, rows=96), KA.t[s, h, 0:96, :], Y.p[2 * k], allp(KA, s), kt_bufs(k))
            load_va64(VA, s, h * 64, par)
            for t in range(NTL):
                q = QTB.next()
                P.dma("sp", q.t[0:96, :], QA.t[s, h, 0:96, tsl(t)], q, [QA.p[pt(s, t)]], [q])
                pO = obk4.next()
                attn_core(lambda j, k=k: kt_ap(k, rows=96, c0=j * 128, n=128), kt_bufs(k), 96, 0, q,
                          lambda j, v=par: va_blk(v, j), va_bufs(par), sc_m, pO, t,
                          fin=lambda pO=pO, par=par, t=t, h=h: finalize64(pO, par, None, [], s, t, h // 2))
        for grp in range(2):
            k = grp
            P.dma("sp", kt_ap(k), KB.t[s, 0, :, :], Y.p[2 * k], allp(KB, s), kt_bufs(k))
            load_va64(VB, s, grp * 64, grp)
            pipe_flush()
            for q in QTBL:
                zero_q_half(q, 1 - grp)
            for j4 in range(4):
                horig = grp * 4 + j4
                for t in range(NTL):
                    q = load_q_half(QB, s, j4, grp, t, QTB.next())
                    pO = obk4.next()
                    attn_core(lambda j, k=k: kt_ap(k, c0=j * 128, n=128), kt_bufs(k),
                              128, 0, q, lambda j, v=grp: va_blk(v, j), va_bufs(grp), 0.125, pO, t, swa=True,
                              fin=lambda pO=pO, grp=grp, horig=horig, t=t, j4=j4: finalize64(
                                  pO, grp, SM.t[grp * 64:grp * 64 + 64, horig:horig + 1], [SM], s, t, 4 + j4))

    def phase_B_odd(l, s):
        i = l // 2
        lam_init = 0.8 - 0.6 * math.exp(-0.3 * l)
        P.dma("sp", CSPB.t[:], CSP.t[s], CSPB, [CSP.p[s]], [CSPB])
        pipe_flush()
        for qi, q in enumerate(QTBL):
            zero_q_half(q, 1 - (qi % 2))
        dcnt = 0
        for h in range(4):
            k = h % 2
            v = h % 2
            P.dma("sp", kt_ap(k), KB.t[s, h, :, :], Y.p[2 * k], allp(KB, s), kt_bufs(k))
            vap = bass.AP(YB, 4096 + v * 2048, [[YROW, 128], [128, NB], [1, 128]])
            P.dma("sp", vap, VB.t[s, :, :, h * 128:(h + 1) * 128].rearrange("b p n -> p b n"), Y.p[4 + 2 * v],
                  allp(VB, s), va_bufs(v))
            for t in range(NTL):
                dcnt += 1
                for mp in range(2):
                    q = load_q_half(QB, s, h, mp, t, QTBL[2 * (dcnt % 2) + mp])
                    pO = obk2.next()
                    pL = PB[5 + mp]

                    def fin(pO=pO, pL=pL, mp=mp, t=t, h=h):
                        f1 = FT.next()
                        ACT(f1.t[:], pL.t[:], AF.Ln, [pL], [f1])
                        f2 = FT.next()
                        ACT(f2.t[:], f1.t[:], AF.Exp, [f1], [f2], scale=-1.0)
                        TTO(CQC.t[:, mp, :], pO.t[:], f2.t[:], ALU.mult, [pO, f2], [CQC])
                        if mp == 0:
                            return
                        od = CQC.t[:, 2, :]
                        STT(od, CQC.t[:, 1, :], SM.t[:, 16 + i:17 + i], CQC.t[:, 0, :], ALU.mult, ALU.add, [CQC, SM], [CQC])
                        nb = PB[7]
                        sumsq_bank([od], [CQC], nb)
                        rs = rstd_from(nb, 128.0)
                        xb = XB.next()
                        STT(xb.t[:], od, CST.t[:, l, 53:54], rs.t[:], ALU.mult, ALU.mult, [CQC, rs, CST], [xb])
                        xb2 = XB.next()
                        TS(xb2.t[:], xb.t[:], 1.0 - lam_init, None, ALU.mult, ALU.bypass, [xb], [xb2])
                        store(OT.t[s, h, :, tsl(t)], xb2.t[:], xb2, [OT.p[pt(s, t)]])
                    attn_core(lambda j, k=k: kt_ap(k, c0=j * 128, n=128), kt_bufs(k),
                              128, 0, q, lambda j, v=v: va_blk(v, j), va_bufs(v), 0.125, pO, t, lbank=pL, fin=fin, deep=False)
        pipe_flush()
        for k in range(2):
            ap = bass.AP(YB, 64 * YROW + k * 2048, [[YROW, 1], [1, 2048]])
            P.op("dve", lambda e, ap=ap: e.memset(ap, 1.0), [], kt_bufs(k))
        set_ones()
        for h in range(8):
            par = h % 2
            k = par
            P.dma("sp", kt_ap(k, rows=64), KA.t[s, h, 0:64, :], Y.p[2 * k], allp(KA, s), kt_bufs(k), partial=True)
            load_va64(VA, s, h * 64, par)
            for t in range(NTL):
                q = QTB.next()
                P.dma("sp", q.t[0:65, :], QA.t[s, h, 0:65, tsl(t)], q, [QA.p[pt(s, t)]], [q])
                pO = obk4.next()
                attn_core(lambda j, k=k: kt_ap(k, rows=65, c0=j * 128, n=128), kt_bufs(k), 65, 0, q,
                          lambda j, v=par: va_blk(v, j), va_bufs(par), 0.125, pO, t,
                          bias=lambda j, h=h: CSPB.t[:, j, h:h + 1],
                          fin=lambda pO=pO, par=par, t=t, h=h: finalize64(pO, par, None, [], s, t, 4 + h // 2))

    def phase_C1(l, s, t):
        st_o = STG.next()
        P.dma("sp", bass.AP(st_o.t, 0, [[4096, 128], [TT, 8], [1, TT]]),
              OT.t[s, :, :, tsl(t)].rearrange("c p t -> p c t"), st_o, [OT.p[pt(s, t)]], [st_o])
        st_g = STG.next()
        P.dma("sp", bass.AP(st_g.t, 0, [[4096, 128], [TT, 8], [1, TT]]),
              GT.t[s, :, :, tsl(t)].rearrange("c p t -> p c t"), st_g, [GT.p[pt(s, t)]], [st_g])
        for c in range(8):
            TTO(HT.t[:, c, :], st_o.t[:, c * TT:(c + 1) * TT], st_g.t[:, c * TT:(c + 1) * TT], ALU.mult,
                [st_o, st_g], [HT])

    def phase_C2(l, s, t):
        load_x_tile(s, t)
        nb = PB[6]
        sqb = PT.items + QTBL
        for g in range(8):
            pb = proj_fm(g * 128, 128, w=WOUT)
            CPY(Y.t[:, g, :], pb.t[:], [pb], [Y.p[g]], eng="act")
            ACT(sqb[g].t[:], pb.t[:], AF.Square, [pb], [sqb[g]])
            if g >= 2:
                MM(nb.t[:], [(ones_b, sqb[g - 2].t[:])], [sqb[g - 2], CMB], [nb], start=(g == 2), stop=False)
        for g in (6, 7):
            MM(nb.t[:], [(ones_b, sqb[g].t[:])], [sqb[g], CMB], [nb], start=False, stop=(g == 7))
        return rstd_from(nb, 1024.0)

    def phase_C3(l, s, t, rs):
        for c in range(8):
            f = FT.next()
            TTO(f.t[:], Y.t[:, c, :], rs.t[:], ALU.mult, [Y.p[c], rs], [f])
            STT(XT.t[:, c, :], f.t[:], AD.t[:, l, s, 2, c:c + 1], XT.t[:, c, :], ALU.mult, ALU.add, [f, AD, XT], [XT])
        P.dma("pool", xT.t[s, :, :, tsl(t)].rearrange("c p t -> p c t"), XT.t[:], XT, [XT], [xT.p[pt(s, t)]])

    def phase_C_all(l):
        tiles = [(s, t) for s in range(2) for t in range(NTL)]
        phase_C1(l, *tiles[0])
        for idx, (s, t) in enumerate(tiles):
            rs = phase_C2(l, s, t)
            if idx + 1 < len(tiles):
                phase_C1(l, *tiles[idx + 1])
            phase_C3(l, s, t, rs)

    for l in range(NL):
        i = l // 2
        even = (l % 2 == 0)
        if even:
            ACT(SM.t[:, 0:8], CST.t[:, l, 45:53], AF.Exp, [CST], [SM])
        else:
            lam_init = 0.8 - 0.6 * math.exp(-0.3 * l)
            f = FT.next()
            TTO(f.t[:, 0:64], CST.t[:, l, 62:126], CST.t[:, l, 126:190], ALU.mult, [CST], [f])
            TTO(f.t[:, 64:128], CST.t[:, l, 190:254], CST.t[:, l, 254:318], ALU.mult, [CST], [f])
            P.op("dve", lambda e, f=f: e.reduce_sum(out=SM.t[:, 20:22],
                                                 in_=f.t[:, 0:128].rearrange("p (a b) -> p a b", a=2),
                                                 axis=mybir.AxisListType.X), [f], [SM])
            ACT(SM.t[:, 22:24], SM.t[:, 20:22], AF.Exp, [SM], [SM])
            TTO(SM.t[:, 24:25], SM.t[:, 23:24], SM.t[:, 22:23], ALU.subtract, [SM], [SM])
            TS(SM.t[:, 16 + i:17 + i], SM.t[:, 24:25], -lam_init, None, ALU.add, ALU.bypass, [SM], [SM])
        if STOP < 2:
            continue
        for s in range(2):
            for t in range(NTL):
                if ONE_TILE and (s, t) != (0, 0):
                    continue
                ada_bg_start(l + 1, s * NTL + t)
                (phase_A_even if even else phase_A_odd)(l, s, t)
                ada_bg_end(l + 1, s * NTL + t)
            if not even and not ONE_TILE:
                fox_post(l, s)
        if l + 1 < NL:
            load_weights(l + 1)
        if STOP < 3:
            continue
        for s in range(2):
            (phase_B_even if even else phase_B_odd)(l, s)
        pipe_flush()
        if STOP < 4:
            continue
        phase_C_all(l)
        if l + 1 < NL:
            load_wout(l + 1)

    for s in range(0 if not SKIP_XPRO else 2, 2):
        for t in range(NTL):
            load_x_tile(s, t)
            for b in range(4):
                for half in range(2):
                    pb = projb.next()

                    def fn(e, b=b, half=half, pb=pb):
                        ins = None
                        for cc in range(4):
                            c = half * 4 + cc
                            ins = e.transpose(out=pb.t[:, cc * 128:(cc + 1) * 128],
                                              in_=XT.t[:, c, b * 128:(b + 1) * 128], identity=ident_f)
                        return ins
                    P.op("pe", fn, [XT, CMF], [pb])
                    ft = FT.next()
                    CPY(ft.t[:], pb.t[:], [pb], [ft], eng=("dve" if half else "act"))
                    tok0 = t * TT + b * 128
                    P.dma("pool", out_d[s, tok0:tok0 + 128, half * 512:(half + 1) * 512], ft.t[:], ft, [ft], [])
    P.emit()
    return nc


def _kc(w):
    K, N = w.shape
    return np.ascontiguousarray(w.reshape(K // 128, 128, N).transpose(1, 0, 2))


def _o_perm_even():
    perm = np.zeros(1024, np.int64)
    for f in range(1024):
        cf, r = divmod(f, 128)
        if cf < 4:
            perm[f] = f
        else:
            j = cf - 4
            head = j if r < 64 else 4 + j
            perm[f] = 512 + head * 64 + (r % 64)
    return perm


def _host_consts():
    pos = np.arange(S, dtype=np.float32)
    tab = np.zeros((128, 4, S), np.float32)
    inv64 = (1.0 / (np.float32(10000.0) ** (np.arange(0, 64, 2, dtype=np.float32) / np.float32(64)))).astype(np.float32)
    inv32 = (1.0 / (np.float32(10000.0) ** (np.arange(0, 32, 2, dtype=np.float32) / np.float32(32)))).astype(np.float32)
    for r in range(128):
        i = r % 64
        ang = (pos * inv64[i % 32]).astype(np.float32)
        tab[r, 0] = np.cos(ang)
        tab[r, 1] = -np.sin(ang) if i < 32 else np.sin(ang)
    for r in range(64, 96):
        j = r - 64
        ang = (pos * inv32[j % 16]).astype(np.float32)
        tab[r, 2] = np.cos(ang)
        tab[r, 3] = -np.sin(ang) if j < 16 else np.sin(ang)
    cmf = np.zeros((128, 384), np.float32)
    cmf[:, 0:128] = np.eye(128, dtype=np.float32)
    k = np.arange(128)
    cmf[:, 128:256] = (k[:, None] <= k[None, :]).astype(np.float32)
    cmf[:, 256:384] = 1.0
    cmb = np.zeros((128, 896), np.float32)
    cmb[:, 0:128] = np.eye(128, dtype=np.float32)
    for m in range(128):
        sw = m + 32 if (m % 64) < 32 else m - 32
        cmb[sw, 128 + m] = 1.0
    for m in range(128):
        if 64 <= m < 96:
            sw = m + 16 if (m - 64) < 16 else m - 16
        else:
            sw = m
        cmb[sw, 256 + m] = 1.0
    cmb[:, 384:512] = np.where(k[:, None] > k[None, :], NEG, 0.0)
    cmb[:, 512:640] = np.where(k[:, None] > k[None, :], NEG, 0.0)
    cmb[:, 640:768] = np.where(k[:, None] <= k[None, :], NEG, 0.0)
    cmb[:, 768:896] = 1.0
    return tab, cmf, cmb


def _prep_shared(inp):
    f = lambda a: np.asarray(a, dtype=np.float32)
    sh = {}
    w_ada = f(inp["w_ada"])
    sh["wada"] = np.ascontiguousarray(w_ada.reshape(4, 8, 128, 3072).transpose(0, 2, 1, 3))
    perm_e = _o_perm_even()
    wE, wUQ, wUK, wUV, wOe = [], [], [], [], []
    for i in range(2):
        w = f(inp["ev_w_in"][i])
        cols = [w[:, 0:384], w[:, 384:640], np.zeros((1024, 64), np.float32), w[:, 640:672]]
        sq = w[:, 672:1184].reshape(1024, 8, 64)
        order = []
        for j in range(4):
            order += [j, 4 + j]
        cols.append(sq[:, order, :].reshape(1024, 512))
        cols.append(w[:, 1184:1312])
        cols.append(w[:, 1312:1440])
        cols.append(w[:, 1440:2464][:, perm_e])
        wE.append(_kc(np.concatenate(cols, axis=1)))
        wUQ.append(_kc(np.concatenate([f(inp["ev_w_uq"][i]), np.zeros((384, 32), np.float32)], axis=1)))
        ukv = f(inp["ev_w_ukv"][i]).reshape(256, 8, 128)
        wUK.append(_kc(np.ascontiguousarray(ukv[:, :, 0:64]).reshape(256, 512)))
        wUV.append(_kc(np.ascontiguousarray(ukv[:, :, 64:128]).reshape(256, 512)))
        wOe.append(_kc(f(inp["ev_w_out"][i])[perm_e, :]))
    sh["wE"] = np.stack(wE)
    sh["wUQ"] = np.stack(wUQ)
    sh["wUK"] = np.stack(wUK)
    sh["wUV"] = np.stack(wUV)
    sh["wOe"] = np.stack(wOe)
    wO, wOo = [], []
    for i in range(2):
        w = f(inp["od_w_in"][i])
        cols = [w[:, 0:3080], np.zeros((1024, 8), np.float32), w[:, 3080:4104]]
        wO.append(_kc(np.concatenate(cols, axis=1)))
        wOo.append(_kc(f(inp["od_w_out"][i])))
    sh["wO"] = np.stack(wO)
    sh["wOo"] = np.stack(wOo)
    cst = np.zeros((128, 4, NC_CST), np.float32)
    for l in range(4):
        i = l // 2
        cst[:, l, 0:8] = f(inp["g_pre"][l]).reshape(8, 128).T
        cst[:, l, 8:16] = f(inp["g_post"][l]).reshape(8, 128).T
        cst[:, l, 16:40] = f(inp["b_ada"][l]).reshape(24, 128).T
        if l % 2 == 0:
            cst[:, l, 40:43] = f(inp["ev_q_norm"][i]).reshape(3, 128).T
            cst[:, l, 43:45] = f(inp["ev_kv_norm"][i]).reshape(2, 128).T
            cst[:, l, 45:53] = f(inp["ev_sinks"][i])[None, :]
        else:
            cst[:, l, 53] = f(inp["od_subln"][i])
            cst[:, l, 54:62] = f(inp["od_forget_bias"][i])[None, :]
            cst[:, l, 62:318] = f(inp["od_lambda"][i]).reshape(256)[None, :]
    sh["cst"] = cst
    tab, cmf, cmb = _host_consts()
    sh["tab"] = tab
    sh["cmf"] = cmf
    sh["cmb"] = cmb
    return sh


_NC_CACHE = {}


def run(inputs, NL=4, cores=8):
    x = np.asarray(inputs["x"], dtype=np.float32)
    c = np.asarray(inputs["c"], dtype=np.float32)
    sh = _prep_shared(inputs)
    in_maps = []
    for core in range(cores):
        m = dict(sh)
        m["x"] = np.ascontiguousarray(x[2 * core:2 * core + 2])
        cc = c[2 * core:2 * core + 2]
        m["cT"] = np.ascontiguousarray(cc.T.reshape(8, 128, 2).transpose(1, 0, 2))
        in_maps.append(m)
    if NL not in _NC_CACHE:
        _NC_CACHE[NL] = build(NL)
    res = run_bass_kernel_spmd(_NC_CACHE[NL], in_maps, core_ids=list(range(cores)))
    if DEBUG:
        return res.results
    return np.concatenate([r["out"] for r in res.results], axis=0).astype(np.float32)


def kernel(**inputs):
    return run(inputs, NL=4, cores=8)
```

```python
from contextlib import ExitStack
import math
import numpy as np
import concourse.bass as bass
import concourse.mybir as mybir
from concourse.bass_utils import run_bass_kernel_spmd

F32 = mybir.dt.float32
BF16 = mybir.dt.bfloat16
AF = mybir.ActivationFunctionType
ALU = mybir.AluOpType

S = 2048
TT = 512
NTL = S // TT
NB = S // 128
EPS = 1e-6
NC_CST = 320
NE = 2528
NO = 4112
NEG = -30000.0
STOP = 9
ONE_TILE = False
DEBUG = False
ASTOP = 99
SKIP_XPRO = False


class Buf:
    __slots__ = ("name", "writers", "readers", "excl")

    def __init__(self, name, excl=False):
        self.name = name
        self.excl = excl
        self.writers = {}
        self.readers = {}


class Tens:
    def __init__(self, t, name, parts, excl=False):
        self.t = t
        self.name = name
        self.p = [Buf(f"{name}.{i}", excl) for i in range(parts)]


ENGS = ("pe", "act", "dve", "pool", "sp")


class Prog:
    def __init__(self, nc, same_engine_sync=True):
        self.nc = nc
        self.es = ExitStack()
        self.ops = {e: [] for e in ENGS}
        self.cnt = {e: 0 for e in ENGS}
        self.known = {e: {} for e in ENGS}
        self.snap = {}
        self.chan_cnt = {}
        self.sem = {}
        self.same_engine_sync = same_engine_sync

    def sbuf(self, name, shape, dtype, parts=1):
        t = self.es.enter_context(self.nc.sbuf_tensor("sb_" + name, list(shape), dtype))
        return Tens(t, name, parts)

    def psum(self, name, shape, dtype, parts=1):
        t = self.es.enter_context(self.nc.psum_tensor("ps_" + name, list(shape), dtype))
        return Tens(t, name, parts, excl=True)

    def dram(self, name, shape, dtype, kind="Internal", parts=1):
        t = self.nc.dram_tensor(name, list(shape), dtype, kind=kind).ap()
        return Tens(t, name, parts)

    @staticmethod
    def _flat(xs):
        out = []
        for x in xs:
            if isinstance(x, Tens):
                out.extend(x.p)
            elif isinstance(x, (list, tuple)):
                out.extend(Prog._flat(x))
            else:
                out.append(x)
        return out

    def _deps(self, eng, reads, writes, skip=None):
        deps = {}

        def add(d):
            for k, v in d.items():
                if deps.get(k, 0) < v:
                    deps[k] = v

        for b in reads:
            add(b.writers)
        for b in writes:
            add(b.writers)
            add(b.readers)
        waits = []
        kn = self.known[eng]
        for k, v in deps.items():
            if k == skip:
                continue
            if k == eng and (eng == "pe" or not self.same_engine_sync):
                continue
            if kn.get(k, 0) >= v:
                continue
            waits.append((k, v))
        for k, v in waits:
            kn[k] = max(kn.get(k, 0), v)
            sn = self.snap.get((k, v))
            if sn:
                for k2, v2 in sn.items():
                    if kn.get(k2, 0) < v2:
                        kn[k2] = v2
        return waits

    def _mark(self, tok, reads, writes, partial):
        k, v = tok
        for b in reads:
            if b.readers.get(k, 0) < v:
                b.readers[k] = v
        for b in writes:
            if partial:
                b.writers[k] = v
            else:
                b.writers = {k: v}
                b.readers = {}

    def op(self, eng, fn, reads=(), writes=()):
        reads = self._flat(reads)
        writes = self._flat(writes)
        writes = writes + [b for b in reads if b.excl and b not in writes]
        reads = [b for b in reads if not b.excl]
        waits = self._deps(eng, reads, writes)
        self.cnt[eng] += 1
        tok = (eng, self.cnt[eng])
        sn = dict(self.known[eng])
        sn[eng] = self.cnt[eng]
        self.snap[tok] = sn
        self._mark(tok, reads, writes, False)
        self.ops[eng].append(("op", waits, fn, tok))
        return tok

    def dma(self, eng, out, in_, sb, reads=(), writes=(), partial=False, **kw):
        reads = self._flat(reads)
        writes = self._flat(writes)
        sbb = self._flat([sb])[0]
        chan = "d_" + sbb.name
        waits = self._deps(eng, reads, writes, skip=(chan if partial else None))
        self.chan_cnt[chan] = self.chan_cnt.get(chan, 0) + 16
        tok = (chan, self.chan_cnt[chan])
        self.snap[tok] = dict(self.known[eng])
        self._mark(tok, reads, writes, partial)
        self.ops[eng].append(("dma", waits, (out, in_, kw), tok))
        return tok

    def emit(self):
        nc = self.nc
        for k in list(ENGS) + list(self.chan_cnt):
            if k not in self.sem:
                self.sem[k] = self.es.enter_context(nc.semaphore("s_" + k.replace(".", "_")))
        final = list(self.chan_cnt.items())

        def run(ename, eng):
            for kind, waits, payload, tok in self.ops[ename]:
                for k, v in waits:
                    eng.wait_ge(self.sem[k], v)
                if kind == "op":
                    payload(eng).then_inc(self.sem[tok[0]], 1)
                else:
                    out, in_, kw = payload
                    eng.dma_start(out=out, in_=in_, **kw).then_inc(self.sem[tok[0]], 16)
            if ename == "sp":
                for c, v in final:
                    eng.wait_ge(self.sem[c], v)

        with nc.Block() as block:
            @block.sync
            def _(e):
                run("sp", e)

            @block.scalar
            def _(e):
                run("act", e)

            @block.vector
            def _(e):
                run("dve", e)

            @block.gpsimd
            def _(e):
                run("pool", e)

            @block.tensor
            def _(e):
                run("pe", e)
        self.es.close()


class Rot:
    def __init__(self, items):
        self.items = items
        self.i = 0

    def next(self):
        x = self.items[self.i % len(self.items)]
        self.i += 1
        return x


def build(NL=4):
    nc = bass.Bass("TRN2", target_bir_lowering=False)
    P = Prog(nc)

    def din(name, shape, dt=F32):
        return nc.dram_tensor(name, list(shape), dt, kind="ExternalInput").ap()

    x_d = din("x", [2, S, 1024])
    cT_d = din("cT", [128, 8, 2])
    wada_d = din("wada", [4, 128, 8, 3072])
    cst_d = din("cst", [128, 4, NC_CST])
    cmf_d = din("cmf", [128, 384])
    cmb_d = din("cmb", [128, 896])
    tab_d = din("tab", [128, 4, S])
    wE_d = din("wE", [2, 128, 8, NE])
    wUQ_d = din("wUQ", [2, 128, 3, 800])
    wUK_d = din("wUK", [2, 128, 2, 512])
    wUV_d = din("wUV", [2, 128, 2, 512])
    wOe_d = din("wOe", [2, 128, 8, 1024])
    wO_d = din("wO", [2, 128, 8, NO])
    wOo_d = din("wOo", [2, 128, 8, 1024])
    out_d = nc.dram_tensor("out", [2, S, 1024], F32, kind="ExternalOutput").ap()

    def scr(name, shape, dt, parts=2 * NTL):
        return P.dram(name, shape, dt, kind=("ExternalOutput" if DEBUG else "Internal"), parts=parts)

    xT = scr("xT", [2, 8, 128, S], F32)
    QA = scr("QA", [2, 8, 128, S], BF16)
    KA = scr("KA", [2, 8, 128, S], BF16)
    QB = scr("QB", [2, 4, 128, S], BF16)
    KB = scr("KB", [2, 4, 128, S], BF16)
    VA = scr("VA", [2, NB, 128, 512], BF16)
    VB = scr("VB", [2, NB, 128, 512], BF16)
    GT = scr("GT", [2, 8, 128, S], BF16)
    OT = scr("OT", [2, 8, 128, S], BF16)
    CSP = scr("CSP", [2, 128, NB, 8], F32, parts=2)

    def pt(s, t):
        return s * NTL + t

    def allp(T, s):
        return [T.p[pt(s, t)] for t in range(NTL)]

    WIN = P.sbuf("WIN", [128, 8, NO], BF16)
    WOUT = P.sbuf("WOUT", [128, 8, 1024], BF16)
    WUQ = P.sbuf("WUQ", [128, 3, 800], BF16)
    WUK = P.sbuf("WUK", [128, 2, 512], BF16)
    WUV = P.sbuf("WUV", [128, 2, 512], BF16)
    CST = P.sbuf("CST", [128, 4, NC_CST], F32)
    AD = P.sbuf("AD", [128, 4, 2, 3, 8], F32)
    CMF = P.sbuf("CMF", [128, 384], F32)
    CMB = P.sbuf("CMB", [128, 896], BF16)
    TAB = P.sbuf("TAB", [128, 4, TT], F32)
    XT = P.sbuf("XT", [128, 8, TT], F32)
    HT = P.sbuf("HT", [128, 8, TT], BF16)
    SQ = Rot([P.sbuf(f"SQ{i}", [128, TT], BF16) for i in range(2)])
    FT = Rot([P.sbuf(f"FT{i}", [128, TT], F32) for i in range(4)])
    RSB = Rot([P.sbuf(f"RSB{i}", [128, TT], F32) for i in range(2)])
    STG = Rot([P.sbuf(f"STG{i}", [128, 4096], BF16) for i in range(2)])
    CQC = P.sbuf("CQC", [128, 5, TT], F32)
    CQN = P.sbuf("CQN", [128, 5, TT], BF16)
    XB = Rot([P.sbuf(f"XB{i}", [128, TT], BF16) for i in range(2)])
    Y = P.sbuf("Y", [128, 8, TT], F32, parts=8)
    YB = Y.t.bitcast(BF16)
    QTBL = [P.sbuf(f"QTB{i}", [128, TT], BF16) for i in range(4)]
    QTB = Rot(QTBL)
    PT = Rot([P.sbuf(f"PT{i}", [128, TT], BF16) for i in range(4)])
    CSPB = P.sbuf("CSPB", [128, NB, 8], F32)
    SPT = P.sbuf("SPT", [128, NB, 8], F32)
    PREF = P.sbuf("PREF", [128, NB, 8], F32)
    FQ8 = P.sbuf("FQ8", [128, S], BF16)
    SM = P.sbuf("SM", [128, 64], F32)
    CTB = P.sbuf("CTB", [128, 8, 2], BF16)
    CTF = P.sbuf("CTF", [128, 8, 2], F32)
    MODF = P.sbuf("MODF", [128, 4, 24, 2], F32)

    PB = [P.psum(f"B{i}", [128, 512], F32) for i in range(8)]

    ident_f = CMF.t[:, 0:128]
    tri_f = CMF.t[:, 128:256]
    ones_f = CMF.t[:, 256:384]
    ident_b = CMB.t[:, 0:128]
    perm64 = CMB.t[:, 128:256]
    perm32 = CMB.t[:, 256:384]
    mc_b = CMB.t[:, 384:512]
    msw_b = CMB.t[:, 512:768]
    ones_b = CMB.t[:, 768:896]

    YROW = 8192

    def kt_ap(k, rows=128, p0=0, c0=0, n=2048):
        return bass.AP(YB, p0 * YROW + k * 2048 + c0, [[YROW, rows], [1, n]])

    def kt_bufs(k):
        return [Y.p[2 * k], Y.p[2 * k + 1]]

    def va_blk(v, j):
        return bass.AP(YB, 4096 + v * 2048 + j * 128, [[YROW, 128], [1, 128]])

    def va_bufs(v):
        return [Y.p[4 + 2 * v], Y.p[5 + 2 * v]]

    def ACT(out, in_, func, r, w, **kw):
        P.op("act", lambda e: e.activation(out=out, in_=in_, func=func, **kw), r, w)

    def TTO(out, a, b, op, r, w, eng="dve"):
        P.op(eng, lambda e: e.tensor_tensor(out=out, in0=a, in1=b, op=op), r, w)

    def TS(out, a, s1, s2, op0, op1, r, w, eng="dve"):
        P.op(eng, lambda e: e.tensor_scalar(out=out, in0=a, scalar1=s1, scalar2=s2, op0=op0, op1=op1), r, w)

    def STT(out, a, s, b, op0, op1, r, w):
        P.op("dve", lambda e: e.scalar_tensor_tensor(out=out, in0=a, scalar=s, in1=b, op0=op0, op1=op1), r, w)

    def CPY(out, in_, r, w, eng="dve"):
        if eng == "act":
            P.op("act", lambda e: e.activation(out=out, in_=in_, func=AF.Identity), r, w)
        else:
            P.op(eng, lambda e: e.tensor_copy(out=out, in_=in_), r, w)

    def MM(out, pairs, r, w, start=True, stop=True):
        pairs = list(pairs)

        def fn(e):
            n = len(pairs)
            ins = None
            for i, (l, rr) in enumerate(pairs):
                ins = e.matmul(out, lhsT=l, rhs=rr, start=(start and i == 0), stop=(stop and i == n - 1))
            return ins
        P.op("pe", fn, r, w)

    def MMS(specs, r, w):
        specs = list(specs)

        def fn(e):
            ins = None
            for (o, l, rr, st, sp) in specs:
                ins = e.matmul(o, lhsT=l, rhs=rr, start=st, stop=sp)
            return ins
        P.op("pe", fn, r, w)

    projb = Rot(PB[0:5])

    def rstd_from(ps, n_feat):
        t1 = FT.next()
        ACT(t1.t[:], ps.t[:], AF.Ln, [ps], [t1], scale=1.0 / n_feat, bias=EPS)
        t2 = RSB.next()
        ACT(t2.t[:], t1.t[:], AF.Exp, [t1], [t2], scale=-0.5)
        return t2

    CASTKW = dict(max_dma_last_dim=4096)

    def load_weights(l):
        i = l // 2
        if l % 2 == 0:
            for c in range(8):
                P.dma("pool", WIN.t[:, c, 0:NE], wE_d[i, :, c, :], WIN, [], [WIN], partial=(c > 0), **CASTKW)
            P.dma("pool", WUQ.t[:], wUQ_d[i], WUQ, [], [WUQ], **CASTKW)
            P.dma("pool", WUK.t[:], wUK_d[i], WUK, [], [WUK], **CASTKW)
            P.dma("pool", WUV.t[:], wUV_d[i], WUV, [], [WUV], **CASTKW)
        else:
            for c in range(8):
                P.dma("pool", WIN.t[:, c, :], wO_d[i, :, c, :], WIN, [], [WIN], partial=(c > 0), **CASTKW)

    def load_wout(l):
        i = l // 2
        src = wOe_d if l % 2 == 0 else wOo_d
        for c in range(8):
            P.dma("pool", WOUT.t[:, c, :], src[i, :, c, :], WOUT, [], [WOUT], partial=(c > 0), **CASTKW)

    P.dma("sp", CST.t[:], cst_d, CST, [], [CST])
    P.dma("sp", CMF.t[:], cmf_d, CMF, [], [CMF])
    P.dma("pool", CMB.t[:], cmb_d, CMB, [], [CMB])
    P.dma("sp", CTF.t[:], cT_d, CTF, [], [CTF])
    ACT(CTB.t[:], CTF.t[:], AF.Silu, [CTF], [CTB])

    load_weights(0)

    xits = [(s, t, hb) for s in range(0 if not SKIP_XPRO else 2, 2) for t in range(NTL) for hb in range(2)]

    def x_load(it):
        s, t, hb = xits[it]
        stg = XT if it % 2 == 0 else Y
        xin = bass.AP(stg.t, 0, [[8 * TT, 128], [1024, 2], [1, 1024]])
        tok0 = t * TT + hb * 256
        src = x_d[s, tok0:tok0 + 256, :].rearrange("(b p) d -> p b d", p=128)
        P.dma("sp", xin, src, stg.p[0], [], [stg])

    if xits:
        x_load(0)
    for it, (s, t, hb) in enumerate(xits):
        if it + 1 < len(xits):
            x_load(it + 1)
        stg = XT if it % 2 == 0 else Y
        tok0 = t * TT + hb * 256
        for c in range(8):
            pb = projb.next()

            def fn(e, c=c, pb=pb, stg=stg):
                ins = None
                for b in range(2):
                    a_in = bass.AP(stg.t, b * 1024 + c * 128, [[8 * TT, 128], [1, 128]])
                    ins = e.transpose(out=pb.t[:, b * 128:(b + 1) * 128], in_=a_in, identity=ident_f)
                return ins
            P.op("pe", fn, [stg, CMF], [pb])
            ft = FT.next()
            CPY(ft.t[:, 0:256], pb.t[:, 0:256], [pb], [ft], eng=("dve" if c % 2 else "act"))
            P.dma("sp", xT.t[s, c, :, tok0:tok0 + 256], ft.t[:, 0:256], ft, [ft], [xT.p[pt(s, t)]], partial=True)

    def ada_piece_dma(l, pc, dst_ap, sbt, bufs):
        P.dma("pool", dst_ap, wada_d[l, :, :, pc * 512:(pc + 1) * 512], sbt, [], bufs, **CASTKW)

    def ada_piece_mm(l, pc, lw_fn, bufs):
        pb = projb.next()
        specs = []
        for g4 in range(4):
            for kc in range(8):
                specs.append((pb.t[:, 2 * g4:2 * g4 + 2], lw_fn(kc, g4), CTB.t[:, kc, :], kc == 0, kc == 7))
        MMS(specs, list(bufs) + [CTB], [pb])
        for g4 in range(4):
            g = pc * 4 + g4
            TS(MODF.t[:, l, g, :], pb.t[:, 2 * g4:2 * g4 + 2], CST.t[:, l, 16 + g:17 + g], None, ALU.add, ALU.bypass,
               [pb, CST], [MODF])

    def ada_finish(l):
        for b in range(2):
            a1 = AD.t[:, l, b, 0, :]
            TS(a1, MODF.t[:, l, 8:16, b], 1.0, None, ALU.add, ALU.bypass, [MODF], [AD])
            TTO(a1, a1, CST.t[:, l, 0:8], ALU.mult, [AD, CST], [AD])
            CPY(AD.t[:, l, b, 1, :], MODF.t[:, l, 0:8, b], [MODF], [AD])
            TTO(AD.t[:, l, b, 2, :], MODF.t[:, l, 16:24, b], CST.t[:, l, 8:16], ALU.mult, [MODF, CST], [AD])

    if NL > 0:
        for pc in range(6):
            wbuf = STG.next()
            ada_piece_dma(0, pc, bass.AP(wbuf.t, 0, [[4096, 128], [512, 8], [1, 512]]), wbuf, [wbuf])
            ada_piece_mm(0, pc, lambda kc, g4, wbuf=wbuf: bass.AP(wbuf.t, kc * 512 + g4 * 128, [[4096, 128], [1, 128]]), [wbuf])
        ada_finish(0)

    def yhalf_bufs(k):
        return [Y.p[4 * k + j] for j in range(4)]

    def ada_bg_start(l, ti):
        if l >= NL or ti >= 6:
            return
        k = ti % 2
        ada_piece_dma(l, ti, bass.AP(YB, k * 4096, [[YROW, 128], [512, 8], [1, 512]]), Y.p[4 * k], yhalf_bufs(k))

    def ada_bg_end(l, ti):
        if l >= NL or ti >= 6:
            return
        k = ti % 2
        ada_piece_mm(l, ti, lambda kc, g4, k=k: bass.AP(YB, k * 4096 + kc * 512 + g4 * 128, [[YROW, 128], [1, 128]]),
                     yhalf_bufs(k))
        if ti == 5:
            ada_finish(l)

    load_wout(0)

    def tsl(t):
        return slice(t * TT, (t + 1) * TT)

    def load_x_tile(s, t):
        P.dma("sp", XT.t[:], xT.t[s, :, :, tsl(t)].rearrange("c p t -> p c t"), XT, [xT.p[pt(s, t)]], [XT])

    def sumsq_bank(srcs, reads, bank):
        n = len(srcs)
        for i, a in enumerate(srcs):
            sq = SQ.next()
            ACT(sq.t[:], a, AF.Square, reads, [sq])
            MM(bank.t[:], [(ones_b, sq.t[:])], [sq, CMB], [bank], start=(i == 0), stop=(i == n - 1))

    rs_next = {}

    pend_mult = []

    def pop_mult(n=1):
        for _ in range(n):
            if pend_mult:
                pend_mult.pop(0)()

    def norm_sq(s, t):
        load_x_tile(s, t)
        bufs = PT.items + QTBL
        for c in range(8):
            ACT(bufs[c].t[:], XT.t[:, c, :], AF.Square, [XT], [bufs[c]])

    def norm_mm(s, t):
        bufs = PT.items + QTBL
        nb = PB[5]
        for c in range(8):
            MM(nb.t[:], [(ones_b, bufs[c].t[:])], [bufs[c], CMB], [nb], start=(c == 0), stop=(c == 7))
        rs = rstd_from(nb, 1024.0)
        for c in range(8):
            pend_mult.append(lambda c=c, rs=rs: TTO(XT.t[:, c, :], XT.t[:, c, :], rs.t[:], ALU.mult, [XT, rs], [XT]))
        rs_next[(s, t)] = rs

    def norm_part(s, t):
        norm_sq(s, t)
        norm_mm(s, t)

    def next_tile(s, t):
        if t + 1 < NTL:
            return (s, t + 1)
        if s == 0:
            return (1, 0)
        return None

    def phase_A_common(l, s, t):
        if (s, t) not in rs_next:
            norm_part(s, t)
        pop_mult(8)
        rs_next.pop((s, t))
        P.dma("sp", TAB.t[:], tab_d[:, :, tsl(t)], TAB, [], [TAB])
        for c in range(8):
            if c % 2 == 0:
                ACT(HT.t[:, c, :], XT.t[:, c, :], AF.Identity, [XT, AD], [HT], scale=AD.t[:, l, s, 0, c:c + 1],
                    bias=AD.t[:, l, s, 1, c:c + 1])
            else:
                TS(HT.t[:, c, :], XT.t[:, c, :], AD.t[:, l, s, 0, c:c + 1], AD.t[:, l, s, 1, c:c + 1], ALU.mult, ALU.add,
                   [XT, AD], [HT])

    def hoist_sq(s, t):
        nt = next_tile(s, t)
        if nt is not None and not ONE_TILE:
            norm_sq(*nt)

    def hoist_mm(s, t):
        nt = next_tile(s, t)
        if nt is not None and not ONE_TILE:
            norm_mm(*nt)

    def proj_fm(col0, M, reads_extra=(), kc=8, w=None, rhs=None):
        w = w or WIN
        pb = projb.next()
        pairs = []
        for c in range(kc):
            r = HT.t[:, c, :] if rhs is None else rhs(c)
            pairs.append((w.t[:, c, col0:col0 + M], r))
        MM(pb.t[0:M, :], pairs, [w, HT] + list(reads_extra), [pb])
        return pb

    def rope(pbx, rows, tabi, perm, out_ap, out_w):
        r0, r1 = rows
        kmax = 128
        xb = XB.next()
        ACT(xb.t[0:kmax, :], pbx.t[0:kmax, :], AF.Identity, [pbx], [xb])
        pb2 = projb.next()
        MM(pb2.t[0:kmax, :], [(perm[0:kmax, 0:kmax], xb.t[0:kmax, :])], [xb, CMB], [pb2])
        f1 = FT.next()
        TTO(f1.t[r0:r1, :], pbx.t[r0:r1, :], TAB.t[r0:r1, tabi, :], ALU.mult, [pbx, TAB], [f1])
        f2 = FT.next()
        TTO(f2.t[r0:r1, :], pb2.t[r0:r1, :], TAB.t[r0:r1, tabi + 1, :], ALU.mult, [pb2, TAB], [f2])
        TTO(out_ap, f1.t[r0:r1, :], f2.t[r0:r1, :], ALU.add, [f1, f2], out_w)

    def store(dst, src, sbt, dparts, partial=True):
        P.dma("pool", dst, src, sbt, [sbt], dparts, partial=partial)

    def gate_proj(l, s, t, col0):
        st = STG.next()
        for c in range(8):
            pb = proj_fm(col0 + c * 128, 128)
            ACT(st.t[:, c * TT:(c + 1) * TT], pb.t[:], AF.Silu, [pb], [st])
            pop_mult()
        store(GT.t[s, :, :, tsl(t)].rearrange("c p t -> p c t"),
              bass.AP(st.t, 0, [[4096, 128], [TT, 8], [1, TT]]), st, [GT.p[pt(s, t)]])

    def v_proj(s, t, col0, ncol, dst, lhs_src=None, w=None, kc=8):
        w = w or WIN
        st = STG.next()
        for b in range(4):
            pb = projb.next()
            pairs = []
            for c in range(kc):
                lt = HT.t[:, c, b * 128:(b + 1) * 128] if lhs_src is None else lhs_src(c, b)
                pairs.append((lt, w.t[:, c, col0:col0 + ncol]))
            MM(pb.t[:, 0:ncol], pairs, [w, HT, CQN], [pb])
            CPY(st.t[:, b * 512:b * 512 + ncol], pb.t[:, 0:ncol], [pb], [st], eng=("dve" if b % 2 else "act"))
        store(dst.t[s, 4 * t:4 * t + 4, :, 0:ncol].rearrange("b p n -> p b n"),
              bass.AP(st.t, 0, [[4096, 128], [512, 4], [1, ncol]]), st, [dst.p[pt(s, t)]])

    def phase_A_even(l, s, t):
        phase_A_common(l, s, t)
        if ASTOP < 2:
            return
        sqb = PT.items + QTBL
        for j in range(5):
            pb = proj_fm((0 if j < 3 else 384 - 3 * 128) + j * 128, 128)
            CPY(CQC.t[:, j, :], pb.t[:], [pb], [CQC], eng="act")
            ACT(sqb[j].t[:], pb.t[:], AF.Square, [pb], [sqb[j]])
        for (c0, nch, ncol, nfeat) in ((0, 3, 40, 384.0), (3, 2, 43, 256.0)):
            nb = PB[6]
            for j in range(nch):
                MM(nb.t[:], [(ones_b, sqb[c0 + j].t[:])], [sqb[c0 + j], CMB], [nb], start=(j == 0), stop=(j == nch - 1))
            rs = rstd_from(nb, nfeat)
            for j in range(nch):
                STT(CQN.t[:, c0 + j, :], CQC.t[:, c0 + j, :], CST.t[:, l, ncol + j:ncol + j + 1], rs.t[:],
                    ALU.mult, ALU.mult, [CQC, CST, rs], [CQN])
        st = STG.next()
        for j in range(4):
            pb = proj_fm(736 + j * 128, 128)
            rope(pb, (0, 128), 0, perm64, st.t[:, j * TT:(j + 1) * TT], [st])
        pb = proj_fm(1248, 128)
        rope(pb, (0, 128), 0, perm64, st.t[:, 4 * TT:5 * TT], [st])
        store(QB.t[s, :, :, tsl(t)].rearrange("c p t -> p c t"),
              bass.AP(st.t, 0, [[4096, 128], [TT, 4], [1, TT]]), st, [QB.p[pt(s, t)]])
        store(KB.t[s, 0, :, tsl(t)], st.t[:, 4 * TT:5 * TT], st, [KB.p[pt(s, t)]])
        if ASTOP < 7:
            return
        hoist_sq(s, t)
        v_proj(s, t, 1376, 128, VB)
        if ASTOP < 8:
            return
        gate_proj(l, s, t, 1504)
        hoist_mm(s, t)
        if ASTOP < 3:
            return
        st = STG.next()
        for h in range(8):
            pb = proj_fm(h * 96, 128, w=WUQ, kc=3, rhs=lambda c: CQN.t[:, c, :], reads_extra=[CQN])
            CPY(st.t[0:64, h * TT:(h + 1) * TT], pb.t[0:64, :], [pb], [st], eng="act")
            rope(pb, (64, 96), 2, perm32, st.t[64:96, h * TT:(h + 1) * TT], [st])
            pop_mult()
        store(QA.t[s, :, 0:96, tsl(t)].rearrange("h p t -> p h t"),
              bass.AP(st.t, 0, [[4096, 96], [TT, 8], [1, TT]]), st, [QA.p[pt(s, t)]])
        if ASTOP < 4:
            return
        st = STG.next()
        for hp in range(4):
            pb = proj_fm(hp * 128, 128, w=WUK, kc=2, rhs=lambda c: CQN.t[:, 3 + c, :], reads_extra=[CQN])
            for e2 in range(2):
                h = 2 * hp + e2
                CPY(st.t[0:64, h * TT:(h + 1) * TT], pb.t[e2 * 64:e2 * 64 + 64, :], [pb], [st],
                    eng=("dve" if e2 else "act"))
        pb = proj_fm(640, 128)
        xk = XB.next()
        rope(pb, (64, 96), 2, perm32, xk.t[64:96, :], [xk])
        store(KA.t[s, :, 0:64, tsl(t)].rearrange("h p t -> p h t"),
              bass.AP(st.t, 0, [[4096, 64], [TT, 8], [1, TT]]), st, [KA.p[pt(s, t)]])
        store(KA.t[s, :, 64:96, tsl(t)].rearrange("h p t -> p h t"),
              bass.AP(xk.t, 64 * TT, [[TT, 32], [0, 8], [1, TT]]), xk, [KA.p[pt(s, t)]])
        if ASTOP < 5:
            return
        v_proj(s, t, 0, 512, VA, lhs_src=lambda c, b: CQN.t[:, 3 + c, b * 128:(b + 1) * 128], w=WUV, kc=2)
        if ASTOP < 6:
            return

    def phase_A_odd(l, s, t):
        phase_A_common(l, s, t)
        for (col0, dst) in ((0, QB), (512, KB)):
            st = STG.next()
            for j in range(4):
                pb = proj_fm(col0 + j * 128, 128)
                rope(pb, (0, 128), 0, perm64, st.t[:, j * TT:(j + 1) * TT], [st])
            store(dst.t[s, :, :, tsl(t)].rearrange("c p t -> p c t"),
                  bass.AP(st.t, 0, [[4096, 128], [TT, 4], [1, TT]]), st, [dst.p[pt(s, t)]])
        v_proj(s, t, 1024, 512, VB)
        hoist_sq(s, t)
        for (col0, dst) in ((1536, QA), (2048, KA)):
            st = STG.next()
            for hp in range(4):
                pb = proj_fm(col0 + hp * 128, 128)
                for e2 in range(2):
                    h = 2 * hp + e2
                    CPY(st.t[0:64, h * TT:(h + 1) * TT], pb.t[e2 * 64:e2 * 64 + 64, :], [pb], [st],
                        eng=("dve" if e2 else "act"))
            store(dst.t[s, :, 0:64, tsl(t)].rearrange("h p t -> p h t"),
                  bass.AP(st.t, 0, [[4096, 64], [TT, 8], [1, TT]]), st, [dst.p[pt(s, t)]])
        v_proj(s, t, 2560, 512, VA)
        pb = projb.next()
        specs = []
        for b in range(4):
            for c in range(8):
                specs.append((pb.t[:, b * 8:(b + 1) * 8], HT.t[:, c, b * 128:(b + 1) * 128], WIN.t[:, c, 3072:3080],
                              c == 0, c == 7))
        MMS(specs, [WIN, HT], [pb])
        f = FT.next()
        for b in range(4):
            TTO(f.t[:, b * 8:(b + 1) * 8], pb.t[:, b * 8:(b + 1) * 8], CST.t[:, l, 54:62], ALU.add, [pb, CST], [f])
        f2 = FT.next()
        ACT(f2.t[:, 0:32], f.t[:, 0:32], AF.Exp, [f], [f2], scale=-1.0)
        ACT(SPT.t[:, 4 * t:4 * t + 4, :].rearrange("p b h -> p (b h)"), f2.t[:, 0:32], AF.Ln, [f2], [SPT], bias=1.0)
        hoist_mm(s, t)
        gate_proj(l, s, t, 3088)

    def fox_post(l, s):
        P.op("dve", lambda e: e.memset(PREF.t[:, 0, :], 0.0), [], [PREF])
        for b in range(1, NB):
            TTO(PREF.t[:, b, :], PREF.t[:, b - 1, :], SPT.t[:, b - 1, :], ALU.add, [PREF, SPT], [PREF])
        pb = projb.next()
        specs = []
        for b in range(NB):
            specs.append((pb.t[:, b * 8:(b + 1) * 8], tri_f, SPT.t[:, b, :], True, False))
            specs.append((pb.t[:, b * 8:(b + 1) * 8], ones_f, PREF.t[:, b, :], False, True))
        MMS(specs, [SPT, PREF, CMF], [pb])
        CPY(CSPB.t[:].rearrange("p b h -> p (b h)"), pb.t[:, 0:NB * 8], [pb], [CSPB])
        P.dma("pool", CSP.t[s], CSPB.t[:], CSPB, [CSPB], [CSP.p[s]])
        for g in range(4):
            pb2 = projb.next()

            def fn(e, g=g, pb2=pb2):
                ins = None
                for bb in range(4):
                    ins = e.transpose(out=pb2.t[0:8, bb * 128:(bb + 1) * 128], in_=CSPB.t[:, g * 4 + bb, :],
                                      identity=ident_f)
                return ins
            P.op("pe", fn, [CSPB, CMF], [pb2])
            ACT(FQ8.t[0:8, g * 512:(g + 1) * 512], pb2.t[0:8, :], AF.Identity, [pb2], [FQ8], scale=-8.0)
        P.dma("pool", QA.t[s, :, 64, :], FQ8.t[0:8, :], FQ8, [FQ8], allp(QA, s), partial=True)

    sbk = Rot(PB[0:3])
    sbk4 = Rot(PB[0:3] + [PB[7]])
    obk4 = Rot(PB[3:7])
    obk2 = Rot(PB[3:5])

    SKEW = 2
    pend = []

    def pipe_flush(keep=0):
        while len(pend) > keep:
            fn = pend.pop(0)
            fn()

    def attn_core(ktf, ktb, krows, kbase, qt, vaf, vab, scale, pO, i, bias=None, swa=False, lbank=None, fin=None, deep=True):
        if swa:
            js = [j for j in range(4 * i - 1, 4 * i + 4) if j >= 0]
        else:
            js = list(range(0, 4 * i + 4))
        for idx, j in enumerate(js):
            m = j - 4 * i
            c0 = 0 if m < 0 else 128 * m
            if swa:
                n = 128 if (m < 0 or m == 3) else 256
            else:
                n = TT - c0
            ps = (sbk4 if deep else sbk).next()
            need_mask = swa or m >= 0
            specs = [(ps.t[:, c0:c0 + n], ktf(j), qt.t[kbase:kbase + krows, c0:c0 + n], True, not need_mask)]
            if swa:
                mk = msw_b[:, 128:256] if m < 0 else msw_b[:, 0:n]
                specs.append((ps.t[:, c0:c0 + n], ident_b, mk, False, True))
            elif m >= 0:
                specs.append((ps.t[:, c0:c0 + 128], ident_b, mc_b, False, True))
            MMS(specs, [qt, CMB] + ktb, [ps])
            p = PT.next()
            kw = dict(scale=scale)
            rd = [ps]
            if bias is not None:
                kw["bias"] = bias(j)
                rd.append(CSPB)
            ACT(p.t[:, c0:c0 + n], ps.t[:, c0:c0 + n], AF.Exp, rd, [p], **kw)
            first = (idx == 0)
            last = (idx == len(js) - 1)

            def pv(p=p, c0=c0, n=n, j=j, first=first, last=last):
                specs = [(pO.t[:, c0:c0 + n], vaf(j), p.t[:, c0:c0 + n], first, last)]
                wr = [pO]
                if lbank is not None:
                    specs.append((lbank.t[:, c0:c0 + n], ones_b, p.t[:, c0:c0 + n], first, last))
                    wr.append(lbank)
                MMS(specs, [p, CMB] + vab, wr)
                if last and fin is not None:
                    fin()
            pend.append(pv)
            pipe_flush(keep=(3 if deep else SKEW))

    def load_va64(src, s, col0, v):
        ap = bass.AP(YB, 4096 + v * 2048 + v * 64, [[YROW, 128], [128, NB], [1, 64]])
        P.dma("sp", ap, src.t[s, :, :, col0:col0 + 64].rearrange("b p n -> p b n"), Y.p[4 + 2 * v],
              allp(src, s), va_bufs(v), partial=True)

    def set_ones():
        for v in range(2):
            ap = bass.AP(YB, 4096 + v * 2048 + (1 - v) * 64, [[YROW, 128], [128, NB], [1, 64]])
            P.op("dve", lambda e, ap=ap: e.memset(ap, 1.0), [], va_bufs(v))

    def zero_q_half(q, half):
        P.op("dve", lambda e, q=q, half=half: e.memset(q.t[half * 64:half * 64 + 64, :], 0.0), [], [q])

    def load_q_half(src, s, ch, half, t, q):
        r0 = half * 64
        P.dma("sp", q.t[r0:r0 + 64, :], src.t[s, ch, r0:r0 + 64, tsl(t)], q, [src.p[pt(s, t)]], [q], partial=True)
        return q

    def finalize64(pO, par, lbias, r_extra, s, t, chunk):
        n0, l0 = par * 64, (1 - par) * 64
        f2 = FT.next()
        if lbias is not None:
            f1 = FT.next()
            ACT(f1.t[n0:n0 + 64, :], pO.t[l0:l0 + 64, :], AF.Ln, [pO] + list(r_extra), [f1], bias=lbias)
            ACT(f2.t[n0:n0 + 64, :], f1.t[n0:n0 + 64, :], AF.Exp, [f1], [f2], scale=-1.0)
        else:
            P.op("dve", lambda e: e.reciprocal(out=f2.t[n0:n0 + 64, :], in_=pO.t[l0:l0 + 64, :]), [pO], [f2])
        xb = XB.next()
        TTO(xb.t[n0:n0 + 64, :], pO.t[n0:n0 + 64, :], f2.t[n0:n0 + 64, :], ALU.mult, [pO, f2], [xb])
        store(OT.t[s, chunk, n0:n0 + 64, tsl(t)], xb.t[n0:n0 + 64, :], xb, [OT.p[pt(s, t)]])

    def phase_B_even(l, s):
        sc_m = 96.0 ** -0.5
        pipe_flush()
        set_ones()
        for h in range(8):
            par = h % 2
            k = par
            P.dma("sp", kt_ap(k, rows=96), KA.t[s, h, 0:96, :], Y.p[2 * k], allp(KA, s), kt_bufs(k))
            load_va64(VA, s, h * 64, par)
            for t in range(NTL):
                q = QTB.next()
                P.dma("sp", q.t[0:96, :], QA.t[s, h, 0:96, tsl(t)], q, [QA.p[pt(s, t)]], [q])
                pO = obk4.next()
                attn_core(lambda j, k=k: kt_ap(k, rows=96, c0=j * 128, n=128), kt_bufs(k), 96, 0, q,
                          lambda j, v=par: va_blk(v, j), va_bufs(par), sc_m, pO, t,
                          fin=lambda pO=pO, par=par, t=t, h=h: finalize64(pO, par, None, [], s, t, h // 2))
        for grp in range(2):
            k = grp
            P.dma("sp", kt_ap(k), KB.t[s, 0, :, :], Y.p[2 * k], allp(KB, s), kt_bufs(k))
            load_va64(VB, s, grp * 64, grp)
            pipe_flush()
            for q in QTBL:
                zero_q_half(q, 1 - grp)
            for j4 in range(4):
                horig = grp * 4 + j4
                for t in range(NTL):
                    q = load_q_half(QB, s, j4, grp, t, QTB.next())
                    pO = obk4.next()
                    attn_core(lambda j, k=k: kt_ap(k, c0=j * 128, n=128), kt_bufs(k),
                              128, 0, q, lambda j, v=grp: va_blk(v, j), va_bufs(grp), 0.125, pO, t, swa=True,
                              fin=lambda pO=pO, grp=grp, horig=horig, t=t, j4=j4: finalize64(
                                  pO, grp, SM.t[grp * 64:grp * 64 + 64, horig:horig + 1], [SM], s, t, 4 + j4))

    def phase_B_odd(l, s):
        i = l // 2
        lam_init = 0.8 - 0.6 * math.exp(-0.3 * l)
        P.dma("sp", CSPB.t[:], CSP.t[s], CSPB, [CSP.p[s]], [CSPB])
        pipe_flush()
        for qi, q in enumerate(QTBL):
            zero_q_half(q, 1 - (qi % 2))
        dcnt = 0
        for h in range(4):
            k = h % 2
            v = h % 2
            P.dma("sp", kt_ap(k), KB.t[s, h, :, :], Y.p[2 * k], allp(KB, s), kt_bufs(k))
            vap = bass.AP(YB, 4096 + v * 2048, [[YROW, 128], [128, NB], [1, 128]])
            P.dma("sp", vap, VB.t[s, :, :, h * 128:(h + 1) * 128].rearrange("b p n -> p b n"), Y.p[4 + 2 * v],
                  allp(VB, s), va_bufs(v))
            for t in range(NTL):
                dcnt += 1
                for mp in range(2):
                    q = load_q_half(QB, s, h, mp, t, QTBL[2 * (dcnt % 2) + mp])
                    pO = obk2.next()
                    pL = PB[5 + mp]

                    def fin(pO=pO, pL=pL, mp=mp, t=t, h=h):
                        f1 = FT.next()
                        ACT(f1.t[:], pL.t[:], AF.Ln, [pL], [f1])
                        f2 = FT.next()
                        ACT(f2.t[:], f1.t[:], AF.Exp, [f1], [f2], scale=-1.0)
                        TTO(CQC.t[:, mp, :], pO.t[:], f2.t[:], ALU.mult, [pO, f2], [CQC])
                        if mp == 0:
                            return
                        od = CQC.t[:, 2, :]
                        STT(od, CQC.t[:, 1, :], SM.t[:, 16 + i:17 + i], CQC.t[:, 0, :], ALU.mult, ALU.add, [CQC, SM], [CQC])
                        nb = PB[7]
                        sumsq_bank([od], [CQC], nb)
                        rs = rstd_from(nb, 128.0)
                        xb = XB.next()
                        STT(xb.t[:], od, CST.t[:, l, 53:54], rs.t[:], ALU.mult, ALU.mult, [CQC, rs, CST], [xb])
                        xb2 = XB.next()
                        TS(xb2.t[:], xb.t[:], 1.0 - lam_init, None, ALU.mult, ALU.bypass, [xb], [xb2])
                        store(OT.t[s, h, :, tsl(t)], xb2.t[:], xb2, [OT.p[pt(s, t)]])
                    attn_core(lambda j, k=k: kt_ap(k, c0=j * 128, n=128), kt_bufs(k),
                              128, 0, q, lambda j, v=v: va_blk(v, j), va_bufs(v), 0.125, pO, t, lbank=pL, fin=fin, deep=False)
        pipe_flush()
        for k in range(2):
            ap = bass.AP(YB, 64 * YROW + k * 2048, [[YROW, 1], [1, 2048]])
            P.op("dve", lambda e, ap=ap: e.memset(ap, 1.0), [], kt_bufs(k))
        set_ones()
        for h in range(8):
            par = h % 2
            k = par
            P.dma("sp", kt_ap(k, rows=64), KA.t[s, h, 0:64, :], Y.p[2 * k], allp(KA, s), kt_bufs(k), partial=True)
            load_va64(VA, s, h * 64, par)
            for t in range(NTL):
                q = QTB.next()
                P.dma("sp", q.t[0:65, :], QA.t[s, h, 0:65, tsl(t)], q, [QA.p[pt(s, t)]], [q])
                pO = obk4.next()
                attn_core(lambda j, k=k: kt_ap(k, rows=65, c0=j * 128, n=128), kt_bufs(k), 65, 0, q,
                          lambda j, v=par: va_blk(v, j), va_bufs(par), 0.125, pO, t,
                          bias=lambda j, h=h: CSPB.t[:, j, h:h + 1],
                          fin=lambda pO=pO, par=par, t=t, h=h: finalize64(pO, par, None, [], s, t, 4 + h // 2))

    c_st = {}

    def phase_C1_load(l, s, t):
        st_o = STG.next()
        P.dma("sp", bass.AP(st_o.t, 0, [[4096, 128], [TT, 8], [1, TT]]),
              OT.t[s, :, :, tsl(t)].rearrange("c p t -> p c t"), st_o, [OT.p[pt(s, t)]], [st_o])
        st_g = STG.next()
        P.dma("sp", bass.AP(st_g.t, 0, [[4096, 128], [TT, 8], [1, TT]]),
              GT.t[s, :, :, tsl(t)].rearrange("c p t -> p c t"), st_g, [GT.p[pt(s, t)]], [st_g])
        c_st[(s, t)] = (st_o, st_g)

    def phase_C1_og(l, s, t):
        st_o, st_g = c_st.pop((s, t))
        for c in range(8):
            TTO(HT.t[:, c, :], st_o.t[:, c * TT:(c + 1) * TT], st_g.t[:, c * TT:(c + 1) * TT], ALU.mult,
                [st_o, st_g], [HT])

    def phase_C2(l, s, t):
        load_x_tile(s, t)
        nb = PB[6]
        sqb = PT.items + QTBL
        for g in range(8):
            pb = proj_fm(g * 128, 128, w=WOUT)
            CPY(Y.t[:, g, :], pb.t[:], [pb], [Y.p[g]], eng="act")
            ACT(sqb[g].t[:], pb.t[:], AF.Square, [pb], [sqb[g]])
            if g >= 2:
                MM(nb.t[:], [(ones_b, sqb[g - 2].t[:])], [sqb[g - 2], CMB], [nb], start=(g == 2), stop=False)
        for g in (6, 7):
            MM(nb.t[:], [(ones_b, sqb[g].t[:])], [sqb[g], CMB], [nb], start=False, stop=(g == 7))
        return rstd_from(nb, 1024.0)

    def phase_C3(l, s, t, rs):
        for c in range(8):
            f = FT.next()
            TTO(f.t[:], Y.t[:, c, :], rs.t[:], ALU.mult, [Y.p[c], rs], [f])
            STT(XT.t[:, c, :], f.t[:], AD.t[:, l, s, 2, c:c + 1], XT.t[:, c, :], ALU.mult, ALU.add, [f, AD, XT], [XT])
        P.dma("pool", xT.t[s, :, :, tsl(t)].rearrange("c p t -> p c t"), XT.t[:], XT, [XT], [xT.p[pt(s, t)]])

    def phase_C_all(l):
        tiles = [(s, t) for s in range(2) for t in range(NTL)]
        phase_C1_load(l, *tiles[0])
        phase_C1_og(l, *tiles[0])
        for idx, (s, t) in enumerate(tiles):
            if idx + 1 < len(tiles):
                phase_C1_load(l, *tiles[idx + 1])
            rs = phase_C2(l, s, t)
            if idx + 1 < len(tiles):
                phase_C1_og(l, *tiles[idx + 1])
            phase_C3(l, s, t, rs)

    for l in range(NL):
        i = l // 2
        even = (l % 2 == 0)
        if even:
            ACT(SM.t[:, 0:8], CST.t[:, l, 45:53], AF.Exp, [CST], [SM])
        else:
            lam_init = 0.8 - 0.6 * math.exp(-0.3 * l)
            f = FT.next()
            TTO(f.t[:, 0:64], CST.t[:, l, 62:126], CST.t[:, l, 126:190], ALU.mult, [CST], [f])
            TTO(f.t[:, 64:128], CST.t[:, l, 190:254], CST.t[:, l, 254:318], ALU.mult, [CST], [f])
            P.op("dve", lambda e, f=f: e.reduce_sum(out=SM.t[:, 20:22],
                                                 in_=f.t[:, 0:128].rearrange("p (a b) -> p a b", a=2),
                                                 axis=mybir.AxisListType.X), [f], [SM])
            ACT(SM.t[:, 22:24], SM.t[:, 20:22], AF.Exp, [SM], [SM])
            TTO(SM.t[:, 24:25], SM.t[:, 23:24], SM.t[:, 22:23], ALU.subtract, [SM], [SM])
            TS(SM.t[:, 16 + i:17 + i], SM.t[:, 24:25], -lam_init, None, ALU.add, ALU.bypass, [SM], [SM])
        if STOP < 2:
            continue
        for s in range(2):
            for t in range(NTL):
                if ONE_TILE and (s, t) != (0, 0):
                    continue
                ada_bg_start(l + 1, s * NTL + t)
                (phase_A_even if even else phase_A_odd)(l, s, t)
                ada_bg_end(l + 1, s * NTL + t)
            if not even and not ONE_TILE:
                fox_post(l, s)
        if l + 1 < NL:
            load_weights(l + 1)
        if STOP < 3:
            continue
        for s in range(2):
            (phase_B_even if even else phase_B_odd)(l, s)
        pipe_flush()
        if STOP < 4:
            continue
        phase_C_all(l)
        if l + 1 < NL:
            load_wout(l + 1)

    for s in range(0 if not SKIP_XPRO else 2, 2):
        for t in range(NTL):
            load_x_tile(s, t)
            for b in range(4):
                for half in range(2):
                    pb = projb.next()

                    def fn(e, b=b, half=half, pb=pb):
                        ins = None
                        for cc in range(4):
                            c = half * 4 + cc
                            ins = e.transpose(out=pb.t[:, cc * 128:(cc + 1) * 128],
                                              in_=XT.t[:, c, b * 128:(b + 1) * 128], identity=ident_f)
                        return ins
                    P.op("pe", fn, [XT, CMF], [pb])
                    ft = FT.next()
                    CPY(ft.t[:], pb.t[:], [pb], [ft], eng=("dve" if half else "act"))
                    tok0 = t * TT + b * 128
                    P.dma("pool", out_d[s, tok0:tok0 + 128, half * 512:(half + 1) * 512], ft.t[:], ft, [ft], [])
    P.emit()
    return nc


def _kc(w):
    K, N = w.shape
    return np.ascontiguousarray(w.reshape(K // 128, 128, N).transpose(1, 0, 2))


def _o_perm_even():
    perm = np.zeros(1024, np.int64)
    for f in range(1024):
        cf, r = divmod(f, 128)
        if cf < 4:
            perm[f] = f
        else:
            j = cf - 4
            head = j if r < 64 else 4 + j
            perm[f] = 512 + head * 64 + (r % 64)
    return perm


def _host_consts():
    pos = np.arange(S, dtype=np.float32)
    tab = np.zeros((128, 4, S), np.float32)
    inv64 = (1.0 / (np.float32(10000.0) ** (np.arange(0, 64, 2, dtype=np.float32) / np.float32(64)))).astype(np.float32)
    inv32 = (1.0 / (np.float32(10000.0) ** (np.arange(0, 32, 2, dtype=np.float32) / np.float32(32)))).astype(np.float32)
    for r in range(128):
        i = r % 64
        ang = (pos * inv64[i % 32]).astype(np.float32)
        tab[r, 0] = np.cos(ang)
        tab[r, 1] = -np.sin(ang) if i < 32 else np.sin(ang)
    for r in range(64, 96):
        j = r - 64
        ang = (pos * inv32[j % 16]).astype(np.float32)
        tab[r, 2] = np.cos(ang)
        tab[r, 3] = -np.sin(ang) if j < 16 else np.sin(ang)
    cmf = np.zeros((128, 384), np.float32)
    cmf[:, 0:128] = np.eye(128, dtype=np.float32)
    k = np.arange(128)
    cmf[:, 128:256] = (k[:, None] <= k[None, :]).astype(np.float32)
    cmf[:, 256:384] = 1.0
    cmb = np.zeros((128, 896), np.float32)
    cmb[:, 0:128] = np.eye(128, dtype=np.float32)
    for m in range(128):
        sw = m + 32 if (m % 64) < 32 else m - 32
        cmb[sw, 128 + m] = 1.0
    for m in range(128):
        if 64 <= m < 96:
            sw = m + 16 if (m - 64) < 16 else m - 16
        else:
            sw = m
        cmb[sw, 256 + m] = 1.0
    cmb[:, 384:512] = np.where(k[:, None] > k[None, :], NEG, 0.0)
    cmb[:, 512:640] = np.where(k[:, None] > k[None, :], NEG, 0.0)
    cmb[:, 640:768] = np.where(k[:, None] <= k[None, :], NEG, 0.0)
    cmb[:, 768:896] = 1.0
    return tab, cmf, cmb


def _prep_shared(inp):
    f = lambda a: np.asarray(a, dtype=np.float32)
    sh = {}
    w_ada = f(inp["w_ada"])
    sh["wada"] = np.ascontiguousarray(w_ada.reshape(4, 8, 128, 3072).transpose(0, 2, 1, 3))
    perm_e = _o_perm_even()
    wE, wUQ, wUK, wUV, wOe = [], [], [], [], []
    for i in range(2):
        w = f(inp["ev_w_in"][i])
        cols = [w[:, 0:384], w[:, 384:640], np.zeros((1024, 64), np.float32), w[:, 640:672]]
        sq = w[:, 672:1184].reshape(1024, 8, 64)
        order = []
        for j in range(4):
            order += [j, 4 + j]
        cols.append(sq[:, order, :].reshape(1024, 512))
        cols.append(w[:, 1184:1312])
        cols.append(w[:, 1312:1440])
        cols.append(w[:, 1440:2464][:, perm_e])
        wE.append(_kc(np.concatenate(cols, axis=1)))
        wUQ.append(_kc(np.concatenate([f(inp["ev_w_uq"][i]), np.zeros((384, 32), np.float32)], axis=1)))
        ukv = f(inp["ev_w_ukv"][i]).reshape(256, 8, 128)
        wUK.append(_kc(np.ascontiguousarray(ukv[:, :, 0:64]).reshape(256, 512)))
        wUV.append(_kc(np.ascontiguousarray(ukv[:, :, 64:128]).reshape(256, 512)))
        wOe.append(_kc(f(inp["ev_w_out"][i])[perm_e, :]))
    sh["wE"] = np.stack(wE)
    sh["wUQ"] = np.stack(wUQ)
    sh["wUK"] = np.stack(wUK)
    sh["wUV"] = np.stack(wUV)
    sh["wOe"] = np.stack(wOe)
    wO, wOo = [], []
    for i in range(2):
        w = f(inp["od_w_in"][i])
        cols = [w[:, 0:3080], np.zeros((1024, 8), np.float32), w[:, 3080:4104]]
        wO.append(_kc(np.concatenate(cols, axis=1)))
        wOo.append(_kc(f(inp["od_w_out"][i])))
    sh["wO"] = np.stack(wO)
    sh["wOo"] = np.stack(wOo)
    cst = np.zeros((128, 4, NC_CST), np.float32)
    for l in range(4):
        i = l // 2
        cst[:, l, 0:8] = f(inp["g_pre"][l]).reshape(8, 128).T
        cst[:, l, 8:16] = f(inp["g_post"][l]).reshape(8, 128).T
        cst[:, l, 16:40] = f(inp["b_ada"][l]).reshape(24, 128).T
        if l % 2 == 0:
            cst[:, l, 40:43] = f(inp["ev_q_norm"][i]).reshape(3, 128).T
            cst[:, l, 43:45] = f(inp["ev_kv_norm"][i]).reshape(2, 128).T
            cst[:, l, 45:53] = f(inp["ev_sinks"][i])[None, :]
        else:
            cst[:, l, 53] = f(inp["od_subln"][i])
            cst[:, l, 54:62] = f(inp["od_forget_bias"][i])[None, :]
            cst[:, l, 62:318] = f(inp["od_lambda"][i]).reshape(256)[None, :]
    sh["cst"] = cst
    tab, cmf, cmb = _host_consts()
    sh["tab"] = tab
    sh["cmf"] = cmf
    sh["cmb"] = cmb
    return sh


_NC_CACHE = {}


def run(inputs, NL=4, cores=8):
    x = np.asarray(inputs["x"], dtype=np.float32)
    c = np.asarray(inputs["c"], dtype=np.float32)
    sh = _prep_shared(inputs)
    in_maps = []
    for core in range(cores):
        m = dict(sh)
        m["x"] = np.ascontiguousarray(x[2 * core:2 * core + 2])
        cc = c[2 * core:2 * core + 2]
        m["cT"] = np.ascontiguousarray(cc.T.reshape(8, 128, 2).transpose(1, 0, 2))
        in_maps.append(m)
    if NL not in _NC_CACHE:
        _NC_CACHE[NL] = build(NL)
    res = run_bass_kernel_spmd(_NC_CACHE[NL], in_maps, core_ids=list(range(cores)))
    if DEBUG:
        return res.results
    return np.concatenate([r["out"] for r in res.results], axis=0).astype(np.float32)


def kernel(**inputs):
    return run(inputs, NL=4, cores=8)
```

```python
from contextlib import ExitStack
import math
import numpy as np
import concourse.bass as bass
import concourse.mybir as mybir
from concourse.bass_utils import run_bass_kernel_spmd

F32 = mybir.dt.float32
BF16 = mybir.dt.bfloat16
AF = mybir.ActivationFunctionType
ALU = mybir.AluOpType

S = 2048
TT = 512
NTL = S // TT
NB = S // 128
EPS = 1e-6
NC_CST = 320
NE = 2528
NO = 4112
NEG = -30000.0
STOP = 9
ONE_TILE = False
DEBUG = False
ASTOP = 99
SKIP_XPRO = False


class Buf:
    __slots__ = ("name", "writers", "readers", "excl")

    def __init__(self, name, excl=False):
        self.name = name
        self.excl = excl
        self.writers = {}
        self.readers = {}


class Tens:
    def __init__(self, t, name, parts, excl=False):
        self.t = t
        self.name = name
        self.p = [Buf(f"{name}.{i}", excl) for i in range(parts)]


ENGS = ("pe", "act", "dve", "pool", "sp")


class Prog:
    def __init__(self, nc, same_engine_sync=True):
        self.nc = nc
        self.es = ExitStack()
        self.ops = {e: [] for e in ENGS}
        self.cnt = {e: 0 for e in ENGS}
        self.known = {e: {} for e in ENGS}
        self.snap = {}
        self.chan_cnt = {}
        self.sem = {}
        self.same_engine_sync = same_engine_sync

    def sbuf(self, name, shape, dtype, parts=1):
        t = self.es.enter_context(self.nc.sbuf_tensor("sb_" + name, list(shape), dtype))
        return Tens(t, name, parts)

    def psum(self, name, shape, dtype, parts=1):
        t = self.es.enter_context(self.nc.psum_tensor("ps_" + name, list(shape), dtype))
        return Tens(t, name, parts, excl=True)

    def dram(self, name, shape, dtype, kind="Internal", parts=1):
        t = self.nc.dram_tensor(name, list(shape), dtype, kind=kind).ap()
        return Tens(t, name, parts)

    @staticmethod
    def _flat(xs):
        out = []
        for x in xs:
            if isinstance(x, Tens):
                out.extend(x.p)
            elif isinstance(x, (list, tuple)):
                out.extend(Prog._flat(x))
            else:
                out.append(x)
        return out

    def _deps(self, eng, reads, writes, skip=None):
        deps = {}

        def add(d):
            for k, v in d.items():
                if deps.get(k, 0) < v:
                    deps[k] = v

        for b in reads:
            add(b.writers)
        for b in writes:
            add(b.writers)
            add(b.readers)
        waits = []
        kn = self.known[eng]
        for k, v in deps.items():
            if k == skip:
                continue
            if k == eng and (eng == "pe" or not self.same_engine_sync):
                continue
            if kn.get(k, 0) >= v:
                continue
            waits.append((k, v))
        for k, v in waits:
            kn[k] = max(kn.get(k, 0), v)
            sn = self.snap.get((k, v))
            if sn:
                for k2, v2 in sn.items():
                    if kn.get(k2, 0) < v2:
                        kn[k2] = v2
        return waits

    def _mark(self, tok, reads, writes, partial):
        k, v = tok
        for b in reads:
            if b.readers.get(k, 0) < v:
                b.readers[k] = v
        for b in writes:
            if partial:
                b.writers[k] = v
            else:
                b.writers = {k: v}
                b.readers = {}

    def op(self, eng, fn, reads=(), writes=()):
        reads = self._flat(reads)
        writes = self._flat(writes)
        writes = writes + [b for b in reads if b.excl and b not in writes]
        reads = [b for b in reads if not b.excl]
        waits = self._deps(eng, reads, writes)
        self.cnt[eng] += 1
        tok = (eng, self.cnt[eng])
        sn = dict(self.known[eng])
        sn[eng] = self.cnt[eng]
        self.snap[tok] = sn
        self._mark(tok, reads, writes, False)
        self.ops[eng].append(("op", waits, fn, tok))
        return tok

    def dma(self, eng, out, in_, sb, reads=(), writes=(), partial=False, **kw):
        reads = self._flat(reads)
        writes = self._flat(writes)
        sbb = self._flat([sb])[0]
        chan = "d_" + sbb.name
        waits = self._deps(eng, reads, writes, skip=(chan if partial else None))
        self.chan_cnt[chan] = self.chan_cnt.get(chan, 0) + 16
        tok = (chan, self.chan_cnt[chan])
        self.snap[tok] = dict(self.known[eng])
        self._mark(tok, reads, writes, partial)
        self.ops[eng].append(("dma", waits, (out, in_, kw), tok))
        return tok

    def emit(self):
        nc = self.nc
        for k in list(ENGS) + list(self.chan_cnt):
            if k not in self.sem:
                self.sem[k] = self.es.enter_context(nc.semaphore("s_" + k.replace(".", "_")))
        final = list(self.chan_cnt.items())

        def run(ename, eng):
            for kind, waits, payload, tok in self.ops[ename]:
                for k, v in waits:
                    eng.wait_ge(self.sem[k], v)
                if kind == "op":
                    payload(eng).then_inc(self.sem[tok[0]], 1)
                else:
                    out, in_, kw = payload
                    eng.dma_start(out=out, in_=in_, **kw).then_inc(self.sem[tok[0]], 16)
            if ename == "sp":
                for c, v in final:
                    eng.wait_ge(self.sem[c], v)

        with nc.Block() as block:
            @block.sync
            def _(e):
                run("sp", e)

            @block.scalar
            def _(e):
                run("act", e)

            @block.vector
            def _(e):
                run("dve", e)

            @block.gpsimd
            def _(e):
                run("pool", e)

            @block.tensor
            def _(e):
                run("pe", e)
        self.es.close()


class Rot:
    def __init__(self, items):
        self.items = items
        self.i = 0

    def next(self):
        x = self.items[self.i % len(self.items)]
        self.i += 1
        return x


def build(NL=4):
    nc = bass.Bass("TRN2", target_bir_lowering=False)
    P = Prog(nc)

    def din(name, shape, dt=F32):
        return nc.dram_tensor(name, list(shape), dt, kind="ExternalInput").ap()

    x_d = din("x", [2, S, 1024])
    cT_d = din("cT", [128, 8, 2])
    wada_d = din("wada", [4, 128, 8, 3072])
    cst_d = din("cst", [128, 4, NC_CST])
    cmf_d = din("cmf", [128, 384])
    cmb_d = din("cmb", [128, 896])
    tab_d = din("tab", [128, 4, S])
    wE_d = din("wE", [2, 128, 8, NE])
    wUQ_d = din("wUQ", [2, 128, 3, 800])
    wUK_d = din("wUK", [2, 128, 2, 512])
    wUV_d = din("wUV", [2, 128, 2, 512])
    wOe_d = din("wOe", [2, 128, 8, 1024])
    wO_d = din("wO", [2, 128, 8, NO])
    wOo_d = din("wOo", [2, 128, 8, 1024])
    out_d = nc.dram_tensor("out", [2, S, 1024], F32, kind="ExternalOutput").ap()

    def scr(name, shape, dt, parts=2 * NTL):
        return P.dram(name, shape, dt, kind=("ExternalOutput" if DEBUG else "Internal"), parts=parts)

    xT = scr("xT", [2, 8, 128, S], F32)
    QA = scr("QA", [2, 8, 128, S], BF16)
    KA = scr("KA", [2, 8, 128, S], BF16)
    QB = scr("QB", [2, 4, 128, S], BF16)
    KB = scr("KB", [2, 4, 128, S], BF16)
    VA = scr("VA", [2, NB, 128, 512], BF16)
    VB = scr("VB", [2, NB, 128, 512], BF16)
    GT = scr("GT", [2, 8, 128, S], BF16)
    OT = scr("OT", [2, 8, 128, S], BF16)
    CSP = scr("CSP", [2, 128, NB, 8], F32, parts=2)

    def pt(s, t):
        return s * NTL + t

    def allp(T, s):
        return [T.p[pt(s, t)] for t in range(NTL)]

    WIN = P.sbuf("WIN", [128, 8, NO], BF16)
    WOUT = P.sbuf("WOUT", [128, 8, 1024], BF16)
    WUQ = P.sbuf("WUQ", [128, 3, 800], BF16)
    WUK = P.sbuf("WUK", [128, 2, 512], BF16)
    WUV = P.sbuf("WUV", [128, 2, 512], BF16)
    CST = P.sbuf("CST", [128, 4, NC_CST], F32)
    AD = P.sbuf("AD", [128, 4, 2, 3, 8], F32)
    CMF = P.sbuf("CMF", [128, 384], F32)
    CMB = P.sbuf("CMB", [128, 896], BF16)
    TAB = P.sbuf("TAB", [128, 4, TT], F32)
    XT = P.sbuf("XT", [128, 8, TT], F32, parts=8)
    HT = P.sbuf("HT", [128, 8, TT], BF16, parts=8)
    SQ = Rot([P.sbuf(f"SQ{i}", [128, TT], BF16) for i in range(2)])
    FT = Rot([P.sbuf(f"FT{i}", [128, TT], F32) for i in range(4)])
    RSB = Rot([P.sbuf(f"RSB{i}", [128, TT], F32) for i in range(2)])
    STG = Rot([P.sbuf(f"STG{i}", [128, 4096], BF16, parts=8) for i in range(2)])
    CQC = P.sbuf("CQC", [128, 5, TT], F32, parts=5)
    CQN = P.sbuf("CQN", [128, 5, TT], BF16, parts=5)
    XB = Rot([P.sbuf(f"XB{i}", [128, TT], BF16) for i in range(2)])
    Y = P.sbuf("Y", [128, 8, TT], F32, parts=8)
    YB = Y.t.bitcast(BF16)
    QTBL = [P.sbuf(f"QTB{i}", [128, TT], BF16) for i in range(4)]
    QTB = Rot(QTBL)
    PT = Rot([P.sbuf(f"PT{i}", [128, TT], BF16) for i in range(4)])
    CSPB = P.sbuf("CSPB", [128, NB, 8], F32)
    SPT = P.sbuf("SPT", [128, NB, 8], F32)
    PREF = P.sbuf("PREF", [128, NB, 8], F32)
    FQ8 = P.sbuf("FQ8", [128, S], BF16)
    SM = P.sbuf("SM", [128, 64], F32)
    CTB = P.sbuf("CTB", [128, 8, 2], BF16)
    CTF = P.sbuf("CTF", [128, 8, 2], F32)
    MODF = P.sbuf("MODF", [128, 4, 24, 2], F32)

    PB = [P.psum(f"B{i}", [128, 512], F32) for i in range(8)]

    ident_f = CMF.t[:, 0:128]
    tri_f = CMF.t[:, 128:256]
    ones_f = CMF.t[:, 256:384]
    ident_b = CMB.t[:, 0:128]
    perm64 = CMB.t[:, 128:256]
    perm32 = CMB.t[:, 256:384]
    mc_b = CMB.t[:, 384:512]
    msw_b = CMB.t[:, 512:768]
    ones_b = CMB.t[:, 768:896]

    YROW = 8192

    def kt_ap(k, rows=128, p0=0, c0=0, n=2048):
        return bass.AP(YB, p0 * YROW + k * 2048 + c0, [[YROW, rows], [1, n]])

    def kt_bufs(k):
        return [Y.p[2 * k], Y.p[2 * k + 1]]

    def va_blk(v, j):
        return bass.AP(YB, 4096 + v * 2048 + j * 128, [[YROW, 128], [1, 128]])

    def va_bufs(v):
        return [Y.p[4 + 2 * v], Y.p[5 + 2 * v]]

    def ACT(out, in_, func, r, w, **kw):
        P.op("act", lambda e: e.activation(out=out, in_=in_, func=func, **kw), r, w)

    def TTO(out, a, b, op, r, w, eng="dve"):
        P.op(eng, lambda e: e.tensor_tensor(out=out, in0=a, in1=b, op=op), r, w)

    def TS(out, a, s1, s2, op0, op1, r, w, eng="dve"):
        P.op(eng, lambda e: e.tensor_scalar(out=out, in0=a, scalar1=s1, scalar2=s2, op0=op0, op1=op1), r, w)

    def STT(out, a, s, b, op0, op1, r, w):
        P.op("dve", lambda e: e.scalar_tensor_tensor(out=out, in0=a, scalar=s, in1=b, op0=op0, op1=op1), r, w)

    def CPY(out, in_, r, w, eng="dve"):
        if eng == "act":
            P.op("act", lambda e: e.activation(out=out, in_=in_, func=AF.Identity), r, w)
        else:
            P.op(eng, lambda e: e.tensor_copy(out=out, in_=in_), r, w)

    def MM(out, pairs, r, w, start=True, stop=True):
        pairs = list(pairs)

        def fn(e):
            n = len(pairs)
            ins = None
            for i, (l, rr) in enumerate(pairs):
                ins = e.matmul(out, lhsT=l, rhs=rr, start=(start and i == 0), stop=(stop and i == n - 1))
            return ins
        P.op("pe", fn, r, w)

    def MMS(specs, r, w):
        specs = list(specs)

        def fn(e):
            ins = None
            for (o, l, rr, st, sp) in specs:
                ins = e.matmul(o, lhsT=l, rhs=rr, start=st, stop=sp)
            return ins
        P.op("pe", fn, r, w)

    projb = Rot(PB[0:5])

    def rstd_from(ps, n_feat):
        t1 = FT.next()
        ACT(t1.t[:], ps.t[:], AF.Ln, [ps], [t1], scale=1.0 / n_feat, bias=EPS)
        t2 = RSB.next()
        ACT(t2.t[:], t1.t[:], AF.Exp, [t1], [t2], scale=-0.5)
        return t2

    CASTKW = dict(max_dma_last_dim=4096)

    def load_weights(l):
        i = l // 2
        if l % 2 == 0:
            for c in range(8):
                P.dma("pool", WIN.t[:, c, 0:NE], wE_d[i, :, c, :], WIN, [], [WIN], partial=(c > 0), **CASTKW)
            P.dma("pool", WUQ.t[:], wUQ_d[i], WUQ, [], [WUQ], **CASTKW)
            P.dma("pool", WUK.t[:], wUK_d[i], WUK, [], [WUK], **CASTKW)
            P.dma("pool", WUV.t[:], wUV_d[i], WUV, [], [WUV], **CASTKW)
        else:
            for c in range(8):
                P.dma("pool", WIN.t[:, c, :], wO_d[i, :, c, :], WIN, [], [WIN], partial=(c > 0), **CASTKW)

    def load_wout(l):
        i = l // 2
        src = wOe_d if l % 2 == 0 else wOo_d
        for c in range(8):
            P.dma("pool", WOUT.t[:, c, :], src[i, :, c, :], WOUT, [], [WOUT], partial=(c > 0), **CASTKW)

    P.dma("sp", CST.t[:], cst_d, CST, [], [CST])
    P.dma("sp", CMF.t[:], cmf_d, CMF, [], [CMF])
    P.dma("pool", CMB.t[:], cmb_d, CMB, [], [CMB])
    P.dma("sp", CTF.t[:], cT_d, CTF, [], [CTF])
    ACT(CTB.t[:], CTF.t[:], AF.Silu, [CTF], [CTB])

    load_weights(0)

    xits = [(s, t, hb) for s in range(0 if not SKIP_XPRO else 2, 2) for t in range(NTL) for hb in range(2)]

    def x_load(it):
        s, t, hb = xits[it]
        stg = XT if it % 2 == 0 else Y
        xin = bass.AP(stg.t, 0, [[8 * TT, 128], [1024, 2], [1, 1024]])
        tok0 = t * TT + hb * 256
        src = x_d[s, tok0:tok0 + 256, :].rearrange("(b p) d -> p b d", p=128)
        P.dma("sp", xin, src, stg.p[0], [], [stg])

    if xits:
        x_load(0)
    for it, (s, t, hb) in enumerate(xits):
        if it + 1 < len(xits):
            x_load(it + 1)
        stg = XT if it % 2 == 0 else Y
        tok0 = t * TT + hb * 256
        for c in range(8):
            pb = projb.next()

            def fn(e, c=c, pb=pb, stg=stg):
                ins = None
                for b in range(2):
                    a_in = bass.AP(stg.t, b * 1024 + c * 128, [[8 * TT, 128], [1, 128]])
                    ins = e.transpose(out=pb.t[:, b * 128:(b + 1) * 128], in_=a_in, identity=ident_f)
                return ins
            P.op("pe", fn, [stg, CMF], [pb])
            ft = FT.next()
            CPY(ft.t[:, 0:256], pb.t[:, 0:256], [pb], [ft], eng=("dve" if c % 2 else "act"))
            P.dma("sp", xT.t[s, c, :, tok0:tok0 + 256], ft.t[:, 0:256], ft, [ft], [xT.p[pt(s, t)]], partial=True)

    def ada_piece_dma(l, pc, dst_ap, sbt, bufs):
        P.dma("pool", dst_ap, wada_d[l, :, :, pc * 512:(pc + 1) * 512], sbt, [], bufs, **CASTKW)

    def ada_piece_mm(l, pc, lw_fn, bufs):
        pb = projb.next()
        specs = []
        for g4 in range(4):
            for kc in range(8):
                specs.append((pb.t[:, 2 * g4:2 * g4 + 2], lw_fn(kc, g4), CTB.t[:, kc, :], kc == 0, kc == 7))
        MMS(specs, list(bufs) + [CTB], [pb])
        for g4 in range(4):
            g = pc * 4 + g4
            TS(MODF.t[:, l, g, :], pb.t[:, 2 * g4:2 * g4 + 2], CST.t[:, l, 16 + g:17 + g], None, ALU.add, ALU.bypass,
               [pb, CST], [MODF])

    def ada_finish(l):
        for b in range(2):
            a1 = AD.t[:, l, b, 0, :]
            TS(a1, MODF.t[:, l, 8:16, b], 1.0, None, ALU.add, ALU.bypass, [MODF], [AD])
            TTO(a1, a1, CST.t[:, l, 0:8], ALU.mult, [AD, CST], [AD])
            CPY(AD.t[:, l, b, 1, :], MODF.t[:, l, 0:8, b], [MODF], [AD])
            TTO(AD.t[:, l, b, 2, :], MODF.t[:, l, 16:24, b], CST.t[:, l, 8:16], ALU.mult, [MODF, CST], [AD])

    if NL > 0:
        for pc in range(6):
            wbuf = STG.next()
            ada_piece_dma(0, pc, bass.AP(wbuf.t, 0, [[4096, 128], [512, 8], [1, 512]]), wbuf, [wbuf])
            ada_piece_mm(0, pc, lambda kc, g4, wbuf=wbuf: bass.AP(wbuf.t, kc * 512 + g4 * 128, [[4096, 128], [1, 128]]), [wbuf])
        ada_finish(0)

    def yhalf_bufs(k):
        return [Y.p[4 * k + j] for j in range(4)]

    def ada_bg_start(l, ti):
        if l >= NL or ti >= 6:
            return
        k = ti % 2
        ada_piece_dma(l, ti, bass.AP(YB, k * 4096, [[YROW, 128], [512, 8], [1, 512]]), Y.p[4 * k], yhalf_bufs(k))

    def ada_bg_end(l, ti):
        if l >= NL or ti >= 6:
            return
        k = ti % 2
        ada_piece_mm(l, ti, lambda kc, g4, k=k: bass.AP(YB, k * 4096 + kc * 512 + g4 * 128, [[YROW, 128], [1, 128]]),
                     yhalf_bufs(k))
        if ti == 5:
            ada_finish(l)

    load_wout(0)

    def tsl(t):
        return slice(t * TT, (t + 1) * TT)

    def load_x_tile(s, t):
        P.dma("sp", XT.t[:], xT.t[s, :, :, tsl(t)].rearrange("c p t -> p c t"), XT, [xT.p[pt(s, t)]], [XT])

    def sumsq_bank(srcs, reads, bank):
        n = len(srcs)
        for i, a in enumerate(srcs):
            sq = SQ.next()
            ACT(sq.t[:], a, AF.Square, reads, [sq])
            MM(bank.t[:], [(ones_b, sq.t[:])], [sq, CMB], [bank], start=(i == 0), stop=(i == n - 1))

    rs_next = {}

    pend_mult = []

    def pop_mult(n=1):
        for _ in range(n):
            if pend_mult:
                pend_mult.pop(0)()

    def norm_sq(s, t):
        load_x_tile(s, t)
        bufs = PT.items + QTBL
        for c in range(8):
            ACT(bufs[c].t[:], XT.t[:, c, :], AF.Square, [XT.p[c]], [bufs[c]])

    def norm_mm(s, t):
        bufs = PT.items + QTBL
        nb = PB[5]
        for c in range(8):
            MM(nb.t[:], [(ones_b, bufs[c].t[:])], [bufs[c], CMB], [nb], start=(c == 0), stop=(c == 7))
        rs = rstd_from(nb, 1024.0)
        for c in range(8):
            pend_mult.append(lambda c=c, rs=rs: TTO(XT.t[:, c, :], XT.t[:, c, :], rs.t[:], ALU.mult, [XT.p[c], rs], [XT.p[c]]))
        rs_next[(s, t)] = rs

    def norm_part(s, t):
        norm_sq(s, t)
        norm_mm(s, t)

    def next_tile(s, t):
        if t + 1 < NTL:
            return (s, t + 1)
        if s == 0:
            return (1, 0)
        return None

    def phase_A_common(l, s, t):
        if (s, t) not in rs_next:
            norm_part(s, t)
        pop_mult(8)
        rs_next.pop((s, t))
        P.dma("sp", TAB.t[:], tab_d[:, :, tsl(t)], TAB, [], [TAB])
        for c in range(8):
            if c % 2 == 0:
                ACT(HT.t[:, c, :], XT.t[:, c, :], AF.Identity, [XT.p[c], AD], [HT.p[c]], scale=AD.t[:, l, s, 0, c:c + 1],
                    bias=AD.t[:, l, s, 1, c:c + 1])
            else:
                TS(HT.t[:, c, :], XT.t[:, c, :], AD.t[:, l, s, 0, c:c + 1], AD.t[:, l, s, 1, c:c + 1], ALU.mult, ALU.add,
                   [XT.p[c], AD], [HT.p[c]])

    def hoist_sq(s, t):
        nt = next_tile(s, t)
        if nt is not None and not ONE_TILE:
            norm_sq(*nt)

    def hoist_mm(s, t):
        nt = next_tile(s, t)
        if nt is not None and not ONE_TILE:
            norm_mm(*nt)

    def proj_fm(col0, M, reads_extra=(), kc=8, w=None, rhs=None):
        w = w or WIN
        pb = projb.next()
        pairs = []
        for c in range(kc):
            r = HT.t[:, c, :] if rhs is None else rhs(c)
            pairs.append((w.t[:, c, col0:col0 + M], r))
        MM(pb.t[0:M, :], pairs, [w, HT] + list(reads_extra), [pb])
        return pb

    def rope(pbx, rows, tabi, perm, out_ap, out_w):
        r0, r1 = rows
        kmax = 128
        xb = XB.next()
        ACT(xb.t[0:kmax, :], pbx.t[0:kmax, :], AF.Identity, [pbx], [xb])
        pb2 = projb.next()
        MM(pb2.t[0:kmax, :], [(perm[0:kmax, 0:kmax], xb.t[0:kmax, :])], [xb, CMB], [pb2])
        f1 = FT.next()
        TTO(f1.t[r0:r1, :], pbx.t[r0:r1, :], TAB.t[r0:r1, tabi, :], ALU.mult, [pbx, TAB], [f1])
        f2 = FT.next()
        TTO(f2.t[r0:r1, :], pb2.t[r0:r1, :], TAB.t[r0:r1, tabi + 1, :], ALU.mult, [pb2, TAB], [f2])
        TTO(out_ap, f1.t[r0:r1, :], f2.t[r0:r1, :], ALU.add, [f1, f2], out_w)

    def store(dst, src, sbt, dparts, partial=True):
        P.dma("pool", dst, src, sbt, [sbt], dparts, partial=partial)

    def gate_proj(l, s, t, col0):
        st = STG.next()
        for c in range(8):
            pb = proj_fm(col0 + c * 128, 128)
            ACT(st.t[:, c * TT:(c + 1) * TT], pb.t[:], AF.Silu, [pb], [st.p[c]])
            pop_mult()
        store(GT.t[s, :, :, tsl(t)].rearrange("c p t -> p c t"),
              bass.AP(st.t, 0, [[4096, 128], [TT, 8], [1, TT]]), st, [GT.p[pt(s, t)]])

    def v_proj(s, t, col0, ncol, dst, lhs_src=None, w=None, kc=8):
        w = w or WIN
        st = STG.next()
        for b in range(4):
            pb = projb.next()
            pairs = []
            for c in range(kc):
                lt = HT.t[:, c, b * 128:(b + 1) * 128] if lhs_src is None else lhs_src(c, b)
                pairs.append((lt, w.t[:, c, col0:col0 + ncol]))
            MM(pb.t[:, 0:ncol], pairs, [w, HT, CQN], [pb])
            CPY(st.t[:, b * 512:b * 512 + ncol], pb.t[:, 0:ncol], [pb], [st.p[b]], eng=("dve" if b % 2 else "act"))
        store(dst.t[s, 4 * t:4 * t + 4, :, 0:ncol].rearrange("b p n -> p b n"),
              bass.AP(st.t, 0, [[4096, 128], [512, 4], [1, ncol]]), st, [dst.p[pt(s, t)]])

    def phase_A_even(l, s, t):
        phase_A_common(l, s, t)
        if ASTOP < 2:
            return
        sqb = PT.items + QTBL
        for j in range(5):
            pb = proj_fm((0 if j < 3 else 384 - 3 * 128) + j * 128, 128)
            CPY(CQC.t[:, j, :], pb.t[:], [pb], [CQC.p[j]], eng="act")
            ACT(sqb[j].t[:], pb.t[:], AF.Square, [pb], [sqb[j]])
        for (c0, nch, ncol, nfeat) in ((0, 3, 40, 384.0), (3, 2, 43, 256.0)):
            nb = PB[6]
            for j in range(nch):
                MM(nb.t[:], [(ones_b, sqb[c0 + j].t[:])], [sqb[c0 + j], CMB], [nb], start=(j == 0), stop=(j == nch - 1))
            rs = rstd_from(nb, nfeat)
            for j in range(nch):
                STT(CQN.t[:, c0 + j, :], CQC.t[:, c0 + j, :], CST.t[:, l, ncol + j:ncol + j + 1], rs.t[:],
                    ALU.mult, ALU.mult, [CQC.p[c0 + j], CST, rs], [CQN.p[c0 + j]])
        st = STG.next()
        for j in range(4):
            pb = proj_fm(736 + j * 128, 128)
            rope(pb, (0, 128), 0, perm64, st.t[:, j * TT:(j + 1) * TT], [st.p[j]])
        pb = proj_fm(1248, 128)
        rope(pb, (0, 128), 0, perm64, st.t[:, 4 * TT:5 * TT], [st.p[4]])
        store(QB.t[s, :, :, tsl(t)].rearrange("c p t -> p c t"),
              bass.AP(st.t, 0, [[4096, 128], [TT, 4], [1, TT]]), st, [QB.p[pt(s, t)]])
        store(KB.t[s, 0, :, tsl(t)], st.t[:, 4 * TT:5 * TT], st, [KB.p[pt(s, t)]])
        if ASTOP < 7:
            return
        hoist_sq(s, t)
        v_proj(s, t, 1376, 128, VB)
        if ASTOP < 8:
            return
        gate_proj(l, s, t, 1504)
        hoist_mm(s, t)
        if ASTOP < 3:
            return
        st = STG.next()
        for h in range(8):
            pb = proj_fm(h * 96, 128, w=WUQ, kc=3, rhs=lambda c: CQN.t[:, c, :], reads_extra=[CQN])
            CPY(st.t[0:64, h * TT:(h + 1) * TT], pb.t[0:64, :], [pb], [st.p[h]], eng="act")
            rope(pb, (64, 96), 2, perm32, st.t[64:96, h * TT:(h + 1) * TT], [st.p[h]])
            pop_mult()
        store(QA.t[s, :, 0:96, tsl(t)].rearrange("h p t -> p h t"),
              bass.AP(st.t, 0, [[4096, 96], [TT, 8], [1, TT]]), st, [QA.p[pt(s, t)]])
        if ASTOP < 4:
            return
        st = STG.next()
        for hp in range(4):
            pb = proj_fm(hp * 128, 128, w=WUK, kc=2, rhs=lambda c: CQN.t[:, 3 + c, :], reads_extra=[CQN])
            for e2 in range(2):
                h = 2 * hp + e2
                CPY(st.t[0:64, h * TT:(h + 1) * TT], pb.t[e2 * 64:e2 * 64 + 64, :], [pb], [st.p[h]],
                    eng=("dve" if e2 else "act"))
        pb = proj_fm(640, 128)
        xk = XB.next()
        rope(pb, (64, 96), 2, perm32, xk.t[64:96, :], [xk])
        store(KA.t[s, :, 0:64, tsl(t)].rearrange("h p t -> p h t"),
              bass.AP(st.t, 0, [[4096, 64], [TT, 8], [1, TT]]), st, [KA.p[pt(s, t)]])
        store(KA.t[s, :, 64:96, tsl(t)].rearrange("h p t -> p h t"),
              bass.AP(xk.t, 64 * TT, [[TT, 32], [0, 8], [1, TT]]), xk, [KA.p[pt(s, t)]])
        if ASTOP < 5:
            return
        v_proj(s, t, 0, 512, VA, lhs_src=lambda c, b: CQN.t[:, 3 + c, b * 128:(b + 1) * 128], w=WUV, kc=2)
        if ASTOP < 6:
            return

    def phase_A_odd(l, s, t):
        phase_A_common(l, s, t)
        for (col0, dst) in ((0, QB), (512, KB)):
            st = STG.next()
            for j in range(4):
                pb = proj_fm(col0 + j * 128, 128)
                rope(pb, (0, 128), 0, perm64, st.t[:, j * TT:(j + 1) * TT], [st.p[j]])
            store(dst.t[s, :, :, tsl(t)].rearrange("c p t -> p c t"),
                  bass.AP(st.t, 0, [[4096, 128], [TT, 4], [1, TT]]), st, [dst.p[pt(s, t)]])
        v_proj(s, t, 1024, 512, VB)
        hoist_sq(s, t)
        for (col0, dst) in ((1536, QA), (2048, KA)):
            st = STG.next()
            for hp in range(4):
                pb = proj_fm(col0 + hp * 128, 128)
                for e2 in range(2):
                    h = 2 * hp + e2
                    CPY(st.t[0:64, h * TT:(h + 1) * TT], pb.t[e2 * 64:e2 * 64 + 64, :], [pb], [st.p[h]],
                        eng=("dve" if e2 else "act"))
            store(dst.t[s, :, 0:64, tsl(t)].rearrange("h p t -> p h t"),
                  bass.AP(st.t, 0, [[4096, 64], [TT, 8], [1, TT]]), st, [dst.p[pt(s, t)]])
        v_proj(s, t, 2560, 512, VA)
        pb = projb.next()
        specs = []
        for b in range(4):
            for c in range(8):
                specs.append((pb.t[:, b * 8:(b + 1) * 8], HT.t[:, c, b * 128:(b + 1) * 128], WIN.t[:, c, 3072:3080],
                              c == 0, c == 7))
        MMS(specs, [WIN, HT], [pb])
        f = FT.next()
        for b in range(4):
            TTO(f.t[:, b * 8:(b + 1) * 8], pb.t[:, b * 8:(b + 1) * 8], CST.t[:, l, 54:62], ALU.add, [pb, CST], [f])
        f2 = FT.next()
        ACT(f2.t[:, 0:32], f.t[:, 0:32], AF.Exp, [f], [f2], scale=-1.0)
        ACT(SPT.t[:, 4 * t:4 * t + 4, :].rearrange("p b h -> p (b h)"), f2.t[:, 0:32], AF.Ln, [f2], [SPT], bias=1.0)
        hoist_mm(s, t)
        gate_proj(l, s, t, 3088)

    def fox_post(l, s):
        P.op("dve", lambda e: e.memset(PREF.t[:, 0, :], 0.0), [], [PREF])
        for b in range(1, NB):
            TTO(PREF.t[:, b, :], PREF.t[:, b - 1, :], SPT.t[:, b - 1, :], ALU.add, [PREF, SPT], [PREF])
        pb = projb.next()
        specs = []
        for b in range(NB):
            specs.append((pb.t[:, b * 8:(b + 1) * 8], tri_f, SPT.t[:, b, :], True, False))
            specs.append((pb.t[:, b * 8:(b + 1) * 8], ones_f, PREF.t[:, b, :], False, True))
        MMS(specs, [SPT, PREF, CMF], [pb])
        CPY(CSPB.t[:].rearrange("p b h -> p (b h)"), pb.t[:, 0:NB * 8], [pb], [CSPB])
        P.dma("pool", CSP.t[s], CSPB.t[:], CSPB, [CSPB], [CSP.p[s]])
        for g in range(4):
            pb2 = projb.next()

            def fn(e, g=g, pb2=pb2):
                ins = None
                for bb in range(4):
                    ins = e.transpose(out=pb2.t[0:8, bb * 128:(bb + 1) * 128], in_=CSPB.t[:, g * 4 + bb, :],
                                      identity=ident_f)
                return ins
            P.op("pe", fn, [CSPB, CMF], [pb2])
            ACT(FQ8.t[0:8, g * 512:(g + 1) * 512], pb2.t[0:8, :], AF.Identity, [pb2], [FQ8], scale=-8.0)
        P.dma("pool", QA.t[s, :, 64, :], FQ8.t[0:8, :], FQ8, [FQ8], allp(QA, s), partial=True)

    sbk = Rot(PB[0:3])
    sbk4 = Rot(PB[0:3] + [PB[7]])
    obk4 = Rot(PB[3:7])
    obk2 = Rot(PB[3:5])

    SKEW = 2
    pend = []

    def pipe_flush(keep=0):
        while len(pend) > keep:
            fn = pend.pop(0)
            fn()

    def attn_core(ktf, ktb, krows, kbase, qt, vaf, vab, scale, pO, i, bias=None, swa=False, lbank=None, fin=None, deep=True):
        if swa:
            js = [j for j in range(4 * i - 1, 4 * i + 4) if j >= 0]
        else:
            js = list(range(0, 4 * i + 4))
        for idx, j in enumerate(js):
            m = j - 4 * i
            c0 = 0 if m < 0 else 128 * m
            if swa:
                n = 128 if (m < 0 or m == 3) else 256
            else:
                n = TT - c0
            ps = (sbk4 if deep else sbk).next()
            need_mask = swa or m >= 0
            specs = [(ps.t[:, c0:c0 + n], ktf(j), qt.t[kbase:kbase + krows, c0:c0 + n], True, not need_mask)]
            if swa:
                mk = msw_b[:, 128:256] if m < 0 else msw_b[:, 0:n]
                specs.append((ps.t[:, c0:c0 + n], ident_b, mk, False, True))
            elif m >= 0:
                specs.append((ps.t[:, c0:c0 + 128], ident_b, mc_b, False, True))
            MMS(specs, [qt, CMB] + ktb, [ps])
            p = PT.next()
            kw = dict(scale=scale)
            rd = [ps]
            if bias is not None:
                kw["bias"] = bias(j)
                rd.append(CSPB)
            ACT(p.t[:, c0:c0 + n], ps.t[:, c0:c0 + n], AF.Exp, rd, [p], **kw)
            first = (idx == 0)
            last = (idx == len(js) - 1)

            def pv(p=p, c0=c0, n=n, j=j, first=first, last=last):
                specs = [(pO.t[:, c0:c0 + n], vaf(j), p.t[:, c0:c0 + n], first, last)]
                wr = [pO]
                if lbank is not None:
                    specs.append((lbank.t[:, c0:c0 + n], ones_b, p.t[:, c0:c0 + n], first, last))
                    wr.append(lbank)
                MMS(specs, [p, CMB] + vab, wr)
                if last and fin is not None:
                    fin()
            pend.append(pv)
            pipe_flush(keep=(3 if deep else SKEW))

    def load_va64(src, s, col0, v):
        ap = bass.AP(YB, 4096 + v * 2048 + v * 64, [[YROW, 128], [128, NB], [1, 64]])
        P.dma("sp", ap, src.t[s, :, :, col0:col0 + 64].rearrange("b p n -> p b n"), Y.p[4 + 2 * v],
              allp(src, s), va_bufs(v), partial=True)

    def set_ones():
        for v in range(2):
            ap = bass.AP(YB, 4096 + v * 2048 + (1 - v) * 64, [[YROW, 128], [128, NB], [1, 64]])
            P.op("dve", lambda e, ap=ap: e.memset(ap, 1.0), [], va_bufs(v))

    def zero_q_half(q, half):
        P.op("dve", lambda e, q=q, half=half: e.memset(q.t[half * 64:half * 64 + 64, :], 0.0), [], [q])

    def load_q_half(src, s, ch, half, t, q):
        r0 = half * 64
        P.dma("sp", q.t[r0:r0 + 64, :], src.t[s, ch, r0:r0 + 64, tsl(t)], q, [src.p[pt(s, t)]], [q], partial=True)
        return q

    def finalize64(pO, par, lbias, r_extra, s, t, chunk):
        n0, l0 = par * 64, (1 - par) * 64
        f2 = FT.next()
        if lbias is not None:
            f1 = FT.next()
            ACT(f1.t[n0:n0 + 64, :], pO.t[l0:l0 + 64, :], AF.Ln, [pO] + list(r_extra), [f1], bias=lbias)
            ACT(f2.t[n0:n0 + 64, :], f1.t[n0:n0 + 64, :], AF.Exp, [f1], [f2], scale=-1.0)
        else:
            P.op("dve", lambda e: e.reciprocal(out=f2.t[n0:n0 + 64, :], in_=pO.t[l0:l0 + 64, :]), [pO], [f2])
        xb = XB.next()
        TTO(xb.t[n0:n0 + 64, :], pO.t[n0:n0 + 64, :], f2.t[n0:n0 + 64, :], ALU.mult, [pO, f2], [xb])
        store(OT.t[s, chunk, n0:n0 + 64, tsl(t)], xb.t[n0:n0 + 64, :], xb, [OT.p[pt(s, t)]])

    def phase_B_even(l, s):
        sc_m = 96.0 ** -0.5
        pipe_flush()
        set_ones()
        for h in range(8):
            par = h % 2
            k = par
            P.dma("sp", kt_ap(k, rows=96), KA.t[s, h, 0:96, :], Y.p[2 * k], allp(KA, s), kt_bufs(k))
            load_va64(VA, s, h * 64, par)
            for t in range(NTL):
                q = QTB.next()
                P.dma("sp", q.t[0:96, :], QA.t[s, h, 0:96, tsl(t)], q, [QA.p[pt(s, t)]], [q])
                pO = obk4.next()
                attn_core(lambda j, k=k: kt_ap(k, rows=96, c0=j * 128, n=128), kt_bufs(k), 96, 0, q,
                          lambda j, v=par: va_blk(v, j), va_bufs(par), sc_m, pO, t,
                          fin=lambda pO=pO, par=par, t=t, h=h: finalize64(pO, par, None, [], s, t, h // 2))
        for grp in range(2):
            k = grp
            P.dma("sp", kt_ap(k), KB.t[s, 0, :, :], Y.p[2 * k], allp(KB, s), kt_bufs(k))
            load_va64(VB, s, grp * 64, grp)
            pipe_flush()
            for q in QTBL:
                zero_q_half(q, 1 - grp)
            for j4 in range(4):
                horig = grp * 4 + j4
                for t in range(NTL):
                    q = load_q_half(QB, s, j4, grp, t, QTB.next())
                    pO = obk4.next()
                    attn_core(lambda j, k=k: kt_ap(k, c0=j * 128, n=128), kt_bufs(k),
                              128, 0, q, lambda j, v=grp: va_blk(v, j), va_bufs(grp), 0.125, pO, t, swa=True,
                              fin=lambda pO=pO, grp=grp, horig=horig, t=t, j4=j4: finalize64(
                                  pO, grp, SM.t[grp * 64:grp * 64 + 64, horig:horig + 1], [SM], s, t, 4 + j4))

    def phase_B_odd(l, s):
        i = l // 2
        lam_init = 0.8 - 0.6 * math.exp(-0.3 * l)
        P.dma("sp", CSPB.t[:], CSP.t[s], CSPB, [CSP.p[s]], [CSPB])
        pipe_flush()
        for qi, q in enumerate(QTBL):
            zero_q_half(q, 1 - (qi % 2))
        dcnt = 0
        for h in range(4):
            k = h % 2
            v = h % 2
            P.dma("sp", kt_ap(k), KB.t[s, h, :, :], Y.p[2 * k], allp(KB, s), kt_bufs(k))
            vap = bass.AP(YB, 4096 + v * 2048, [[YROW, 128], [128, NB], [1, 128]])
            P.dma("sp", vap, VB.t[s, :, :, h * 128:(h + 1) * 128].rearrange("b p n -> p b n"), Y.p[4 + 2 * v],
                  allp(VB, s), va_bufs(v))
            for t in range(NTL):
                dcnt += 1
                for mp in range(2):
                    q = load_q_half(QB, s, h, mp, t, QTBL[2 * (dcnt % 2) + mp])
                    pO = obk2.next()
                    pL = PB[5 + mp]

                    def fin(pO=pO, pL=pL, mp=mp, t=t, h=h):
                        f1 = FT.next()
                        ACT(f1.t[:], pL.t[:], AF.Ln, [pL], [f1])
                        f2 = FT.next()
                        ACT(f2.t[:], f1.t[:], AF.Exp, [f1], [f2], scale=-1.0)
                        TTO(CQC.t[:, mp, :], pO.t[:], f2.t[:], ALU.mult, [pO, f2], [CQC])
                        if mp == 0:
                            return
                        od = CQC.t[:, 2, :]
                        STT(od, CQC.t[:, 1, :], SM.t[:, 16 + i:17 + i], CQC.t[:, 0, :], ALU.mult, ALU.add, [CQC, SM], [CQC])
                        nb = PB[7]
                        sumsq_bank([od], [CQC], nb)
                        rs = rstd_from(nb, 128.0)
                        xb = XB.next()
                        STT(xb.t[:], od, CST.t[:, l, 53:54], rs.t[:], ALU.mult, ALU.mult, [CQC, rs, CST], [xb])
                        xb2 = XB.next()
                        TS(xb2.t[:], xb.t[:], 1.0 - lam_init, None, ALU.mult, ALU.bypass, [xb], [xb2])
                        store(OT.t[s, h, :, tsl(t)], xb2.t[:], xb2, [OT.p[pt(s, t)]])
                    attn_core(lambda j, k=k: kt_ap(k, c0=j * 128, n=128), kt_bufs(k),
                              128, 0, q, lambda j, v=v: va_blk(v, j), va_bufs(v), 0.125, pO, t, lbank=pL, fin=fin, deep=False)
        pipe_flush()
        for k in range(2):
            ap = bass.AP(YB, 64 * YROW + k * 2048, [[YROW, 1], [1, 2048]])
            P.op("dve", lambda e, ap=ap: e.memset(ap, 1.0), [], kt_bufs(k))
        set_ones()
        for h in range(8):
            par = h % 2
            k = par
            P.dma("sp", kt_ap(k, rows=64), KA.t[s, h, 0:64, :], Y.p[2 * k], allp(KA, s), kt_bufs(k), partial=True)
            load_va64(VA, s, h * 64, par)
            for t in range(NTL):
                q = QTB.next()
                P.dma("sp", q.t[0:65, :], QA.t[s, h, 0:65, tsl(t)], q, [QA.p[pt(s, t)]], [q])
                pO = obk4.next()
                attn_core(lambda j, k=k: kt_ap(k, rows=65, c0=j * 128, n=128), kt_bufs(k), 65, 0, q,
                          lambda j, v=par: va_blk(v, j), va_bufs(par), 0.125, pO, t,
                          bias=lambda j, h=h: CSPB.t[:, j, h:h + 1],
                          fin=lambda pO=pO, par=par, t=t, h=h: finalize64(pO, par, None, [], s, t, 4 + h // 2))

    c_st = {}

    def phase_C1_load(l, s, t):
        st_o = STG.next()
        P.dma("sp", bass.AP(st_o.t, 0, [[4096, 128], [TT, 8], [1, TT]]),
              OT.t[s, :, :, tsl(t)].rearrange("c p t -> p c t"), st_o, [OT.p[pt(s, t)]], [st_o])
        st_g = STG.next()
        P.dma("sp", bass.AP(st_g.t, 0, [[4096, 128], [TT, 8], [1, TT]]),
              GT.t[s, :, :, tsl(t)].rearrange("c p t -> p c t"), st_g, [GT.p[pt(s, t)]], [st_g])
        c_st[(s, t)] = (st_o, st_g)

    def phase_C1_og(l, s, t):
        st_o, st_g = c_st.pop((s, t))
        for c in range(8):
            TTO(HT.t[:, c, :], st_o.t[:, c * TT:(c + 1) * TT], st_g.t[:, c * TT:(c + 1) * TT], ALU.mult,
                [st_o.p[c], st_g.p[c]], [HT.p[c]])

    def phase_C2(l, s, t):
        load_x_tile(s, t)
        nb = PB[6]
        sqb = PT.items + QTBL
        for g in range(8):
            pb = proj_fm(g * 128, 128, w=WOUT)
            CPY(Y.t[:, g, :], pb.t[:], [pb], [Y.p[g]], eng="act")
            ACT(sqb[g].t[:], pb.t[:], AF.Square, [pb], [sqb[g]])
            if g >= 2:
                MM(nb.t[:], [(ones_b, sqb[g - 2].t[:])], [sqb[g - 2], CMB], [nb], start=(g == 2), stop=False)
        for g in (6, 7):
            MM(nb.t[:], [(ones_b, sqb[g].t[:])], [sqb[g], CMB], [nb], start=False, stop=(g == 7))
        return rstd_from(nb, 1024.0)

    def phase_C3(l, s, t, rs):
        for c in range(8):
            f = FT.next()
            TTO(f.t[:], Y.t[:, c, :], rs.t[:], ALU.mult, [Y.p[c], rs], [f])
            STT(XT.t[:, c, :], f.t[:], AD.t[:, l, s, 2, c:c + 1], XT.t[:, c, :], ALU.mult, ALU.add, [f, AD, XT.p[c]], [XT.p[c]])
        P.dma("pool", xT.t[s, :, :, tsl(t)].rearrange("c p t -> p c t"), XT.t[:], XT, [XT], [xT.p[pt(s, t)]])

    def phase_C_all(l):
        tiles = [(s, t) for s in range(2) for t in range(NTL)]
        phase_C1_load(l, *tiles[0])
        phase_C1_og(l, *tiles[0])
        for idx, (s, t) in enumerate(tiles):
            if idx + 1 < len(tiles):
                phase_C1_load(l, *tiles[idx + 1])
            rs = phase_C2(l, s, t)
            if idx + 1 < len(tiles):
                phase_C1_og(l, *tiles[idx + 1])
            phase_C3(l, s, t, rs)

    for l in range(NL):
        i = l // 2
        even = (l % 2 == 0)
        if even:
            ACT(SM.t[:, 0:8], CST.t[:, l, 45:53], AF.Exp, [CST], [SM])
        else:
            lam_init = 0.8 - 0.6 * math.exp(-0.3 * l)
            f = FT.next()
            TTO(f.t[:, 0:64], CST.t[:, l, 62:126], CST.t[:, l, 126:190], ALU.mult, [CST], [f])
            TTO(f.t[:, 64:128], CST.t[:, l, 190:254], CST.t[:, l, 254:318], ALU.mult, [CST], [f])
            P.op("dve", lambda e, f=f: e.reduce_sum(out=SM.t[:, 20:22],
                                                 in_=f.t[:, 0:128].rearrange("p (a b) -> p a b", a=2),
                                                 axis=mybir.AxisListType.X), [f], [SM])
            ACT(SM.t[:, 22:24], SM.t[:, 20:22], AF.Exp, [SM], [SM])
            TTO(SM.t[:, 24:25], SM.t[:, 23:24], SM.t[:, 22:23], ALU.subtract, [SM], [SM])
            TS(SM.t[:, 16 + i:17 + i], SM.t[:, 24:25], -lam_init, None, ALU.add, ALU.bypass, [SM], [SM])
        if STOP < 2:
            continue
        for s in range(2):
            for t in range(NTL):
                if ONE_TILE and (s, t) != (0, 0):
                    continue
                ada_bg_start(l + 1, s * NTL + t)
                (phase_A_even if even else phase_A_odd)(l, s, t)
                ada_bg_end(l + 1, s * NTL + t)
            if not even and not ONE_TILE:
                fox_post(l, s)
        if l + 1 < NL:
            load_weights(l + 1)
        if STOP < 3:
            continue
        for s in range(2):
            (phase_B_even if even else phase_B_odd)(l, s)
        pipe_flush()
        if STOP < 4:
            continue
        phase_C_all(l)
        if l + 1 < NL:
            load_wout(l + 1)

    for s in range(0 if not SKIP_XPRO else 2, 2):
        for t in range(NTL):
            load_x_tile(s, t)
            for b in range(4):
                for half in range(2):
                    pb = projb.next()

                    def fn(e, b=b, half=half, pb=pb):
                        ins = None
                        for cc in range(4):
                            c = half * 4 + cc
                            ins = e.transpose(out=pb.t[:, cc * 128:(cc + 1) * 128],
                                              in_=XT.t[:, c, b * 128:(b + 1) * 128], identity=ident_f)
                        return ins
                    P.op("pe", fn, [XT, CMF], [pb])
                    ft = FT.next()
                    CPY(ft.t[:], pb.t[:], [pb], [ft], eng=("dve" if half else "act"))
                    tok0 = t * TT + b * 128
                    P.dma("pool", out_d[s, tok0:tok0 + 128, half * 512:(half + 1) * 512], ft.t[:], ft, [ft], [])
    P.emit()
    return nc


def _kc(w):
    K, N = w.shape
    return np.ascontiguousarray(w.reshape(K // 128, 128, N).transpose(1, 0, 2))


def _o_perm_even():
    perm = np.zeros(1024, np.int64)
    for f in range(1024):
        cf, r = divmod(f, 128)
        if cf < 4:
            perm[f] = f
        else:
            j = cf - 4
            head = j if r < 64 else 4 + j
            perm[f] = 512 + head * 64 + (r % 64)
    return perm


def _host_consts():
    pos = np.arange(S, dtype=np.float32)
    tab = np.zeros((128, 4, S), np.float32)
    inv64 = (1.0 / (np.float32(10000.0) ** (np.arange(0, 64, 2, dtype=np.float32) / np.float32(64)))).astype(np.float32)
    inv32 = (1.0 / (np.float32(10000.0) ** (np.arange(0, 32, 2, dtype=np.float32) / np.float32(32)))).astype(np.float32)
    for r in range(128):
        i = r % 64
        ang = (pos * inv64[i % 32]).astype(np.float32)
        tab[r, 0] = np.cos(ang)
        tab[r, 1] = -np.sin(ang) if i < 32 else np.sin(ang)
    for r in range(64, 96):
        j = r - 64
        ang = (pos * inv32[j % 16]).astype(np.float32)
        tab[r, 2] = np.cos(ang)
        tab[r, 3] = -np.sin(ang) if j < 16 else np.sin(ang)
    cmf = np.zeros((128, 384), np.float32)
    cmf[:, 0:128] = np.eye(128, dtype=np.float32)
    k = np.arange(128)
    cmf[:, 128:256] = (k[:, None] <= k[None, :]).astype(np.float32)
    cmf[:, 256:384] = 1.0
    cmb = np.zeros((128, 896), np.float32)
    cmb[:, 0:128] = np.eye(128, dtype=np.float32)
    for m in range(128):
        sw = m + 32 if (m % 64) < 32 else m - 32
        cmb[sw, 128 + m] = 1.0
    for m in range(128):
        if 64 <= m < 96:
            sw = m + 16 if (m - 64) < 16 else m - 16
        else:
            sw = m
        cmb[sw, 256 + m] = 1.0
    cmb[:, 384:512] = np.where(k[:, None] > k[None, :], NEG, 0.0)
    cmb[:, 512:640] = np.where(k[:, None] > k[None, :], NEG, 0.0)
    cmb[:, 640:768] = np.where(k[:, None] <= k[None, :], NEG, 0.0)
    cmb[:, 768:896] = 1.0
    return tab, cmf, cmb


def _prep_shared(inp):
    f = lambda a: np.asarray(a, dtype=np.float32)
    sh = {}
    w_ada = f(inp["w_ada"])
    sh["wada"] = np.ascontiguousarray(w_ada.reshape(4, 8, 128, 3072).transpose(0, 2, 1, 3))
    perm_e = _o_perm_even()
    wE, wUQ, wUK, wUV, wOe = [], [], [], [], []
    for i in range(2):
        w = f(inp["ev_w_in"][i])
        cols = [w[:, 0:384], w[:, 384:640], np.zeros((1024, 64), np.float32), w[:, 640:672]]
        sq = w[:, 672:1184].reshape(1024, 8, 64)
        order = []
        for j in range(4):
            order += [j, 4 + j]
        cols.append(sq[:, order, :].reshape(1024, 512))
        cols.append(w[:, 1184:1312])
        cols.append(w[:, 1312:1440])
        cols.append(w[:, 1440:2464][:, perm_e])
        wE.append(_kc(np.concatenate(cols, axis=1)))
        wUQ.append(_kc(np.concatenate([f(inp["ev_w_uq"][i]), np.zeros((384, 32), np.float32)], axis=1)))
        ukv = f(inp["ev_w_ukv"][i]).reshape(256, 8, 128)
        wUK.append(_kc(np.ascontiguousarray(ukv[:, :, 0:64]).reshape(256, 512)))
        wUV.append(_kc(np.ascontiguousarray(ukv[:, :, 64:128]).reshape(256, 512)))
        wOe.append(_kc(f(inp["ev_w_out"][i])[perm_e, :]))
    sh["wE"] = np.stack(wE)
    sh["wUQ"] = np.stack(wUQ)
    sh["wUK"] = np.stack(wUK)
    sh["wUV"] = np.stack(wUV)
    sh["wOe"] = np.stack(wOe)
    wO, wOo = [], []
    for i in range(2):
        w = f(inp["od_w_in"][i])
        cols = [w[:, 0:3080], np.zeros((1024, 8), np.float32), w[:, 3080:4104]]
        wO.append(_kc(np.concatenate(cols, axis=1)))
        wOo.append(_kc(f(inp["od_w_out"][i])))
    sh["wO"] = np.stack(wO)
    sh["wOo"] = np.stack(wOo)
    cst = np.zeros((128, 4, NC_CST), np.float32)
    for l in range(4):
        i = l // 2
        cst[:, l, 0:8] = f(inp["g_pre"][l]).reshape(8, 128).T
        cst[:, l, 8:16] = f(inp["g_post"][l]).reshape(8, 128).T
        cst[:, l, 16:40] = f(inp["b_ada"][l]).reshape(24, 128).T
        if l % 2 == 0:
            cst[:, l, 40:43] = f(inp["ev_q_norm"][i]).reshape(3, 128).T
            cst[:, l, 43:45] = f(inp["ev_kv_norm"][i]).reshape(2, 128).T
            cst[:, l, 45:53] = f(inp["ev_sinks"][i])[None, :]
        else:
            cst[:, l, 53] = f(inp["od_subln"][i])
            cst[:, l, 54:62] = f(inp["od_forget_bias"][i])[None, :]
            cst[:, l, 62:318] = f(inp["od_lambda"][i]).reshape(256)[None, :]
    sh["cst"] = cst
    tab, cmf, cmb = _host_consts()
    sh["tab"] = tab
    sh["cmf"] = cmf
    sh["cmb"] = cmb
    return sh


_NC_CACHE = {}


def run(inputs, NL=4, cores=8):
    x = np.asarray(inputs["x"], dtype=np.float32)
    c = np.asarray(inputs["c"], dtype=np.float32)
    sh = _prep_shared(inputs)
    in_maps = []
    for core in range(cores):
        m = dict(sh)
        m["x"] = np.ascontiguousarray(x[2 * core:2 * core + 2])
        cc = c[2 * core:2 * core + 2]
        m["cT"] = np.ascontiguousarray(cc.T.reshape(8, 128, 2).transpose(1, 0, 2))
        in_maps.append(m)
    if NL not in _NC_CACHE:
        _NC_CACHE[NL] = build(NL)
    res = run_bass_kernel_spmd(_NC_CACHE[NL], in_maps, core_ids=list(range(cores)))
    if DEBUG:
        return res.results
    return np.concatenate([r["out"] for r in res.results], axis=0).astype(np.float32)


def kernel(**inputs):
    return run(inputs, NL=4, cores=8)
```

```python
from contextlib import ExitStack
import math
import numpy as np
import concourse.bass as bass
import concourse.mybir as mybir
from concourse.bass_utils import run_bass_kernel_spmd

F32 = mybir.dt.float32
BF16 = mybir.dt.bfloat16
AF = mybir.ActivationFunctionType
ALU = mybir.AluOpType

S = 2048
TT = 512
NTL = S // TT
NB = S // 128
EPS = 1e-6
NC_CST = 320
NE = 2528
NO = 4112
NEG = -30000.0
STOP = 9
ONE_TILE = False
DEBUG = False
ASTOP = 99
SKIP_XPRO = False


class Buf:
    __slots__ = ("name", "writers", "readers", "excl")

    def __init__(self, name, excl=False):
        self.name = name
        self.excl = excl
        self.writers = {}
        self.readers = {}


class Tens:
    def __init__(self, t, name, parts, excl=False):
        self.t = t
        self.name = name
        self.p = [Buf(f"{name}.{i}", excl) for i in range(parts)]


ENGS = ("pe", "act", "dve", "pool", "sp")


class Prog:
    def __init__(self, nc, same_engine_sync=True):
        self.nc = nc
        self.es = ExitStack()
        self.ops = {e: [] for e in ENGS}
        self.cnt = {e: 0 for e in ENGS}
        self.known = {e: {} for e in ENGS}
        self.snap = {}
        self.chan_cnt = {}
        self.sem = {}
        self.same_engine_sync = same_engine_sync

    def sbuf(self, name, shape, dtype, parts=1):
        t = self.es.enter_context(self.nc.sbuf_tensor("sb_" + name, list(shape), dtype))
        return Tens(t, name, parts)

    def psum(self, name, shape, dtype, parts=1):
        t = self.es.enter_context(self.nc.psum_tensor("ps_" + name, list(shape), dtype))
        return Tens(t, name, parts, excl=True)

    def dram(self, name, shape, dtype, kind="Internal", parts=1):
        t = self.nc.dram_tensor(name, list(shape), dtype, kind=kind).ap()
        return Tens(t, name, parts)

    @staticmethod
    def _flat(xs):
        out = []
        for x in xs:
            if isinstance(x, Tens):
                out.extend(x.p)
            elif isinstance(x, (list, tuple)):
                out.extend(Prog._flat(x))
            else:
                out.append(x)
        return out

    def _deps(self, eng, reads, writes, skip=None):
        deps = {}

        def add(d):
            for k, v in d.items():
                if deps.get(k, 0) < v:
                    deps[k] = v

        for b in reads:
            add(b.writers)
        for b in writes:
            add(b.writers)
            add(b.readers)
        waits = []
        kn = self.known[eng]
        for k, v in deps.items():
            if k == skip:
                continue
            if k == eng and (eng == "pe" or not self.same_engine_sync):
                continue
            if kn.get(k, 0) >= v:
                continue
            waits.append((k, v))
        for k, v in waits:
            kn[k] = max(kn.get(k, 0), v)
            sn = self.snap.get((k, v))
            if sn:
                for k2, v2 in sn.items():
                    if kn.get(k2, 0) < v2:
                        kn[k2] = v2
        return waits

    def _mark(self, tok, reads, writes, partial):
        k, v = tok
        for b in reads:
            if b.readers.get(k, 0) < v:
                b.readers[k] = v
        for b in writes:
            if partial:
                b.writers[k] = v
            else:
                b.writers = {k: v}
                b.readers = {}

    def op(self, eng, fn, reads=(), writes=()):
        reads = self._flat(reads)
        writes = self._flat(writes)
        writes = writes + [b for b in reads if b.excl and b not in writes]
        reads = [b for b in reads if not b.excl]
        waits = self._deps(eng, reads, writes)
        self.cnt[eng] += 1
        tok = (eng, self.cnt[eng])
        sn = dict(self.known[eng])
        sn[eng] = self.cnt[eng]
        self.snap[tok] = sn
        self._mark(tok, reads, writes, False)
        self.ops[eng].append(("op", waits, fn, tok))
        return tok

    def dma(self, eng, out, in_, sb, reads=(), writes=(), partial=False, **kw):
        reads = self._flat(reads)
        writes = self._flat(writes)
        sbb = self._flat([sb])[0]
        chan = "d_" + sbb.name
        waits = self._deps(eng, reads, writes, skip=(chan if partial else None))
        self.chan_cnt[chan] = self.chan_cnt.get(chan, 0) + 16
        tok = (chan, self.chan_cnt[chan])
        self.snap[tok] = dict(self.known[eng])
        self._mark(tok, reads, writes, partial)
        self.ops[eng].append(("dma", waits, (out, in_, kw), tok))
        return tok

    def emit(self):
        nc = self.nc
        for k in list(ENGS) + list(self.chan_cnt):
            if k not in self.sem:
                self.sem[k] = self.es.enter_context(nc.semaphore("s_" + k.replace(".", "_")))
        final = list(self.chan_cnt.items())

        def run(ename, eng):
            for kind, waits, payload, tok in self.ops[ename]:
                for k, v in waits:
                    eng.wait_ge(self.sem[k], v)
                if kind == "op":
                    payload(eng).then_inc(self.sem[tok[0]], 1)
                else:
                    out, in_, kw = payload
                    eng.dma_start(out=out, in_=in_, **kw).then_inc(self.sem[tok[0]], 16)
            if ename == "sp":
                for c, v in final:
                    eng.wait_ge(self.sem[c], v)

        with nc.Block() as block:
            @block.sync
            def _(e):
                run("sp", e)

            @block.scalar
            def _(e):
                run("act", e)

            @block.vector
            def _(e):
                run("dve", e)

            @block.gpsimd
            def _(e):
                run("pool", e)

            @block.tensor
            def _(e):
                run("pe", e)
        self.es.close()


class Rot:
    def __init__(self, items):
        self.items = items
        self.i = 0

    def next(self):
        x = self.items[self.i % len(self.items)]
        self.i += 1
        return x


def build(NL=4):
    nc = bass.Bass("TRN2", target_bir_lowering=False)
    P = Prog(nc)

    def din(name, shape, dt=F32):
        return nc.dram_tensor(name, list(shape), dt, kind="ExternalInput").ap()

    x_d = din("x", [2, S, 1024])
    cT_d = din("cT", [128, 8, 2])
    wada_d = din("wada", [4, 128, 8, 3072])
    cst_d = din("cst", [128, 4, NC_CST])
    cmf_d = din("cmf", [128, 384])
    cmb_d = din("cmb", [128, 896])
    tab_d = din("tab", [128, 4, S])
    wE_d = din("wE", [2, 128, 8, NE])
    wUQ_d = din("wUQ", [2, 128, 3, 800])
    wUK_d = din("wUK", [2, 128, 2, 512])
    wUV_d = din("wUV", [2, 128, 2, 512])
    wOe_d = din("wOe", [2, 128, 8, 1024])
    wO_d = din("wO", [2, 128, 8, NO])
    wOo_d = din("wOo", [2, 128, 8, 1024])
    out_d = nc.dram_tensor("out", [2, S, 1024], F32, kind="ExternalOutput").ap()

    def scr(name, shape, dt, parts=2 * NTL):
        return P.dram(name, shape, dt, kind=("ExternalOutput" if DEBUG else "Internal"), parts=parts)

    xT = scr("xT", [2, 8, 128, S], F32)
    QA = scr("QA", [2, 8, 128, S], BF16)
    KA = scr("KA", [2, 8, 128, S], BF16)
    QB = scr("QB", [2, 4, 128, S], BF16)
    KB = scr("KB", [2, 4, 128, S], BF16)
    VA = scr("VA", [2, NB, 128, 512], BF16)
    VB = scr("VB", [2, NB, 128, 512], BF16)
    GT = scr("GT", [2, 8, 128, S], BF16)
    OT = scr("OT", [2, 8, 128, S], BF16)
    CSP = scr("CSP", [2, 128, NB, 8], F32, parts=2)

    def pt(s, t):
        return s * NTL + t

    def allp(T, s):
        return [T.p[pt(s, t)] for t in range(NTL)]

    WIN = P.sbuf("WIN", [128, 8, NO], BF16)
    WOUT = P.sbuf("WOUT", [128, 8, 1024], BF16)
    WUQ = P.sbuf("WUQ", [128, 3, 800], BF16)
    WUK = P.sbuf("WUK", [128, 2, 512], BF16)
    WUV = P.sbuf("WUV", [128, 2, 512], BF16)
    CST = P.sbuf("CST", [128, 4, NC_CST], F32)
    AD = P.sbuf("AD", [128, 4, 2, 3, 8], F32)
    CMF = P.sbuf("CMF", [128, 384], F32)
    CMB = P.sbuf("CMB", [128, 896], BF16)
    TAB = P.sbuf("TAB", [128, 4, TT], F32)
    XT = P.sbuf("XT", [128, 8, TT], F32, parts=8)
    HT = P.sbuf("HT", [128, 8, TT], BF16, parts=8)
    SQ = Rot([P.sbuf(f"SQ{i}", [128, TT], BF16) for i in range(2)])
    FT = Rot([P.sbuf(f"FT{i}", [128, TT], F32) for i in range(4)])
    RSB = Rot([P.sbuf(f"RSB{i}", [128, TT], F32) for i in range(2)])
    STG = Rot([P.sbuf(f"STG{i}", [128, 4096], BF16, parts=8) for i in range(2)])
    CQC = P.sbuf("CQC", [128, 5, TT], F32, parts=5)
    CQN = P.sbuf("CQN", [128, 5, TT], BF16, parts=5)
    XB = Rot([P.sbuf(f"XB{i}", [128, TT], BF16) for i in range(2)])
    Y = P.sbuf("Y", [128, 8, TT], F32, parts=8)
    YB = Y.t.bitcast(BF16)
    QTBL = [P.sbuf(f"QTB{i}", [128, TT], BF16) for i in range(4)]
    QTB = Rot(QTBL)
    PT = Rot([P.sbuf(f"PT{i}", [128, TT], BF16) for i in range(4)])
    CSPB = P.sbuf("CSPB", [128, NB, 8], F32)
    SPT = P.sbuf("SPT", [128, NB, 8], F32)
    PREF = P.sbuf("PREF", [128, NB, 8], F32)
    FQ8 = P.sbuf("FQ8", [128, S], BF16)
    SM = P.sbuf("SM", [128, 64], F32)
    CTB = P.sbuf("CTB", [128, 8, 2], BF16)
    CTF = P.sbuf("CTF", [128, 8, 2], F32)
    MODF = P.sbuf("MODF", [128, 4, 24, 2], F32)

    PB = [P.psum(f"B{i}", [128, 512], F32) for i in range(8)]

    ident_f = CMF.t[:, 0:128]
    tri_f = CMF.t[:, 128:256]
    ones_f = CMF.t[:, 256:384]
    ident_b = CMB.t[:, 0:128]
    perm64 = CMB.t[:, 128:256]
    perm32 = CMB.t[:, 256:384]
    mc_b = CMB.t[:, 384:512]
    msw_b = CMB.t[:, 512:768]
    ones_b = CMB.t[:, 768:896]

    YROW = 8192

    def kt_ap(k, rows=128, p0=0, c0=0, n=2048):
        return bass.AP(YB, p0 * YROW + k * 2048 + c0, [[YROW, rows], [1, n]])

    def kt_bufs(k):
        return [Y.p[2 * k], Y.p[2 * k + 1]]

    def va_blk(v, j):
        return bass.AP(YB, 4096 + v * 2048 + j * 128, [[YROW, 128], [1, 128]])

    def va_bufs(v):
        return [Y.p[4 + 2 * v], Y.p[5 + 2 * v]]

    def ACT(out, in_, func, r, w, **kw):
        P.op("act", lambda e: e.activation(out=out, in_=in_, func=func, **kw), r, w)

    def TTO(out, a, b, op, r, w, eng="dve"):
        P.op(eng, lambda e: e.tensor_tensor(out=out, in0=a, in1=b, op=op), r, w)

    def TS(out, a, s1, s2, op0, op1, r, w, eng="dve"):
        P.op(eng, lambda e: e.tensor_scalar(out=out, in0=a, scalar1=s1, scalar2=s2, op0=op0, op1=op1), r, w)

    def STT(out, a, s, b, op0, op1, r, w):
        P.op("dve", lambda e: e.scalar_tensor_tensor(out=out, in0=a, scalar=s, in1=b, op0=op0, op1=op1), r, w)

    def CPY(out, in_, r, w, eng="dve"):
        if eng == "act":
            P.op("act", lambda e: e.activation(out=out, in_=in_, func=AF.Identity), r, w)
        else:
            P.op(eng, lambda e: e.tensor_copy(out=out, in_=in_), r, w)

    def MM(out, pairs, r, w, start=True, stop=True):
        pairs = list(pairs)

        def fn(e):
            n = len(pairs)
            ins = None
            for i, (l, rr) in enumerate(pairs):
                ins = e.matmul(out, lhsT=l, rhs=rr, start=(start and i == 0), stop=(stop and i == n - 1))
            return ins
        P.op("pe", fn, r, w)

    def MMS(specs, r, w):
        specs = list(specs)

        def fn(e):
            ins = None
            for (o, l, rr, st, sp) in specs:
                ins = e.matmul(o, lhsT=l, rhs=rr, start=st, stop=sp)
            return ins
        P.op("pe", fn, r, w)

    projb = Rot(PB[0:5])

    def rstd_from(ps, n_feat):
        t1 = FT.next()
        ACT(t1.t[:], ps.t[:], AF.Ln, [ps], [t1], scale=1.0 / n_feat, bias=EPS)
        t2 = RSB.next()
        ACT(t2.t[:], t1.t[:], AF.Exp, [t1], [t2], scale=-0.5)
        return t2

    CASTKW = dict(max_dma_last_dim=4096)

    def load_weights(l):
        i = l // 2
        if l % 2 == 0:
            for c in range(8):
                P.dma("pool", WIN.t[:, c, 0:NE], wE_d[i, :, c, :], WIN, [], [WIN], partial=(c > 0), **CASTKW)
            P.dma("pool", WUQ.t[:], wUQ_d[i], WUQ, [], [WUQ], **CASTKW)
            P.dma("pool", WUK.t[:], wUK_d[i], WUK, [], [WUK], **CASTKW)
            P.dma("pool", WUV.t[:], wUV_d[i], WUV, [], [WUV], **CASTKW)
        else:
            for c in range(8):
                P.dma("pool", WIN.t[:, c, :], wO_d[i, :, c, :], WIN, [], [WIN], partial=(c > 0), **CASTKW)

    def load_wout(l):
        i = l // 2
        src = wOe_d if l % 2 == 0 else wOo_d
        for c in range(8):
            P.dma("pool", WOUT.t[:, c, :], src[i, :, c, :], WOUT, [], [WOUT], partial=(c > 0), **CASTKW)

    P.dma("sp", CST.t[:], cst_d, CST, [], [CST])
    P.dma("sp", CMF.t[:], cmf_d, CMF, [], [CMF])
    P.dma("pool", CMB.t[:], cmb_d, CMB, [], [CMB])
    P.dma("sp", CTF.t[:], cT_d, CTF, [], [CTF])
    ACT(CTB.t[:], CTF.t[:], AF.Silu, [CTF], [CTB])

    load_weights(0)

    xits = [(s, t, hb) for s in range(0 if not SKIP_XPRO else 2, 2) for t in range(NTL) for hb in range(2)]

    def x_load(it):
        s, t, hb = xits[it]
        stg = XT if it % 2 == 0 else Y
        xin = bass.AP(stg.t, 0, [[8 * TT, 128], [1024, 2], [1, 1024]])
        tok0 = t * TT + hb * 256
        src = x_d[s, tok0:tok0 + 256, :].rearrange("(b p) d -> p b d", p=128)
        P.dma("sp", xin, src, stg.p[0], [], [stg])

    if xits:
        x_load(0)
    for it, (s, t, hb) in enumerate(xits):
        if it + 1 < len(xits):
            x_load(it + 1)
        stg = XT if it % 2 == 0 else Y
        tok0 = t * TT + hb * 256
        for c in range(8):
            pb = projb.next()

            def fn(e, c=c, pb=pb, stg=stg):
                ins = None
                for b in range(2):
                    a_in = bass.AP(stg.t, b * 1024 + c * 128, [[8 * TT, 128], [1, 128]])
                    ins = e.transpose(out=pb.t[:, b * 128:(b + 1) * 128], in_=a_in, identity=ident_f)
                return ins
            P.op("pe", fn, [stg, CMF], [pb])
            ft = FT.next()
            CPY(ft.t[:, 0:256], pb.t[:, 0:256], [pb], [ft], eng=("dve" if c % 2 else "act"))
            P.dma("sp", xT.t[s, c, :, tok0:tok0 + 256], ft.t[:, 0:256], ft, [ft], [xT.p[pt(s, t)]], partial=True)

    def ada_piece_dma(l, pc, dst_ap, sbt, bufs):
        P.dma("pool", dst_ap, wada_d[l, :, :, pc * 512:(pc + 1) * 512], sbt, [], bufs, **CASTKW)

    def ada_piece_mm(l, pc, lw_fn, bufs):
        pb = projb.next()
        specs = []
        for g4 in range(4):
            for kc in range(8):
                specs.append((pb.t[:, 2 * g4:2 * g4 + 2], lw_fn(kc, g4), CTB.t[:, kc, :], kc == 0, kc == 7))
        MMS(specs, list(bufs) + [CTB], [pb])
        for g4 in range(4):
            g = pc * 4 + g4
            TS(MODF.t[:, l, g, :], pb.t[:, 2 * g4:2 * g4 + 2], CST.t[:, l, 16 + g:17 + g], None, ALU.add, ALU.bypass,
               [pb, CST], [MODF])

    def ada_finish(l):
        for b in range(2):
            a1 = AD.t[:, l, b, 0, :]
            TS(a1, MODF.t[:, l, 8:16, b], 1.0, None, ALU.add, ALU.bypass, [MODF], [AD])
            TTO(a1, a1, CST.t[:, l, 0:8], ALU.mult, [AD, CST], [AD])
            CPY(AD.t[:, l, b, 1, :], MODF.t[:, l, 0:8, b], [MODF], [AD])
            TTO(AD.t[:, l, b, 2, :], MODF.t[:, l, 16:24, b], CST.t[:, l, 8:16], ALU.mult, [MODF, CST], [AD])

    if NL > 0:
        for pc in range(6):
            wbuf = STG.next()
            ada_piece_dma(0, pc, bass.AP(wbuf.t, 0, [[4096, 128], [512, 8], [1, 512]]), wbuf, [wbuf])
            ada_piece_mm(0, pc, lambda kc, g4, wbuf=wbuf: bass.AP(wbuf.t, kc * 512 + g4 * 128, [[4096, 128], [1, 128]]), [wbuf])
        ada_finish(0)

    def yhalf_bufs(k):
        return [Y.p[4 * k + j] for j in range(4)]

    def ada_bg_start(l, ti):
        if l >= NL or ti >= 6:
            return
        k = ti % 2
        ada_piece_dma(l, ti, bass.AP(YB, k * 4096, [[YROW, 128], [512, 8], [1, 512]]), Y.p[4 * k], yhalf_bufs(k))

    def ada_bg_end(l, ti):
        if l >= NL or ti >= 6:
            return
        k = ti % 2
        ada_piece_mm(l, ti, lambda kc, g4, k=k: bass.AP(YB, k * 4096 + kc * 512 + g4 * 128, [[YROW, 128], [1, 128]]),
                     yhalf_bufs(k))
        if ti == 5:
            ada_finish(l)

    load_wout(0)

    def tsl(t):
        return slice(t * TT, (t + 1) * TT)

    def load_x_tile(s, t):
        P.dma("sp", XT.t[:], xT.t[s, :, :, tsl(t)].rearrange("c p t -> p c t"), XT, [xT.p[pt(s, t)]], [XT])

    def sumsq_bank(srcs, reads, bank):
        n = len(srcs)
        for i, a in enumerate(srcs):
            sq = SQ.next()
            ACT(sq.t[:], a, AF.Square, reads, [sq])
            MM(bank.t[:], [(ones_b, sq.t[:])], [sq, CMB], [bank], start=(i == 0), stop=(i == n - 1))

    rs_next = {}

    pend_mult = []

    def pop_mult(n=1):
        for _ in range(n):
            if pend_mult:
                pend_mult.pop(0)()

    def norm_sq(s, t):
        load_x_tile(s, t)
        bufs = PT.items + QTBL
        for c in range(8):
            ACT(bufs[c].t[:], XT.t[:, c, :], AF.Square, [XT.p[c]], [bufs[c]])

    def norm_mm(s, t):
        bufs = PT.items + QTBL
        nb = PB[5]
        for c in range(8):
            MM(nb.t[:], [(ones_b, bufs[c].t[:])], [bufs[c], CMB], [nb], start=(c == 0), stop=(c == 7))
        rs = rstd_from(nb, 1024.0)
        for c in range(8):
            pend_mult.append(lambda c=c, rs=rs: TTO(XT.t[:, c, :], XT.t[:, c, :], rs.t[:], ALU.mult, [XT.p[c], rs], [XT.p[c]]))
        rs_next[(s, t)] = rs

    def norm_part(s, t):
        norm_sq(s, t)
        norm_mm(s, t)

    def next_tile(s, t):
        if t + 1 < NTL:
            return (s, t + 1)
        if s == 0:
            return (1, 0)
        return None

    def phase_A_common(l, s, t):
        if (s, t) not in rs_next:
            norm_part(s, t)
        pop_mult(8)
        rs_next.pop((s, t))
        P.dma("sp", TAB.t[:], tab_d[:, :, tsl(t)], TAB, [], [TAB])
        for c in range(8):
            if c % 2 == 0:
                ACT(HT.t[:, c, :], XT.t[:, c, :], AF.Identity, [XT.p[c], AD], [HT.p[c]], scale=AD.t[:, l, s, 0, c:c + 1],
                    bias=AD.t[:, l, s, 1, c:c + 1])
            else:
                TS(HT.t[:, c, :], XT.t[:, c, :], AD.t[:, l, s, 0, c:c + 1], AD.t[:, l, s, 1, c:c + 1], ALU.mult, ALU.add,
                   [XT.p[c], AD], [HT.p[c]])

    def hoist_sq(s, t):
        nt = next_tile(s, t)
        if nt is not None and not ONE_TILE:
            norm_sq(*nt)

    def hoist_mm(s, t):
        nt = next_tile(s, t)
        if nt is not None and not ONE_TILE:
            norm_mm(*nt)

    def proj_fm(col0, M, reads_extra=(), kc=8, w=None, rhs=None):
        w = w or WIN
        pb = projb.next()
        pairs = []
        for c in range(kc):
            r = HT.t[:, c, :] if rhs is None else rhs(c)
            pairs.append((w.t[:, c, col0:col0 + M], r))
        MM(pb.t[0:M, :], pairs, [w, HT] + list(reads_extra), [pb])
        return pb

    def rope(pbx, rows, tabi, perm, out_ap, out_w):
        r0, r1 = rows
        kmax = 128
        xb = XB.next()
        ACT(xb.t[0:kmax, :], pbx.t[0:kmax, :], AF.Identity, [pbx], [xb])
        pb2 = projb.next()
        MM(pb2.t[0:kmax, :], [(perm[0:kmax, 0:kmax], xb.t[0:kmax, :])], [xb, CMB], [pb2])
        f1 = FT.next()
        TTO(f1.t[r0:r1, :], pbx.t[r0:r1, :], TAB.t[r0:r1, tabi, :], ALU.mult, [pbx, TAB], [f1])
        f2 = FT.next()
        TTO(f2.t[r0:r1, :], pb2.t[r0:r1, :], TAB.t[r0:r1, tabi + 1, :], ALU.mult, [pb2, TAB], [f2])
        TTO(out_ap, f1.t[r0:r1, :], f2.t[r0:r1, :], ALU.add, [f1, f2], out_w)

    def store(dst, src, sbt, dparts, partial=True):
        P.dma("pool", dst, src, sbt, [sbt], dparts, partial=partial)

    def gate_proj(l, s, t, col0):
        st = STG.next()
        for c in range(8):
            pb = proj_fm(col0 + c * 128, 128)
            ACT(st.t[:, c * TT:(c + 1) * TT], pb.t[:], AF.Silu, [pb], [st.p[c]])
            pop_mult()
        store(GT.t[s, :, :, tsl(t)].rearrange("c p t -> p c t"),
              bass.AP(st.t, 0, [[4096, 128], [TT, 8], [1, TT]]), st, [GT.p[pt(s, t)]])

    def v_proj(s, t, col0, ncol, dst, lhs_src=None, w=None, kc=8):
        w = w or WIN
        st = STG.next()
        for b in range(4):
            pb = projb.next()
            pairs = []
            for c in range(kc):
                lt = HT.t[:, c, b * 128:(b + 1) * 128] if lhs_src is None else lhs_src(c, b)
                pairs.append((lt, w.t[:, c, col0:col0 + ncol]))
            MM(pb.t[:, 0:ncol], pairs, [w, HT, CQN], [pb])
            CPY(st.t[:, b * 512:b * 512 + ncol], pb.t[:, 0:ncol], [pb], [st.p[b]], eng=("dve" if b % 2 else "act"))
        store(dst.t[s, 4 * t:4 * t + 4, :, 0:ncol].rearrange("b p n -> p b n"),
              bass.AP(st.t, 0, [[4096, 128], [512, 4], [1, ncol]]), st, [dst.p[pt(s, t)]])

    def phase_A_even(l, s, t):
        phase_A_common(l, s, t)
        if ASTOP < 2:
            return
        sqb = PT.items + QTBL
        for j in range(5):
            pb = proj_fm((0 if j < 3 else 384 - 3 * 128) + j * 128, 128)
            CPY(CQC.t[:, j, :], pb.t[:], [pb], [CQC.p[j]], eng="act")
            ACT(sqb[j].t[:], pb.t[:], AF.Square, [pb], [sqb[j]])
        for (c0, nch, ncol, nfeat) in ((0, 3, 40, 384.0), (3, 2, 43, 256.0)):
            nb = PB[6]
            for j in range(nch):
                MM(nb.t[:], [(ones_b, sqb[c0 + j].t[:])], [sqb[c0 + j], CMB], [nb], start=(j == 0), stop=(j == nch - 1))
            rs = rstd_from(nb, nfeat)
            for j in range(nch):
                STT(CQN.t[:, c0 + j, :], CQC.t[:, c0 + j, :], CST.t[:, l, ncol + j:ncol + j + 1], rs.t[:],
                    ALU.mult, ALU.mult, [CQC.p[c0 + j], CST, rs], [CQN.p[c0 + j]])
        st = STG.next()
        for j in range(4):
            pb = proj_fm(736 + j * 128, 128)
            rope(pb, (0, 128), 0, perm64, st.t[:, j * TT:(j + 1) * TT], [st.p[j]])
        pb = proj_fm(1248, 128)
        rope(pb, (0, 128), 0, perm64, st.t[:, 4 * TT:5 * TT], [st.p[4]])
        store(QB.t[s, :, :, tsl(t)].rearrange("c p t -> p c t"),
              bass.AP(st.t, 0, [[4096, 128], [TT, 4], [1, TT]]), st, [QB.p[pt(s, t)]])
        store(KB.t[s, 0, :, tsl(t)], st.t[:, 4 * TT:5 * TT], st, [KB.p[pt(s, t)]])
        if ASTOP < 7:
            return
        hoist_sq(s, t)
        v_proj(s, t, 1376, 128, VB)
        if ASTOP < 8:
            return
        gate_proj(l, s, t, 1504)
        hoist_mm(s, t)
        if ASTOP < 3:
            return
        st = STG.next()
        for h in range(8):
            pb = proj_fm(h * 96, 128, w=WUQ, kc=3, rhs=lambda c: CQN.t[:, c, :], reads_extra=[CQN])
            CPY(st.t[0:64, h * TT:(h + 1) * TT], pb.t[0:64, :], [pb], [st.p[h]], eng="act")
            rope(pb, (64, 96), 2, perm32, st.t[64:96, h * TT:(h + 1) * TT], [st.p[h]])
            pop_mult()
        store(QA.t[s, :, 0:96, tsl(t)].rearrange("h p t -> p h t"),
              bass.AP(st.t, 0, [[4096, 96], [TT, 8], [1, TT]]), st, [QA.p[pt(s, t)]])
        if ASTOP < 4:
            return
        st = STG.next()
        for hp in range(4):
            pb = proj_fm(hp * 128, 128, w=WUK, kc=2, rhs=lambda c: CQN.t[:, 3 + c, :], reads_extra=[CQN])
            for e2 in range(2):
                h = 2 * hp + e2
                CPY(st.t[0:64, h * TT:(h + 1) * TT], pb.t[e2 * 64:e2 * 64 + 64, :], [pb], [st.p[h]],
                    eng=("dve" if e2 else "act"))
        pb = proj_fm(640, 128)
        xk = XB.next()
        rope(pb, (64, 96), 2, perm32, xk.t[64:96, :], [xk])
        store(KA.t[s, :, 0:64, tsl(t)].rearrange("h p t -> p h t"),
              bass.AP(st.t, 0, [[4096, 64], [TT, 8], [1, TT]]), st, [KA.p[pt(s, t)]])
        store(KA.t[s, :, 64:96, tsl(t)].rearrange("h p t -> p h t"),
              bass.AP(xk.t, 64 * TT, [[TT, 32], [0, 8], [1, TT]]), xk, [KA.p[pt(s, t)]])
        if ASTOP < 5:
            return
        v_proj(s, t, 0, 512, VA, lhs_src=lambda c, b: CQN.t[:, 3 + c, b * 128:(b + 1) * 128], w=WUV, kc=2)
        if ASTOP < 6:
            return

    def phase_A_odd(l, s, t):
        phase_A_common(l, s, t)
        for (col0, dst) in ((0, QB), (512, KB)):
            st = STG.next()
            for j in range(4):
                pb = proj_fm(col0 + j * 128, 128)
                rope(pb, (0, 128), 0, perm64, st.t[:, j * TT:(j + 1) * TT], [st.p[j]])
            store(dst.t[s, :, :, tsl(t)].rearrange("c p t -> p c t"),
                  bass.AP(st.t, 0, [[4096, 128], [TT, 4], [1, TT]]), st, [dst.p[pt(s, t)]])
        v_proj(s, t, 1024, 512, VB)
        hoist_sq(s, t)
        for (col0, dst) in ((1536, QA), (2048, KA)):
            st = STG.next()
            for hp in range(4):
                pb = proj_fm(col0 + hp * 128, 128)
                for e2 in range(2):
                    h = 2 * hp + e2
                    CPY(st.t[0:64, h * TT:(h + 1) * TT], pb.t[e2 * 64:e2 * 64 + 64, :], [pb], [st.p[h]],
                        eng=("dve" if e2 else "act"))
            store(dst.t[s, :, 0:64, tsl(t)].rearrange("h p t -> p h t"),
                  bass.AP(st.t, 0, [[4096, 64], [TT, 8], [1, TT]]), st, [dst.p[pt(s, t)]])
        v_proj(s, t, 2560, 512, VA)
        pb = projb.next()
        specs = []
        for b in range(4):
            for c in range(8):
                specs.append((pb.t[:, b * 8:(b + 1) * 8], HT.t[:, c, b * 128:(b + 1) * 128], WIN.t[:, c, 3072:3080],
                              c == 0, c == 7))
        MMS(specs, [WIN, HT], [pb])
        f = FT.next()
        for b in range(4):
            TTO(f.t[:, b * 8:(b + 1) * 8], pb.t[:, b * 8:(b + 1) * 8], CST.t[:, l, 54:62], ALU.add, [pb, CST], [f])
        f2 = FT.next()
        ACT(f2.t[:, 0:32], f.t[:, 0:32], AF.Exp, [f], [f2], scale=-1.0)
        ACT(SPT.t[:, 4 * t:4 * t + 4, :].rearrange("p b h -> p (b h)"), f2.t[:, 0:32], AF.Ln, [f2], [SPT], bias=1.0)
        hoist_mm(s, t)
        gate_proj(l, s, t, 3088)

    def fox_post(l, s):
        P.op("dve", lambda e: e.memset(PREF.t[:, 0, :], 0.0), [], [PREF])
        for b in range(1, NB):
            TTO(PREF.t[:, b, :], PREF.t[:, b - 1, :], SPT.t[:, b - 1, :], ALU.add, [PREF, SPT], [PREF])
        pb = projb.next()
        specs = []
        for b in range(NB):
            specs.append((pb.t[:, b * 8:(b + 1) * 8], tri_f, SPT.t[:, b, :], True, False))
            specs.append((pb.t[:, b * 8:(b + 1) * 8], ones_f, PREF.t[:, b, :], False, True))
        MMS(specs, [SPT, PREF, CMF], [pb])
        CPY(CSPB.t[:].rearrange("p b h -> p (b h)"), pb.t[:, 0:NB * 8], [pb], [CSPB])
        P.dma("pool", CSP.t[s], CSPB.t[:], CSPB, [CSPB], [CSP.p[s]])
        for g in range(4):
            pb2 = projb.next()

            def fn(e, g=g, pb2=pb2):
                ins = None
                for bb in range(4):
                    ins = e.transpose(out=pb2.t[0:8, bb * 128:(bb + 1) * 128], in_=CSPB.t[:, g * 4 + bb, :],
                                      identity=ident_f)
                return ins
            P.op("pe", fn, [CSPB, CMF], [pb2])
            ACT(FQ8.t[0:8, g * 512:(g + 1) * 512], pb2.t[0:8, :], AF.Identity, [pb2], [FQ8], scale=-8.0)
        P.dma("pool", QA.t[s, :, 64, :], FQ8.t[0:8, :], FQ8, [FQ8], allp(QA, s), partial=True)

    sbk = Rot(PB[0:3])
    sbk4 = Rot(PB[0:3] + [PB[7]])
    obk4 = Rot(PB[3:7])
    obk2 = Rot(PB[3:5])

    SKEW = 2
    pend = []

    def pipe_flush(keep=0):
        while len(pend) > keep:
            fn = pend.pop(0)
            fn()

    def attn_core(ktf, ktb, krows, kbase, qt, vaf, vab, scale, pO, i, bias=None, swa=False, lbank=None, fin=None, deep=True):
        if swa:
            js = [j for j in range(4 * i - 1, 4 * i + 4) if j >= 0]
        else:
            js = list(range(0, 4 * i + 4))
        for idx, j in enumerate(js):
            m = j - 4 * i
            c0 = 0 if m < 0 else 128 * m
            if swa:
                n = 128 if (m < 0 or m == 3) else 256
            else:
                n = TT - c0
            ps = (sbk4 if deep else sbk).next()
            need_mask = swa or m >= 0
            specs = [(ps.t[:, c0:c0 + n], ktf(j), qt.t[kbase:kbase + krows, c0:c0 + n], True, not need_mask)]
            if swa:
                mk = msw_b[:, 128:256] if m < 0 else msw_b[:, 0:n]
                specs.append((ps.t[:, c0:c0 + n], ident_b, mk, False, True))
            elif m >= 0:
                specs.append((ps.t[:, c0:c0 + 128], ident_b, mc_b, False, True))
            MMS(specs, [qt, CMB] + ktb, [ps])
            p = PT.next()
            kw = dict(scale=scale)
            rd = [ps]
            if bias is not None:
                kw["bias"] = bias(j)
                rd.append(CSPB)
            ACT(p.t[:, c0:c0 + n], ps.t[:, c0:c0 + n], AF.Exp, rd, [p], **kw)
            first = (idx == 0)
            last = (idx == len(js) - 1)

            def pv(p=p, c0=c0, n=n, j=j, first=first, last=last):
                specs = [(pO.t[:, c0:c0 + n], vaf(j), p.t[:, c0:c0 + n], first, last)]
                wr = [pO]
                if lbank is not None:
                    specs.append((lbank.t[:, c0:c0 + n], ones_b, p.t[:, c0:c0 + n], first, last))
                    wr.append(lbank)
                MMS(specs, [p, CMB] + vab, wr)
                if last and fin is not None:
                    fin()
            pend.append(pv)
            pipe_flush(keep=(3 if deep else SKEW))

    def load_va64(src, s, col0, v):
        ap = bass.AP(YB, 4096 + v * 2048 + v * 64, [[YROW, 128], [128, NB], [1, 64]])
        P.dma("sp", ap, src.t[s, :, :, col0:col0 + 64].rearrange("b p n -> p b n"), Y.p[4 + 2 * v],
              allp(src, s), va_bufs(v), partial=True)

    def set_ones():
        for v in range(2):
            ap = bass.AP(YB, 4096 + v * 2048 + (1 - v) * 64, [[YROW, 128], [128, NB], [1, 64]])
            P.op("dve", lambda e, ap=ap: e.memset(ap, 1.0), [], va_bufs(v))

    def zero_q_half(q, half):
        P.op("dve", lambda e, q=q, half=half: e.memset(q.t[half * 64:half * 64 + 64, :], 0.0), [], [q])

    def load_q_half(src, s, ch, half, t, q):
        r0 = half * 64
        P.dma("sp", q.t[r0:r0 + 64, :], src.t[s, ch, r0:r0 + 64, tsl(t)], q, [src.p[pt(s, t)]], [q], partial=True)
        return q

    def finalize64(pO, par, lbias, r_extra, s, t, chunk):
        n0, l0 = par * 64, (1 - par) * 64
        f2 = FT.next()
        if lbias is not None:
            f1 = FT.next()
            ACT(f1.t[n0:n0 + 64, :], pO.t[l0:l0 + 64, :], AF.Ln, [pO] + list(r_extra), [f1], bias=lbias)
            ACT(f2.t[n0:n0 + 64, :], f1.t[n0:n0 + 64, :], AF.Exp, [f1], [f2], scale=-1.0)
        else:
            P.op("dve", lambda e: e.reciprocal(out=f2.t[n0:n0 + 64, :], in_=pO.t[l0:l0 + 64, :]), [pO], [f2])
        xb = XB.next()
        TTO(xb.t[n0:n0 + 64, :], pO.t[n0:n0 + 64, :], f2.t[n0:n0 + 64, :], ALU.mult, [pO, f2], [xb])
        store(OT.t[s, chunk, n0:n0 + 64, tsl(t)], xb.t[n0:n0 + 64, :], xb, [OT.p[pt(s, t)]])

    def phase_B_even(l, s):
        sc_m = 96.0 ** -0.5
        pipe_flush()
        set_ones()
        for h in range(8):
            par = h % 2
            k = par
            P.dma("sp", kt_ap(k, rows=96), KA.t[s, h, 0:96, :], Y.p[2 * k], allp(KA, s), kt_bufs(k))
            load_va64(VA, s, h * 64, par)
            for t in range(NTL):
                q = QTB.next()
                P.dma("sp", q.t[0:96, :], QA.t[s, h, 0:96, tsl(t)], q, [QA.p[pt(s, t)]], [q])
                pO = obk4.next()
                attn_core(lambda j, k=k: kt_ap(k, rows=96, c0=j * 128, n=128), kt_bufs(k), 96, 0, q,
                          lambda j, v=par: va_blk(v, j), va_bufs(par), sc_m, pO, t,
                          fin=lambda pO=pO, par=par, t=t, h=h: finalize64(pO, par, None, [], s, t, h // 2))
        for grp in range(2):
            k = grp
            P.dma("sp", kt_ap(k), KB.t[s, 0, :, :], Y.p[2 * k], allp(KB, s), kt_bufs(k))
            load_va64(VB, s, grp * 64, grp)
            pipe_flush()
            for q in QTBL:
                zero_q_half(q, 1 - grp)
            for j4 in range(4):
                horig = grp * 4 + j4
                for t in range(NTL):
                    q = load_q_half(QB, s, j4, grp, t, QTB.next())
                    pO = obk4.next()
                    attn_core(lambda j, k=k: kt_ap(k, c0=j * 128, n=128), kt_bufs(k),
                              128, 0, q, lambda j, v=grp: va_blk(v, j), va_bufs(grp), 0.125, pO, t, swa=True,
                              fin=lambda pO=pO, grp=grp, horig=horig, t=t, j4=j4: finalize64(
                                  pO, grp, SM.t[grp * 64:grp * 64 + 64, horig:horig + 1], [SM], s, t, 4 + j4))

    def phase_B_odd(l, s):
        i = l // 2
        lam_init = 0.8 - 0.6 * math.exp(-0.3 * l)
        P.dma("sp", CSPB.t[:], CSP.t[s], CSPB, [CSP.p[s]], [CSPB])
        pipe_flush()
        for qi, q in enumerate(QTBL):
            zero_q_half(q, 1 - (qi % 2))
        dcnt = 0
        for h in range(4):
            k = h % 2
            v = h % 2
            P.dma("sp", kt_ap(k), KB.t[s, h, :, :], Y.p[2 * k], allp(KB, s), kt_bufs(k))
            vap = bass.AP(YB, 4096 + v * 2048, [[YROW, 128], [128, NB], [1, 128]])
            P.dma("sp", vap, VB.t[s, :, :, h * 128:(h + 1) * 128].rearrange("b p n -> p b n"), Y.p[4 + 2 * v],
                  allp(VB, s), va_bufs(v))
            for t in range(NTL):
                dcnt += 1
                for mp in range(2):
                    q = load_q_half(QB, s, h, mp, t, QTBL[2 * (dcnt % 2) + mp])
                    pO = obk2.next()
                    pL = PB[5 + mp]

                    def fin(pO=pO, pL=pL, mp=mp, t=t, h=h):
                        f1 = FT.next()
                        ACT(f1.t[:], pL.t[:], AF.Ln, [pL], [f1])
                        f2 = FT.next()
                        ACT(f2.t[:], f1.t[:], AF.Exp, [f1], [f2], scale=-1.0)
                        TTO(CQC.t[:, mp, :], pO.t[:], f2.t[:], ALU.mult, [pO, f2], [CQC])
                        if mp == 0:
                            return
                        od = CQC.t[:, 2, :]
                        STT(od, CQC.t[:, 1, :], SM.t[:, 16 + i:17 + i], CQC.t[:, 0, :], ALU.mult, ALU.add, [CQC, SM], [CQC])
                        nb = PB[7]
                        sumsq_bank([od], [CQC], nb)
                        rs = rstd_from(nb, 128.0)
                        xb = XB.next()
                        STT(xb.t[:], od, CST.t[:, l, 53:54], rs.t[:], ALU.mult, ALU.mult, [CQC, rs, CST], [xb])
                        xb2 = XB.next()
                        TS(xb2.t[:], xb.t[:], 1.0 - lam_init, None, ALU.mult, ALU.bypass, [xb], [xb2])
                        store(OT.t[s, h, :, tsl(t)], xb2.t[:], xb2, [OT.p[pt(s, t)]])
                    attn_core(lambda j, k=k: kt_ap(k, c0=j * 128, n=128), kt_bufs(k),
                              128, 0, q, lambda j, v=v: va_blk(v, j), va_bufs(v), 0.125, pO, t, lbank=pL, fin=fin, deep=False)
        pipe_flush()
        for k in range(2):
            ap = bass.AP(YB, 64 * YROW + k * 2048, [[YROW, 1], [1, 2048]])
            P.op("dve", lambda e, ap=ap: e.memset(ap, 1.0), [], kt_bufs(k))
        set_ones()
        for h in range(8):
            par = h % 2
            k = par
            P.dma("sp", kt_ap(k, rows=64), KA.t[s, h, 0:64, :], Y.p[2 * k], allp(KA, s), kt_bufs(k), partial=True)
            load_va64(VA, s, h * 64, par)
            for t in range(NTL):
                q = QTB.next()
                P.dma("sp", q.t[0:65, :], QA.t[s, h, 0:65, tsl(t)], q, [QA.p[pt(s, t)]], [q])
                pO = obk4.next()
                attn_core(lambda j, k=k: kt_ap(k, rows=65, c0=j * 128, n=128), kt_bufs(k), 65, 0, q,
                          lambda j, v=par: va_blk(v, j), va_bufs(par), 0.125, pO, t,
                          bias=lambda j, h=h: CSPB.t[:, j, h:h + 1],
                          fin=lambda pO=pO, par=par, t=t, h=h: finalize64(pO, par, None, [], s, t, 4 + h // 2))

    c_st = {}

    def phase_C1_load(l, s, t):
        st_o = STG.next()
        P.dma("sp", bass.AP(st_o.t, 0, [[4096, 128], [TT, 8], [1, TT]]),
              OT.t[s, :, :, tsl(t)].rearrange("c p t -> p c t"), st_o, [OT.p[pt(s, t)]], [st_o])
        st_g = STG.next()
        P.dma("sp", bass.AP(st_g.t, 0, [[4096, 128], [TT, 8], [1, TT]]),
              GT.t[s, :, :, tsl(t)].rearrange("c p t -> p c t"), st_g, [GT.p[pt(s, t)]], [st_g])
        c_st[(s, t)] = (st_o, st_g)

    def phase_C1_og(l, s, t):
        st_o, st_g = c_st.pop((s, t))
        for c in range(8):
            TTO(HT.t[:, c, :], st_o.t[:, c * TT:(c + 1) * TT], st_g.t[:, c * TT:(c + 1) * TT], ALU.mult,
                [st_o.p[c], st_g.p[c]], [HT.p[c]])

    def phase_C2(l, s, t):
        load_x_tile(s, t)
        nb = PB[6]
        sqb = PT.items + QTBL
        for g in range(8):
            pb = proj_fm(g * 128, 128, w=WOUT)
            CPY(Y.t[:, g, :], pb.t[:], [pb], [Y.p[g]], eng="act")
            ACT(sqb[g].t[:], pb.t[:], AF.Square, [pb], [sqb[g]])
            if g >= 2:
                MM(nb.t[:], [(ones_b, sqb[g - 2].t[:])], [sqb[g - 2], CMB], [nb], start=(g == 2), stop=False)
        for g in (6, 7):
            MM(nb.t[:], [(ones_b, sqb[g].t[:])], [sqb[g], CMB], [nb], start=False, stop=(g == 7))
        return rstd_from(nb, 1024.0)

    def phase_C3(l, s, t, rs):
        for c in range(8):
            f = FT.next()
            TTO(f.t[:], Y.t[:, c, :], rs.t[:], ALU.mult, [Y.p[c], rs], [f])
            STT(XT.t[:, c, :], f.t[:], AD.t[:, l, s, 2, c:c + 1], XT.t[:, c, :], ALU.mult, ALU.add, [f, AD, XT.p[c]], [XT.p[c]])
        if l == NL - 1:
            out_tile(s, t)
        else:
            P.dma("pool", xT.t[s, :, :, tsl(t)].rearrange("c p t -> p c t"), XT.t[:], XT, [XT], [xT.p[pt(s, t)]])

    def out_tile(s, t):
        for b in range(4):
            for half in range(2):
                pb = projb.next()

                def fn(e, b=b, half=half, pb=pb):
                    ins = None
                    for cc in range(4):
                        c = half * 4 + cc
                        ins = e.transpose(out=pb.t[:, cc * 128:(cc + 1) * 128],
                                          in_=XT.t[:, c, b * 128:(b + 1) * 128], identity=ident_f)
                    return ins
                P.op("pe", fn, [XT, CMF], [pb])
                ft = FT.next()
                CPY(ft.t[:], pb.t[:], [pb], [ft], eng=("dve" if half else "act"))
                tok0 = t * TT + b * 128
                P.dma("pool", out_d[s, tok0:tok0 + 128, half * 512:(half + 1) * 512], ft.t[:], ft, [ft], [])

    def phase_C_all(l):
        tiles = [(s, t) for s in range(2) for t in range(NTL)]
        phase_C1_load(l, *tiles[0])
        phase_C1_og(l, *tiles[0])
        for idx, (s, t) in enumerate(tiles):
            if idx + 1 < len(tiles):
                phase_C1_load(l, *tiles[idx + 1])
            rs = phase_C2(l, s, t)
            if idx + 1 < len(tiles):
                phase_C1_og(l, *tiles[idx + 1])
            phase_C3(l, s, t, rs)

    for l in range(NL):
        i = l // 2
        even = (l % 2 == 0)
        if even:
            ACT(SM.t[:, 0:8], CST.t[:, l, 45:53], AF.Exp, [CST], [SM])
        else:
            lam_init = 0.8 - 0.6 * math.exp(-0.3 * l)
            f = FT.next()
            TTO(f.t[:, 0:64], CST.t[:, l, 62:126], CST.t[:, l, 126:190], ALU.mult, [CST], [f])
            TTO(f.t[:, 64:128], CST.t[:, l, 190:254], CST.t[:, l, 254:318], ALU.mult, [CST], [f])
            P.op("dve", lambda e, f=f: e.reduce_sum(out=SM.t[:, 20:22],
                                                 in_=f.t[:, 0:128].rearrange("p (a b) -> p a b", a=2),
                                                 axis=mybir.AxisListType.X), [f], [SM])
            ACT(SM.t[:, 22:24], SM.t[:, 20:22], AF.Exp, [SM], [SM])
            TTO(SM.t[:, 24:25], SM.t[:, 23:24], SM.t[:, 22:23], ALU.subtract, [SM], [SM])
            TS(SM.t[:, 16 + i:17 + i], SM.t[:, 24:25], -lam_init, None, ALU.add, ALU.bypass, [SM], [SM])
        if STOP < 2:
            continue
        for s in range(2):
            for t in range(NTL):
                if ONE_TILE and (s, t) != (0, 0):
                    continue
                ada_bg_start(l + 1, s * NTL + t)
                (phase_A_even if even else phase_A_odd)(l, s, t)
                ada_bg_end(l + 1, s * NTL + t)
            if not even and not ONE_TILE:
                fox_post(l, s)
        if l + 1 < NL:
            load_weights(l + 1)
        if STOP < 3:
            continue
        for s in range(2):
            (phase_B_even if even else phase_B_odd)(l, s)
        pipe_flush()
        if STOP < 4:
            continue
        phase_C_all(l)
        if l + 1 < NL:
            load_wout(l + 1)

    if NL == 0 or STOP < 4:
        for s in range(0 if not SKIP_XPRO else 2, 2):
            for t in range(NTL):
                load_x_tile(s, t)
                out_tile(s, t)
    P.emit()
    return nc


def _kc(w):
    K, N = w.shape
    return np.ascontiguousarray(w.reshape(K // 128, 128, N).transpose(1, 0, 2))


def _o_perm_even():
    perm = np.zeros(1024, np.int64)
    for f in range(1024):
        cf, r = divmod(f, 128)
        if cf < 4:
            perm[f] = f
        else:
            j = cf - 4
            head = j if r < 64 else 4 + j
            perm[f] = 512 + head * 64 + (r % 64)
    return perm


def _host_consts():
    pos = np.arange(S, dtype=np.float32)
    tab = np.zeros((128, 4, S), np.float32)
    inv64 = (1.0 / (np.float32(10000.0) ** (np.arange(0, 64, 2, dtype=np.float32) / np.float32(64)))).astype(np.float32)
    inv32 = (1.0 / (np.float32(10000.0) ** (np.arange(0, 32, 2, dtype=np.float32) / np.float32(32)))).astype(np.float32)
    for r in range(128):
        i = r % 64
        ang = (pos * inv64[i % 32]).astype(np.float32)
        tab[r, 0] = np.cos(ang)
        tab[r, 1] = -np.sin(ang) if i < 32 else np.sin(ang)
    for r in range(64, 96):
        j = r - 64
        ang = (pos * inv32[j % 16]).astype(np.float32)
        tab[r, 2] = np.cos(ang)
        tab[r, 3] = -np.sin(ang) if j < 16 else np.sin(ang)
    cmf = np.zeros((128, 384), np.float32)
    cmf[:, 0:128] = np.eye(128, dtype=np.float32)
    k = np.arange(128)
    cmf[:, 128:256] = (k[:, None] <= k[None, :]).astype(np.float32)
    cmf[:, 256:384] = 1.0
    cmb = np.zeros((128, 896), np.float32)
    cmb[:, 0:128] = np.eye(128, dtype=np.float32)
    for m in range(128):
        sw = m + 32 if (m % 64) < 32 else m - 32
        cmb[sw, 128 + m] = 1.0
    for m in range(128):
        if 64 <= m < 96:
            sw = m + 16 if (m - 64) < 16 else m - 16
        else:
            sw = m
        cmb[sw, 256 + m] = 1.0
    cmb[:, 384:512] = np.where(k[:, None] > k[None, :], NEG, 0.0)
    cmb[:, 512:640] = np.where(k[:, None] > k[None, :], NEG, 0.0)
    cmb[:, 640:768] = np.where(k[:, None] <= k[None, :], NEG, 0.0)
    cmb[:, 768:896] = 1.0
    return tab, cmf, cmb


def _prep_shared(inp):
    f = lambda a: np.asarray(a, dtype=np.float32)
    sh = {}
    w_ada = f(inp["w_ada"])
    sh["wada"] = np.ascontiguousarray(w_ada.reshape(4, 8, 128, 3072).transpose(0, 2, 1, 3))
    perm_e = _o_perm_even()
    wE, wUQ, wUK, wUV, wOe = [], [], [], [], []
    for i in range(2):
        w = f(inp["ev_w_in"][i])
        cols = [w[:, 0:384], w[:, 384:640], np.zeros((1024, 64), np.float32), w[:, 640:672]]
        sq = w[:, 672:1184].reshape(1024, 8, 64)
        order = []
        for j in range(4):
            order += [j, 4 + j]
        cols.append(sq[:, order, :].reshape(1024, 512))
        cols.append(w[:, 1184:1312])
        cols.append(w[:, 1312:1440])
        cols.append(w[:, 1440:2464][:, perm_e])
        wE.append(_kc(np.concatenate(cols, axis=1)))
        wUQ.append(_kc(np.concatenate([f(inp["ev_w_uq"][i]), np.zeros((384, 32), np.float32)], axis=1)))
        ukv = f(inp["ev_w_ukv"][i]).reshape(256, 8, 128)
        wUK.append(_kc(np.ascontiguousarray(ukv[:, :, 0:64]).reshape(256, 512)))
        wUV.append(_kc(np.ascontiguousarray(ukv[:, :, 64:128]).reshape(256, 512)))
        wOe.append(_kc(f(inp["ev_w_out"][i])[perm_e, :]))
    sh["wE"] = np.stack(wE)
    sh["wUQ"] = np.stack(wUQ)
    sh["wUK"] = np.stack(wUK)
    sh["wUV"] = np.stack(wUV)
    sh["wOe"] = np.stack(wOe)
    wO, wOo = [], []
    for i in range(2):
        w = f(inp["od_w_in"][i])
        cols = [w[:, 0:3080], np.zeros((1024, 8), np.float32), w[:, 3080:4104]]
        wO.append(_kc(np.concatenate(cols, axis=1)))
        wOo.append(_kc(f(inp["od_w_out"][i])))
    sh["wO"] = np.stack(wO)
    sh["wOo"] = np.stack(wOo)
    cst = np.zeros((128, 4, NC_CST), np.float32)
    for l in range(4):
        i = l // 2
        cst[:, l, 0:8] = f(inp["g_pre"][l]).reshape(8, 128).T
        cst[:, l, 8:16] = f(inp["g_post"][l]).reshape(8, 128).T
        cst[:, l, 16:40] = f(inp["b_ada"][l]).reshape(24, 128).T
        if l % 2 == 0:
            cst[:, l, 40:43] = f(inp["ev_q_norm"][i]).reshape(3, 128).T
            cst[:, l, 43:45] = f(inp["ev_kv_norm"][i]).reshape(2, 128).T
            cst[:, l, 45:53] = f(inp["ev_sinks"][i])[None, :]
        else:
            cst[:, l, 53] = f(inp["od_subln"][i])
            cst[:, l, 54:62] = f(inp["od_forget_bias"][i])[None, :]
            cst[:, l, 62:318] = f(inp["od_lambda"][i]).reshape(256)[None, :]
    sh["cst"] = cst
    tab, cmf, cmb = _host_consts()
    sh["tab"] = tab
    sh["cmf"] = cmf
    sh["cmb"] = cmb
    return sh


_NC_CACHE = {}


def run(inputs, NL=4, cores=8):
    x = np.asarray(inputs["x"], dtype=np.float32)
    c = np.asarray(inputs["c"], dtype=np.float32)
    sh = _prep_shared(inputs)
    in_maps = []
    for core in range(cores):
        m = dict(sh)
        m["x"] = np.ascontiguousarray(x[2 * core:2 * core + 2])
        cc = c[2 * core:2 * core + 2]
        m["cT"] = np.ascontiguousarray(cc.T.reshape(8, 128, 2).transpose(1, 0, 2))
        in_maps.append(m)
    if NL not in _NC_CACHE:
        _NC_CACHE[NL] = build(NL)
    res = run_bass_kernel_spmd(_NC_CACHE[NL], in_maps, core_ids=list(range(cores)))
    if DEBUG:
        return res.results
    return np.concatenate([r["out"] for r in res.results], axis=0).astype(np.float32)


def kernel(**inputs):
    return run(inputs, NL=4, cores=8)
```

```python
from contextlib import ExitStack
import math
import numpy as np
import concourse.bass as bass
import concourse.mybir as mybir
from concourse.bass_utils import run_bass_kernel_spmd

F32 = mybir.dt.float32
BF16 = mybir.dt.bfloat16
AF = mybir.ActivationFunctionType
ALU = mybir.AluOpType

S = 2048
TT = 512
NTL = S // TT
NB = S // 128
EPS = 1e-6
NC_CST = 320
NE = 2528
NO = 4112
NEG = -30000.0
STOP = 9
ONE_TILE = False
DEBUG = False
ASTOP = 99
SKIP_XPRO = False


class Buf:
    __slots__ = ("name", "writers", "readers", "excl")

    def __init__(self, name, excl=False):
        self.name = name
        self.excl = excl
        self.writers = {}
        self.readers = {}


class Tens:
    def __init__(self, t, name, parts, excl=False):
        self.t = t
        self.name = name
        self.p = [Buf(f"{name}.{i}", excl) for i in range(parts)]


ENGS = ("pe", "act", "dve", "pool", "sp")


class Prog:
    def __init__(self, nc, same_engine_sync=True):
        self.nc = nc
        self.es = ExitStack()
        self.ops = {e: [] for e in ENGS}
        self.cnt = {e: 0 for e in ENGS}
        self.known = {e: {} for e in ENGS}
        self.snap = {}
        self.chan_cnt = {}
        self.sem = {}
        self.same_engine_sync = same_engine_sync

    def sbuf(self, name, shape, dtype, parts=1):
        t = self.es.enter_context(self.nc.sbuf_tensor("sb_" + name, list(shape), dtype))
        return Tens(t, name, parts)

    def psum(self, name, shape, dtype, parts=1):
        t = self.es.enter_context(self.nc.psum_tensor("ps_" + name, list(shape), dtype))
        return Tens(t, name, parts, excl=True)

    def dram(self, name, shape, dtype, kind="Internal", parts=1):
        t = self.nc.dram_tensor(name, list(shape), dtype, kind=kind).ap()
        return Tens(t, name, parts)

    @staticmethod
    def _flat(xs):
        out = []
        for x in xs:
            if isinstance(x, Tens):
                out.extend(x.p)
            elif isinstance(x, (list, tuple)):
                out.extend(Prog._flat(x))
            else:
                out.append(x)
        return out

    def _deps(self, eng, reads, writes, skip=None, partial=False):
        deps = {}

        def add(d):
            for k, v in d.items():
                if deps.get(k, 0) < v:
                    deps[k] = v

        for b in reads:
            add(b.writers)
        for b in writes:
            if not partial:
                add(b.writers)
            add(b.readers)
        waits = []
        kn = self.known[eng]
        for k, v in deps.items():
            if k == skip:
                continue
            if k == eng and (eng == "pe" or not self.same_engine_sync):
                continue
            if kn.get(k, 0) >= v:
                continue
            waits.append((k, v))
        for k, v in waits:
            kn[k] = max(kn.get(k, 0), v)
            sn = self.snap.get((k, v))
            if sn:
                for k2, v2 in sn.items():
                    if kn.get(k2, 0) < v2:
                        kn[k2] = v2
        return waits

    def _mark(self, tok, reads, writes, partial):
        k, v = tok
        for b in reads:
            if b.readers.get(k, 0) < v:
                b.readers[k] = v
        for b in writes:
            if partial:
                b.writers[k] = v
            else:
                b.writers = {k: v}
                b.readers = {}

    def op(self, eng, fn, reads=(), writes=()):
        reads = self._flat(reads)
        writes = self._flat(writes)
        writes = writes + [b for b in reads if b.excl and b not in writes]
        reads = [b for b in reads if not b.excl]
        waits = self._deps(eng, reads, writes)
        self.cnt[eng] += 1
        tok = (eng, self.cnt[eng])
        sn = dict(self.known[eng])
        sn[eng] = self.cnt[eng]
        self.snap[tok] = sn
        self._mark(tok, reads, writes, False)
        self.ops[eng].append(("op", waits, fn, tok))
        return tok

    def dma(self, eng, out, in_, sb, reads=(), writes=(), partial=False, nowaw=False, **kw):
        reads = self._flat(reads)
        writes = self._flat(writes)
        sbb = self._flat([sb])[0]
        chan = "d_" + sbb.name
        waits = self._deps(eng, reads, writes, skip=(chan if partial else None), partial=(partial and nowaw))
        self.chan_cnt[chan] = self.chan_cnt.get(chan, 0) + 16
        tok = (chan, self.chan_cnt[chan])
        self.snap[tok] = dict(self.known[eng])
        self._mark(tok, reads, writes, partial)
        self.ops[eng].append(("dma", waits, (out, in_, kw), tok))
        return tok

    def emit(self):
        nc = self.nc
        for k in list(ENGS) + list(self.chan_cnt):
            if k not in self.sem:
                self.sem[k] = self.es.enter_context(nc.semaphore("s_" + k.replace(".", "_")))
        final = list(self.chan_cnt.items())

        def run(ename, eng):
            for kind, waits, payload, tok in self.ops[ename]:
                for k, v in waits:
                    eng.wait_ge(self.sem[k], v)
                if kind == "op":
                    payload(eng).then_inc(self.sem[tok[0]], 1)
                else:
                    out, in_, kw = payload
                    eng.dma_start(out=out, in_=in_, **kw).then_inc(self.sem[tok[0]], 16)
            if ename == "sp":
                for c, v in final:
                    eng.wait_ge(self.sem[c], v)

        with nc.Block() as block:
            @block.sync
            def _(e):
                run("sp", e)

            @block.scalar
            def _(e):
                run("act", e)

            @block.vector
            def _(e):
                run("dve", e)

            @block.gpsimd
            def _(e):
                run("pool", e)

            @block.tensor
            def _(e):
                run("pe", e)
        self.es.close()


class Rot:
    def __init__(self, items):
        self.items = items
        self.i = 0

    def next(self):
        x = self.items[self.i % len(self.items)]
        self.i += 1
        return x


def build(NL=4):
    nc = bass.Bass("TRN2", target_bir_lowering=False)
    P = Prog(nc)

    def din(name, shape, dt=F32):
        return nc.dram_tensor(name, list(shape), dt, kind="ExternalInput").ap()

    x_d = din("x", [2, S, 1024])
    cT_d = din("cT", [128, 8, 2])
    wada_d = din("wada", [4, 128, 8, 3072])
    cst_d = din("cst", [128, 4, NC_CST])
    cmf_d = din("cmf", [128, 384])
    cmb_d = din("cmb", [128, 896])
    tab_d = din("tab", [128, 4, S])
    wE_d = din("wE", [2, 128, 8, NE])
    wUQ_d = din("wUQ", [2, 128, 3, 800])
    wUK_d = din("wUK", [2, 128, 2, 512])
    wUV_d = din("wUV", [2, 128, 2, 512])
    wOe_d = din("wOe", [2, 128, 8, 1024])
    wO_d = din("wO", [2, 128, 8, NO])
    wOo_d = din("wOo", [2, 128, 8, 1024])
    out_d = nc.dram_tensor("out", [2, S, 1024], F32, kind="ExternalOutput").ap()

    def scr(name, shape, dt, parts=2 * NTL):
        return P.dram(name, shape, dt, kind=("ExternalOutput" if DEBUG else "Internal"), parts=parts)

    xT = scr("xT", [2, 8, 128, S], F32)
    QA = scr("QA", [2, 8, 128, S], BF16)
    KA = scr("KA", [2, 8, 128, S], BF16)
    QB = scr("QB", [2, 4, 128, S], BF16)
    KB = scr("KB", [2, 4, 128, S], BF16)
    VA = scr("VA", [2, NB, 128, 512], BF16)
    VB = scr("VB", [2, NB, 128, 512], BF16)
    GT = scr("GT", [2, 8, 128, S], BF16)
    OT = scr("OT", [2, 8, 128, S], BF16)
    CSP = scr("CSP", [2, 128, NB, 8], F32, parts=2)

    def pt(s, t):
        return s * NTL + t

    def allp(T, s):
        return [T.p[pt(s, t)] for t in range(NTL)]

    WIN = P.sbuf("WIN", [128, 8, NO], BF16)
    WOUT = P.sbuf("WOUT", [128, 8, 1024], BF16)
    WUQ = P.sbuf("WUQ", [128, 3, 800], BF16)
    WUK = P.sbuf("WUK", [128, 2, 512], BF16)
    WUV = P.sbuf("WUV", [128, 2, 512], BF16)
    CST = P.sbuf("CST", [128, 4, NC_CST], F32)
    AD = P.sbuf("AD", [128, 4, 2, 3, 8], F32)
    CMF = P.sbuf("CMF", [128, 384], F32)
    CMB = P.sbuf("CMB", [128, 896], BF16)
    TAB = P.sbuf("TAB", [128, 4, TT], F32)
    XT = P.sbuf("XT", [128, 8, TT], F32, parts=8)
    HT = P.sbuf("HT", [128, 8, TT], BF16, parts=8)
    SQ = Rot([P.sbuf(f"SQ{i}", [128, TT], BF16) for i in range(2)])
    FT = Rot([P.sbuf(f"FT{i}", [128, TT], F32) for i in range(4)])
    RSB = Rot([P.sbuf(f"RSB{i}", [128, TT], F32) for i in range(2)])
    STG = Rot([P.sbuf(f"STG{i}", [128, 4096], BF16, parts=8) for i in range(2)])
    CQC = P.sbuf("CQC", [128, 5, TT], F32, parts=5)
    CQN = P.sbuf("CQN", [128, 5, TT], BF16, parts=5)
    XB = Rot([P.sbuf(f"XB{i}", [128, TT], BF16) for i in range(2)])
    Y = P.sbuf("Y", [128, 8, TT], F32, parts=8)
    YB = Y.t.bitcast(BF16)
    QTBL = [P.sbuf(f"QTB{i}", [128, TT], BF16) for i in range(4)]
    QTB = Rot(QTBL)
    PT = Rot([P.sbuf(f"PT{i}", [128, TT], BF16) for i in range(4)])
    CSPB = P.sbuf("CSPB", [128, NB, 8], F32)
    SPT = P.sbuf("SPT", [128, NB, 8], F32)
    PREF = P.sbuf("PREF", [128, NB, 8], F32)
    FQ8 = P.sbuf("FQ8", [128, S], BF16)
    SM = P.sbuf("SM", [128, 64], F32)
    CTB = P.sbuf("CTB", [128, 8, 2], BF16)
    CTF = P.sbuf("CTF", [128, 8, 2], F32)
    MODF = P.sbuf("MODF", [128, 4, 24, 2], F32)

    PB = [P.psum(f"B{i}", [128, 512], F32) for i in range(8)]

    ident_f = CMF.t[:, 0:128]
    tri_f = CMF.t[:, 128:256]
    ones_f = CMF.t[:, 256:384]
    ident_b = CMB.t[:, 0:128]
    perm64 = CMB.t[:, 128:256]
    perm32 = CMB.t[:, 256:384]
    mc_b = CMB.t[:, 384:512]
    msw_b = CMB.t[:, 512:768]
    ones_b = CMB.t[:, 768:896]

    YROW = 8192

    def kt_ap(k, rows=128, p0=0, c0=0, n=2048):
        return bass.AP(YB, p0 * YROW + k * 2048 + c0, [[YROW, rows], [1, n]])

    def kt_bufs(k):
        return [Y.p[2 * k], Y.p[2 * k + 1]]

    def va_blk(v, j):
        return bass.AP(YB, 4096 + v * 2048 + j * 128, [[YROW, 128], [1, 128]])

    def va_bufs(v):
        return [Y.p[4 + 2 * v], Y.p[5 + 2 * v]]

    def ACT(out, in_, func, r, w, **kw):
        P.op("act", lambda e: e.activation(out=out, in_=in_, func=func, **kw), r, w)

    def TTO(out, a, b, op, r, w, eng="dve"):
        P.op(eng, lambda e: e.tensor_tensor(out=out, in0=a, in1=b, op=op), r, w)

    def TS(out, a, s1, s2, op0, op1, r, w, eng="dve"):
        P.op(eng, lambda e: e.tensor_scalar(out=out, in0=a, scalar1=s1, scalar2=s2, op0=op0, op1=op1), r, w)

    def STT(out, a, s, b, op0, op1, r, w):
        P.op("dve", lambda e: e.scalar_tensor_tensor(out=out, in0=a, scalar=s, in1=b, op0=op0, op1=op1), r, w)

    def CPY(out, in_, r, w, eng="dve"):
        if eng == "act":
            P.op("act", lambda e: e.activation(out=out, in_=in_, func=AF.Identity), r, w)
        else:
            P.op(eng, lambda e: e.tensor_copy(out=out, in_=in_), r, w)

    def MM(out, pairs, r, w, start=True, stop=True):
        pairs = list(pairs)

        def fn(e):
            n = len(pairs)
            ins = None
            for i, (l, rr) in enumerate(pairs):
                ins = e.matmul(out, lhsT=l, rhs=rr, start=(start and i == 0), stop=(stop and i == n - 1))
            return ins
        P.op("pe", fn, r, w)

    def MMS(specs, r, w):
        specs = list(specs)

        def fn(e):
            ins = None
            for (o, l, rr, st, sp) in specs:
                ins = e.matmul(o, lhsT=l, rhs=rr, start=st, stop=sp)
            return ins
        P.op("pe", fn, r, w)

    projb = Rot(PB[0:5])

    def rstd_from(ps, n_feat):
        t1 = FT.next()
        ACT(t1.t[:], ps.t[:], AF.Ln, [ps], [t1], scale=1.0 / n_feat, bias=EPS)
        t2 = RSB.next()
        ACT(t2.t[:], t1.t[:], AF.Exp, [t1], [t2], scale=-0.5)
        return t2

    CASTKW = dict(max_dma_last_dim=4096)

    def load_weights(l):
        i = l // 2
        if l % 2 == 0:
            for c in range(8):
                P.dma("pool", WIN.t[:, c, 0:NE], wE_d[i, :, c, :], WIN, [], [WIN], partial=(c > 0), **CASTKW)
            P.dma("pool", WUQ.t[:], wUQ_d[i], WUQ, [], [WUQ], **CASTKW)
            P.dma("pool", WUK.t[:], wUK_d[i], WUK, [], [WUK], **CASTKW)
            P.dma("pool", WUV.t[:], wUV_d[i], WUV, [], [WUV], **CASTKW)
        else:
            for c in range(8):
                P.dma("pool", WIN.t[:, c, :], wO_d[i, :, c, :], WIN, [], [WIN], partial=(c > 0), **CASTKW)

    def load_wout(l):
        i = l // 2
        src = wOe_d if l % 2 == 0 else wOo_d
        for c in range(8):
            P.dma("pool", WOUT.t[:, c, :], src[i, :, c, :], WOUT, [], [WOUT], partial=(c > 0), **CASTKW)

    P.dma("sp", CST.t[:], cst_d, CST, [], [CST])
    P.dma("sp", CMF.t[:], cmf_d, CMF, [], [CMF])
    P.dma("pool", CMB.t[:], cmb_d, CMB, [], [CMB])
    P.dma("sp", CTF.t[:], cT_d, CTF, [], [CTF])
    ACT(CTB.t[:], CTF.t[:], AF.Silu, [CTF], [CTB])

    load_weights(0)

    xits = [(s, t, hb) for s in range(0 if not SKIP_XPRO else 2, 2) for t in range(NTL) for hb in range(2)]

    def x_load(it):
        s, t, hb = xits[it]
        stg = XT if it % 2 == 0 else Y
        xin = bass.AP(stg.t, 0, [[8 * TT, 128], [1024, 2], [1, 1024]])
        tok0 = t * TT + hb * 256
        src = x_d[s, tok0:tok0 + 256, :].rearrange("(b p) d -> p b d", p=128)
        P.dma("sp", xin, src, stg.p[0], [], [stg])

    if xits:
        x_load(0)
    for it, (s, t, hb) in enumerate(xits):
        if it + 1 < len(xits):
            x_load(it + 1)
        stg = XT if it % 2 == 0 else Y
        tok0 = t * TT + hb * 256
        for c in range(8):
            pb = projb.next()

            def fn(e, c=c, pb=pb, stg=stg):
                ins = None
                for b in range(2):
                    a_in = bass.AP(stg.t, b * 1024 + c * 128, [[8 * TT, 128], [1, 128]])
                    ins = e.transpose(out=pb.t[:, b * 128:(b + 1) * 128], in_=a_in, identity=ident_f)
                return ins
            P.op("pe", fn, [stg, CMF], [pb])
            ft = FT.next()
            CPY(ft.t[:, 0:256], pb.t[:, 0:256], [pb], [ft], eng=("dve" if c % 2 else "act"))
            P.dma("sp", xT.t[s, c, :, tok0:tok0 + 256], ft.t[:, 0:256], ft, [ft], [xT.p[pt(s, t)]], partial=True, nowaw=True)

    def ada_piece_dma(l, pc, dst_ap, sbt, bufs):
        P.dma("pool", dst_ap, wada_d[l, :, :, pc * 512:(pc + 1) * 512], sbt, [], bufs, **CASTKW)

    def ada_piece_mm(l, pc, lw_fn, bufs):
        pb = projb.next()
        specs = []
        for g4 in range(4):
            for kc in range(8):
                specs.append((pb.t[:, 2 * g4:2 * g4 + 2], lw_fn(kc, g4), CTB.t[:, kc, :], kc == 0, kc == 7))
        MMS(specs, list(bufs) + [CTB], [pb])
        for g4 in range(4):
            g = pc * 4 + g4
            TS(MODF.t[:, l, g, :], pb.t[:, 2 * g4:2 * g4 + 2], CST.t[:, l, 16 + g:17 + g], None, ALU.add, ALU.bypass,
               [pb, CST], [MODF])

    def ada_finish(l):
        for b in range(2):
            a1 = AD.t[:, l, b, 0, :]
            TS(a1, MODF.t[:, l, 8:16, b], 1.0, None, ALU.add, ALU.bypass, [MODF], [AD])
            TTO(a1, a1, CST.t[:, l, 0:8], ALU.mult, [AD, CST], [AD])
            CPY(AD.t[:, l, b, 1, :], MODF.t[:, l, 0:8, b], [MODF], [AD])
            TTO(AD.t[:, l, b, 2, :], MODF.t[:, l, 16:24, b], CST.t[:, l, 8:16], ALU.mult, [MODF, CST], [AD])

    if NL > 0:
        for pc in range(6):
            wbuf = STG.next()
            ada_piece_dma(0, pc, bass.AP(wbuf.t, 0, [[4096, 128], [512, 8], [1, 512]]), wbuf, [wbuf])
            ada_piece_mm(0, pc, lambda kc, g4, wbuf=wbuf: bass.AP(wbuf.t, kc * 512 + g4 * 128, [[4096, 128], [1, 128]]), [wbuf])
        ada_finish(0)

    def yhalf_bufs(k):
        return [Y.p[4 * k + j] for j in range(4)]

    def ada_bg_start(l, ti):
        if l >= NL or ti >= 6:
            return
        k = ti % 2
        ada_piece_dma(l, ti, bass.AP(YB, k * 4096, [[YROW, 128], [512, 8], [1, 512]]), Y.p[4 * k], yhalf_bufs(k))

    def ada_bg_end(l, ti):
        if l >= NL or ti >= 6:
            return
        k = ti % 2
        ada_piece_mm(l, ti, lambda kc, g4, k=k: bass.AP(YB, k * 4096 + kc * 512 + g4 * 128, [[YROW, 128], [1, 128]]),
                     yhalf_bufs(k))
        if ti == 5:
            ada_finish(l)

    load_wout(0)

    def tsl(t):
        return slice(t * TT, (t + 1) * TT)

    def load_x_tile(s, t):
        P.dma("sp", XT.t[:], xT.t[s, :, :, tsl(t)].rearrange("c p t -> p c t"), XT, [xT.p[pt(s, t)]], [XT])

    def sumsq_bank(srcs, reads, bank):
        n = len(srcs)
        for i, a in enumerate(srcs):
            sq = SQ.next()
            ACT(sq.t[:], a, AF.Square, reads, [sq])
            MM(bank.t[:], [(ones_b, sq.t[:])], [sq, CMB], [bank], start=(i == 0), stop=(i == n - 1))

    rs_next = {}

    pend_mult = []

    def pop_mult(n=1):
        for _ in range(n):
            if pend_mult:
                pend_mult.pop(0)()

    def norm_sq(s, t):
        load_x_tile(s, t)
        bufs = PT.items + QTBL
        for c in range(8):
            ACT(bufs[c].t[:], XT.t[:, c, :], AF.Square, [XT.p[c]], [bufs[c]])

    def norm_mm(s, t):
        bufs = PT.items + QTBL
        nb = PB[5]
        for c in range(8):
            MM(nb.t[:], [(ones_b, bufs[c].t[:])], [bufs[c], CMB], [nb], start=(c == 0), stop=(c == 7))
        rs = rstd_from(nb, 1024.0)
        for c in range(8):
            pend_mult.append(lambda c=c, rs=rs: TTO(XT.t[:, c, :], XT.t[:, c, :], rs.t[:], ALU.mult, [XT.p[c], rs], [XT.p[c]]))
        rs_next[(s, t)] = rs

    def norm_part(s, t):
        norm_sq(s, t)
        norm_mm(s, t)

    def next_tile(s, t):
        if t + 1 < NTL:
            return (s, t + 1)
        if s == 0:
            return (1, 0)
        return None

    def phase_A_common(l, s, t):
        if (s, t) not in rs_next:
            norm_part(s, t)
        pop_mult(8)
        rs_next.pop((s, t))
        P.dma("sp", TAB.t[:], tab_d[:, :, tsl(t)], TAB, [], [TAB])
        for c in range(8):
            if c % 2 == 0:
                ACT(HT.t[:, c, :], XT.t[:, c, :], AF.Identity, [XT.p[c], AD], [HT.p[c]], scale=AD.t[:, l, s, 0, c:c + 1],
                    bias=AD.t[:, l, s, 1, c:c + 1])
            else:
                TS(HT.t[:, c, :], XT.t[:, c, :], AD.t[:, l, s, 0, c:c + 1], AD.t[:, l, s, 1, c:c + 1], ALU.mult, ALU.add,
                   [XT.p[c], AD], [HT.p[c]])

    def hoist_sq(s, t):
        nt = next_tile(s, t)
        if nt is not None and not ONE_TILE:
            norm_sq(*nt)

    def hoist_mm(s, t):
        nt = next_tile(s, t)
        if nt is not None and not ONE_TILE:
            norm_mm(*nt)

    def proj_fm(col0, M, reads_extra=(), kc=8, w=None, rhs=None):
        w = w or WIN
        pb = projb.next()
        pairs = []
        for c in range(kc):
            r = HT.t[:, c, :] if rhs is None else rhs(c)
            pairs.append((w.t[:, c, col0:col0 + M], r))
        MM(pb.t[0:M, :], pairs, [w, HT] + list(reads_extra), [pb])
        return pb

    def rope(pbx, rows, tabi, perm, out_ap, out_w):
        r0, r1 = rows
        kmax = 128
        xb = XB.next()
        ACT(xb.t[0:kmax, :], pbx.t[0:kmax, :], AF.Identity, [pbx], [xb])
        pb2 = projb.next()
        MM(pb2.t[0:kmax, :], [(perm[0:kmax, 0:kmax], xb.t[0:kmax, :])], [xb, CMB], [pb2])
        f1 = FT.next()
        TTO(f1.t[r0:r1, :], pbx.t[r0:r1, :], TAB.t[r0:r1, tabi, :], ALU.mult, [pbx, TAB], [f1])
        f2 = FT.next()
        TTO(f2.t[r0:r1, :], pb2.t[r0:r1, :], TAB.t[r0:r1, tabi + 1, :], ALU.mult, [pb2, TAB], [f2])
        TTO(out_ap, f1.t[r0:r1, :], f2.t[r0:r1, :], ALU.add, [f1, f2], out_w)

    def store(dst, src, sbt, dparts, partial=True):
        P.dma("pool", dst, src, sbt, [sbt], dparts, partial=partial, nowaw=partial)

    def gate_proj(l, s, t, col0):
        st = STG.next()
        for c in range(8):
            pb = proj_fm(col0 + c * 128, 128)
            ACT(st.t[:, c * TT:(c + 1) * TT], pb.t[:], AF.Silu, [pb], [st.p[c]])
            pop_mult()
        store(GT.t[s, :, :, tsl(t)].rearrange("c p t -> p c t"),
              bass.AP(st.t, 0, [[4096, 128], [TT, 8], [1, TT]]), st, [GT.p[pt(s, t)]])

    def v_proj(s, t, col0, ncol, dst, lhs_src=None, w=None, kc=8):
        w = w or WIN
        st = STG.next()
        for b in range(4):
            pb = projb.next()
            pairs = []
            for c in range(kc):
                lt = HT.t[:, c, b * 128:(b + 1) * 128] if lhs_src is None else lhs_src(c, b)
                pairs.append((lt, w.t[:, c, col0:col0 + ncol]))
            MM(pb.t[:, 0:ncol], pairs, [w, HT, CQN], [pb])
            CPY(st.t[:, b * 512:b * 512 + ncol], pb.t[:, 0:ncol], [pb], [st.p[b]], eng=("dve" if b % 2 else "act"))
        store(dst.t[s, 4 * t:4 * t + 4, :, 0:ncol].rearrange("b p n -> p b n"),
              bass.AP(st.t, 0, [[4096, 128], [512, 4], [1, ncol]]), st, [dst.p[pt(s, t)]])

    def phase_A_even(l, s, t):
        phase_A_common(l, s, t)
        if ASTOP < 2:
            return
        sqb = PT.items + QTBL
        for j in range(5):
            pb = proj_fm((0 if j < 3 else 384 - 3 * 128) + j * 128, 128)
            CPY(CQC.t[:, j, :], pb.t[:], [pb], [CQC.p[j]], eng="act")
            ACT(sqb[j].t[:], pb.t[:], AF.Square, [pb], [sqb[j]])
        for (c0, nch, ncol, nfeat) in ((0, 3, 40, 384.0), (3, 2, 43, 256.0)):
            nb = PB[6]
            for j in range(nch):
                MM(nb.t[:], [(ones_b, sqb[c0 + j].t[:])], [sqb[c0 + j], CMB], [nb], start=(j == 0), stop=(j == nch - 1))
            rs = rstd_from(nb, nfeat)
            for j in range(nch):
                STT(CQN.t[:, c0 + j, :], CQC.t[:, c0 + j, :], CST.t[:, l, ncol + j:ncol + j + 1], rs.t[:],
                    ALU.mult, ALU.mult, [CQC.p[c0 + j], CST, rs], [CQN.p[c0 + j]])
        st = STG.next()
        for j in range(4):
            pb = proj_fm(736 + j * 128, 128)
            rope(pb, (0, 128), 0, perm64, st.t[:, j * TT:(j + 1) * TT], [st.p[j]])
        pb = proj_fm(1248, 128)
        rope(pb, (0, 128), 0, perm64, st.t[:, 4 * TT:5 * TT], [st.p[4]])
        store(QB.t[s, :, :, tsl(t)].rearrange("c p t -> p c t"),
              bass.AP(st.t, 0, [[4096, 128], [TT, 4], [1, TT]]), st, [QB.p[pt(s, t)]])
        store(KB.t[s, 0, :, tsl(t)], st.t[:, 4 * TT:5 * TT], st, [KB.p[pt(s, t)]])
        if ASTOP < 7:
            return
        hoist_sq(s, t)
        v_proj(s, t, 1376, 128, VB)
        if ASTOP < 8:
            return
        gate_proj(l, s, t, 1504)
        hoist_mm(s, t)
        if ASTOP < 3:
            return
        st = STG.next()
        for h in range(8):
            pb = proj_fm(h * 96, 128, w=WUQ, kc=3, rhs=lambda c: CQN.t[:, c, :], reads_extra=[CQN])
            CPY(st.t[0:64, h * TT:(h + 1) * TT], pb.t[0:64, :], [pb], [st.p[h]], eng="act")
            rope(pb, (64, 96), 2, perm32, st.t[64:96, h * TT:(h + 1) * TT], [st.p[h]])
            pop_mult()
        store(QA.t[s, :, 0:96, tsl(t)].rearrange("h p t -> p h t"),
              bass.AP(st.t, 0, [[4096, 96], [TT, 8], [1, TT]]), st, [QA.p[pt(s, t)]])
        if ASTOP < 4:
            return
        st = STG.next()
        for hp in range(4):
            pb = proj_fm(hp * 128, 128, w=WUK, kc=2, rhs=lambda c: CQN.t[:, 3 + c, :], reads_extra=[CQN])
            for e2 in range(2):
                h = 2 * hp + e2
                CPY(st.t[0:64, h * TT:(h + 1) * TT], pb.t[e2 * 64:e2 * 64 + 64, :], [pb], [st.p[h]],
                    eng=("dve" if e2 else "act"))
        pb = proj_fm(640, 128)
        xk = XB.next()
        rope(pb, (64, 96), 2, perm32, xk.t[64:96, :], [xk])
        store(KA.t[s, :, 0:64, tsl(t)].rearrange("h p t -> p h t"),
              bass.AP(st.t, 0, [[4096, 64], [TT, 8], [1, TT]]), st, [KA.p[pt(s, t)]])
        store(KA.t[s, :, 64:96, tsl(t)].rearrange("h p t -> p h t"),
              bass.AP(xk.t, 64 * TT, [[TT, 32], [0, 8], [1, TT]]), xk, [KA.p[pt(s, t)]])
        if ASTOP < 5:
            return
        v_proj(s, t, 0, 512, VA, lhs_src=lambda c, b: CQN.t[:, 3 + c, b * 128:(b + 1) * 128], w=WUV, kc=2)
        if ASTOP < 6:
            return

    def phase_A_odd(l, s, t):
        phase_A_common(l, s, t)
        for (col0, dst) in ((0, QB), (512, KB)):
            st = STG.next()
            for j in range(4):
                pb = proj_fm(col0 + j * 128, 128)
                rope(pb, (0, 128), 0, perm64, st.t[:, j * TT:(j + 1) * TT], [st.p[j]])
            store(dst.t[s, :, :, tsl(t)].rearrange("c p t -> p c t"),
                  bass.AP(st.t, 0, [[4096, 128], [TT, 4], [1, TT]]), st, [dst.p[pt(s, t)]])
        v_proj(s, t, 1024, 512, VB)
        hoist_sq(s, t)
        for (col0, dst) in ((1536, QA), (2048, KA)):
            st = STG.next()
            for hp in range(4):
                pb = proj_fm(col0 + hp * 128, 128)
                for e2 in range(2):
                    h = 2 * hp + e2
                    CPY(st.t[0:64, h * TT:(h + 1) * TT], pb.t[e2 * 64:e2 * 64 + 64, :], [pb], [st.p[h]],
                        eng=("dve" if e2 else "act"))
            store(dst.t[s, :, 0:64, tsl(t)].rearrange("h p t -> p h t"),
                  bass.AP(st.t, 0, [[4096, 64], [TT, 8], [1, TT]]), st, [dst.p[pt(s, t)]])
        v_proj(s, t, 2560, 512, VA)
        pb = projb.next()
        specs = []
        for b in range(4):
            for c in range(8):
                specs.append((pb.t[:, b * 8:(b + 1) * 8], HT.t[:, c, b * 128:(b + 1) * 128], WIN.t[:, c, 3072:3080],
                              c == 0, c == 7))
        MMS(specs, [WIN, HT], [pb])
        f = FT.next()
        for b in range(4):
            TTO(f.t[:, b * 8:(b + 1) * 8], pb.t[:, b * 8:(b + 1) * 8], CST.t[:, l, 54:62], ALU.add, [pb, CST], [f])
        f2 = FT.next()
        ACT(f2.t[:, 0:32], f.t[:, 0:32], AF.Exp, [f], [f2], scale=-1.0)
        ACT(SPT.t[:, 4 * t:4 * t + 4, :].rearrange("p b h -> p (b h)"), f2.t[:, 0:32], AF.Ln, [f2], [SPT], bias=1.0)
        hoist_mm(s, t)
        gate_proj(l, s, t, 3088)

    def fox_post(l, s):
        P.op("dve", lambda e: e.memset(PREF.t[:, 0, :], 0.0), [], [PREF])
        for b in range(1, NB):
            TTO(PREF.t[:, b, :], PREF.t[:, b - 1, :], SPT.t[:, b - 1, :], ALU.add, [PREF, SPT], [PREF])
        pb = projb.next()
        specs = []
        for b in range(NB):
            specs.append((pb.t[:, b * 8:(b + 1) * 8], tri_f, SPT.t[:, b, :], True, False))
            specs.append((pb.t[:, b * 8:(b + 1) * 8], ones_f, PREF.t[:, b, :], False, True))
        MMS(specs, [SPT, PREF, CMF], [pb])
        CPY(CSPB.t[:].rearrange("p b h -> p (b h)"), pb.t[:, 0:NB * 8], [pb], [CSPB])
        P.dma("pool", CSP.t[s], CSPB.t[:], CSPB, [CSPB], [CSP.p[s]])
        for g in range(4):
            pb2 = projb.next()

            def fn(e, g=g, pb2=pb2):
                ins = None
                for bb in range(4):
                    ins = e.transpose(out=pb2.t[0:8, bb * 128:(bb + 1) * 128], in_=CSPB.t[:, g * 4 + bb, :],
                                      identity=ident_f)
                return ins
            P.op("pe", fn, [CSPB, CMF], [pb2])
            ACT(FQ8.t[0:8, g * 512:(g + 1) * 512], pb2.t[0:8, :], AF.Identity, [pb2], [FQ8], scale=-8.0)
        P.dma("pool", QA.t[s, :, 64, :], FQ8.t[0:8, :], FQ8, [FQ8], allp(QA, s), partial=True)

    sbk = Rot(PB[0:3])
    sbk4 = Rot(PB[0:3] + [PB[7]])
    obk4 = Rot(PB[3:7])
    obk2 = Rot(PB[3:5])

    SKEW = 2
    pend = []

    def pipe_flush(keep=0):
        while len(pend) > keep:
            fn = pend.pop(0)
            fn()

    def attn_core(ktf, ktb, krows, kbase, qt, vaf, vab, scale, pO, i, bias=None, swa=False, lbank=None, fin=None, deep=True):
        if swa:
            js = [j for j in range(4 * i - 1, 4 * i + 4) if j >= 0]
        else:
            js = list(range(0, 4 * i + 4))
        for idx, j in enumerate(js):
            m = j - 4 * i
            c0 = 0 if m < 0 else 128 * m
            if swa:
                n = 128 if (m < 0 or m == 3) else 256
            else:
                n = TT - c0
            ps = (sbk4 if deep else sbk).next()
            need_mask = swa or m >= 0
            specs = [(ps.t[:, c0:c0 + n], ktf(j), qt.t[kbase:kbase + krows, c0:c0 + n], True, not need_mask)]
            if swa:
                mk = msw_b[:, 128:256] if m < 0 else msw_b[:, 0:n]
                specs.append((ps.t[:, c0:c0 + n], ident_b, mk, False, True))
            elif m >= 0:
                specs.append((ps.t[:, c0:c0 + 128], ident_b, mc_b, False, True))
            MMS(specs, [qt, CMB] + ktb, [ps])
            p = PT.next()
            kw = dict(scale=scale)
            rd = [ps]
            if bias is not None:
                kw["bias"] = bias(j)
                rd.append(CSPB)
            ACT(p.t[:, c0:c0 + n], ps.t[:, c0:c0 + n], AF.Exp, rd, [p], **kw)
            first = (idx == 0)
            last = (idx == len(js) - 1)

            def pv(p=p, c0=c0, n=n, j=j, first=first, last=last):
                specs = [(pO.t[:, c0:c0 + n], vaf(j), p.t[:, c0:c0 + n], first, last)]
                wr = [pO]
                if lbank is not None:
                    specs.append((lbank.t[:, c0:c0 + n], ones_b, p.t[:, c0:c0 + n], first, last))
                    wr.append(lbank)
                MMS(specs, [p, CMB] + vab, wr)
                if last and fin is not None:
                    fin()
            pend.append(pv)
            pipe_flush(keep=(3 if deep else SKEW))

    def load_va64(src, s, col0, v):
        ap = bass.AP(YB, 4096 + v * 2048 + v * 64, [[YROW, 128], [128, NB], [1, 64]])
        P.dma("sp", ap, src.t[s, :, :, col0:col0 + 64].rearrange("b p n -> p b n"), Y.p[4 + 2 * v],
              allp(src, s), va_bufs(v), partial=True)

    def set_ones():
        for v in range(2):
            ap = bass.AP(YB, 4096 + v * 2048 + (1 - v) * 64, [[YROW, 128], [128, NB], [1, 64]])
            P.op("dve", lambda e, ap=ap: e.memset(ap, 1.0), [], va_bufs(v))

    def zero_q_half(q, half):
        P.op("dve", lambda e, q=q, half=half: e.memset(q.t[half * 64:half * 64 + 64, :], 0.0), [], [q])

    def load_q_half(src, s, ch, half, t, q):
        r0 = half * 64
        P.dma("sp", q.t[r0:r0 + 64, :], src.t[s, ch, r0:r0 + 64, tsl(t)], q, [src.p[pt(s, t)]], [q], partial=True)
        return q

    def finalize64(pO, par, lbias, r_extra, s, t, chunk):
        n0, l0 = par * 64, (1 - par) * 64
        f2 = FT.next()
        if lbias is not None:
            f1 = FT.next()
            ACT(f1.t[n0:n0 + 64, :], pO.t[l0:l0 + 64, :], AF.Ln, [pO] + list(r_extra), [f1], bias=lbias)
            ACT(f2.t[n0:n0 + 64, :], f1.t[n0:n0 + 64, :], AF.Exp, [f1], [f2], scale=-1.0)
        else:
            P.op("dve", lambda e: e.reciprocal(out=f2.t[n0:n0 + 64, :], in_=pO.t[l0:l0 + 64, :]), [pO], [f2])
        xb = XB.next()
        TTO(xb.t[n0:n0 + 64, :], pO.t[n0:n0 + 64, :], f2.t[n0:n0 + 64, :], ALU.mult, [pO, f2], [xb])
        store(OT.t[s, chunk, n0:n0 + 64, tsl(t)], xb.t[n0:n0 + 64, :], xb, [OT.p[pt(s, t)]])

    def phase_B_even(l, s):
        sc_m = 96.0 ** -0.5
        pipe_flush()
        set_ones()
        for h in range(8):
            par = h % 2
            k = par
            P.dma("sp", kt_ap(k, rows=96), KA.t[s, h, 0:96, :], Y.p[2 * k], allp(KA, s), kt_bufs(k))
            load_va64(VA, s, h * 64, par)
            for t in range(NTL):
                q = QTB.next()
                P.dma("sp", q.t[0:96, :], QA.t[s, h, 0:96, tsl(t)], q, [QA.p[pt(s, t)]], [q])
                pO = obk4.next()
                attn_core(lambda j, k=k: kt_ap(k, rows=96, c0=j * 128, n=128), kt_bufs(k), 96, 0, q,
                          lambda j, v=par: va_blk(v, j), va_bufs(par), sc_m, pO, t,
                          fin=lambda pO=pO, par=par, t=t, h=h: finalize64(pO, par, None, [], s, t, h // 2))
        for grp in range(2):
            k = grp
            P.dma("sp", kt_ap(k), KB.t[s, 0, :, :], Y.p[2 * k], allp(KB, s), kt_bufs(k))
            load_va64(VB, s, grp * 64, grp)
            pipe_flush()
            for q in QTBL:
                zero_q_half(q, 1 - grp)
            for j4 in range(4):
                horig = grp * 4 + j4
                for t in range(NTL):
                    q = load_q_half(QB, s, j4, grp, t, QTB.next())
                    pO = obk4.next()
                    attn_core(lambda j, k=k: kt_ap(k, c0=j * 128, n=128), kt_bufs(k),
                              128, 0, q, lambda j, v=grp: va_blk(v, j), va_bufs(grp), 0.125, pO, t, swa=True,
                              fin=lambda pO=pO, grp=grp, horig=horig, t=t, j4=j4: finalize64(
                                  pO, grp, SM.t[grp * 64:grp * 64 + 64, horig:horig + 1], [SM], s, t, 4 + j4))

    def phase_B_odd(l, s):
        i = l // 2
        lam_init = 0.8 - 0.6 * math.exp(-0.3 * l)
        P.dma("sp", CSPB.t[:], CSP.t[s], CSPB, [CSP.p[s]], [CSPB])
        pipe_flush()
        for qi, q in enumerate(QTBL):
            zero_q_half(q, 1 - (qi % 2))
        dcnt = 0
        for h in range(4):
            k = h % 2
            v = h % 2
            P.dma("sp", kt_ap(k), KB.t[s, h, :, :], Y.p[2 * k], allp(KB, s), kt_bufs(k))
            vap = bass.AP(YB, 4096 + v * 2048, [[YROW, 128], [128, NB], [1, 128]])
            P.dma("sp", vap, VB.t[s, :, :, h * 128:(h + 1) * 128].rearrange("b p n -> p b n"), Y.p[4 + 2 * v],
                  allp(VB, s), va_bufs(v))
            for t in range(NTL):
                dcnt += 1
                for mp in range(2):
                    q = load_q_half(QB, s, h, mp, t, QTBL[2 * (dcnt % 2) + mp])
                    pO = obk2.next()
                    pL = PB[5 + mp]

                    def fin(pO=pO, pL=pL, mp=mp, t=t, h=h):
                        f1 = FT.next()
                        ACT(f1.t[:], pL.t[:], AF.Ln, [pL], [f1])
                        f2 = FT.next()
                        ACT(f2.t[:], f1.t[:], AF.Exp, [f1], [f2], scale=-1.0)
                        TTO(CQC.t[:, mp, :], pO.t[:], f2.t[:], ALU.mult, [pO, f2], [CQC])
                        if mp == 0:
                            return
                        od = CQC.t[:, 2, :]
                        STT(od, CQC.t[:, 1, :], SM.t[:, 16 + i:17 + i], CQC.t[:, 0, :], ALU.mult, ALU.add, [CQC, SM], [CQC])
                        nb = PB[7]
                        sumsq_bank([od], [CQC], nb)
                        rs = rstd_from(nb, 128.0)
                        xb = XB.next()
                        STT(xb.t[:], od, CST.t[:, l, 53:54], rs.t[:], ALU.mult, ALU.mult, [CQC, rs, CST], [xb])
                        xb2 = XB.next()
                        TS(xb2.t[:], xb.t[:], 1.0 - lam_init, None, ALU.mult, ALU.bypass, [xb], [xb2])
                        store(OT.t[s, h, :, tsl(t)], xb2.t[:], xb2, [OT.p[pt(s, t)]])
                    attn_core(lambda j, k=k: kt_ap(k, c0=j * 128, n=128), kt_bufs(k),
                              128, 0, q, lambda j, v=v: va_blk(v, j), va_bufs(v), 0.125, pO, t, lbank=pL, fin=fin, deep=False)
        pipe_flush()
        for k in range(2):
            ap = bass.AP(YB, 64 * YROW + k * 2048, [[YROW, 1], [1, 2048]])
            P.op("dve", lambda e, ap=ap: e.memset(ap, 1.0), [], kt_bufs(k))
        set_ones()
        for h in range(8):
            par = h % 2
            k = par
            P.dma("sp", kt_ap(k, rows=64), KA.t[s, h, 0:64, :], Y.p[2 * k], allp(KA, s), kt_bufs(k), partial=True)
            load_va64(VA, s, h * 64, par)
            for t in range(NTL):
                q = QTB.next()
                P.dma("sp", q.t[0:65, :], QA.t[s, h, 0:65, tsl(t)], q, [QA.p[pt(s, t)]], [q])
                pO = obk4.next()
                attn_core(lambda j, k=k: kt_ap(k, rows=65, c0=j * 128, n=128), kt_bufs(k), 65, 0, q,
                          lambda j, v=par: va_blk(v, j), va_bufs(par), 0.125, pO, t,
                          bias=lambda j, h=h: CSPB.t[:, j, h:h + 1],
                          fin=lambda pO=pO, par=par, t=t, h=h: finalize64(pO, par, None, [], s, t, 4 + h // 2))

    c_st = {}

    def phase_C1_load(l, s, t):
        st_o = STG.next()
        P.dma("sp", bass.AP(st_o.t, 0, [[4096, 128], [TT, 8], [1, TT]]),
              OT.t[s, :, :, tsl(t)].rearrange("c p t -> p c t"), st_o, [OT.p[pt(s, t)]], [st_o])
        st_g = STG.next()
        P.dma("sp", bass.AP(st_g.t, 0, [[4096, 128], [TT, 8], [1, TT]]),
              GT.t[s, :, :, tsl(t)].rearrange("c p t -> p c t"), st_g, [GT.p[pt(s, t)]], [st_g])
        c_st[(s, t)] = (st_o, st_g)

    def phase_C1_og(l, s, t):
        st_o, st_g = c_st.pop((s, t))
        for c in range(8):
            TTO(HT.t[:, c, :], st_o.t[:, c * TT:(c + 1) * TT], st_g.t[:, c * TT:(c + 1) * TT], ALU.mult,
                [st_o.p[c], st_g.p[c]], [HT.p[c]])

    def phase_C2(l, s, t):
        load_x_tile(s, t)
        nb = PB[6]
        sqb = PT.items + QTBL
        for g in range(8):
            pb = proj_fm(g * 128, 128, w=WOUT)
            CPY(Y.t[:, g, :], pb.t[:], [pb], [Y.p[g]], eng="act")
            ACT(sqb[g].t[:], pb.t[:], AF.Square, [pb], [sqb[g]])
            if g >= 2:
                MM(nb.t[:], [(ones_b, sqb[g - 2].t[:])], [sqb[g - 2], CMB], [nb], start=(g == 2), stop=False)
        for g in (6, 7):
            MM(nb.t[:], [(ones_b, sqb[g].t[:])], [sqb[g], CMB], [nb], start=False, stop=(g == 7))
        return rstd_from(nb, 1024.0)

    def phase_C3(l, s, t, rs):
        for c in range(8):
            f = FT.next()
            TTO(f.t[:], Y.t[:, c, :], rs.t[:], ALU.mult, [Y.p[c], rs], [f])
            STT(XT.t[:, c, :], f.t[:], AD.t[:, l, s, 2, c:c + 1], XT.t[:, c, :], ALU.mult, ALU.add, [f, AD, XT.p[c]], [XT.p[c]])
        if l == NL - 1:
            out_tile(s, t)
        else:
            P.dma("pool", xT.t[s, :, :, tsl(t)].rearrange("c p t -> p c t"), XT.t[:], XT, [XT], [xT.p[pt(s, t)]])

    def out_tile(s, t):
        for b in range(4):
            for half in range(2):
                pb = projb.next()

                def fn(e, b=b, half=half, pb=pb):
                    ins = None
                    for cc in range(4):
                        c = half * 4 + cc
                        ins = e.transpose(out=pb.t[:, cc * 128:(cc + 1) * 128],
                                          in_=XT.t[:, c, b * 128:(b + 1) * 128], identity=ident_f)
                    return ins
                P.op("pe", fn, [XT, CMF], [pb])
                ft = FT.next()
                CPY(ft.t[:], pb.t[:], [pb], [ft], eng=("dve" if half else "act"))
                tok0 = t * TT + b * 128
                P.dma("pool", out_d[s, tok0:tok0 + 128, half * 512:(half + 1) * 512], ft.t[:], ft, [ft], [])

    def phase_C_all(l):
        tiles = [(s, t) for s in range(2) for t in range(NTL)]
        phase_C1_load(l, *tiles[0])
        phase_C1_og(l, *tiles[0])
        for idx, (s, t) in enumerate(tiles):
            if idx + 1 < len(tiles):
                phase_C1_load(l, *tiles[idx + 1])
            rs = phase_C2(l, s, t)
            if idx + 1 < len(tiles):
                phase_C1_og(l, *tiles[idx + 1])
            phase_C3(l, s, t, rs)

    for l in range(NL):
        i = l // 2
        even = (l % 2 == 0)
        if even:
            ACT(SM.t[:, 0:8], CST.t[:, l, 45:53], AF.Exp, [CST], [SM])
        else:
            lam_init = 0.8 - 0.6 * math.exp(-0.3 * l)
            f = FT.next()
            TTO(f.t[:, 0:64], CST.t[:, l, 62:126], CST.t[:, l, 126:190], ALU.mult, [CST], [f])
            TTO(f.t[:, 64:128], CST.t[:, l, 190:254], CST.t[:, l, 254:318], ALU.mult, [CST], [f])
            P.op("dve", lambda e, f=f: e.reduce_sum(out=SM.t[:, 20:22],
                                                 in_=f.t[:, 0:128].rearrange("p (a b) -> p a b", a=2),
                                                 axis=mybir.AxisListType.X), [f], [SM])
            ACT(SM.t[:, 22:24], SM.t[:, 20:22], AF.Exp, [SM], [SM])
            TTO(SM.t[:, 24:25], SM.t[:, 23:24], SM.t[:, 22:23], ALU.subtract, [SM], [SM])
            TS(SM.t[:, 16 + i:17 + i], SM.t[:, 24:25], -lam_init, None, ALU.add, ALU.bypass, [SM], [SM])
        if STOP < 2:
            continue
        for s in range(2):
            for t in range(NTL):
                if ONE_TILE and (s, t) != (0, 0):
                    continue
                ada_bg_start(l + 1, s * NTL + t)
                (phase_A_even if even else phase_A_odd)(l, s, t)
                ada_bg_end(l + 1, s * NTL + t)
            if not even and not ONE_TILE:
                fox_post(l, s)
        if l + 1 < NL:
            load_weights(l + 1)
        if STOP < 3:
            continue
        for s in range(2):
            (phase_B_even if even else phase_B_odd)(l, s)
        pipe_flush()
        if STOP < 4:
            continue
        phase_C_all(l)
        if l + 1 < NL:
            load_wout(l + 1)

    if NL == 0 or STOP < 4:
        for s in range(0 if not SKIP_XPRO else 2, 2):
            for t in range(NTL):
                load_x_tile(s, t)
                out_tile(s, t)
    P.emit()
    return nc


def _kc(w):
    K, N = w.shape
    return np.ascontiguousarray(w.reshape(K // 128, 128, N).transpose(1, 0, 2))


def _o_perm_even():
    perm = np.zeros(1024, np.int64)
    for f in range(1024):
        cf, r = divmod(f, 128)
        if cf < 4:
            perm[f] = f
        else:
            j = cf - 4
            head = j if r < 64 else 4 + j
            perm[f] = 512 + head * 64 + (r % 64)
    return perm


def _host_consts():
    pos = np.arange(S, dtype=np.float32)
    tab = np.zeros((128, 4, S), np.float32)
    inv64 = (1.0 / (np.float32(10000.0) ** (np.arange(0, 64, 2, dtype=np.float32) / np.float32(64)))).astype(np.float32)
    inv32 = (1.0 / (np.float32(10000.0) ** (np.arange(0, 32, 2, dtype=np.float32) / np.float32(32)))).astype(np.float32)
    for r in range(128):
        i = r % 64
        ang = (pos * inv64[i % 32]).astype(np.float32)
        tab[r, 0] = np.cos(ang)
        tab[r, 1] = -np.sin(ang) if i < 32 else np.sin(ang)
    for r in range(64, 96):
        j = r - 64
        ang = (pos * inv32[j % 16]).astype(np.float32)
        tab[r, 2] = np.cos(ang)
        tab[r, 3] = -np.sin(ang) if j < 16 else np.sin(ang)
    cmf = np.zeros((128, 384), np.float32)
    cmf[:, 0:128] = np.eye(128, dtype=np.float32)
    k = np.arange(128)
    cmf[:, 128:256] = (k[:, None] <= k[None, :]).astype(np.float32)
    cmf[:, 256:384] = 1.0
    cmb = np.zeros((128, 896), np.float32)
    cmb[:, 0:128] = np.eye(128, dtype=np.float32)
    for m in range(128):
        sw = m + 32 if (m % 64) < 32 else m - 32
        cmb[sw, 128 + m] = 1.0
    for m in range(128):
        if 64 <= m < 96:
            sw = m + 16 if (m - 64) < 16 else m - 16
        else:
            sw = m
        cmb[sw, 256 + m] = 1.0
    cmb[:, 384:512] = np.where(k[:, None] > k[None, :], NEG, 0.0)
    cmb[:, 512:640] = np.where(k[:, None] > k[None, :], NEG, 0.0)
    cmb[:, 640:768] = np.where(k[:, None] <= k[None, :], NEG, 0.0)
    cmb[:, 768:896] = 1.0
    return tab, cmf, cmb


def _prep_shared(inp):
    f = lambda a: np.asarray(a, dtype=np.float32)
    sh = {}
    w_ada = f(inp["w_ada"])
    sh["wada"] = np.ascontiguousarray(w_ada.reshape(4, 8, 128, 3072).transpose(0, 2, 1, 3))
    perm_e = _o_perm_even()
    wE, wUQ, wUK, wUV, wOe = [], [], [], [], []
    for i in range(2):
        w = f(inp["ev_w_in"][i])
        cols = [w[:, 0:384], w[:, 384:640], np.zeros((1024, 64), np.float32), w[:, 640:672]]
        sq = w[:, 672:1184].reshape(1024, 8, 64)
        order = []
        for j in range(4):
            order += [j, 4 + j]
        cols.append(sq[:, order, :].reshape(1024, 512))
        cols.append(w[:, 1184:1312])
        cols.append(w[:, 1312:1440])
        cols.append(w[:, 1440:2464][:, perm_e])
        wE.append(_kc(np.concatenate(cols, axis=1)))
        wUQ.append(_kc(np.concatenate([f(inp["ev_w_uq"][i]), np.zeros((384, 32), np.float32)], axis=1)))
        ukv = f(inp["ev_w_ukv"][i]).reshape(256, 8, 128)
        wUK.append(_kc(np.ascontiguousarray(ukv[:, :, 0:64]).reshape(256, 512)))
        wUV.append(_kc(np.ascontiguousarray(ukv[:, :, 64:128]).reshape(256, 512)))
        wOe.append(_kc(f(inp["ev_w_out"][i])[perm_e, :]))
    sh["wE"] = np.stack(wE)
    sh["wUQ"] = np.stack(wUQ)
    sh["wUK"] = np.stack(wUK)
    sh["wUV"] = np.stack(wUV)
    sh["wOe"] = np.stack(wOe)
    wO, wOo = [], []
    for i in range(2):
        w = f(inp["od_w_in"][i])
        cols = [w[:, 0:3080], np.zeros((1024, 8), np.float32), w[:, 3080:4104]]
        wO.append(_kc(np.concatenate(cols, axis=1)))
        wOo.append(_kc(f(inp["od_w_out"][i])))
    sh["wO"] = np.stack(wO)
    sh["wOo"] = np.stack(wOo)
    cst = np.zeros((128, 4, NC_CST), np.float32)
    for l in range(4):
        i = l // 2
        cst[:, l, 0:8] = f(inp["g_pre"][l]).reshape(8, 128).T
        cst[:, l, 8:16] = f(inp["g_post"][l]).reshape(8, 128).T
        cst[:, l, 16:40] = f(inp["b_ada"][l]).reshape(24, 128).T
        if l % 2 == 0:
            cst[:, l, 40:43] = f(inp["ev_q_norm"][i]).reshape(3, 128).T
            cst[:, l, 43:45] = f(inp["ev_kv_norm"][i]).reshape(2, 128).T
            cst[:, l, 45:53] = f(inp["ev_sinks"][i])[None, :]
        else:
            cst[:, l, 53] = f(inp["od_subln"][i])
            cst[:, l, 54:62] = f(inp["od_forget_bias"][i])[None, :]
            cst[:, l, 62:318] = f(inp["od_lambda"][i]).reshape(256)[None, :]
    sh["cst"] = cst
    tab, cmf, cmb = _host_consts()
    sh["tab"] = tab
    sh["cmf"] = cmf
    sh["cmb"] = cmb
    return sh


_NC_CACHE = {}


def run(inputs, NL=4, cores=8):
    x = np.asarray(inputs["x"], dtype=np.float32)
    c = np.asarray(inputs["c"], dtype=np.float32)
    sh = _prep_shared(inputs)
    in_maps = []
    for core in range(cores):
        m = dict(sh)
        m["x"] = np.ascontiguousarray(x[2 * core:2 * core + 2])
        cc = c[2 * core:2 * core + 2]
        m["cT"] = np.ascontiguousarray(cc.T.reshape(8, 128, 2).transpose(1, 0, 2))
        in_maps.append(m)
    if NL not in _NC_CACHE:
        _NC_CACHE[NL] = build(NL)
    res = run_bass_kernel_spmd(_NC_CACHE[NL], in_maps, core_ids=list(range(cores)))
    if DEBUG:
        return res.results
    return np.concatenate([r["out"] for r in res.results], axis=0).astype(np.float32)


def kernel(**inputs):
    return run(inputs, NL=4, cores=8)
```
